# Optimizing a Trainium2 kernel written in Bass

```python
import math
import jax, jax.numpy as jnp
from jax import lax
import numpy as np

D_MODEL = 2048
BATCH = 1
SEQ = 16384
DEPTH = 2

N_MEM = 256
HG_HEADS = 16
HG_KDIM = 128
HG_VDIM = 128
D_HG = HG_HEADS * HG_KDIM
D_HG_V = HG_HEADS * HG_VDIM
M2_HEADS = 32
M2_HEADDIM = 64
D_M2 = M2_HEADS * M2_HEADDIM
M2_GROUPS = 8
M2_DSTATE = 128
M2_CONV = 4
D_XBC = D_M2 + 2 * M2_GROUPS * M2_DSTATE
CHUNK = 64
XA_HEADS = 4
XA_HEADDIM = D_MODEL // XA_HEADS
D_FF = 5632
FFN_CONV = 3
NORM_EPS = 1e-6
M2_NORM_EPS = 1e-5
LB_FLOOR = 1e-30
IN_SPLITS = (D_HG, D_HG, D_HG_V, D_HG_V, D_M2, D_XBC, M2_HEADS, D_MODEL, D_MODEL)
N_IN = D_HG + D_HG + D_HG_V + D_HG_V + D_M2 + D_XBC + M2_HEADS + D_MODEL + D_MODEL

kernel_name = "hybrid_hgrn2_mamba2_gated_xattn_convglu"


def rms_norm(x, g, eps=NORM_EPS):
    xf = x.astype(jnp.float32)
    y = xf * lax.rsqrt(jnp.mean(xf * xf, axis=-1, keepdims=True) + eps)
    return (y * g.astype(jnp.float32)).astype(x.dtype)


def causal_dwconv(x, w, b):
    k_width, ch = w.shape
    y = lax.conv_general_dilated(x, w[:, None, :].astype(x.dtype), window_strides=(1,),
                                 padding=[(k_width - 1, 0)],
                                 dimension_numbers=("NWC", "WIO", "NWC"),
                                 feature_group_count=ch)
    return y + b.astype(x.dtype)


def to_chunks(t):
    b, s = t.shape[:2]
    return t.reshape((b, s // CHUNK, CHUNK) + t.shape[2:]).swapaxes(0, 1)


def from_chunks(t):
    t = t.swapaxes(0, 1)
    return t.reshape((t.shape[0], t.shape[1] * t.shape[2]) + t.shape[3:])


def masked_decay(diff):
    mask = jnp.tril(jnp.ones((CHUNK, CHUNK), dtype=bool))
    mask = mask.reshape((1, CHUNK, CHUNK) + (1,) * (diff.ndim - 3))
    return jnp.where(mask, jnp.exp(jnp.where(mask, diff, 0.0)), 0.0)


def gla_chunk_scan(q, k, v, log_f):
    bsz, _, h, kd = q.shape
    vd = v.shape[-1]

    def step(state, inp):
        qc, kc, vc, gc = inp
        b = jnp.cumsum(gc, axis=1)
        o_inter = jnp.einsum('bthk,bhkv->bthv', qc * jnp.exp(b), state)
        decay = masked_decay(b[:, :, None] - b[:, None, :])
        scores = jnp.einsum('bthk,btshk,bshk->bhts', qc, decay, kc)
        o_intra = jnp.einsum('bhts,bshv->bthv', scores, vc)
        b_last = b[:, -1]
        state = (jnp.exp(b_last)[..., None] * state
                 + jnp.einsum('bshk,bshv->bhkv', kc * jnp.exp(b_last[:, None] - b), vc))
        return state, o_inter + o_intra

    init = jnp.zeros((bsz, h, kd, vd), jnp.float32)
    _, o = lax.scan(step, init, (to_chunks(q), to_chunks(k), to_chunks(v), to_chunks(log_f)))
    return from_chunks(o)


def ssd_chunk_scan(xdt, a, bm, cm):
    bsz, _, g, hg, p = xdt.shape
    n = bm.shape[-1]

    def step(state, inp):
        xc, ac, bc, cc = inp
        cum = jnp.cumsum(ac, axis=1)
        lmat = masked_decay(cum[:, :, None] - cum[:, None, :])
        cb = jnp.einsum('btgn,bsgn->btsg', cc, bc)
        y_intra = jnp.einsum('btsg,btsgh,bsghp->btghp', cb, lmat, xc)
        y_inter = jnp.einsum('btgn,bghpn->btghp', cc, state) * jnp.exp(cum)[..., None]
        last = cum[:, -1]
        state = (jnp.exp(last)[..., None, None] * state
                 + jnp.einsum('bsgn,bsgh,bsghp->bghpn', bc, jnp.exp(last[:, None] - cum), xc))
        return state, y_intra + y_inter

    init = jnp.zeros((bsz, g, hg, p, n), jnp.float32)
    _, y = lax.scan(step, init, (to_chunks(xdt), to_chunks(a), to_chunks(bm), to_chunks(cm)))
    return from_chunks(y)


def hgrn2_branch(q_raw, f_raw, i_raw, og_raw, lb, norm_g):
    bsz, s, _ = q_raw.shape
    f32 = jnp.float32
    q = q_raw.astype(f32).reshape(bsz, s, HG_HEADS, HG_KDIM) * (HG_KDIM ** -0.5)
    z = f_raw.astype(f32).reshape(bsz, s, HG_HEADS, HG_KDIM)
    lb = lb.astype(f32).reshape(HG_HEADS, HG_KDIM)
    log_f = jnp.logaddexp(jax.nn.log_sigmoid(z),
                          jnp.log(jnp.maximum(lb, LB_FLOOR)) + jax.nn.log_sigmoid(-z))
    k = (1.0 - lb) * jax.nn.sigmoid(-z)
    v = i_raw.astype(f32).reshape(bsz, s, HG_HEADS, HG_VDIM)
    o = gla_chunk_scan(q, k, v, log_f)
    o = o * jax.nn.sigmoid(og_raw.astype(f32).reshape(bsz, s, HG_HEADS, HG_VDIM))
    o = o * lax.rsqrt(jnp.mean(o * o, axis=-1, keepdims=True) + NORM_EPS)
    o = o * norm_g.astype(f32).reshape(HG_HEADS, HG_VDIM)
    return o.reshape(bsz, s, D_HG_V).astype(q_raw.dtype)


def mamba2_branch(z, xbc_raw, dt_raw, conv_w, conv_b, dt_bias, a_log, d_skip, norm_g):
    bsz, s, _ = z.shape
    f32 = jnp.float32
    xbc = jax.nn.silu(causal_dwconv(xbc_raw, conv_w, conv_b))
    xs, bm, cm = jnp.split(xbc, [D_M2, D_M2 + M2_GROUPS * M2_DSTATE], axis=-1)
    hpg = M2_HEADS // M2_GROUPS
    xs = xs.astype(f32).reshape(bsz, s, M2_GROUPS, hpg, M2_HEADDIM)
    bm = bm.astype(f32).reshape(bsz, s, M2_GROUPS, M2_DSTATE)
    cm = cm.astype(f32).reshape(bsz, s, M2_GROUPS, M2_DSTATE)
    dt = jax.nn.softplus(dt_raw.astype(f32) + dt_bias.astype(f32))
    dt = dt.reshape(bsz, s, M2_GROUPS, hpg)
    a_neg = -jnp.exp(a_log.astype(f32)).reshape(M2_GROUPS, hpg)
    y = ssd_chunk_scan(xs * dt[..., None], dt * a_neg, bm, cm)
    y = y + d_skip.astype(f32).reshape(M2_GROUPS, hpg)[:, :, None] * xs
    y = y.reshape(bsz, s, D_M2) * jax.nn.silu(z.astype(f32))
    yg = y.reshape(bsz, s, M2_GROUPS, D_M2 // M2_GROUPS)
    yg = yg * lax.rsqrt(jnp.mean(yg * yg, axis=-1, keepdims=True) + M2_NORM_EPS)
    y = yg.reshape(bsz, s, D_M2) * norm_g.astype(f32)
    return y.astype(z.dtype)


def token_mixer(u, w_in, lb, hg_norm_g, m2_conv_w, m2_conv_b, m2_dt_bias, m2_a_log, m2_d,
                m2_norm_g, w_branch_hg, w_branch_m2, w_out):
    proj = u @ w_in
    idx = [int(i) for i in np.cumsum(IN_SPLITS)[:-1]]
    hq, hf, hi, hog, mz, mxbc, mdt, g_hg, g_m2 = jnp.split(proj, idx, axis=-1)
    y_hg = hgrn2_branch(hq, hf, hi, hog, lb, hg_norm_g) @ w_branch_hg
    y_m2 = mamba2_branch(mz, mxbc, mdt, m2_conv_w, m2_conv_b, m2_dt_bias, m2_a_log, m2_d,
                         m2_norm_g) @ w_branch_m2
    merged = jax.nn.sigmoid(g_hg) * y_hg + jax.nn.sigmoid(g_m2) * y_m2
    return merged @ w_out


def memory_cross_attention(u, memn, wq, wkv, wo):
    bsz, s, _ = u.shape
    q = (u @ wq).reshape(bsz, s, XA_HEADS, XA_HEADDIM)
    k, v = jnp.split(memn @ wkv, 2, axis=-1)
    k = k.reshape(bsz, -1, XA_HEADS, XA_HEADDIM)
    v = v.reshape(bsz, -1, XA_HEADS, XA_HEADDIM)
    scores = jnp.einsum('bshd,bmhd->bhsm', q, k).astype(jnp.float32) * (XA_HEADDIM ** -0.5)
    probs = jax.nn.softmax(scores, axis=-1).astype(v.dtype)
    o = jnp.einsum('bhsm,bmhd->bshd', probs, v).reshape(bsz, s, D_MODEL)
    return o @ wo


def conv_glu_ffn(u, w_up, conv_w, conv_b, w_down):
    gate, up = jnp.split(u @ w_up, 2, axis=-1)
    gate = causal_dwconv(gate, conv_w, conv_b)
    return (jax.nn.gelu(gate, approximate=False) * up) @ w_down


def setup_inputs(seed: int = 0) -> dict:
    key = jax.random.key(seed)
    ks = jax.random.split(key, 32)
    L = DEPTH
    f32 = jnp.float32

    def nrm(k, shape, scale):
        return jax.random.normal(k, shape, f32) * scale

    dt0 = jnp.exp(jax.random.uniform(ks[8], (L, M2_HEADS), f32) * (math.log(0.1) - math.log(0.001))
                  + math.log(0.001))
    return {
        "x": nrm(ks[0], (BATCH, SEQ, D_MODEL), 1.0),
        "mem": nrm(ks[1], (BATCH, N_MEM, D_MODEL), 1.0),
        "mix_norm_g": 1.0 + nrm(ks[2], (L, D_MODEL), 0.02),
        "w_in": nrm(ks[3], (L, D_MODEL, N_IN), D_MODEL ** -0.5),
        "hg_lb_logits": 1.0 + nrm(ks[4], (L, D_HG), 0.3),
        "hg_norm_g": 1.0 + nrm(ks[5], (L, D_HG_V), 0.02),
        "m2_conv_w": nrm(ks[6], (L, M2_CONV, D_XBC), M2_CONV ** -0.5),
        "m2_conv_b": nrm(ks[7], (L, D_XBC), 0.02),
        "m2_dt_bias": dt0 + jnp.log(-jnp.expm1(-dt0)),
        "m2_A_log": jnp.log(jax.random.uniform(ks[9], (L, M2_HEADS), f32, 1.0, 16.0)),
        "m2_D": 1.0 + nrm(ks[10], (L, M2_HEADS), 0.1),
        "m2_norm_g": 1.0 + nrm(ks[11], (L, D_M2), 0.02),
        "w_branch_hg": nrm(ks[12], (L, D_HG_V, D_MODEL), D_HG_V ** -0.5),
        "w_branch_m2": nrm(ks[13], (L, D_M2, D_MODEL), D_M2 ** -0.5),
        "w_out": nrm(ks[14], (L, D_MODEL, D_MODEL), D_MODEL ** -0.5),
        "mem_norm_g": 1.0 + nrm(ks[15], (D_MODEL,), 0.02),
        "xa_norm_g": 1.0 + nrm(ks[16], (L, D_MODEL), 0.02),
        "xa_wq": nrm(ks[17], (L, D_MODEL, D_MODEL), D_MODEL ** -0.5),
        "xa_wkv": nrm(ks[18], (L, D_MODEL, 2 * D_MODEL), D_MODEL ** -0.5),
        "xa_wo": nrm(ks[19], (L, D_MODEL, D_MODEL), D_MODEL ** -0.5),
        "ffn_norm_g": 1.0 + nrm(ks[20], (L, D_MODEL), 0.02),
        "ffn_w_up": nrm(ks[21], (L, D_MODEL, 2 * D_FF), D_MODEL ** -0.5),
        "ffn_conv_w": nrm(ks[22], (L, FFN_CONV, D_FF), FFN_CONV ** -0.5),
        "ffn_conv_b": nrm(ks[23], (L, D_FF), 0.02),
        "ffn_w_down": nrm(ks[24], (L, D_FF, D_MODEL), D_FF ** -0.5),
        "final_norm_g": 1.0 + nrm(ks[25], (D_MODEL,), 0.02),
    }


def reference(x, mem, mix_norm_g, w_in, hg_lb_logits, hg_norm_g, m2_conv_w, m2_conv_b,
              m2_dt_bias, m2_A_log, m2_D, m2_norm_g, w_branch_hg, w_branch_m2, w_out,
              mem_norm_g, xa_norm_g, xa_wq, xa_wkv, xa_wo, ffn_norm_g, ffn_w_up, ffn_conv_w,
              ffn_conv_b, ffn_w_down, final_norm_g):
    p = jax.nn.softmax(hg_lb_logits.astype(jnp.float32), axis=0)
    lower_bounds = jnp.cumsum(p, axis=0) - p[0]
    memn = rms_norm(mem, mem_norm_g)
    h = x
    for l in range(DEPTH):
        u = rms_norm(h, mix_norm_g[l])
        h = h + token_mixer(u, w_in[l], lower_bounds[l], hg_norm_g[l], m2_conv_w[l],
                            m2_conv_b[l], m2_dt_bias[l], m2_A_log[l], m2_D[l], m2_norm_g[l],
                            w_branch_hg[l], w_branch_m2[l], w_out[l])
        u = rms_norm(h, xa_norm_g[l])
        h = h + memory_cross_attention(u, memn, xa_wq[l], xa_wkv[l], xa_wo[l])
        u = rms_norm(h, ffn_norm_g[l])
        h = h + conv_glu_ffn(u, ffn_w_up[l], ffn_conv_w[l], ffn_conv_b[l], ffn_w_down[l])
    return rms_norm(h, final_norm_g)
```

```python
import contextlib
import numpy as np
import concourse.bass as bass
import concourse.mybir as mybir
from concourse.bass_utils import run_bass_kernel_spmd

F32 = mybir.dt.float32
BF16 = mybir.dt.bfloat16
AF = mybir.ActivationFunctionType
ALU = mybir.AluOpType
AX = mybir.AxisListType

D = 2048
KC = D // 128
DEPTH = 2
N_MEM = 256
HG_H = 16
M2_H = 32
M2_G = 8
D_FF = 5632
N_IN = 18464
C_Q, C_F, C_I, C_OG = 0, 2048, 4096, 6144
C_Z = 8192
C_X = 10240
C_B = C_X + 2048
C_C = C_B + 1024
C_DT = 14336
C_GHG = 14368
C_GM2 = 16416
EPS = 1e-6
M2_EPS = 1e-5

ENGS = ["pe", "act", "dve", "pool", "sp"]
DMA_WINDOW = 8


class Res:
    __slots__ = ("name", "last_write", "readers")

    def __init__(self, name=""):
        self.name = name
        self.last_write = None
        self.readers = []


class Op:
    __slots__ = ("eng", "fn", "deps", "signaled", "is_dma", "dma_idx", "tok")

    def __init__(self, eng, fn, is_dma=False):
        self.eng = eng
        self.fn = fn
        self.deps = []
        self.signaled = False
        self.is_dma = is_dma
        self.dma_idx = -1
        self.tok = None


class T:
    def __init__(self, t, res):
        self.t = t
        self.r = res

    def __getitem__(self, k):
        return self.t[k]


class Prog:
    def __init__(self, nc):
        self.nc = nc
        self.ops = {e: [] for e in ENGS}
        self.dmas = {e: [] for e in ENGS}
        self.stack = contextlib.ExitStack()
        self.scopes = [self.stack]
        self.n = 0
        self.pending = {e: [] for e in ENGS}

    def sb(self, shape, dt, name=None):
        self.n += 1
        name = (name or "sb") + f"_{self.n}"
        t = self.scopes[-1].enter_context(self.nc.sbuf_tensor(name, list(shape), dt))
        return T(t, Res(name))

    @contextlib.contextmanager
    def scope(self):
        st = contextlib.ExitStack()
        self.scopes.append(st)
        try:
            yield
        finally:
            self.scopes.pop()
            self.barrier()
            st.close()

    def ps(self, shape, dt, name=None):
        self.n += 1
        name = (name or "ps") + f"_{self.n}"
        t = self.stack.enter_context(self.nc.psum_tensor(name, list(shape), dt))
        return T(t, Res(name))

    def dram(self, shape, dt, name=None):
        self.n += 1
        name = (name or "dr") + f"_{self.n}"
        t = self.nc.dram_tensor(name, list(shape), dt)
        return T(t.ap(), Res(name))

    def _record(self, o, reads, writes):
        deps = []
        seen = set()

        def add(d):
            if d is None or d is o or id(d) in seen:
                return
            if d.eng == "pe" and o.eng == "pe" and not d.is_dma:
                return
            seen.add(id(d))
            deps.append(d)

        for r in reads:
            add(r.r.last_write)
        for w in writes:
            add(w.r.last_write)
            for rd in w.r.readers:
                add(rd)
        for d in self.pending[o.eng]:
            add(d)
        self.pending[o.eng] = []
        if o.is_dma:
            q = self.dmas[o.eng]
            o.dma_idx = len(q)
            if o.dma_idx >= DMA_WINDOW:
                add(q[o.dma_idx - DMA_WINDOW])
            q.append(o)
        for d in deps:
            d.signaled = True
        o.deps = deps
        for r in reads:
            r.r.readers.append(o)
        for w in writes:
            w.r.last_write = o
            w.r.readers = []
        self.ops[o.eng].append(o)
        return o

    def op(self, eng, fn, reads=(), writes=()):
        return self._record(Op(eng, fn), reads, writes)

    def dma(self, eng, fn, reads=(), writes=()):
        return self._record(Op(eng, fn, is_dma=True), reads, writes)

    def barrier(self):
        tails = []
        for e in ENGS:
            if self.ops[e]:
                tails.append(self.ops[e][-1])
            tails.extend(self.dmas[e][-DMA_WINDOW:])
        for e in ENGS:
            self.pending[e] = list(tails)

    def emit(self, final_deps):
        nc = self.nc
        st = self.stack
        sems = {e: st.enter_context(nc.semaphore(f"s_{e}")) for e in ENGS}
        dsem = {
            e: [st.enter_context(nc.semaphore(f"d_{e}{i}")) for i in range(DMA_WINDOW)]
            for e in ENGS
            if self.dmas[e]
        }
        fin = Op("sp", None)
        fin.deps = list(final_deps)
        for d in fin.deps:
            d.signaled = True
        self.ops["sp"].append(fin)
        for e in ENGS:
            c = 0
            for o in self.ops[e]:
                if o.is_dma:
                    o.tok = (dsem[e][o.dma_idx % DMA_WINDOW], 16 * (o.dma_idx // DMA_WINDOW + 1))
                elif o.signaled:
                    c += 1
                    o.tok = (sems[e], c)
        engobj = {"pe": "tensor", "act": "scalar", "dve": "vector", "pool": "gpsimd", "sp": "sync"}
        stats = {}
        with nc.Block() as block:
            for e in ENGS:
                ops = self.ops[e]
                if not ops:
                    continue
                nwait = [0]

                def body(eng, ops=ops, nwait=nwait):
                    waited = {}
                    for o in ops:
                        need = {}
                        for d in o.deps:
                            s, v = d.tok
                            k = id(s)
                            if waited.get(k, 0) < v and need.get(k, (None, 0))[1] < v:
                                need[k] = (s, v)
                        for k, (s, v) in need.items():
                            eng.wait_ge(s, v)
                            waited[k] = v
                            nwait[0] += 1
                        if o.fn is None:
                            continue
                        ins = o.fn(eng)
                        if o.is_dma:
                            ins.then_inc(o.tok[0], 16)
                        elif o.signaled:
                            ins.then_inc(o.tok[0], 1)

                getattr(block, engobj[e])(body)
                stats[e] = (len(ops), nwait[0])
        return stats


class Ring:
    def __init__(self, items):
        self.items = items
        self.i = 0

    def next(self):
        x = self.items[self.i % len(self.items)]
        self.i += 1
        return x


class K:
    def __init__(self, nc, NT):
        self.nc = nc
        self.P = Prog(nc)
        self.NT = NT
        self.TT = NT * 128
        self.segs = []
        t = 0
        while t < NT:
            n = min(6, NT - t)
            self.segs.append((t, n))
            t += n
        self.consts()

    def act(self, out, in_, func, reads, writes, **kw):
        return self.P.op("act", lambda e: e.activation(out=out, in_=in_, func=func, **kw), reads, writes)

    def tt(self, out, in0, in1, op, reads, writes, eng="dve"):
        return self.P.op(eng, lambda e: e.tensor_tensor(out=out, in0=in0, in1=in1, op=op), reads, writes)

    def ts(self, out, in0, s1, s2, op0, op1, reads, writes, eng="dve"):
        if op1 is None:
            return self.P.op(eng, lambda e: e.tensor_scalar(out=out, in0=in0, scalar1=s1, scalar2=None, op0=op0), reads, writes)
        return self.P.op(eng, lambda e: e.tensor_scalar(out=out, in0=in0, scalar1=s1, scalar2=s2, op0=op0, op1=op1), reads, writes)

    def stt(self, out, in0, scalar, in1, op0, op1, reads, writes):
        return self.P.op("dve", lambda e: e.scalar_tensor_tensor(out=out, in0=in0, scalar=scalar, in1=in1, op0=op0, op1=op1), reads, writes)

    def mm(self, out, lhsT, rhs, start, stop, reads, writes):
        return self.P.op("pe", lambda e: e.matmul(out, lhsT, rhs, start=start, stop=stop), reads, writes)

    def tr(self, out, in_, ident, reads, writes):
        return self.P.op("pe", lambda e: e.transpose(out, in_, ident), reads, writes)

    def dma(self, out, in_, reads, writes, eng="sp"):
        return self.P.dma(eng, lambda e: e.dma_start(out=out, in_=in_), reads, writes)

    def consts(self):
        P = self.P
        self.idf = P.sb([128, 128], F32, "idf")
        self.idb = P.sb([128, 128], BF16, "idb")
        self.ones_f = P.sb([128, 128], F32, "ones_f")
        self.ones_b = P.sb([128, 128], BF16, "ones_b")
        self.triu = P.sb([128, 128], F32, "triu")
        self.epsc = P.sb([128, 4], F32, "epsc")
        iot = P.sb([128, 128], F32, "iot")
        P.op("pool", lambda e: e.iota(iot[:], [[1, 128]], base=0, channel_multiplier=-1,
                                      allow_small_or_imprecise_dtypes=True), (), [iot])
        self.ts(self.idf[:], iot[:], 0.0, None, ALU.is_equal, None, [iot], [self.idf])
        self.ts(self.idb[:], iot[:], 0.0, None, ALU.is_equal, None, [iot], [self.idb])
        self.ts(self.triu[:], iot[:], 0.0, None, ALU.is_ge, None, [iot], [self.triu])
        P.op("dve", lambda e: e.memset(self.ones_f[:], 1.0), (), [self.ones_f])
        P.op("dve", lambda e: e.memset(self.ones_b[:], 1.0), (), [self.ones_b])
        P.op("dve", lambda e: e.memset(self.epsc[:, 0:1], EPS), (), [self.epsc])
        P.op("dve", lambda e: e.memset(self.epsc[:, 1:2], M2_EPS), (), [self.epsc])
        P.op("dve", lambda e: e.memset(self.epsc[:, 2:3], 1.0), (), [self.epsc])
        self.MAXS = 768
        self.rmask = P.sb([128, self.MAXS], F32, "rmask")
        P.op("dve", lambda e: e.memset(self.rmask[:], 1.0), (), [self.rmask])
        P.op("dve", lambda e: e.memset(self.rmask[:].rearrange("p (c j) -> p c j", j=64)[:, :, 0:1], 0.0), (), [self.rmask])
        self.pp = Ring([P.ps([128, 512], F32, f"pp{i}") for i in range(4)])
        self.pq = Ring([P.ps([128, 512], F32, f"pq{i}") for i in range(2)])
        self.pt = Ring([P.ps([128, 1024], BF16, f"pt{i}") for i in range(2)])

    def load_cols(self, vec_ap, n, name):
        P = self.P
        rows = P.sb([n, 128], F32, name + "_r")
        cols = P.sb([128, n], F32, name)
        self.dma(rows[:], vec_ap, [], [rows])
        ps = self.pq.next()
        self.tr(ps[:, 0:n], rows[:], self.idf[0:n, 0:n], [rows, self.idf], [ps])
        self.act(cols[:], ps[:, 0:n], AF.Copy, [ps], [cols])
        return cols

    def norm_to_uT(self, h_ap, hres, g_ap, uT_d, tiles=None):
        P = self.P
        gbc = P.sb([128, D], F32, "gbc")
        self.dma(gbc[:], g_ap.partition_broadcast(128), [], [gbc])
        hb = Ring([P.sb([128, D], F32, "hb") for _ in range(2)])
        ub = Ring([P.sb([128, D], BF16, "ub") for _ in range(2)])
        junk = P.sb([128, D], BF16, "junk")
        stt_ = Ring([P.sb([128, 4], F32, "nst") for _ in range(2)])
        uo = Ring([P.sb([128, KC, 128], BF16, "uo") for _ in range(2)])
        for i in (tiles if tiles is not None else range(self.NT)):
            h = hb.next(); u = ub.next(); s = stt_.next(); o = uo.next()
            self.dma(h[:], h_ap[i * 128:(i + 1) * 128, :], [hres], [h])
            self.act(junk[:], h[:], AF.Square, [h], [junk, s], accum_out=s[:, 0:1])
            self.act(s[:, 1:2], s[:, 0:1], AF.Ln, [s, self.epsc], [s], scale=1.0 / D, bias=self.epsc[:, 0:1])
            self.act(s[:, 2:3], s[:, 1:2], AF.Exp, [s], [s], scale=-0.5)
            self.stt(u[:], h[:], s[:, 2:3], gbc[:], ALU.mult, ALU.mult, [h, s, gbc], [u])
            for half in range(2):
                pt = self.pt.next()
                for j in range(8):
                    kc = half * 8 + j
                    self.tr(pt[:, j * 128:(j + 1) * 128], u[:, kc * 128:(kc + 1) * 128], self.idb[:], [u, self.idb], [pt])
                self.act(o[:, half * 8:(half + 1) * 8, :], pt[:].rearrange("p (j t) -> p j t", j=8), AF.Copy, [pt], [o])
            self.dma(uT_d[:, :, i * 128:(i + 1) * 128].rearrange("k p t -> p k t"), o[:], [o], [uT_d])

    def hg_setup(self, lbl_ap, l, hgng_ap):
        lg0 = self.load_cols(lbl_ap[0:1, :].rearrange("o (h p) -> (o h) p", p=128), 16, "lg0")
        lg1 = self.load_cols(lbl_ap[1:2, :].rearrange("o (h p) -> (o h) p", p=128), 16, "lg1")
        self.lb = self.P.sb([128, 16], F32, "lb")
        self.oml = self.P.sb([128, 16], F32, "oml")
        if l == 0:
            self.tt(self.lb[:], lg0[:], lg0[:], ALU.subtract, [lg0], [self.lb])
        else:
            self.tt(self.lb[:], lg1[:], lg0[:], ALU.subtract, [lg0, lg1], [self.lb])
            self.act(self.lb[:], self.lb[:], AF.Sigmoid, [self.lb], [self.lb])
        self.ts(self.oml[:], self.lb[:], -1.0, 1.0, ALU.mult, ALU.add, [self.lb], [self.oml])
        if hgng_ap is not None:
            self.hgng = self.load_cols(hgng_ap.rearrange("o (h p) -> (o h) p", p=128), 16, "hgng")

    def hg_setup_lb(self, lbl_ap, l):
        self.hg_setup(lbl_ap, l, None)

    def load_uT_seg(self, uT_d, t0, nt, ring):
        u = ring.next()
        self.dma(u[:, :, 0:nt * 128], uT_d[:, :, t0 * 128:(t0 + nt) * 128].rearrange("k p t -> p k t"), [uT_d], [u])
        return u

    def hg_mixer(self, uT_d, w_ap, mask_ap, S, hgT_d, state_only=False, dec_out=None):
        P = self.P
        MAXS = self.MAXS
        useg = Ring([P.sb([128, KC, MAXS], BF16, "useg") for _ in range(1)])
        wts = Ring([P.sb([128, KC, 4, 128], BF16, "hgw") for _ in range(2)])
        mk = Ring([P.sb([128, MAXS], F32, "mk") for _ in range(2)])

        def ring2(dt, nm):
            return Ring([P.sb([128, MAXS], dt, nm) for _ in range(2)])

        rA, rB, rC, rD, rE, rSG, rO = (ring2(F32, n) for n in ("hA", "hB", "hC", "hD", "hE", "hSG", "hO"))
        rF, rG, rI, rHo = (ring2(BF16, n) for n in ("hF", "hG", "hI", "hHo"))
        rcs = Ring([P.sb([128, 4, 16], F32, "hcs") for _ in range(2)])
        rkv = Ring([P.sb([64, 256], BF16, "hkv") for _ in range(3)])
        rsT = Ring([P.sb([64, 64], BF16, "hsT") for _ in range(3)])
        rSp = Ring([P.sb([128, 128], BF16, "hSp") for _ in range(3)])
        rtmp = Ring([P.sb([128, 128], F32, "htmp") for _ in range(3)])
        rrs = Ring([P.sb([128, 512], F32, "hrs") for _ in range(2)])
        rsq = ring2(F32, "hsq")
        scale = 128 ** -0.5
        for (t0, nt) in self.segs:
            NS = nt * 128
            NCH = NS // 64
            u = self.load_uT_seg(uT_d, t0, nt, useg)
            mask = mk.next()
            self.dma(mask[:, 0:NS], mask_ap[0:1, t0 * 128:t0 * 128 + NS].partition_broadcast(128), [], [mask])
            chunks = [(n0, min(512, NS - n0)) for n0 in range(0, NS, 512)]
            for hd in range(HG_H):
                wt = wts.next()
                mats = [(1, C_F), (2, C_I)] if state_only else [(1, C_F), (0, C_Q), (2, C_I), (3, C_OG)]
                for (m, c0) in mats:
                    self.dma(wt[:, :, m, :], w_ap[:, c0 + hd * 128:c0 + (hd + 1) * 128].rearrange("(k p) c -> p k c", p=128),
                             [], [wt], eng="pool")
                A, B, C, Dd, E, SG, O = (r.next() for r in (rA, rB, rC, rD, rE, rSG, rO))
                F, G, I, Ho = (r.next() for r in (rF, rG, rI, rHo))
                for (m, c0) in mats:
                    for (n0, nn) in chunks:
                        ps = self.pp.next()
                        for kc in range(KC):
                            self.mm(ps[:, 0:nn], wt[:, kc, m, :], u[:, kc, n0:n0 + nn], kc == 0, kc == KC - 1, [wt, u], [ps])
                        if m == 1:
                            self.act(A[:, n0:n0 + nn], ps[:, 0:nn], AF.Sigmoid, [ps], [A])
                        elif m == 0:
                            self.act(E[:, n0:n0 + nn], ps[:, 0:nn], AF.Copy, [ps], [E], scale=scale)
                        elif m == 2:
                            self.act(I[:, n0:n0 + nn], ps[:, 0:nn], AF.Copy, [ps], [I])
                        else:
                            self.act(SG[:, n0:n0 + nn], ps[:, 0:nn], AF.Sigmoid, [ps], [SG])
                self.ts(A[:, 0:NS], A[:, 0:NS], self.oml[:, hd:hd + 1], self.lb[:, hd:hd + 1], ALU.mult, ALU.add, [A, self.oml, self.lb], [A])
                self.ts(B[:, 0:NS], A[:, 0:NS], -1.0, 1.0, ALU.mult, ALU.add, [A], [B])
                self.tt(B[:, 0:NS], B[:, 0:NS], mask[:, 0:NS], ALU.mult, [B, mask], [B])
                self.act(A[:, 0:NS], A[:, 0:NS], AF.Ln, [A], [A])
                self.tt(A[:, 0:NS], A[:, 0:NS], mask[:, 0:NS], ALU.mult, [A, mask], [A])
                P.op("dve", lambda e, C=C, A=A, NS=NS: e.tensor_tensor_scan(out=C[:, 0:NS], data0=self.rmask[:, 0:NS], data1=A[:, 0:NS],
                                                                        initial=0.0, op0=ALU.mult, op1=ALU.add), [A, self.rmask], [C])
                C3 = C[:, 0:NS].rearrange("p (c j) -> p c j", j=64)
                A3 = A[:, 0:NS].rearrange("p (c j) -> p c j", j=64)
                self.tt(A3, C3, C3[:, :, 31:32].to_broadcast([128, NCH, 64]), ALU.subtract, [C], [A])
                cs = rcs.next()
                self.tt(cs[:, 3, 0:NCH], C3[:, :, 63], C3[:, :, 31], ALU.subtract, [C], [cs])
                self.act(cs[:, 0, 0:NCH], C3[:, :, 63], AF.Exp, [C], [cs])
                self.act(cs[:, 1, 0:NCH], cs[:, 3, 0:NCH], AF.Exp, [cs], [cs])
                self.act(cs[:, 2, 0:NCH], C3[:, :, 31], AF.Exp, [C], [cs])
                if dec_out is not None:
                    P.op("dve", lambda e, cs=cs, C3=C3, NCH=NCH: e.tensor_reduce(out=cs[:, 3, 0:1], in_=C3[:, :, 63], axis=AX.X, op=ALU.add), [C], [cs])
                    self.tt(dec_out[:, hd:hd + 1], dec_out[:, hd:hd + 1], cs[:, 3, 0:1], ALU.add, [dec_out, cs], [dec_out])
                self.act(Dd[:, 0:NS], A[:, 0:NS], AF.Exp, [A], [Dd], scale=-1.0)
                self.tt(G[:, 0:NS], B[:, 0:NS], Dd[:, 0:NS], ALU.mult, [B, Dd], [G])
                if not state_only:
                    self.act(A[:, 0:NS], A[:, 0:NS], AF.Exp, [A], [A])
                    self.tt(F[:, 0:NS], E[:, 0:NS], A[:, 0:NS], ALU.mult, [E, A], [F])
                Sh = S[:, hd, :]
                for c in range(NCH):
                    c0 = c * 64
                    pt = self.pt.next()
                    self.tr(pt[0:64, 0:128], G[:, c0:c0 + 64], self.idb[:], [G, self.idb], [pt])
                    self.tr(pt[0:64, 128:256], I[:, c0:c0 + 64], self.idb[:], [I, self.idb], [pt])
                    kv = rkv.next()
                    self.act(kv[:], pt[0:64, 0:256], AF.Copy, [pt], [kv])
                    if not state_only:
                        psc = self.pq.next()
                        self.mm(psc[0:64, 0:64], G[:, c0:c0 + 64], F[:, c0:c0 + 64], True, True, [G, F], [psc])
                        sT = rsT.next()
                        self.tt(sT[:], psc[0:64, 0:64], self.triu[0:64, 0:64], ALU.mult, [psc, self.triu], [sT])
                        Sp = rSp.next()
                        self.ts(Sp[:], Sh, cs[:, 2, c:c + 1], None, ALU.mult, None, [S, cs], [Sp])
                        po = self.pq.next()
                        self.mm(po[:, 0:64], kv[:, 128:256], sT[:], True, False, [kv, sT], [po])
                        self.mm(po[:, 0:64], Sp[:], F[:, c0:c0 + 64], False, True, [Sp, F], [po])
                        self.tt(O[:, c0:c0 + 64], po[:, 0:64], SG[:, c0:c0 + 64], ALU.mult, [po, SG], [O])
                    pst = self.pq.next()
                    self.mm(pst[:, 0:128], kv[:, 0:128], kv[:, 128:256], True, True, [kv], [pst])
                    tmp = rtmp.next()
                    self.ts(tmp[:], pst[:, 0:128], cs[:, 1, c:c + 1], None, ALU.mult, None, [pst, cs], [tmp])
                    self.stt(Sh, Sh, cs[:, 0, c:c + 1], tmp[:], ALU.mult, ALU.add, [S, cs, tmp], [S])
                if state_only:
                    continue
                sq = rsq.next()
                self.tt(sq[:, 0:NS], O[:, 0:NS], O[:, 0:NS], ALU.mult, [O], [sq])
                for (n0, nn) in chunks:
                    ps = self.pp.next()
                    self.mm(ps[:, 0:nn], self.ones_f[:], sq[:, n0:n0 + nn], True, True, [self.ones_f, sq], [ps])
                    rs = rrs.next()
                    self.act(rs[:, 0:nn], ps[:, 0:nn], AF.Ln, [ps, self.epsc], [rs], scale=1.0 / 128, bias=self.epsc[:, 0:1])
                    self.act(rs[:, 0:nn], rs[:, 0:nn], AF.Exp, [rs], [rs], scale=-0.5)
                    self.stt(Ho[:, n0:n0 + nn], O[:, n0:n0 + nn], self.hgng[:, hd:hd + 1], rs[:, 0:nn], ALU.mult, ALU.mult,
                             [O, self.hgng, rs], [Ho])
                self.dma(hgT_d[hd, :, t0 * 128:t0 * 128 + NS], Ho[:, 0:NS], [Ho], [hgT_d])

    def m2_setup(self, w_ap, convw_ap, convb_ap, dtb_ap, alog_ap, dsk_ap, ng_ap, mask_ap):
        P = self.P
        self.cw = [self.load_cols(convw_ap[k:k + 1, :].rearrange("o (j p) -> (o j) p", p=128), 32, f"cw{k}") for k in range(4)]
        self.cbias = self.load_cols(convb_ap.rearrange("o (j p) -> (o j) p", p=128), 32, "cbias")
        self.maskT = self.load_cols(mask_ap.rearrange("o (j p) -> (o j) p", p=128), self.NT, "maskT")
        self.dtb = P.sb([128, 32], F32, "dtb")
        self.negA = P.sb([128, 32], F32, "negA")
        self.dsk = P.sb([128, 32], F32, "dsk")
        self.m2ng = P.sb([128, 2048], F32, "m2ng")
        self.dma(self.dtb[:], dtb_ap.partition_broadcast(128), [], [self.dtb])
        self.dma(self.negA[:], alog_ap.partition_broadcast(128), [], [self.negA])
        if dsk_ap is not None:
            self.dma(self.dsk[:], dsk_ap.partition_broadcast(128), [], [self.dsk])
            self.dma(self.m2ng[:], ng_ap.partition_broadcast(128), [], [self.m2ng])
        self.act(self.negA[:], self.negA[:], AF.Exp, [self.negA], [self.negA])
        self.ts(self.negA[:], self.negA[:], -1.0, None, ALU.mult, None, [self.negA], [self.negA])
        self.wdt = P.sb([128, KC, 32], BF16, "wdt")
        self.dma(self.wdt[:], w_ap[:, C_DT:C_DT + 32].rearrange("(k p) c -> p k c", p=128), [], [self.wdt], eng="pool")
        self.strict = P.sb([128, 128], F32, "strict")
        self.ts(self.strict[:], self.triu[:], -1.0, 1.0, ALU.mult, ALU.add, [self.triu], [self.strict])
        self.carry = P.sb([128, 8, 4, 3], F32, "carry")
        P.op("dve", lambda e: e.memset(self.carry[:], 0.0), (), [self.carry])

    def m2_mixer(self, uT_d, w_ap, S, Sbf, m2T_d, state_only=False, dec_out=None):
        P = self.P
        MAXS = self.MAXS
        MT = MAXS // 128
        useg = Ring([P.sb([128, KC, MAXS], BF16, "useg") for _ in range(1)])
        wts = Ring([P.sb([128, KC, 768], BF16, "m2w") for _ in range(2)])
        rxr = Ring([P.sb([128, 4, 3 + MAXS], F32, "xr") for _ in range(2)])
        racc = Ring([P.sb([128, 2, MAXS], F32, "xacc") for _ in range(2)])
        rtmpc = Ring([P.sb([128, MAXS], F32, "ctmp") for _ in range(2)])
        rBT = Ring([P.sb([128, MAXS], BF16, "BT") for _ in range(2)])
        rCT = Ring([P.sb([128, MAXS], BF16, "CT") for _ in range(2)])
        rzs = Ring([P.sb([128, MT, 256], F32, "zs") for _ in range(2)])
        rm2o = Ring([P.sb([128, 2, MAXS], BF16, "m2o") for _ in range(2)])
        dtS = P.sb([128, MT, 32], F32, "dtS")
        aS = P.sb([128, MT, 32], F32, "aS")
        cumS = P.sb([128, MT, 32], F32, "cumS")
        lastS = P.sb([128, MT, 32], F32, "lastS")
        ecS = P.sb([128, MT, 32], F32, "ecS")
        elS = P.sb([128, MT, 32], F32, "elS")
        ddS = P.sb([128, MT, 32], F32, "ddS")
        rxs = Ring([P.sb([128, 256], F32, "xs") for _ in range(2)])
        rxdt = Ring([P.sb([128, 4, 64], BF16, "xdt") for _ in range(2)])
        rxdd = Ring([P.sb([128, 4, 64], BF16, "xdd") for _ in range(2)])
        rBtm = Ring([P.sb([128, 128], BF16, "Btm") for _ in range(2)])
        rcbm = Ring([P.sb([128, 128], F32, "cbm") for _ in range(2)])
        rM1 = Ring([P.sb([128, 4, 128], F32, "M1") for _ in range(2)])
        rEL = Ring([P.sb([128, 4, 128], F32, "EL") for _ in range(2)])
        rWT = Ring([P.sb([128, 4, 128], BF16, "WT") for _ in range(2)])
        ry = Ring([P.sb([128, 256], F32, "y") for _ in range(2)])
        ry2 = Ring([P.sb([128, 256], F32, "y2") for _ in range(2)])
        ryn = Ring([P.sb([128, 256], BF16, "yn") for _ in range(2)])
        rst = Ring([P.sb([128, 4], F32, "mst") for _ in range(2)])
        rtS = Ring([P.sb([128, 4, 64], F32, "tS") for _ in range(2)])
        junk = P.sb([128, 256], BF16, "mjunk")
        for (t0, nt) in self.segs:
            NS = nt * 128
            u = self.load_uT_seg(uT_d, t0, nt, useg)
            chunks = [(n0, min(512, NS - n0)) for n0 in range(0, NS, 512)]
            for ti in range(nt):
                ps = self.pq.next()
                for kc in range(KC):
                    self.mm(ps[:, 0:32], u[:, kc, ti * 128:(ti + 1) * 128], self.wdt[:, kc, :], kc == 0, kc == KC - 1, [u, self.wdt], [ps])
                self.tt(dtS[:, ti, :], ps[:, 0:32], self.dtb[:], ALU.add, [ps, self.dtb], [dtS])
            self.act(dtS[:, 0:nt, :], dtS[:, 0:nt, :], AF.Exp, [dtS], [dtS])
            self.act(dtS[:, 0:nt, :], dtS[:, 0:nt, :], AF.Ln, [dtS, self.epsc], [dtS], bias=self.epsc[:, 2:3])
            for ti in range(nt):
                self.ts(dtS[:, ti, :], dtS[:, ti, :], self.maskT[:, t0 + ti:t0 + ti + 1], None, ALU.mult, None, [dtS, self.maskT], [dtS])
                self.tt(aS[:, ti, :], dtS[:, ti, :], self.negA[:], ALU.mult, [dtS, self.negA], [aS])
            for ti in range(nt):
                ps = self.pq.next()
                self.mm(ps[:, 0:32], self.triu[:], aS[:, ti, :], True, True, [self.triu, aS], [ps])
                self.mm(ps[:, 32:64], self.ones_f[:], aS[:, ti, :], True, True, [self.ones_f, aS], [ps])
                self.act(cumS[:, ti, :], ps[:, 0:32], AF.Copy, [ps], [cumS])
                self.act(lastS[:, ti, :], ps[:, 32:64], AF.Copy, [ps], [lastS])
                if dec_out is not None:
                    self.tt(dec_out[:], dec_out[:], lastS[:, ti, :], ALU.add, [dec_out, lastS], [dec_out])
            self.act(ecS[:, 0:nt, :], cumS[:, 0:nt, :], AF.Exp, [cumS], [ecS])
            self.act(elS[:, 0:nt, :], lastS[:, 0:nt, :], AF.Exp, [lastS], [elS])
            self.tt(ddS[:, 0:nt, :], lastS[:, 0:nt, :], cumS[:, 0:nt, :], ALU.subtract, [lastS, cumS], [ddS])
            self.act(ddS[:, 0:nt, :], ddS[:, 0:nt, :], AF.Exp, [ddS], [ddS])
            self.tt(ddS[:, 0:nt, :], ddS[:, 0:nt, :], dtS[:, 0:nt, :], ALU.mult, [ddS, dtS], [ddS])
            for gi in range(M2_G):
                wt = wts.next()
                srcs = [(0, C_Z + gi * 256, 256), (256, C_X + gi * 256, 256), (512, C_B + gi * 128, 128), (640, C_C + gi * 128, 128)]
                for (o0, c0, w_) in srcs:
                    if state_only and o0 in (0, 640):
                        continue
                    self.dma(wt[:, :, o0:o0 + w_], w_ap[:, c0:c0 + w_].rearrange("(k p) c -> p k c", p=128), [], [wt], eng="pool")
                xr = rxr.next(); acc = racc.next(); BT = rBT.next(); CT = rCT.next(); zs = rzs.next(); m2o = rm2o.next()
                blks = [0, 1, 2] if state_only else [0, 1, 2, 3]
                if not state_only:
                    for ti in range(nt):
                        ps = self.pp.next()
                        for kc in range(KC):
                            self.mm(ps[:, 0:256], u[:, kc, ti * 128:(ti + 1) * 128], wt[:, kc, 0:256], kc == 0, kc == KC - 1, [u, wt], [ps])
                        self.act(zs[:, ti, :], ps[:, 0:256], AF.Silu, [ps], [zs])
                for b in blks:
                    j = (gi * 2 + b) if b < 2 else (16 + gi if b == 2 else 24 + gi)
                    self.act(xr[:, b, 0:3], self.carry[:, gi, b, :], AF.Copy, [self.carry], [xr])
                    for (n0, nn) in chunks:
                        ps = self.pp.next()
                        for kc in range(KC):
                            self.mm(ps[:, 0:nn], wt[:, kc, 256 + b * 128:256 + (b + 1) * 128], u[:, kc, n0:n0 + nn], kc == 0, kc == KC - 1, [wt, u], [ps])
                        self.act(xr[:, b, 3 + n0:3 + n0 + nn], ps[:, 0:nn], AF.Copy, [ps], [xr])
                    self.act(self.carry[:, gi, b, :], xr[:, b, NS:NS + 3], AF.Copy, [xr], [self.carry])
                    tmpc = rtmpc.next()
                    self.ts(tmpc[:, 0:NS], xr[:, b, 3:3 + NS], self.cw[3][:, j:j + 1], self.cbias[:, j:j + 1], ALU.mult, ALU.add,
                            [xr, self.cw[3], self.cbias], [tmpc])
                    for k in range(3):
                        self.stt(tmpc[:, 0:NS], xr[:, b, k:k + NS], self.cw[k][:, j:j + 1], tmpc[:, 0:NS], ALU.mult, ALU.add,
                                 [xr, self.cw[k], tmpc], [tmpc])
                    if b < 2:
                        self.act(acc[:, b, 0:NS], tmpc[:, 0:NS], AF.Silu, [tmpc], [acc])
                    elif b == 2:
                        self.act(BT[:, 0:NS], tmpc[:, 0:NS], AF.Silu, [tmpc], [BT])
                    else:
                        self.act(CT[:, 0:NS], tmpc[:, 0:NS], AF.Silu, [tmpc], [CT])
                Sg = S[:, gi * 4:(gi + 1) * 4, :]
                Sbg = Sbf[:, gi * 4:(gi + 1) * 4, :]
                hs = slice(gi * 4, gi * 4 + 4)
                for ti in range(nt):
                    tk = slice(ti * 128, (ti + 1) * 128)
                    px = self.pq.next()
                    self.tr(px[:, 0:128], acc[:, 0, tk], self.idf[:], [acc, self.idf], [px])
                    self.tr(px[:, 128:256], acc[:, 1, tk], self.idf[:], [acc, self.idf], [px])
                    pb = self.pt.next()
                    self.tr(pb[:, 0:128], BT[:, tk], self.idb[:], [BT, self.idb], [pb])
                    xs = rxs.next(); xdt = rxdt.next(); xdd = rxdd.next(); Btm = rBtm.next()
                    self.act(xs[:], px[:, 0:256], AF.Copy, [px], [xs])
                    self.act(Btm[:], pb[:, 0:128], AF.Copy, [pb], [Btm])
                    xs3 = xs[:].rearrange("p (h q) -> p h q", q=64)
                    self.tt(xdd[:], xs3, ddS[:, ti, hs].unsqueeze(2).to_broadcast([128, 4, 64]), ALU.mult, [xs, ddS], [xdd])
                    if not state_only:
                        self.tt(xdt[:], xs3, dtS[:, ti, hs].unsqueeze(2).to_broadcast([128, 4, 64]), ALU.mult, [xs, dtS], [xdt])
                        pcb = self.pq.next()
                        self.mm(pcb[:, 0:128], BT[:, tk], CT[:, tk], True, True, [BT, CT], [pcb])
                        cbm = rcbm.next()
                        self.tt(cbm[:], pcb[:, 0:128], self.triu[:], ALU.mult, [pcb, self.triu], [cbm])
                        M1 = rM1.next()
                        self.tt(M1[:], self.strict[:].unsqueeze(1).to_broadcast([128, 4, 128]),
                                aS[:, ti, hs].unsqueeze(2).to_broadcast([128, 4, 128]), ALU.mult, [self.strict, aS], [M1])
                        pD = self.pq.next()
                        for h in range(4):
                            self.mm(pD[:, h * 128:(h + 1) * 128], M1[:, h, :], self.triu[:], True, True, [M1, self.triu], [pD])
                        EL = rEL.next()
                        self.act(EL[:], pD[:].rearrange("p (h t) -> p h t", h=4), AF.Exp, [pD], [EL])
                        WT = rWT.next()
                        self.tt(WT[:], EL[:], cbm[:].unsqueeze(1).to_broadcast([128, 4, 128]), ALU.mult, [EL, cbm], [WT])
                        py = self.pq.next()
                        for h in range(4):
                            self.mm(py[:, h * 64:(h + 1) * 64], WT[:, h, :], xdt[:, h, :], True, True, [WT, xdt], [py])
                        self.mm(py[:, 256:512], CT[:, tk], Sbg.rearrange("p h q -> p (h q)"), True, True, [CT, Sbf], [py])
                        y = ry.next(); y2 = ry2.next()
                        self.tt(y[:].rearrange("p (h q) -> p h q", q=64), py[:, 256:512].rearrange("p (h q) -> p h q", q=64),
                                ecS[:, ti, hs].unsqueeze(2).to_broadcast([128, 4, 64]), ALU.mult, [py, ecS], [y])
                        self.tt(y[:], y[:], py[:, 0:256], ALU.add, [y, py], [y])
                        self.tt(y2[:].rearrange("p (h q) -> p h q", q=64), xs3, self.dsk[:, hs].unsqueeze(2).to_broadcast([128, 4, 64]),
                                ALU.mult, [xs, self.dsk], [y2])
                        self.tt(y[:], y[:], y2[:], ALU.add, [y, y2], [y])
                        self.tt(y[:], y[:], zs[:, ti, :], ALU.mult, [y, zs], [y])
                        st = rst.next()
                        self.act(junk[:], y[:], AF.Square, [y], [junk, st], accum_out=st[:, 0:1])
                        self.act(st[:, 1:2], st[:, 0:1], AF.Ln, [st, self.epsc], [st], scale=1.0 / 256, bias=self.epsc[:, 1:2])
                        self.act(st[:, 2:3], st[:, 1:2], AF.Exp, [st], [st], scale=-0.5)
                        yn = ryn.next()
                        self.stt(yn[:], y[:], st[:, 2:3], self.m2ng[:, gi * 256:(gi + 1) * 256], ALU.mult, ALU.mult, [y, st, self.m2ng], [yn])
                        po = self.pt.next()
                        self.tr(po[:, 0:128], yn[:, 0:128], self.idb[:], [yn, self.idb], [po])
                        self.tr(po[:, 128:256], yn[:, 128:256], self.idb[:], [yn, self.idb], [po])
                        self.act(m2o[:, :, tk], po[:, 0:256].rearrange("p (j t) -> p j t", j=2), AF.Copy, [po], [m2o])
                    pS = self.pq.next()
                    self.mm(pS[:, 0:256], Btm[:], xdd[:].rearrange("p h q -> p (h q)"), True, True, [Btm, xdd], [pS])
                    tS = rtS.next()
                    self.tt(tS[:], Sg, elS[:, ti, hs].unsqueeze(2).to_broadcast([128, 4, 64]), ALU.mult, [S, elS], [tS])
                    self.tt(Sg, tS[:], pS[:, 0:256].rearrange("p (h q) -> p h q", q=64), ALU.add, [tS, pS], [S])
                    self.act(Sbg, Sg, AF.Copy, [S], [Sbf])
                if not state_only:
                    self.dma(m2T_d[gi * 2:gi * 2 + 2, :, t0 * 128:t0 * 128 + NS].rearrange("j p t -> p j t"), m2o[:, :, 0:NS], [m2o], [m2T_d])

    def tok_chunks(self, n0, n1):
        return [(a, min(512, n1 - a)) for a in range(n0, n1, 512)]

    def dense_A_res(self, xT_d, KCx, W_ap, res_ap, res_r, out_d, seg_tiles):
        P = self.P
        NTs = seg_tiles
        xs_ = Ring([P.sb([128, KCx, NTs * 128], BF16, "dax") for _ in range(1)])
        wr = Ring([P.sb([128, KCx, 512], BF16, "daw") for _ in range(2)])
        rr = Ring([P.sb([128, 512], F32, "dar") for _ in range(3)])
        orr = Ring([P.sb([128, 512], F32, "dao") for _ in range(3)])
        t = 0
        while t < self.NT:
            nt = min(NTs, self.NT - t)
            x = xs_.next()
            self.dma(x[:, :, 0:nt * 128], xT_d[:, :, t * 128:(t + nt) * 128].rearrange("k p t -> p k t"), [xT_d], [x])
            for cb in range(4):
                w = wr.next()
                self.dma(w[:], W_ap[:, cb * 512:(cb + 1) * 512].rearrange("(k p) c -> p k c", p=128), [], [w], eng="pool")
                for ti in range(nt):
                    r = rr.next(); o = orr.next()
                    tok = slice((t + ti) * 128, (t + ti + 1) * 128)
                    self.dma(r[:], res_ap[tok, cb * 512:(cb + 1) * 512], [res_r], [r])
                    ps = self.pp.next()
                    for kc in range(KCx):
                        self.mm(ps[:], x[:, kc, ti * 128:(ti + 1) * 128], w[:, kc, :], kc == 0, kc == KCx - 1, [x, w], [ps])
                    self.tt(o[:], ps[:], r[:], ALU.add, [ps, r], [o])
                    self.dma(out_d[tok, cb * 512:(cb + 1) * 512], o[:], [o], [out_d])
            t += nt

    def dense_B(self, x, W_ap, c0, ncols, evac, ntok=None):
        P = self.P
        wr = self._dbw
        ntok = self.TT if ntok is None else ntok
        for c in range(0, ncols, 512):
            wc = min(512, ncols - c)
            w = wr.next()
            self.dma(w[:, :, 0:wc], W_ap[:, c0 + c:c0 + c + wc].rearrange("(k p) c -> p k c", p=128), [], [w], eng="pool")
            for b in range(wc // 128):
                for (n0, nn) in self.tok_chunks(0, ntok):
                    ps = self.pp.next()
                    for kc in range(KC):
                        self.mm(ps[:, 0:nn], w[:, kc, b * 128:(b + 1) * 128], x[:, kc, n0:n0 + nn], kc == 0, kc == KC - 1, [w, x], [ps])
                    evac(ps, (c // 128) + b, n0, nn)

    def halves(self):
        a = ((self.NT + 1) // 2) * 128
        return [(0, a), (a, self.TT)] if a < self.TT else [(0, self.TT)]

    def load_xT(self, xT_d, name):
        x = self.P.sb([128, KC, self.TT], BF16, name)
        self.dma(x[:], xT_d[:, :, :].rearrange("k p t -> p k t"), [xT_d], [x])
        return x

    def mixer_merge(self, uT_d, hgT_d, m2T_d, w_ap, wbh_ap, wbm_ap, mT_d):
        P = self.P
        hv = self.halves()
        NSM = max(b - a for a, b in hv)
        ru = Ring([P.sb([128, KC, NSM], BF16, "mu") for _ in range(1)])
        rh = Ring([P.sb([128, KC, NSM], BF16, "mh") for _ in range(1)])
        rm = Ring([P.sb([128, KC, NSM], BF16, "mm") for _ in range(1)])
        rw = Ring([P.sb([128, KC, 4, 128], BF16, "mw") for _ in range(2)])
        rg = Ring([P.sb([128, 2, 512], F32, "mg") for _ in range(2)])
        rt = Ring([P.sb([128, 2, 512], F32, "mt") for _ in range(2)])
        ro = Ring([P.sb([128, NSM], BF16, "mo") for _ in range(2)])
        for (h0, h1) in hv:
            nh = h1 - h0
            u = ru.next(); hg = rh.next(); m2 = rm.next()
            for (dst, src) in ((u, uT_d), (hg, hgT_d), (m2, m2T_d)):
                self.dma(dst[:, :, 0:nh], src[:, :, h0:h1].rearrange("k p t -> p k t"), [src], [dst])
            for kb in range(16):
                c = kb * 128
                w = rw.next()
                for mi, (ap, cc) in enumerate(((w_ap, C_GHG + c), (wbh_ap, c), (w_ap, C_GM2 + c), (wbm_ap, c))):
                    self.dma(w[:, :, mi, :], ap[:, cc:cc + 128].rearrange("(k p) c -> p k c", p=128), [], [w], eng="pool")
                o = ro.next()
                for (n0, nn) in self.tok_chunks(0, nh):
                    g = rg.next(); t_ = rt.next()
                    pss = []
                    for mi, x in enumerate((u, hg, u, m2)):
                        ps = self.pp.next()
                        for kc in range(KC):
                            self.mm(ps[:, 0:nn], w[:, kc, mi, :], x[:, kc, n0:n0 + nn], kc == 0, kc == KC - 1, [w, x], [ps])
                        pss.append(ps)
                    self.act(g[:, 0, 0:nn], pss[0][:, 0:nn], AF.Sigmoid, [pss[0]], [g])
                    self.act(g[:, 1, 0:nn], pss[2][:, 0:nn], AF.Sigmoid, [pss[2]], [g])
                    self.tt(t_[:, 0, 0:nn], pss[1][:, 0:nn], g[:, 0, 0:nn], ALU.mult, [pss[1], g], [t_])
                    self.tt(t_[:, 1, 0:nn], pss[3][:, 0:nn], g[:, 1, 0:nn], ALU.mult, [pss[3], g], [t_])
                    self.tt(o[:, n0:n0 + nn], t_[:, 0, 0:nn], t_[:, 1, 0:nn], ALU.add, [t_], [o])
                self.dma(mT_d[kb, :, h0:h1], o[:, 0:nh], [o], [mT_d])

    def xattn(self, uT_d, memnT_d, wq_ap, wkv_ap, oT_d):
        P = self.P
        qT = P.sb([128, KC, self.TT], BF16, "xa_q")
        kT = P.sb([128, KC, 256], BF16, "xa_k")
        v = P.sb([128, 2, 2048], BF16, "xa_v")
        sc = 512 ** -0.5
        with P.scope():
            self._dbw = Ring([P.sb([128, KC, 512], BF16, "dbw") for _ in range(2)])
            mn = P.sb([128, KC, 256], BF16, "xa_mn")
            self.dma(mn[:], memnT_d[:, :, :].rearrange("k p t -> p k t"), [memnT_d], [mn])
            for c in range(0, 2048, 512):
                w = self._dbw.next()
                self.dma(w[:], wkv_ap[:, c:c + 512].rearrange("(k p) c -> p k c", p=128), [], [w], eng="pool")
                for b in range(4):
                    ps = self.pp.next()
                    for kc in range(KC):
                        self.mm(ps[:, 0:256], w[:, kc, b * 128:(b + 1) * 128], mn[:, kc, :], kc == 0, kc == KC - 1, [w, mn], [ps])
                    self.act(kT[:, c // 128 + b, :], ps[:, 0:256], AF.Copy, [ps], [kT])
            for c in range(0, 2048, 512):
                w = self._dbw.next()
                self.dma(w[:], wkv_ap[:, 2048 + c:2048 + c + 512].rearrange("(k p) c -> p k c", p=128), [], [w], eng="pool")
                for mb in range(2):
                    ps = self.pp.next()
                    for kc in range(KC):
                        self.mm(ps[:], mn[:, kc, mb * 128:(mb + 1) * 128], w[:, kc, :], kc == 0, kc == KC - 1, [w, mn], [ps])
                    self.act(v[:, mb, c:c + 512], ps[:], AF.Copy, [ps], [v])
            hv = self.halves()
            xr_ = Ring([P.sb([128, KC, max(b - a for a, b in hv)], BF16, "xa_u") for _ in range(1)])
            for (h0, h1) in hv:
                x = xr_.next()
                self.dma(x[:, :, 0:h1 - h0], uT_d[:, :, h0:h1].rearrange("k p t -> p k t"), [uT_d], [x])

                def evq(ps, cb, n0, nn, h0=h0):
                    self.act(qT[:, cb, h0 + n0:h0 + n0 + nn], ps[:, 0:nn], AF.Copy, [ps], [qT], scale=sc)
                self.dense_B(x, wq_ap, 0, 2048, evq, ntok=h1 - h0)
        rPT = Ring([P.sb([128, 2, 512], BF16, "xa_pt") for _ in range(2)])
        rden = Ring([P.sb([128, 512], F32, "xa_den") for _ in range(2)])
        ro = Ring([P.sb([128, 4, 512], BF16, "xa_o") for _ in range(2)])
        for hh in range(4):
            for (n0, nn) in self.tok_chunks(0, self.TT):
                PT = rPT.next()
                for mb in range(2):
                    ps = self.pp.next()
                    for dc in range(4):
                        self.mm(ps[:, 0:nn], kT[:, hh * 4 + dc, mb * 128:(mb + 1) * 128], qT[:, hh * 4 + dc, n0:n0 + nn], dc == 0, dc == 3, [kT, qT], [ps])
                    self.act(PT[:, mb, 0:nn], ps[:, 0:nn], AF.Exp, [ps], [PT])
                psd = self.pp.next()
                for mb in range(2):
                    self.mm(psd[:, 0:nn], self.ones_b[:], PT[:, mb, 0:nn], mb == 0, mb == 1, [self.ones_b, PT], [psd])
                den = rden.next()
                P.op("dve", lambda e, den=den, psd=psd, nn=nn: e.reciprocal(out=den[:, 0:nn], in_=psd[:, 0:nn]), [psd], [den])
                o = ro.next()
                for dc in range(4):
                    ps = self.pp.next()
                    for mb in range(2):
                        self.mm(ps[:, 0:nn], v[:, mb, hh * 512 + dc * 128:hh * 512 + (dc + 1) * 128], PT[:, mb, 0:nn], mb == 0, mb == 1, [v, PT], [ps])
                    self.tt(o[:, dc, 0:nn], ps[:, 0:nn], den[:, 0:nn], ALU.mult, [ps, den], [o])
                self.dma(oT_d[hh * 4:hh * 4 + 4, :, n0:n0 + nn].rearrange("j p t -> p j t"), o[:, :, 0:nn], [o], [oT_d])

    def ffn_up(self, uT_d, wup_ap, cw_ap, cb_ap, maskF_ap, aT_d):
        P = self.P
        TT = self.TT
        x = self.load_xT(uT_d, "ff_u")
        cw = [self.load_cols(cw_ap[k:k + 1, :].rearrange("o (j p) -> (o j) p", p=128), 44, f"fcw{k}") for k in range(3)]
        cbs = self.load_cols(cb_ap.rearrange("o (j p) -> (o j) p", p=128), 44, "fcb")
        mF = P.sb([128, 128], F32, "maskF")
        self.dma(mF[:], maskF_ap.partition_broadcast(128), [], [mF])
        rw = Ring([P.sb([128, KC, 2, 256], BF16, "fw") for _ in range(2)])
        rgr = Ring([P.sb([128, 2 + TT], F32, "fgr") for _ in range(2)])
        rup = Ring([P.sb([128, TT], F32, "fup") for _ in range(2)])
        rac = Ring([P.sb([128, TT], F32, "fac") for _ in range(2)])
        rao = Ring([P.sb([128, TT], BF16, "fao") for _ in range(2)])
        for jp in range(0, 44, 2):
            w = rw.next()
            self.dma(w[:, :, 0, :], wup_ap[:, jp * 128:jp * 128 + 256].rearrange("(k p) c -> p k c", p=128), [], [w], eng="pool")
            self.dma(w[:, :, 1, :], wup_ap[:, D_FF + jp * 128:D_FF + jp * 128 + 256].rearrange("(k p) c -> p k c", p=128), [], [w], eng="pool")
            for b in range(2):
                j = jp + b
                gr = rgr.next(); up = rup.next(); ac = rac.next(); ao = rao.next()
                P.op("dve", lambda e, gr=gr: e.memset(gr[:, 0:2], 0.0), (), [gr])
                for (n0, nn) in self.tok_chunks(0, TT):
                    ps = self.pp.next()
                    for kc in range(KC):
                        self.mm(ps[:, 0:nn], w[:, kc, 0, b * 128:(b + 1) * 128], x[:, kc, n0:n0 + nn], kc == 0, kc == KC - 1, [w, x], [ps])
                    self.act(gr[:, 2 + n0:2 + n0 + nn], ps[:, 0:nn], AF.Copy, [ps], [gr])
                    ps2 = self.pp.next()
                    for kc in range(KC):
                        self.mm(ps2[:, 0:nn], w[:, kc, 1, b * 128:(b + 1) * 128], x[:, kc, n0:n0 + nn], kc == 0, kc == KC - 1, [w, x], [ps2])
                    self.act(up[:, n0:n0 + nn], ps2[:, 0:nn], AF.Copy, [ps2], [up])
                self.tt(gr[:, 2:130], gr[:, 2:130], mF[:], ALU.mult, [gr, mF], [gr])
                self.ts(ac[:], gr[:, 2:2 + TT], cw[2][:, j:j + 1], cbs[:, j:j + 1], ALU.mult, ALU.add, [gr, cw[2], cbs], [ac])
                for k in range(2):
                    self.stt(ac[:], gr[:, k:k + TT], cw[k][:, j:j + 1], ac[:], ALU.mult, ALU.add, [gr, cw[k], ac], [ac])
                self.act(ac[:], ac[:], AF.Gelu, [ac], [ac])
                self.tt(ao[:], ac[:], up[:], ALU.mult, [ac, up], [ao])
                self.dma(aT_d[j, :, :], ao[:], [ao], [aT_d])

    def final_norm(self, h_ap, hres, g_ap, out_ap, out_r, tiles):
        P = self.P
        gbc = P.sb([128, D], F32, "fgbc")
        self.dma(gbc[:], g_ap.partition_broadcast(128), [], [gbc])
        hb = Ring([P.sb([128, D], F32, "fhb") for _ in range(2)])
        ob = Ring([P.sb([128, D], F32, "fob") for _ in range(2)])
        junk = P.sb([128, D], BF16, "fjunk")
        stt_ = Ring([P.sb([128, 4], F32, "fst") for _ in range(2)])
        outs = []
        for i in tiles:
            h = hb.next(); o = ob.next(); s = stt_.next()
            self.dma(h[:], h_ap[i * 128:(i + 1) * 128, :], [hres], [h])
            self.act(junk[:], h[:], AF.Square, [h], [junk, s], accum_out=s[:, 0:1])
            self.act(s[:, 1:2], s[:, 0:1], AF.Ln, [s, self.epsc], [s], scale=1.0 / D, bias=self.epsc[:, 0:1])
            self.act(s[:, 2:3], s[:, 1:2], AF.Exp, [s], [s], scale=-0.5)
            self.stt(o[:], h[:], s[:, 2:3], gbc[:], ALU.mult, ALU.mult, [h, s, gbc], [o])
            outs.append(self.dma(out_ap[i * 128:(i + 1) * 128, :], o[:], [o], [out_r]))
        return outs


def _inp(nc, name, shape):
    return nc.dram_tensor(name, list(shape), F32, kind="ExternalInput").ap()


def _outp(nc, name, shape):
    return nc.dram_tensor(name, list(shape), F32, kind="ExternalOutput").ap()


def build_state(l, NT):
    nc = bass.Bass("TRN2", target_bir_lowering=False)
    TT = NT * 128
    h = _inp(nc, "h", [TT, D]); mask = _inp(nc, "mask", [1, TT]); g = _inp(nc, "mix_g", [1, D])
    w_in = _inp(nc, "w_in", [D, N_IN]); lbl = _inp(nc, "lbl", [2, 2048])
    cw = _inp(nc, "m2_cw", [4, 4096]); cb = _inp(nc, "m2_cb", [1, 4096])
    dtb = _inp(nc, "m2_dtb", [1, 32]); alog = _inp(nc, "m2_alog", [1, 32])
    o_shg = _outp(nc, "o_shg", [128, 16, 128]); o_dhg = _outp(nc, "o_dhg", [128, 16])
    o_sm2 = _outp(nc, "o_sm2", [128, 32, 64]); o_dm2 = _outp(nc, "o_dm2", [128, 32])
    k = K(nc, NT); P = k.P
    uT_d = P.dram([KC, 128, TT], BF16, "uT")
    with P.scope():
        k.norm_to_uT(h, T(h, Res()), g, uT_d)
    k.hg_setup_lb(lbl, l)
    Shg = P.sb([128, 16, 128], F32, "Shg"); dhg = P.sb([128, 16], F32, "dhg")
    P.op("dve", lambda e: e.memset(Shg[:], 0.0), (), [Shg])
    P.op("dve", lambda e: e.memset(dhg[:], 0.0), (), [dhg])
    with P.scope():
        k.hg_mixer(uT_d, w_in, mask, Shg, None, state_only=True, dec_out=dhg)
    outs = [k.dma(o_shg, Shg[:], [Shg], [T(o_shg, Res())]), k.dma(o_dhg, dhg[:], [dhg], [T(o_dhg, Res())])]
    k.m2_setup(w_in, cw, cb, dtb, alog, None, None, mask)
    Sm2 = P.sb([128, 32, 64], F32, "Sm2"); Sbf = P.sb([128, 32, 64], BF16, "Sm2b"); dm2 = P.sb([128, 32], F32, "dm2")
    P.op("dve", lambda e: e.memset(Sm2[:], 0.0), (), [Sm2])
    P.op("dve", lambda e: e.memset(dm2[:], 0.0), (), [dm2])
    with P.scope():
        k.m2_mixer(uT_d, w_in, Sm2, Sbf, None, state_only=True, dec_out=dm2)
    outs += [k.dma(o_sm2, Sm2[:], [Sm2], [T(o_sm2, Res())]), k.dma(o_dm2, dm2[:], [dm2], [T(o_dm2, Res())])]
    P.emit(outs)
    return nc


def build_main(l, NT, last):
    nc = bass.Bass("TRN2", target_bir_lowering=False)
    TT = NT * 128
    h = _inp(nc, "h", [TT, D]); mask = _inp(nc, "mask", [1, TT]); maskF = _inp(nc, "maskF", [1, 128])
    g = _inp(nc, "mix_g", [1, D]); w_in = _inp(nc, "w_in", [D, N_IN]); lbl = _inp(nc, "lbl", [2, 2048])
    hgng = _inp(nc, "hg_ng", [1, 2048])
    cw = _inp(nc, "m2_cw", [4, 4096]); cb = _inp(nc, "m2_cb", [1, 4096])
    dtb = _inp(nc, "m2_dtb", [1, 32]); alog = _inp(nc, "m2_alog", [1, 32]); dsk = _inp(nc, "m2_dsk", [1, 32])
    m2ng = _inp(nc, "m2_ng", [1, 2048])
    wbh = _inp(nc, "w_bhg", [2048, D]); wbm = _inp(nc, "w_bm2", [2048, D]); wout = _inp(nc, "w_out", [D, D])
    mem = _inp(nc, "mem", [N_MEM, D]); memg = _inp(nc, "mem_g", [1, D]); xag = _inp(nc, "xa_g", [1, D])
    wq = _inp(nc, "xa_wq", [D, D]); wkv = _inp(nc, "xa_wkv", [D, 2 * D]); wo = _inp(nc, "xa_wo", [D, D])
    ffg = _inp(nc, "ffn_g", [1, D]); wup = _inp(nc, "ffn_wup", [D, 2 * D_FF])
    fcw = _inp(nc, "ffn_cw", [3, D_FF]); fcb = _inp(nc, "ffn_cb", [1, D_FF]); wdn = _inp(nc, "ffn_wdn", [D_FF, D])
    ps_hg = _inp(nc, "ps_hg", [7, 128, 16, 128]); pd_hg = _inp(nc, "pd_hg", [7, 128, 16])
    ps_m2 = _inp(nc, "ps_m2", [7, 128, 32, 64]); pd_m2 = _inp(nc, "pd_m2", [7, 128, 32])
    if last:
        fing = _inp(nc, "fin_g", [1, D])
    h_out = _outp(nc, "h_out", [TT, D])
    k = K(nc, NT); P = k.P
    hin = T(h, Res())
    uT_d = P.dram([KC, 128, TT], BF16, "uT")
    hgT_d = P.dram([KC, 128, TT], BF16, "hgT")
    m2T_d = P.dram([KC, 128, TT], BF16, "m2T")
    mT_d = P.dram([KC, 128, TT], BF16, "mT")
    h1_d = P.dram([TT, D], F32, "h1")
    h2_d = P.dram([TT, D], F32, "h2")
    h3_d = T(h_out, Res()) if not last else P.dram([TT, D], F32, "h3")
    memT_d = P.dram([KC, 128, N_MEM], BF16, "memT")
    oT_d = P.dram([KC, 128, TT], BF16, "oT")
    aT_d = P.dram([D_FF // 128, 128, TT], BF16, "aT")
    with P.scope():
        k.norm_to_uT(h, hin, g, uT_d)
    with P.scope():
        k.hg_setup(lbl, l, hgng)
        Shg = P.sb([128, 16, 128], F32, "Shg")
        P.op("dve", lambda e: e.memset(Shg[:], 0.0), (), [Shg])
        Sm2 = P.sb([128, 32, 64], F32, "Sm2"); Sbf = P.sb([128, 32, 64], BF16, "Sm2b")
        P.op("dve", lambda e: e.memset(Sm2[:], 0.0), (), [Sm2])
        with P.scope():
            t1 = Ring([P.sb([128, 16, 128], F32, "pst") for _ in range(2)])
            d1 = Ring([P.sb([128, 16], F32, "pdt") for _ in range(2)])
            t2 = Ring([P.sb([128, 32, 64], F32, "pst2") for _ in range(2)])
            d2 = Ring([P.sb([128, 32], F32, "pdt2") for _ in range(2)])
            for j in range(7):
                a = t1.next(); b = d1.next(); c = t2.next(); d_ = d2.next()
                k.dma(a[:], ps_hg[j], [], [a]); k.dma(b[:], pd_hg[j], [], [b])
                k.dma(c[:], ps_m2[j], [], [c]); k.dma(d_[:], pd_m2[j], [], [d_])
                k.act(b[:], b[:], AF.Exp, [b], [b]); k.act(d_[:], d_[:], AF.Exp, [d_], [d_])
                k.tt(Shg[:], Shg[:], b[:].unsqueeze(2).to_broadcast([128, 16, 128]), ALU.mult, [Shg, b], [Shg])
                k.tt(Shg[:], Shg[:], a[:], ALU.add, [Shg, a], [Shg])
                k.tt(Sm2[:], Sm2[:], d_[:].unsqueeze(2).to_broadcast([128, 32, 64]), ALU.mult, [Sm2, d_], [Sm2])
                k.tt(Sm2[:], Sm2[:], c[:], ALU.add, [Sm2, c], [Sm2])
        k.act(Sbf[:], Sm2[:], AF.Copy, [Sm2], [Sbf])
        with P.scope():
            k.hg_mixer(uT_d, w_in, mask, Shg, hgT_d)
        k.m2_setup(w_in, cw, cb, dtb, alog, dsk, m2ng, mask)
        with P.scope():
            k.m2_mixer(uT_d, w_in, Sm2, Sbf, m2T_d)
    with P.scope():
        k.mixer_merge(uT_d, hgT_d, m2T_d, w_in, wbh, wbm, mT_d)
    with P.scope():
        k.dense_A_res(mT_d, KC, wout, h, hin, h1_d, NT)
    u3_d = P.dram([KC, 128, TT], BF16, "u3T")
    with P.scope():
        k.norm_to_uT(h1_d[:, :], h1_d, xag, u3_d)
    with P.scope():
        k.norm_to_uT(mem, T(mem, Res()), memg, memT_d, tiles=range(N_MEM // 128))
    with P.scope():
        k.xattn(u3_d, memT_d, wq, wkv, oT_d)
    with P.scope():
        k.dense_A_res(oT_d, KC, wo, h1_d[:, :], h1_d, h2_d, NT)
    u4_d = P.dram([KC, 128, TT], BF16, "u4T")
    with P.scope():
        k.norm_to_uT(h2_d[:, :], h2_d, ffg, u4_d)
    with P.scope():
        k.ffn_up(u4_d, wup, fcw, fcb, maskF, aT_d)
    with P.scope():
        k.dense_A_res(aT_d, D_FF // 128, wdn, h2_d[:, :], h2_d, h3_d, 6)
    if last:
        with P.scope():
            outs = k.final_norm(h3_d[:, :], h3_d, fing, h_out, T(h_out, Res()), range(NT))
    else:
        outs = [h3_d.r.last_write]
    stats = P.emit(outs)
    return nc, stats


_PROG_CACHE = {}


def _prog(kind, l, NT, last=False):
    key = (kind, l, NT, last)
    if key not in _PROG_CACHE:
        if kind == "state":
            _PROG_CACHE[key] = build_state(l, NT)
        else:
            _PROG_CACHE[key] = build_main(l, NT, last)[0]
    return _PROG_CACHE[key]


def _halo_slices(hfull, n_cores, TOWN):
    out = []
    for c in range(n_cores):
        s = c * TOWN
        if c == 0:
            blk = np.concatenate([np.zeros((128, hfull.shape[1]), np.float32), hfull[0:TOWN]], 0)
        else:
            blk = hfull[s - 128:s + TOWN]
        out.append(np.ascontiguousarray(blk, dtype=np.float32))
    return out


def kernel_impl(inputs, n_cores=8):
    f32 = np.float32
    x = np.asarray(inputs["x"], f32)
    SEQ = x.shape[1]
    TOWN = SEQ // n_cores
    NT = TOWN // 128 + 1
    TT = NT * 128
    mem = np.ascontiguousarray(np.asarray(inputs["mem"], f32)[0])
    g = lambda k: np.asarray(inputs[k], f32)
    row = lambda a: np.ascontiguousarray(a.reshape(1, -1))
    cores = list(range(n_cores))
    m_state, m_main, m_f = [], [], []
    for c in cores:
        ms = np.zeros((1, TT), f32); mm_ = np.zeros((1, TT), f32)
        lo = 128 if c == 0 else 3
        ms[0, lo:TOWN + 3] = 1.0
        mm_[0, lo:] = 1.0
        m_state.append(ms); m_main.append(mm_)
        m_f.append(np.zeros((1, 128), f32) if c == 0 else np.ones((1, 128), f32))
    hfull = np.ascontiguousarray(x[0])
    for l in range(DEPTH):
        hs = _halo_slices(hfull, n_cores, TOWN)
        common_state = dict(mix_g=row(g("mix_norm_g")[l]), w_in=np.ascontiguousarray(g("w_in")[l]), lbl=np.ascontiguousarray(g("hg_lb_logits")),
                            m2_cw=np.ascontiguousarray(g("m2_conv_w")[l]), m2_cb=row(g("m2_conv_b")[l]),
                            m2_dtb=row(g("m2_dt_bias")[l]), m2_alog=row(g("m2_A_log")[l]))
        nc = _prog("state", l, NT)
        res = run_bass_kernel_spmd(nc, [dict(common_state, h=hs[c], mask=m_state[c]) for c in cores], core_ids=cores)
        st = res.results
        last = (l == DEPTH - 1)
        common = dict(common_state, hg_ng=row(g("hg_norm_g")[l]), m2_dsk=row(g("m2_D")[l]), m2_ng=row(g("m2_norm_g")[l]),
                      w_bhg=np.ascontiguousarray(g("w_branch_hg")[l]), w_bm2=np.ascontiguousarray(g("w_branch_m2")[l]),
                      w_out=np.ascontiguousarray(g("w_out")[l]), mem=mem, mem_g=row(g("mem_norm_g")), xa_g=row(g("xa_norm_g")[l]),
                      xa_wq=np.ascontiguousarray(g("xa_wq")[l]), xa_wkv=np.ascontiguousarray(g("xa_wkv")[l]),
                      xa_wo=np.ascontiguousarray(g("xa_wo")[l]), ffn_g=row(g("ffn_norm_g")[l]),
                      ffn_wup=np.ascontiguousarray(g("ffn_w_up")[l]), ffn_cw=np.ascontiguousarray(g("ffn_conv_w")[l]),
                      ffn_cb=row(g("ffn_conv_b")[l]), ffn_wdn=np.ascontiguousarray(g("ffn_w_down")[l]))
        if last:
            common["fin_g"] = row(g("final_norm_g"))
        maps = []
        for c in cores:
            ps_hg = np.zeros((7, 128, 16, 128), f32); pd_hg = np.zeros((7, 128, 16), f32)
            ps_m2 = np.zeros((7, 128, 32, 64), f32); pd_m2 = np.zeros((7, 128, 32), f32)
            for j in range(7):
                src = c - 7 + j
                if src >= 0:
                    ps_hg[j] = st[src]["o_shg"]; pd_hg[j] = st[src]["o_dhg"]
                    ps_m2[j] = st[src]["o_sm2"]; pd_m2[j] = st[src]["o_dm2"]
            maps.append(dict(common, h=hs[c], mask=m_main[c], maskF=m_f[c], ps_hg=ps_hg, pd_hg=pd_hg, ps_m2=ps_m2, pd_m2=pd_m2))
        nc = _prog("main", l, NT, last)
        res = run_bass_kernel_spmd(nc, maps, core_ids=cores)
        hfull = np.concatenate([np.asarray(r["h_out"])[128:] for r in res.results], 0)
    return np.ascontiguousarray(hfull.reshape(1, SEQ, D).astype(np.float32))


def kernel(**inputs):
    return kernel_impl(inputs, 8)
```

```python
import contextlib
import numpy as np
import concourse.bass as bass
import concourse.mybir as mybir
from concourse.bass_utils import run_bass_kernel_spmd

F32 = mybir.dt.float32
BF16 = mybir.dt.bfloat16
AF = mybir.ActivationFunctionType
ALU = mybir.AluOpType
AX = mybir.AxisListType

D = 2048
KC = D // 128
DEPTH = 2
N_MEM = 256
HG_H = 16
M2_H = 32
M2_G = 8
D_FF = 5632
N_IN = 18464
C_Q, C_F, C_I, C_OG = 0, 2048, 4096, 6144
C_Z = 8192
C_X = 10240
C_B = C_X + 2048
C_C = C_B + 1024
C_DT = 14336
C_GHG = 14368
C_GM2 = 16416
EPS = 1e-6
M2_EPS = 1e-5

ENGS = ["pe", "act", "dve", "pool", "sp"]
DMA_WINDOW = 8


class Res:
    __slots__ = ("name", "last_write", "readers")

    def __init__(self, name=""):
        self.name = name
        self.last_write = None
        self.readers = []


class Op:
    __slots__ = ("eng", "fn", "deps", "signaled", "is_dma", "dma_idx", "tok", "cc")

    def __init__(self, eng, fn, is_dma=False):
        self.eng = eng
        self.fn = fn
        self.deps = []
        self.signaled = False
        self.is_dma = is_dma
        self.dma_idx = -1
        self.tok = None
        self.cc = False


class T:
    def __init__(self, t, res):
        self.t = t
        self.r = res

    def __getitem__(self, k):
        return self.t[k]


class Prog:
    def __init__(self, nc):
        self.nc = nc
        self.ops = {e: [] for e in ENGS}
        self.dmas = {e: [] for e in ENGS}
        self.stack = contextlib.ExitStack()
        self.scopes = [self.stack]
        self.n = 0
        self.pending = {e: [] for e in ENGS}

    def sb(self, shape, dt, name=None):
        self.n += 1
        name = (name or "sb") + f"_{self.n}"
        t = self.scopes[-1].enter_context(self.nc.sbuf_tensor(name, list(shape), dt))
        return T(t, Res(name))

    @contextlib.contextmanager
    def scope(self):
        st = contextlib.ExitStack()
        self.scopes.append(st)
        try:
            yield
        finally:
            self.scopes.pop()
            self.barrier()
            st.close()

    def ps(self, shape, dt, name=None):
        self.n += 1
        name = (name or "ps") + f"_{self.n}"
        t = self.stack.enter_context(self.nc.psum_tensor(name, list(shape), dt))
        return T(t, Res(name))

    def dram(self, shape, dt, name=None):
        self.n += 1
        name = (name or "dr") + f"_{self.n}"
        t = self.nc.dram_tensor(name, list(shape), dt)
        return T(t.ap(), Res(name))

    def _record(self, o, reads, writes):
        deps = []
        seen = set()

        def add(d):
            if d is None or d is o or id(d) in seen:
                return
            if d.eng == "pe" and o.eng == "pe" and not d.is_dma:
                return
            seen.add(id(d))
            deps.append(d)

        for r in reads:
            add(r.r.last_write)
        for w in writes:
            add(w.r.last_write)
            for rd in w.r.readers:
                add(rd)
        for d in self.pending[o.eng]:
            add(d)
        self.pending[o.eng] = []
        if o.is_dma:
            q = self.dmas[o.eng]
            o.dma_idx = len(q)
            if o.dma_idx >= DMA_WINDOW:
                add(q[o.dma_idx - DMA_WINDOW])
            q.append(o)
        for d in deps:
            d.signaled = True
        o.deps = deps
        for r in reads:
            r.r.readers.append(o)
        for w in writes:
            w.r.last_write = o
            w.r.readers = []
        self.ops[o.eng].append(o)
        return o

    def op(self, eng, fn, reads=(), writes=()):
        return self._record(Op(eng, fn), reads, writes)

    def dma(self, eng, fn, reads=(), writes=()):
        return self._record(Op(eng, fn, is_dma=True), reads, writes)

    def barrier(self):
        tails = []
        for e in ENGS:
            if self.ops[e]:
                tails.append(self.ops[e][-1])
            tails.extend(self.dmas[e][-DMA_WINDOW:])
        for e in ENGS:
            self.pending[e] = list(tails)

    def emit(self, final_deps):
        nc = self.nc
        st = self.stack
        sems = {e: st.enter_context(nc.semaphore(f"s_{e}")) for e in ENGS}
        dsem = {
            e: [st.enter_context(nc.semaphore(f"d_{e}{i}")) for i in range(DMA_WINDOW)]
            for e in ENGS
            if self.dmas[e]
        }
        fin = Op("sp", None)
        fin.deps = list(final_deps)
        for d in fin.deps:
            d.signaled = True
        self.ops["sp"].append(fin)
        ccs = []
        for e in ENGS:
            c = 0
            for o in self.ops[e]:
                if o.is_dma:
                    o.tok = (dsem[e][o.dma_idx % DMA_WINDOW], 16 * (o.dma_idx // DMA_WINDOW + 1))
                elif o.cc:
                    o.tok = (st.enter_context(nc.semaphore(f"cc{len(ccs)}")), 1)
                    ccs.append(o)
                elif o.signaled:
                    c += 1
                    o.tok = (sems[e], c)
        engobj = {"pe": "tensor", "act": "scalar", "dve": "vector", "pool": "gpsimd", "sp": "sync"}
        stats = {}
        with nc.Block() as block:
            for e in ENGS:
                ops = self.ops[e]
                if not ops:
                    continue
                nwait = [0]

                def body(eng, ops=ops, nwait=nwait):
                    waited = {}
                    for o in ops:
                        need = {}
                        for d in o.deps:
                            s, v = d.tok
                            k = id(s)
                            if waited.get(k, 0) < v and need.get(k, (None, 0))[1] < v:
                                need[k] = (s, v)
                        for k, (s, v) in need.items():
                            eng.wait_ge(s, v)
                            waited[k] = v
                            nwait[0] += 1
                        if o.fn is None:
                            continue
                        ins = o.fn(eng)
                        if o.is_dma:
                            ins.then_inc(o.tok[0], 16)
                        elif o.cc:
                            ins.then_inc(o.tok[0], 1)
                        elif o.signaled:
                            ins.then_inc(o.tok[0], 1)

                getattr(block, engobj[e])(body)
                stats[e] = (len(ops), nwait[0])
        return stats


class Ring:
    def __init__(self, items):
        self.items = items
        self.i = 0

    def next(self):
        x = self.items[self.i % len(self.items)]
        self.i += 1
        return x


class K:
    def __init__(self, nc, NT):
        self.nc = nc
        self.P = Prog(nc)
        self.NT = NT
        self.TT = NT * 128
        self.consts()

    def act(self, out, in_, func, reads, writes, **kw):
        return self.P.op("act", lambda e: e.activation(out=out, in_=in_, func=func, **kw), reads, writes)

    def tt(self, out, in0, in1, op, reads, writes, eng="dve"):
        return self.P.op(eng, lambda e: e.tensor_tensor(out=out, in0=in0, in1=in1, op=op), reads, writes)

    def ts(self, out, in0, s1, s2, op0, op1, reads, writes, eng="dve"):
        if op1 is None:
            return self.P.op(eng, lambda e: e.tensor_scalar(out=out, in0=in0, scalar1=s1, scalar2=None, op0=op0), reads, writes)
        return self.P.op(eng, lambda e: e.tensor_scalar(out=out, in0=in0, scalar1=s1, scalar2=s2, op0=op0, op1=op1), reads, writes)

    def stt(self, out, in0, scalar, in1, op0, op1, reads, writes):
        return self.P.op("dve", lambda e: e.scalar_tensor_tensor(out=out, in0=in0, scalar=scalar, in1=in1, op0=op0, op1=op1), reads, writes)

    def mm(self, out, lhsT, rhs, start, stop, reads, writes):
        return self.P.op("pe", lambda e: e.matmul(out, lhsT, rhs, start=start, stop=stop), reads, writes)

    def tr(self, out, in_, ident, reads, writes):
        return self.P.op("pe", lambda e: e.transpose(out, in_, ident), reads, writes)

    def dma(self, out, in_, reads, writes, eng="sp"):
        return self.P.dma(eng, lambda e: e.dma_start(out=out, in_=in_), reads, writes)

    def consts(self):
        P = self.P
        self.idf = P.sb([128, 128], F32, "idf")
        self.idb = P.sb([128, 128], BF16, "idb")
        self.ones_f = P.sb([128, 128], F32, "ones_f")
        self.ones_b = P.sb([128, 128], BF16, "ones_b")
        self.triu = P.sb([128, 128], F32, "triu")
        self.epsc = P.sb([128, 4], F32, "epsc")
        iot = P.sb([128, 128], F32, "iot")
        P.op("pool", lambda e: e.iota(iot[:], [[1, 128]], base=0, channel_multiplier=-1,
                                      allow_small_or_imprecise_dtypes=True), (), [iot])
        self.ts(self.idf[:], iot[:], 0.0, None, ALU.is_equal, None, [iot], [self.idf])
        self.ts(self.idb[:], iot[:], 0.0, None, ALU.is_equal, None, [iot], [self.idb])
        self.ts(self.triu[:], iot[:], 0.0, None, ALU.is_ge, None, [iot], [self.triu])
        P.op("dve", lambda e: e.memset(self.ones_f[:], 1.0), (), [self.ones_f])
        P.op("dve", lambda e: e.memset(self.ones_b[:], 1.0), (), [self.ones_b])
        P.op("dve", lambda e: e.memset(self.epsc[:, 0:1], EPS), (), [self.epsc])
        P.op("dve", lambda e: e.memset(self.epsc[:, 1:2], M2_EPS), (), [self.epsc])
        P.op("dve", lambda e: e.memset(self.epsc[:, 2:3], 1.0), (), [self.epsc])
        self.MAXS = 768
        self.rmask = P.sb([128, self.MAXS], F32, "rmask")
        P.op("dve", lambda e: e.memset(self.rmask[:], 1.0), (), [self.rmask])
        P.op("dve", lambda e: e.memset(self.rmask[:].rearrange("p (c j) -> p c j", j=64)[:, :, 0:1], 0.0), (), [self.rmask])
        self.pp = Ring([P.ps([128, 512], F32, f"pp{i}") for i in range(4)])
        self.pq = Ring([P.ps([128, 512], F32, f"pq{i}") for i in range(2)])
        self.pt = Ring([P.ps([128, 1024], BF16, f"pt{i}") for i in range(2)])

    def load_cols(self, vec_ap, n, name):
        P = self.P
        rows = P.sb([n, 128], F32, name + "_r")
        cols = P.sb([128, n], F32, name)
        self.dma(rows[:], vec_ap, [], [rows])
        ps = self.pq.next()
        self.tr(ps[:, 0:n], rows[:], self.idf[0:n, 0:n], [rows, self.idf], [ps])
        self.act(cols[:], ps[:, 0:n], AF.Copy, [ps], [cols])
        return cols

    def norm_to_uT(self, h_ap, hres, g_ap, uT_d, tiles=None):
        P = self.P
        gbc = P.sb([128, D], F32, "gbc")
        self.dma(gbc[:], g_ap.partition_broadcast(128), [], [gbc])
        hb = Ring([P.sb([128, D], F32, "hb") for _ in range(4)])
        ub = Ring([P.sb([128, D], BF16, "ub") for _ in range(4)])
        junk = P.sb([128, D], BF16, "junk")
        stt_ = Ring([P.sb([128, 4], F32, "nst") for _ in range(4)])
        uo = Ring([P.sb([128, KC, 128], BF16, "uo") for _ in range(4)])
        for i in (tiles if tiles is not None else range(self.NT)):
            h = hb.next(); u = ub.next(); s = stt_.next(); o = uo.next()
            self.dma(h[:], h_ap[i * 128:(i + 1) * 128, :], [hres], [h])
            self.act(junk[:], h[:], AF.Square, [h], [junk, s], accum_out=s[:, 0:1])
            self.act(s[:, 1:2], s[:, 0:1], AF.Ln, [s, self.epsc], [s], scale=1.0 / D, bias=self.epsc[:, 0:1])
            self.act(s[:, 2:3], s[:, 1:2], AF.Exp, [s], [s], scale=-0.5)
            self.stt(u[:], h[:], s[:, 2:3], gbc[:], ALU.mult, ALU.mult, [h, s, gbc], [u])
            for half in range(2):
                pt = self.pt.next()
                for j in range(8):
                    kc = half * 8 + j
                    self.tr(pt[:, j * 128:(j + 1) * 128], u[:, kc * 128:(kc + 1) * 128], self.idb[:], [u, self.idb], [pt])
                self.act(o[:, half * 8:(half + 1) * 8, :], pt[:].rearrange("p (j t) -> p j t", j=8), AF.Copy, [pt], [o])
            self.dma(uT_d[:, :, i * 128:(i + 1) * 128].rearrange("k p t -> p k t"), o[:], [o], [uT_d])

    def hg_setup(self, lbl_ap, l, hgng_ap):
        lg0 = self.load_cols(lbl_ap[0:1, :].rearrange("o (h p) -> (o h) p", p=128), 16, "lg0")
        lg1 = self.load_cols(lbl_ap[1:2, :].rearrange("o (h p) -> (o h) p", p=128), 16, "lg1")
        self.lb = self.P.sb([128, 16], F32, "lb")
        self.oml = self.P.sb([128, 16], F32, "oml")
        if l == 0:
            self.tt(self.lb[:], lg0[:], lg0[:], ALU.subtract, [lg0], [self.lb])
        else:
            self.tt(self.lb[:], lg1[:], lg0[:], ALU.subtract, [lg0, lg1], [self.lb])
            self.act(self.lb[:], self.lb[:], AF.Sigmoid, [self.lb], [self.lb])
        self.ts(self.oml[:], self.lb[:], -1.0, 1.0, ALU.mult, ALU.add, [self.lb], [self.oml])
        if hgng_ap is not None:
            self.hgng = self.load_cols(hgng_ap.rearrange("o (h p) -> (o h) p", p=128), 16, "hgng")

    def hg_setup_lb(self, lbl_ap, l):
        self.hg_setup(lbl_ap, l, None)

    def load_uT_seg(self, uT_d, t0, nt, ring):
        u = ring.next()
        self.dma(u[:, :, 0:nt * 128], uT_d[:, :, t0 * 128:(t0 + nt) * 128].rearrange("k p t -> p k t"), [uT_d], [u])
        return u

    @staticmethod
    def interleave(gens):
        gens = list(gens)
        while gens:
            for g_ in list(gens):
                try:
                    next(g_)
                except StopIteration:
                    gens.remove(g_)

    def pipeline(self, units, *stages):
        units = list(units)
        ns = len(stages)
        for i in range(len(units) + ns - 1):
            gens = []
            for j in range(ns - 1, -1, -1):
                if 0 <= i - j < len(units):
                    gens.append(stages[j](units[i - j]))
            self.interleave(gens)

    def make_segs(self, maxt):
        n = -(-self.NT // maxt)
        base, extra = divmod(self.NT, n)
        segs, t = [], 0
        for i in range(n):
            c = base + (1 if i < extra else 0)
            segs.append((t, c)); t += c
        return segs

    def hg_mixer(self, uT_d, w_ap, mask_ap, S, hgT_d, state_only=False, dec_out=None):
        P = self.P
        MAXS = self.MAXS
        MC = MAXS // 64
        so = state_only
        useg = Ring([P.sb([128, KC, MAXS], BF16, "useg") for _ in range(1)])
        wts = Ring([P.sb([128, KC, 4, 128], BF16, "hgw") for _ in range(2)])
        mk = Ring([P.sb([128, MAXS], F32, "mk") for _ in range(2)])

        def ring(n, shape, dt, nm):
            return Ring([P.sb(shape, dt, nm) for _ in range(n)])

        rA = ring(2, [128, MAXS], F32, "hA")
        rB, rC, rD = (ring(1, [128, MAXS], F32, n) for n in ("hB", "hC", "hD"))
        rE = ring(2, [128, 2 if so else MAXS], F32, "hE")
        rG = ring(1, [128, MAXS], BF16, "hG")
        rI = ring(2, [128, MAXS], BF16, "hI")
        q_ = 2 if so else MAXS
        rSG = ring(3, [128, q_], F32, "hSG")
        rO = ring(1, [128, q_], F32, "hO")
        rF, rHo = (ring(2, [128, q_], BF16, n) for n in ("hF", "hHo"))
        rsq = ring(1, [128, q_], F32, "hsq")
        rcs = ring(2, [128, 4, 16], F32, "hcs")
        rkv = ring(2, [64, MC, 256], BF16, "hkv")
        rsT = ring(2, [64, MC, 2 if so else 64], BF16, "hsT")
        rtm = ring(2, [128, MC, 128], F32, "htm")
        rSp = ring(3, [128, 128], BF16, "hSp")
        rrs = ring(2, [128, 512], F32, "hrs")
        scale = 128 ** -0.5
        seg = {}

        def stage0(un):
            (t0, nt, hd) = un
            NS = nt * 128
            NCH = NS // 64
            if hd == 0:
                seg["u"] = self.load_uT_seg(uT_d, t0, nt, useg)
                m_ = mk.next()
                self.dma(m_[:, 0:NS], mask_ap[0:1, t0 * 128:t0 * 128 + NS].partition_broadcast(128), [], [m_])
                seg["mask"] = m_
            u = seg["u"]; mask = seg["mask"]
            chunks = [(n0, min(512, NS - n0)) for n0 in range(0, NS, 512)]
            wt = wts.next()
            mats = [(1, C_F), (2, C_I)] if so else [(1, C_F), (0, C_Q), (2, C_I), (3, C_OG)]
            for (m, c0) in mats:
                self.dma(wt[:, :, m, :], w_ap[:, c0 + hd * 128:c0 + (hd + 1) * 128].rearrange("(k p) c -> p k c", p=128),
                         [], [wt], eng="pool")
            A, E, I, SG = (r.next() for r in (rA, rE, rI, rSG))
            st_ = dict(NS=NS, NCH=NCH, chunks=chunks, SG=SG, A=A, E=E, I=I, mask=mask, t0=t0, hd=hd)
            seg[("st", t0, hd)] = st_
            for (m, c0) in mats:
                for (n0, nn) in chunks:
                    ps = self.pp.next()
                    for kc in range(KC):
                        self.mm(ps[:, 0:nn], wt[:, kc, m, :], u[:, kc, n0:n0 + nn], kc == 0, kc == KC - 1, [wt, u], [ps])
                    if m == 1:
                        self.act(A[:, n0:n0 + nn], ps[:, 0:nn], AF.Sigmoid, [ps], [A])
                    elif m == 0:
                        self.act(E[:, n0:n0 + nn], ps[:, 0:nn], AF.Copy, [ps], [E], scale=scale)
                    elif m == 2:
                        self.act(I[:, n0:n0 + nn], ps[:, 0:nn], AF.Copy, [ps], [I])
                    else:
                        self.act(SG[:, n0:n0 + nn], ps[:, 0:nn], AF.Sigmoid, [ps], [SG])
                    yield

        def stage1(un):
            (t0, nt, hd) = un
            st_ = seg[("st", t0, hd)]
            NS, NCH, A, E, I, mask = (st_[k_] for k_ in ("NS", "NCH", "A", "E", "I", "mask"))
            B, C, Dd, G = (r.next() for r in (rB, rC, rD, rG))
            F = rF.next()
            cs = rcs.next(); kv = rkv.next(); sTa = rsT.next(); tm = rtm.next()
            st_.update(F=F, cs=cs, kv=kv, sT=sTa, tm=tm)
            self.ts(A[:, 0:NS], A[:, 0:NS], self.oml[:, hd:hd + 1], self.lb[:, hd:hd + 1], ALU.mult, ALU.add, [A, self.oml, self.lb], [A])
            self.ts(B[:, 0:NS], A[:, 0:NS], -1.0, 1.0, ALU.mult, ALU.add, [A], [B])
            self.tt(B[:, 0:NS], B[:, 0:NS], mask[:, 0:NS], ALU.mult, [B, mask], [B])
            self.act(A[:, 0:NS], A[:, 0:NS], AF.Ln, [A], [A])
            self.tt(A[:, 0:NS], A[:, 0:NS], mask[:, 0:NS], ALU.mult, [A, mask], [A])
            P.op("dve", lambda e, C=C, A=A, NS=NS: e.tensor_tensor_scan(out=C[:, 0:NS], data0=self.rmask[:, 0:NS], data1=A[:, 0:NS],
                                                                    initial=0.0, op0=ALU.mult, op1=ALU.add), [A, self.rmask], [C])
            C3 = C[:, 0:NS].rearrange("p (c j) -> p c j", j=64)
            A3 = A[:, 0:NS].rearrange("p (c j) -> p c j", j=64)
            self.tt(A3, C3, C3[:, :, 31:32].to_broadcast([128, NCH, 64]), ALU.subtract, [C], [A])
            self.tt(cs[:, 3, 0:NCH], C3[:, :, 63], C3[:, :, 31], ALU.subtract, [C], [cs])
            self.act(cs[:, 0, 0:NCH], C3[:, :, 63], AF.Exp, [C], [cs])
            self.act(cs[:, 1, 0:NCH], cs[:, 3, 0:NCH], AF.Exp, [cs], [cs])
            self.act(cs[:, 2, 0:NCH], C3[:, :, 31], AF.Exp, [C], [cs])
            if dec_out is not None:
                P.op("dve", lambda e, cs=cs, C3=C3, NCH=NCH: e.tensor_reduce(out=cs[:, 3, 0:1], in_=C3[:, :, 63], axis=AX.X, op=ALU.add), [C], [cs])
                self.tt(dec_out[:, hd:hd + 1], dec_out[:, hd:hd + 1], cs[:, 3, 0:1], ALU.add, [dec_out, cs], [dec_out])
            self.act(Dd[:, 0:NS], A[:, 0:NS], AF.Exp, [A], [Dd], scale=-1.0)
            self.tt(G[:, 0:NS], B[:, 0:NS], Dd[:, 0:NS], ALU.mult, [B, Dd], [G])
            if not so:
                self.act(A[:, 0:NS], A[:, 0:NS], AF.Exp, [A], [A])
                self.tt(F[:, 0:NS], E[:, 0:NS], A[:, 0:NS], ALU.mult, [E, A], [F])
            yield
            for c in range(NCH):
                c0 = c * 64
                pt = self.pt.next()
                self.tr(pt[0:64, 0:128], G[:, c0:c0 + 64], self.idb[:], [G, self.idb], [pt])
                self.tr(pt[0:64, 128:256], I[:, c0:c0 + 64], self.idb[:], [I, self.idb], [pt])
                self.act(kv[:, c, :], pt[0:64, 0:256], AF.Copy, [pt], [kv])
                if not so:
                    psc = self.pq.next()
                    self.mm(psc[0:64, 0:64], G[:, c0:c0 + 64], F[:, c0:c0 + 64], True, True, [G, F], [psc])
                    self.tt(sTa[:, c, :], psc[0:64, 0:64], self.triu[0:64, 0:64], ALU.mult, [psc, self.triu], [sTa])
                pst = self.pq.next()
                self.mm(pst[:, 0:128], kv[:, c, 0:128], kv[:, c, 128:256], True, True, [kv], [pst])
                self.ts(tm[:, c, :], pst[:, 0:128], cs[:, 1, c:c + 1], None, ALU.mult, None, [pst, cs], [tm])
                yield

        def stage2(un):
            (t0, nt, hd) = un
            d = seg.pop(("st", t0, hd))
            NS, NCH, chunks = d["NS"], d["NCH"], d["chunks"]
            SG, F, cs, kv, sTa, tm = (d[k_] for k_ in ("SG", "F", "cs", "kv", "sT", "tm"))
            O = rO.next(); Ho = rHo.next()
            Sh = S[:, hd, :]
            for c in range(NCH):
                c0 = c * 64
                if not so:
                    Sp = rSp.next()
                    self.ts(Sp[:], Sh, cs[:, 2, c:c + 1], None, ALU.mult, None, [S, cs], [Sp])
                self.stt(Sh, Sh, cs[:, 0, c:c + 1], tm[:, c, :], ALU.mult, ALU.add, [S, cs, tm], [S])
                if not so:
                    po = self.pq.next()
                    self.mm(po[:, 0:64], kv[:, c, 128:256], sTa[:, c, :], True, False, [kv, sTa], [po])
                    self.mm(po[:, 0:64], Sp[:], F[:, c0:c0 + 64], False, True, [Sp, F], [po])
                    self.tt(O[:, c0:c0 + 64], po[:, 0:64], SG[:, c0:c0 + 64], ALU.mult, [po, SG], [O])
                yield
            if so:
                return
            sq = rsq.next()
            self.tt(sq[:, 0:NS], O[:, 0:NS], O[:, 0:NS], ALU.mult, [O], [sq])
            for (n0, nn) in chunks:
                ps = self.pp.next()
                self.mm(ps[:, 0:nn], self.ones_f[:], sq[:, n0:n0 + nn], True, True, [self.ones_f, sq], [ps])
                rs = rrs.next()
                self.act(rs[:, 0:nn], ps[:, 0:nn], AF.Ln, [ps, self.epsc], [rs], scale=1.0 / 128, bias=self.epsc[:, 0:1])
                self.act(rs[:, 0:nn], rs[:, 0:nn], AF.Exp, [rs], [rs], scale=-0.5)
                self.stt(Ho[:, n0:n0 + nn], O[:, n0:n0 + nn], self.hgng[:, hd:hd + 1], rs[:, 0:nn], ALU.mult, ALU.mult,
                         [O, self.hgng, rs], [Ho])
                yield
            self.dma(hgT_d[hd, :, t0 * 128:t0 * 128 + NS], Ho[:, 0:NS], [Ho], [hgT_d])

        units = [(t0, nt, hd) for (t0, nt) in self.make_segs(MAXS // 128) for hd in range(HG_H)]
        self.pipeline(units, stage0, stage1, stage2)

    def m2_setup(self, w_ap, convw_ap, convb_ap, dtb_ap, alog_ap, dsk_ap, ng_ap, mask_ap):
        P = self.P
        self.cw = [self.load_cols(convw_ap[k:k + 1, :].rearrange("o (j p) -> (o j) p", p=128), 32, f"cw{k}") for k in range(4)]
        self.cbias = self.load_cols(convb_ap.rearrange("o (j p) -> (o j) p", p=128), 32, "cbias")
        self.maskT = self.load_cols(mask_ap.rearrange("o (j p) -> (o j) p", p=128), self.NT, "maskT")
        self.dtb = P.sb([128, 32], F32, "dtb")
        self.negA = P.sb([128, 32], F32, "negA")
        self.dsk = P.sb([128, 32], F32, "dsk")
        self.m2ng_ap = ng_ap
        self.dma(self.dtb[:], dtb_ap.partition_broadcast(128), [], [self.dtb])
        self.dma(self.negA[:], alog_ap.partition_broadcast(128), [], [self.negA])
        if dsk_ap is not None:
            self.dma(self.dsk[:], dsk_ap.partition_broadcast(128), [], [self.dsk])
        self.act(self.negA[:], self.negA[:], AF.Exp, [self.negA], [self.negA])
        self.ts(self.negA[:], self.negA[:], -1.0, None, ALU.mult, None, [self.negA], [self.negA])
        self.wdt = P.sb([128, KC, 32], BF16, "wdt")
        self.dma(self.wdt[:], w_ap[:, C_DT:C_DT + 32].rearrange("(k p) c -> p k c", p=128), [], [self.wdt], eng="pool")
        self.strict = P.sb([128, 128], F32, "strict")
        self.ts(self.strict[:], self.triu[:], -1.0, 1.0, ALU.mult, ALU.add, [self.triu], [self.strict])
        self.carry = P.sb([128, 8, 4, 3], F32, "carry")
        P.op("dve", lambda e: e.memset(self.carry[:], 0.0), (), [self.carry])

    def m2_mixer(self, uT_d, w_ap, S, Sbf, m2T_d, state_only=False, dec_out=None):
        P = self.P
        MT = 5
        MAXS = MT * 128
        so = state_only

        def ring(n, shape, dt, nm):
            return Ring([P.sb(shape, dt, nm) for _ in range(n)])

        useg = ring(1, [128, KC, MAXS], BF16, "useg")
        wts = ring(2, [128, KC, 768], BF16, "m2w")
        rxr = ring(2, [128, 4, 3 + MAXS], F32, "xr")
        racc = ring(1, [128, 2, MAXS], F32, "xacc")
        rtmpc = ring(2, [128, MAXS], F32, "ctmp")
        rBT = ring(1, [128, MAXS], BF16, "BT")
        rCT = ring(2, [128, 2 if so else MAXS], BF16, "CT")
        rzs = ring(3, [128, 1 if so else MT, 256], F32, "zs")
        rm2o = ring(2, [128, 2, 2 if so else MAXS], BF16, "m2o")
        rng = ring(2, [128, 2 if so else 256], F32, "m2ngs")
        rybuf = ring(2, [128, 1 if so else MT, 256], F32, "ybuf")
        rxddA = ring(2, [128, MT, 256], BF16, "xddA")
        rBtmA = ring(2, [128, MT, 128], BF16, "BtmA")
        rps = Ring([dict((n, P.sb([128, MT, 32], F32, n)) for n in ("dtS", "aS", "cumS", "lastS", "ecS", "elS", "ddS")) for _ in range(2)])
        rxs = ring(2, [128, 256], F32, "xs")
        rxdt = ring(2, [128, 4, 64], BF16, "xdt")
        q_ = 2 if so else 128
        rcbm = ring(2, [128, q_], F32, "cbm")
        rM1 = ring(2, [128, 4, q_], F32, "M1")
        rEL = ring(2, [128, 4, q_], F32, "EL")
        rWT = ring(2, [128, 4, q_], BF16, "WT")
        ry = ring(2, [128, 2 * q_], F32, "y")
        ry2 = ring(2, [128, 2 * q_], F32, "y2")
        ryn = ring(2, [128, 2 * q_], BF16, "yn")
        rst = ring(2, [128, 4], F32, "mst")
        rtS = ring(2, [128, 4, 64], F32, "tS")
        junk = P.sb([128, 256], BF16, "mjunk")
        seg = {}

        def seg_prep(t0, nt):
            u = self.load_uT_seg(uT_d, t0, nt, useg)
            seg["u"] = u
            ps_ = rps.next()
            seg["ps"] = ps_
            dtS, aS, cumS, lastS, ecS, elS, ddS = (ps_[n] for n in ("dtS", "aS", "cumS", "lastS", "ecS", "elS", "ddS"))
            for ti in range(nt):
                ps = self.pq.next()
                for kc in range(KC):
                    self.mm(ps[:, 0:32], u[:, kc, ti * 128:(ti + 1) * 128], self.wdt[:, kc, :], kc == 0, kc == KC - 1, [u, self.wdt], [ps])
                self.tt(dtS[:, ti, :], ps[:, 0:32], self.dtb[:], ALU.add, [ps, self.dtb], [dtS])
            self.act(dtS[:, 0:nt, :], dtS[:, 0:nt, :], AF.Exp, [dtS], [dtS])
            self.act(dtS[:, 0:nt, :], dtS[:, 0:nt, :], AF.Ln, [dtS, self.epsc], [dtS], bias=self.epsc[:, 2:3])
            for ti in range(nt):
                self.ts(dtS[:, ti, :], dtS[:, ti, :], self.maskT[:, t0 + ti:t0 + ti + 1], None, ALU.mult, None, [dtS, self.maskT], [dtS])
                self.tt(aS[:, ti, :], dtS[:, ti, :], self.negA[:], ALU.mult, [dtS, self.negA], [aS])
            for ti in range(nt):
                ps = self.pq.next()
                self.mm(ps[:, 0:32], self.triu[:], aS[:, ti, :], True, True, [self.triu, aS], [ps])
                self.mm(ps[:, 32:64], self.ones_f[:], aS[:, ti, :], True, True, [self.ones_f, aS], [ps])
                self.act(cumS[:, ti, :], ps[:, 0:32], AF.Copy, [ps], [cumS])
                self.act(lastS[:, ti, :], ps[:, 32:64], AF.Copy, [ps], [lastS])
                if dec_out is not None:
                    self.tt(dec_out[:], dec_out[:], lastS[:, ti, :], ALU.add, [dec_out, lastS], [dec_out])
            self.act(ecS[:, 0:nt, :], cumS[:, 0:nt, :], AF.Exp, [cumS], [ecS])
            self.act(elS[:, 0:nt, :], lastS[:, 0:nt, :], AF.Exp, [lastS], [elS])
            self.tt(ddS[:, 0:nt, :], lastS[:, 0:nt, :], cumS[:, 0:nt, :], ALU.subtract, [lastS, cumS], [ddS])
            self.act(ddS[:, 0:nt, :], ddS[:, 0:nt, :], AF.Exp, [ddS], [ddS])
            self.tt(ddS[:, 0:nt, :], ddS[:, 0:nt, :], dtS[:, 0:nt, :], ALU.mult, [ddS, dtS], [ddS])

        def stage0(un):
            (t0, nt, gi) = un
            NS = nt * 128
            if gi == 0:
                seg_prep(t0, nt)
                yield
            u = seg["u"]
            chunks = [(n0, min(512, NS - n0)) for n0 in range(0, NS, 512)]
            wt = wts.next()
            srcs = [(0, C_Z + gi * 256, 256), (256, C_X + gi * 256, 256), (512, C_B + gi * 128, 128), (640, C_C + gi * 128, 128)]
            for (o0, c0, w_) in srcs:
                if so and o0 in (0, 640):
                    continue
                self.dma(wt[:, :, o0:o0 + w_], w_ap[:, c0:c0 + w_].rearrange("(k p) c -> p k c", p=128), [], [wt], eng="pool")
            xr = rxr.next(); zs = rzs.next()
            seg[("st", t0, gi)] = dict(xr=xr, zs=zs, ps=seg["ps"], chunks=chunks)
            blks = [0, 1, 2] if so else [0, 1, 2, 3]
            if not so:
                for ti in range(nt):
                    ps = self.pp.next()
                    for kc in range(KC):
                        self.mm(ps[:, 0:256], u[:, kc, ti * 128:(ti + 1) * 128], wt[:, kc, 0:256], kc == 0, kc == KC - 1, [u, wt], [ps])
                    self.act(zs[:, ti, :], ps[:, 0:256], AF.Silu, [ps], [zs])
                    yield
            for b in blks:
                for (n0, nn) in chunks:
                    ps = self.pp.next()
                    for kc in range(KC):
                        self.mm(ps[:, 0:nn], wt[:, kc, 256 + b * 128:256 + (b + 1) * 128], u[:, kc, n0:n0 + nn], kc == 0, kc == KC - 1, [wt, u], [ps])
                    self.act(xr[:, b, 3 + n0:3 + n0 + nn], ps[:, 0:nn], AF.Copy, [ps], [xr])
                    yield

        def stage1(un):
            (t0, nt, gi) = un
            NS = nt * 128
            st_ = seg[("st", t0, gi)]
            xr = st_["xr"]; ps_ = st_["ps"]
            dtS, aS, ddS = ps_["dtS"], ps_["aS"], ps_["ddS"]
            acc = racc.next(); BT = rBT.next(); CT = rCT.next(); m2o = rm2o.next()
            ybuf = rybuf.next(); xddA = rxddA.next(); BtmA = rBtmA.next(); ngs = rng.next()
            if not so:
                self.dma(ngs[:], self.m2ng_ap[0:1, gi * 256:(gi + 1) * 256].partition_broadcast(128), [], [ngs])
            st_.update(CT=CT, m2o=m2o, ybuf=ybuf, xddA=xddA, BtmA=BtmA, ngs=ngs)
            blks = [0, 1, 2] if so else [0, 1, 2, 3]
            for b in blks:
                j = (gi * 2 + b) if b < 2 else (16 + gi if b == 2 else 24 + gi)
                self.act(xr[:, b, 0:3], self.carry[:, gi, b, :], AF.Copy, [self.carry], [xr])
                self.act(self.carry[:, gi, b, :], xr[:, b, NS:NS + 3], AF.Copy, [xr], [self.carry])
                tmpc = rtmpc.next()
                self.ts(tmpc[:, 0:NS], xr[:, b, 3:3 + NS], self.cw[3][:, j:j + 1], self.cbias[:, j:j + 1], ALU.mult, ALU.add,
                        [xr, self.cw[3], self.cbias], [tmpc])
                for k_ in range(3):
                    self.stt(tmpc[:, 0:NS], xr[:, b, k_:k_ + NS], self.cw[k_][:, j:j + 1], tmpc[:, 0:NS], ALU.mult, ALU.add,
                             [xr, self.cw[k_], tmpc], [tmpc])
                if b < 2:
                    self.act(acc[:, b, 0:NS], tmpc[:, 0:NS], AF.Silu, [tmpc], [acc])
                elif b == 2:
                    self.act(BT[:, 0:NS], tmpc[:, 0:NS], AF.Silu, [tmpc], [BT])
                else:
                    self.act(CT[:, 0:NS], tmpc[:, 0:NS], AF.Silu, [tmpc], [CT])
                yield
            hs = slice(gi * 4, gi * 4 + 4)
            for ti in range(nt):
                tk = slice(ti * 128, (ti + 1) * 128)
                px = self.pq.next()
                self.tr(px[:, 0:128], acc[:, 0, tk], self.idf[:], [acc, self.idf], [px])
                self.tr(px[:, 128:256], acc[:, 1, tk], self.idf[:], [acc, self.idf], [px])
                pb = self.pt.next()
                self.tr(pb[:, 0:128], BT[:, tk], self.idb[:], [BT, self.idb], [pb])
                xs = rxs.next(); xdt = rxdt.next()
                self.act(xs[:], px[:, 0:256], AF.Copy, [px], [xs])
                self.act(BtmA[:, ti, :], pb[:, 0:128], AF.Copy, [pb], [BtmA])
                xs3 = xs[:].rearrange("p (h q) -> p h q", q=64)
                self.tt(xddA[:, ti, :].rearrange("p (h q) -> p h q", q=64), xs3, ddS[:, ti, hs].unsqueeze(2).to_broadcast([128, 4, 64]),
                        ALU.mult, [xs, ddS], [xddA])
                if not so:
                    self.tt(xdt[:], xs3, dtS[:, ti, hs].unsqueeze(2).to_broadcast([128, 4, 64]), ALU.mult, [xs, dtS], [xdt])
                    pcb = self.pq.next()
                    self.mm(pcb[:, 0:128], BT[:, tk], CT[:, tk], True, True, [BT, CT], [pcb])
                    cbm = rcbm.next()
                    self.tt(cbm[:], pcb[:, 0:128], self.triu[:], ALU.mult, [pcb, self.triu], [cbm])
                    M1 = rM1.next()
                    self.tt(M1[:], self.strict[:].unsqueeze(1).to_broadcast([128, 4, 128]),
                            aS[:, ti, hs].unsqueeze(2).to_broadcast([128, 4, 128]), ALU.mult, [self.strict, aS], [M1])
                    pD = self.pq.next()
                    for h in range(4):
                        self.mm(pD[:, h * 128:(h + 1) * 128], M1[:, h, :], self.triu[:], True, True, [M1, self.triu], [pD])
                    EL = rEL.next()
                    self.act(EL[:], pD[:].rearrange("p (h t) -> p h t", h=4), AF.Exp, [pD], [EL])
                    WT = rWT.next()
                    self.tt(WT[:], EL[:], cbm[:].unsqueeze(1).to_broadcast([128, 4, 128]), ALU.mult, [EL, cbm], [WT])
                    py = self.pq.next()
                    for h in range(4):
                        self.mm(py[:, h * 64:(h + 1) * 64], WT[:, h, :], xdt[:, h, :], True, True, [WT, xdt], [py])
                    y2 = ry2.next()
                    self.tt(y2[:].rearrange("p (h q) -> p h q", q=64), xs3, self.dsk[:, hs].unsqueeze(2).to_broadcast([128, 4, 64]),
                            ALU.mult, [xs, self.dsk], [y2])
                    self.tt(ybuf[:, ti, :], y2[:], py[:, 0:256], ALU.add, [y2, py], [ybuf])
                yield

        def stage2(un):
            (t0, nt, gi) = un
            NS = nt * 128
            d = seg.pop(("st", t0, gi))
            zs = d["zs"]; ps_ = d["ps"]
            ecS, elS = ps_["ecS"], ps_["elS"]
            CT, m2o, ybuf, xddA, BtmA, ngs = (d[k_] for k_ in ("CT", "m2o", "ybuf", "xddA", "BtmA", "ngs"))
            Sg = S[:, gi * 4:(gi + 1) * 4, :]
            Sbg = Sbf[:, gi * 4:(gi + 1) * 4, :]
            hs = slice(gi * 4, gi * 4 + 4)
            for ti in range(nt):
                tk = slice(ti * 128, (ti + 1) * 128)
                pS = self.pq.next()
                self.mm(pS[:, 0:256], BtmA[:, ti, :], xddA[:, ti, :], True, True, [BtmA, xddA], [pS])
                if not so:
                    py = self.pq.next()
                    self.mm(py[:, 0:256], CT[:, tk], Sbg.rearrange("p h q -> p (h q)"), True, True, [CT, Sbf], [py])
                tS = rtS.next()
                self.tt(tS[:], Sg, elS[:, ti, hs].unsqueeze(2).to_broadcast([128, 4, 64]), ALU.mult, [S, elS], [tS])
                self.tt(Sg, tS[:], pS[:, 0:256].rearrange("p (h q) -> p h q", q=64), ALU.add, [tS, pS], [S])
                if not so:
                    self.act(Sbg, Sg, AF.Copy, [S], [Sbf])
                    y = ry.next()
                    self.tt(y[:].rearrange("p (h q) -> p h q", q=64), py[:, 0:256].rearrange("p (h q) -> p h q", q=64),
                            ecS[:, ti, hs].unsqueeze(2).to_broadcast([128, 4, 64]), ALU.mult, [py, ecS], [y])
                    self.tt(y[:], y[:], ybuf[:, ti, :], ALU.add, [y, ybuf], [y])
                    self.tt(y[:], y[:], zs[:, ti, :], ALU.mult, [y, zs], [y])
                    st = rst.next()
                    self.act(junk[:], y[:], AF.Square, [y], [junk, st], accum_out=st[:, 0:1])
                    self.act(st[:, 1:2], st[:, 0:1], AF.Ln, [st, self.epsc], [st], scale=1.0 / 256, bias=self.epsc[:, 1:2])
                    self.act(st[:, 2:3], st[:, 1:2], AF.Exp, [st], [st], scale=-0.5)
                    yn = ryn.next()
                    self.stt(yn[:], y[:], st[:, 2:3], ngs[:], ALU.mult, ALU.mult, [y, st, ngs], [yn])
                    po = self.pt.next()
                    self.tr(po[:, 0:128], yn[:, 0:128], self.idb[:], [yn, self.idb], [po])
                    self.tr(po[:, 128:256], yn[:, 128:256], self.idb[:], [yn, self.idb], [po])
                    self.act(m2o[:, :, tk], po[:, 0:256].rearrange("p (j t) -> p j t", j=2), AF.Copy, [po], [m2o])
                yield
            if not so:
                self.dma(m2T_d[gi * 2:gi * 2 + 2, :, t0 * 128:t0 * 128 + NS].rearrange("j p t -> p j t"), m2o[:, :, 0:NS], [m2o], [m2T_d])

        units = [(t0, nt, gi) for (t0, nt) in self.make_segs(MT) for gi in range(M2_G)]
        self.pipeline(units, stage0, stage1, stage2)

    def hg_state(self, uT_d, w_ap, mask_ap, S, dec_out):
        P = self.P
        MT = 6
        MAXS = MT * 128

        def ring(n, shape, dt, nm):
            return Ring([P.sb(shape, dt, nm) for _ in range(n)])

        useg = ring(1, [128, KC, MAXS], BF16, "useg")
        wts = ring(2, [128, KC, 2, 128], BF16, "hsw")
        mk = ring(2, [128, MAXS], F32, "mk")
        rA = ring(2, [128, MAXS], F32, "sA")
        rI = ring(2, [128, MAXS], BF16, "sI")
        rB, rC = (ring(1, [128, MAXS], F32, n) for n in ("sB", "sC"))
        rG = ring(2, [128, MAXS], BF16, "sG")
        rkv = ring(2, [128, MT, 256], BF16, "skv")
        rcs = ring(2, [128, 4], F32, "scs")
        onesr = P.sb([128, MAXS], F32, "onesr")
        P.op("dve", lambda e: e.memset(onesr[:], 1.0), (), [onesr])
        seg = {}

        def stage0(un):
            (t0, nt, hd) = un
            NS = nt * 128
            if hd == 0:
                seg["u"] = self.load_uT_seg(uT_d, t0, nt, useg)
                m_ = mk.next()
                self.dma(m_[:, 0:NS], mask_ap[0:1, t0 * 128:t0 * 128 + NS].partition_broadcast(128), [], [m_])
                seg["mask"] = m_
            u = seg["u"]
            wt = wts.next()
            for (m, c0) in ((0, C_F), (1, C_I)):
                self.dma(wt[:, :, m, :], w_ap[:, c0 + hd * 128:c0 + (hd + 1) * 128].rearrange("(k p) c -> p k c", p=128), [], [wt], eng="pool")
            A = rA.next(); I = rI.next()
            seg[("st", t0, hd)] = dict(A=A, I=I, mask=seg["mask"])
            for m in range(2):
                for (n0, nn) in [(a, min(512, NS - a)) for a in range(0, NS, 512)]:
                    ps = self.pp.next()
                    for kc in range(KC):
                        self.mm(ps[:, 0:nn], wt[:, kc, m, :], u[:, kc, n0:n0 + nn], kc == 0, kc == KC - 1, [wt, u], [ps])
                    if m == 0:
                        self.act(A[:, n0:n0 + nn], ps[:, 0:nn], AF.Sigmoid, [ps], [A])
                    else:
                        self.act(I[:, n0:n0 + nn], ps[:, 0:nn], AF.Copy, [ps], [I])
                    yield

        def stage1(un):
            (t0, nt, hd) = un
            NS = nt * 128
            d = seg.pop(("st", t0, hd))
            A, I, mask = d["A"], d["I"], d["mask"]
            B = rB.next(); C = rC.next(); G = rG.next(); kv = rkv.next(); cs = rcs.next()
            self.ts(A[:, 0:NS], A[:, 0:NS], self.oml[:, hd:hd + 1], self.lb[:, hd:hd + 1], ALU.mult, ALU.add, [A, self.oml, self.lb], [A])
            self.ts(B[:, 0:NS], A[:, 0:NS], -1.0, 1.0, ALU.mult, ALU.add, [A], [B])
            self.tt(B[:, 0:NS], B[:, 0:NS], mask[:, 0:NS], ALU.mult, [B, mask], [B])
            self.act(A[:, 0:NS], A[:, 0:NS], AF.Ln, [A], [A])
            self.tt(A[:, 0:NS], A[:, 0:NS], mask[:, 0:NS], ALU.mult, [A, mask], [A])
            yield
            P.op("dve", lambda e, C=C, A=A, NS=NS: e.tensor_tensor_scan(out=C[:, 0:NS], data0=onesr[:, 0:NS], data1=A[:, 0:NS],
                                                                    initial=0.0, op0=ALU.mult, op1=ALU.add), [A, onesr], [C])
            self.act(A[:, 0:NS], C[:, 0:NS], AF.Exp, [C], [A], scale=-1.0, bias=C[:, NS - 1:NS])
            self.act(cs[:, 0:1], C[:, NS - 1:NS], AF.Exp, [C], [cs])
            self.tt(dec_out[:, hd:hd + 1], dec_out[:, hd:hd + 1], C[:, NS - 1:NS], ALU.add, [dec_out, C], [dec_out])
            self.tt(G[:, 0:NS], B[:, 0:NS], A[:, 0:NS], ALU.mult, [B, A], [G])
            yield
            for ti in range(nt):
                tk = slice(ti * 128, (ti + 1) * 128)
                pt = self.pt.next()
                self.tr(pt[:, 0:128], G[:, tk], self.idb[:], [G, self.idb], [pt])
                self.tr(pt[:, 128:256], I[:, tk], self.idb[:], [I, self.idb], [pt])
                self.act(kv[:, ti, :], pt[:, 0:256], AF.Copy, [pt], [kv])
                if ti % 2 == 1:
                    yield
            pst = self.pq.next()
            for ti in range(nt):
                self.mm(pst[:, 0:128], kv[:, ti, 0:128], kv[:, ti, 128:256], ti == 0, ti == nt - 1, [kv], [pst])
            Sh = S[:, hd, :]
            self.stt(Sh, Sh, cs[:, 0:1], pst[:, 0:128], ALU.mult, ALU.add, [S, cs, pst], [S])
            yield

        units = [(t0, nt, hd) for (t0, nt) in self.make_segs(MT) for hd in range(HG_H)]
        self.pipeline(units, stage0, stage1)

    def m2_state(self, uT_d, w_ap, S, dec_out):
        P = self.P
        MT = 6
        MAXS = MT * 128

        def ring(n, shape, dt, nm):
            return Ring([P.sb(shape, dt, nm) for _ in range(n)])

        useg = ring(1, [128, KC, MAXS], BF16, "useg")
        wts = ring(2, [128, KC, 384], BF16, "msw")
        rxr = ring(2, [128, 3, 3 + MAXS], F32, "xr")
        racc = ring(1, [128, 2, MAXS], F32, "xacc")
        rtmpc = ring(2, [128, MAXS], F32, "ctmp")
        rBT = ring(1, [128, MAXS], BF16, "BT")
        rps = Ring([dict((n, P.sb([128, MT, 32], F32, n)) for n in ("dtS", "aS", "cumS", "lastS", "ddS")) for _ in range(2)])
        rtot = ring(2, [128, 2, 32], F32, "mtot")
        rxs = ring(2, [128, 256], F32, "xs")
        rxdd = ring(2, [128, MT, 256], BF16, "xddA")
        rBtm = ring(2, [128, MT, 128], BF16, "BtmA")
        rtS = ring(2, [128, 4, 64], F32, "tS")
        seg = {}

        def seg_prep(t0, nt):
            u = self.load_uT_seg(uT_d, t0, nt, useg)
            seg["u"] = u
            ps_ = rps.next(); tot = rtot.next()
            seg["ps"] = ps_; seg["tot"] = tot
            dtS, aS, cumS, lastS, ddS = (ps_[n] for n in ("dtS", "aS", "cumS", "lastS", "ddS"))
            for ti in range(nt):
                ps = self.pq.next()
                for kc in range(KC):
                    self.mm(ps[:, 0:32], u[:, kc, ti * 128:(ti + 1) * 128], self.wdt[:, kc, :], kc == 0, kc == KC - 1, [u, self.wdt], [ps])
                self.tt(dtS[:, ti, :], ps[:, 0:32], self.dtb[:], ALU.add, [ps, self.dtb], [dtS])
            self.act(dtS[:, 0:nt, :], dtS[:, 0:nt, :], AF.Exp, [dtS], [dtS])
            self.act(dtS[:, 0:nt, :], dtS[:, 0:nt, :], AF.Ln, [dtS, self.epsc], [dtS], bias=self.epsc[:, 2:3])
            for ti in range(nt):
                self.ts(dtS[:, ti, :], dtS[:, ti, :], self.maskT[:, t0 + ti:t0 + ti + 1], None, ALU.mult, None, [dtS, self.maskT], [dtS])
                self.tt(aS[:, ti, :], dtS[:, ti, :], self.negA[:], ALU.mult, [dtS, self.negA], [aS])
            for ti in range(nt):
                ps = self.pq.next()
                self.mm(ps[:, 0:32], self.triu[:], aS[:, ti, :], True, True, [self.triu, aS], [ps])
                self.mm(ps[:, 32:64], self.ones_f[:], aS[:, ti, :], True, True, [self.ones_f, aS], [ps])
                self.act(cumS[:, ti, :], ps[:, 0:32], AF.Copy, [ps], [cumS])
                self.act(lastS[:, ti, :], ps[:, 32:64], AF.Copy, [ps], [lastS])
            self.tt(ddS[:, 0:nt, :], lastS[:, 0:nt, :], cumS[:, 0:nt, :], ALU.subtract, [lastS, cumS], [ddS])
            P.op("dve", lambda e, tot=tot: e.memset(tot[:, 0, :], 0.0), (), [tot])
            for ti in range(nt - 1, -1, -1):
                self.tt(ddS[:, ti, :], ddS[:, ti, :], tot[:, 0, :], ALU.add, [ddS, tot], [ddS])
                self.tt(tot[:, 0, :], tot[:, 0, :], lastS[:, ti, :], ALU.add, [tot, lastS], [tot])
            self.tt(dec_out[:], dec_out[:], tot[:, 0, :], ALU.add, [dec_out, tot], [dec_out])
            self.act(tot[:, 1, :], tot[:, 0, :], AF.Exp, [tot], [tot])
            self.act(ddS[:, 0:nt, :], ddS[:, 0:nt, :], AF.Exp, [ddS], [ddS])
            self.tt(ddS[:, 0:nt, :], ddS[:, 0:nt, :], dtS[:, 0:nt, :], ALU.mult, [ddS, dtS], [ddS])

        def stage0(un):
            (t0, nt, gi) = un
            NS = nt * 128
            if gi == 0:
                seg_prep(t0, nt)
                yield
            u = seg["u"]
            wt = wts.next()
            self.dma(wt[:, :, 0:256], w_ap[:, C_X + gi * 256:C_X + (gi + 1) * 256].rearrange("(k p) c -> p k c", p=128), [], [wt], eng="pool")
            self.dma(wt[:, :, 256:384], w_ap[:, C_B + gi * 128:C_B + (gi + 1) * 128].rearrange("(k p) c -> p k c", p=128), [], [wt], eng="pool")
            xr = rxr.next()
            seg[("st", t0, gi)] = dict(xr=xr, ps=seg["ps"], tot=seg["tot"])
            for b in range(3):
                for (n0, nn) in [(a, min(512, NS - a)) for a in range(0, NS, 512)]:
                    ps = self.pp.next()
                    for kc in range(KC):
                        self.mm(ps[:, 0:nn], wt[:, kc, b * 128:(b + 1) * 128], u[:, kc, n0:n0 + nn], kc == 0, kc == KC - 1, [wt, u], [ps])
                    self.act(xr[:, b, 3 + n0:3 + n0 + nn], ps[:, 0:nn], AF.Copy, [ps], [xr])
                    yield

        def stage1(un):
            (t0, nt, gi) = un
            NS = nt * 128
            d = seg.pop(("st", t0, gi))
            xr = d["xr"]; ddS = d["ps"]["ddS"]; tot = d["tot"]
            acc = racc.next(); BT = rBT.next(); xdd = rxdd.next(); Btm = rBtm.next()
            for b in range(3):
                j = (gi * 2 + b) if b < 2 else 16 + gi
                self.act(xr[:, b, 0:3], self.carry[:, gi, b, :], AF.Copy, [self.carry], [xr])
                self.act(self.carry[:, gi, b, :], xr[:, b, NS:NS + 3], AF.Copy, [xr], [self.carry])
                tmpc = rtmpc.next()
                self.ts(tmpc[:, 0:NS], xr[:, b, 3:3 + NS], self.cw[3][:, j:j + 1], self.cbias[:, j:j + 1], ALU.mult, ALU.add,
                        [xr, self.cw[3], self.cbias], [tmpc])
                for k_ in range(3):
                    self.stt(tmpc[:, 0:NS], xr[:, b, k_:k_ + NS], self.cw[k_][:, j:j + 1], tmpc[:, 0:NS], ALU.mult, ALU.add,
                             [xr, self.cw[k_], tmpc], [tmpc])
                if b < 2:
                    self.act(acc[:, b, 0:NS], tmpc[:, 0:NS], AF.Silu, [tmpc], [acc])
                else:
                    self.act(BT[:, 0:NS], tmpc[:, 0:NS], AF.Silu, [tmpc], [BT])
                yield
            hs = slice(gi * 4, gi * 4 + 4)
            for ti in range(nt):
                tk = slice(ti * 128, (ti + 1) * 128)
                px = self.pq.next()
                self.tr(px[:, 0:128], acc[:, 0, tk], self.idf[:], [acc, self.idf], [px])
                self.tr(px[:, 128:256], acc[:, 1, tk], self.idf[:], [acc, self.idf], [px])
                pb = self.pt.next()
                self.tr(pb[:, 0:128], BT[:, tk], self.idb[:], [BT, self.idb], [pb])
                xs = rxs.next()
                self.act(xs[:], px[:, 0:256], AF.Copy, [px], [xs])
                self.act(Btm[:, ti, :], pb[:, 0:128], AF.Copy, [pb], [Btm])
                self.tt(xdd[:, ti, :].rearrange("p (h q) -> p h q", q=64), xs[:].rearrange("p (h q) -> p h q", q=64),
                        ddS[:, ti, hs].unsqueeze(2).to_broadcast([128, 4, 64]), ALU.mult, [xs, ddS], [xdd])
                yield
            pS = self.pq.next()
            for ti in range(nt):
                self.mm(pS[:, 0:256], Btm[:, ti, :], xdd[:, ti, :], ti == 0, ti == nt - 1, [Btm, xdd], [pS])
            Sg = S[:, gi * 4:(gi + 1) * 4, :]
            tS = rtS.next()
            self.tt(tS[:], Sg, tot[:, 1, hs].unsqueeze(2).to_broadcast([128, 4, 64]), ALU.mult, [S, tot], [tS])
            self.tt(Sg, tS[:], pS[:, 0:256].rearrange("p (h q) -> p h q", q=64), ALU.add, [tS, pS], [S])
            yield

        units = [(t0, nt, gi) for (t0, nt) in self.make_segs(MT) for gi in range(M2_G)]
        self.pipeline(units, stage0, stage1)

    def tok_chunks(self, n0, n1):
        return [(a, min(512, n1 - a)) for a in range(n0, n1, 512)]

    def dense_A_res(self, xT_d, KCx, W_ap, res_ap, res_r, out_d, seg_tiles):
        P = self.P
        NTs = seg_tiles
        xs_ = Ring([P.sb([128, KCx, NTs * 128], BF16, "dax") for _ in range(1)])
        wr = Ring([P.sb([128, KCx, 512], BF16, "daw") for _ in range(2)])
        rr = Ring([P.sb([128, 512], F32, "dar") for _ in range(3)])
        orr = Ring([P.sb([128, 512], F32, "dao") for _ in range(3)])
        t = 0
        while t < self.NT:
            nt = min(NTs, self.NT - t)
            x = xs_.next()
            self.dma(x[:, :, 0:nt * 128], xT_d[:, :, t * 128:(t + nt) * 128].rearrange("k p t -> p k t"), [xT_d], [x])
            for cb in range(4):
                w = wr.next()
                self.dma(w[:], W_ap[:, cb * 512:(cb + 1) * 512].rearrange("(k p) c -> p k c", p=128), [], [w], eng="pool")
                for ti in range(nt):
                    r = rr.next(); o = orr.next()
                    tok = slice((t + ti) * 128, (t + ti + 1) * 128)
                    self.dma(r[:], res_ap[tok, cb * 512:(cb + 1) * 512], [res_r], [r])
                    ps = self.pp.next()
                    for kc in range(KCx):
                        self.mm(ps[:], x[:, kc, ti * 128:(ti + 1) * 128], w[:, kc, :], kc == 0, kc == KCx - 1, [x, w], [ps])
                    self.tt(o[:], ps[:], r[:], ALU.add, [ps, r], [o])
                    self.dma(out_d[tok, cb * 512:(cb + 1) * 512], o[:], [o], [out_d])
            t += nt

    def dense_B(self, x, W_ap, c0, ncols, evac, ntok=None):
        P = self.P
        wr = self._dbw
        ntok = self.TT if ntok is None else ntok
        for c in range(0, ncols, 512):
            wc = min(512, ncols - c)
            w = wr.next()
            self.dma(w[:, :, 0:wc], W_ap[:, c0 + c:c0 + c + wc].rearrange("(k p) c -> p k c", p=128), [], [w], eng="pool")
            for b in range(wc // 128):
                for (n0, nn) in self.tok_chunks(0, ntok):
                    ps = self.pp.next()
                    for kc in range(KC):
                        self.mm(ps[:, 0:nn], w[:, kc, b * 128:(b + 1) * 128], x[:, kc, n0:n0 + nn], kc == 0, kc == KC - 1, [w, x], [ps])
                    evac(ps, (c // 128) + b, n0, nn)

    def halves(self):
        a = ((self.NT + 1) // 2) * 128
        return [(0, a), (a, self.TT)] if a < self.TT else [(0, self.TT)]

    def load_xT(self, xT_d, name):
        x = self.P.sb([128, KC, self.TT], BF16, name)
        self.dma(x[:], xT_d[:, :, :].rearrange("k p t -> p k t"), [xT_d], [x])
        return x

    def mixer_merge(self, uT_d, hgT_d, m2T_d, w_ap, wbh_ap, wbm_ap, mT_d):
        P = self.P
        hv = self.halves()
        NSM = max(b - a for a, b in hv)
        ru = Ring([P.sb([128, KC, NSM], BF16, "mu") for _ in range(1)])
        rh = Ring([P.sb([128, KC, NSM], BF16, "mh") for _ in range(1)])
        rm = Ring([P.sb([128, KC, NSM], BF16, "mm") for _ in range(1)])
        rw = Ring([P.sb([128, KC, 4, 128], BF16, "mw") for _ in range(2)])
        rg = Ring([P.sb([128, 2, 512], F32, "mg") for _ in range(2)])
        rt = Ring([P.sb([128, 2, 512], F32, "mt") for _ in range(2)])
        ro = Ring([P.sb([128, NSM], BF16, "mo") for _ in range(2)])
        for (h0, h1) in hv:
            nh = h1 - h0
            u = ru.next(); hg = rh.next(); m2 = rm.next()
            for (dst, src) in ((u, uT_d), (hg, hgT_d), (m2, m2T_d)):
                self.dma(dst[:, :, 0:nh], src[:, :, h0:h1].rearrange("k p t -> p k t"), [src], [dst])
            for kb in range(16):
                c = kb * 128
                w = rw.next()
                for mi, (ap, cc) in enumerate(((w_ap, C_GHG + c), (wbh_ap, c), (w_ap, C_GM2 + c), (wbm_ap, c))):
                    self.dma(w[:, :, mi, :], ap[:, cc:cc + 128].rearrange("(k p) c -> p k c", p=128), [], [w], eng="pool")
                o = ro.next()
                for (n0, nn) in self.tok_chunks(0, nh):
                    g = rg.next(); t_ = rt.next()
                    pss = []
                    for mi, x in enumerate((u, hg, u, m2)):
                        ps = self.pp.next()
                        for kc in range(KC):
                            self.mm(ps[:, 0:nn], w[:, kc, mi, :], x[:, kc, n0:n0 + nn], kc == 0, kc == KC - 1, [w, x], [ps])
                        pss.append(ps)
                    self.act(g[:, 0, 0:nn], pss[0][:, 0:nn], AF.Sigmoid, [pss[0]], [g])
                    self.act(g[:, 1, 0:nn], pss[2][:, 0:nn], AF.Sigmoid, [pss[2]], [g])
                    self.tt(t_[:, 0, 0:nn], pss[1][:, 0:nn], g[:, 0, 0:nn], ALU.mult, [pss[1], g], [t_])
                    self.tt(t_[:, 1, 0:nn], pss[3][:, 0:nn], g[:, 1, 0:nn], ALU.mult, [pss[3], g], [t_])
                    self.tt(o[:, n0:n0 + nn], t_[:, 0, 0:nn], t_[:, 1, 0:nn], ALU.add, [t_], [o])
                self.dma(mT_d[kb, :, h0:h1], o[:, 0:nh], [o], [mT_d])

    def xattn(self, uT_d, memnT_d, wq_ap, wkv_ap, oT_d):
        P = self.P
        qT = P.sb([128, KC, self.TT], BF16, "xa_q")
        kT = P.sb([128, KC, 256], BF16, "xa_k")
        v = P.sb([128, 2, 2048], BF16, "xa_v")
        sc = 512 ** -0.5
        with P.scope():
            self._dbw = Ring([P.sb([128, KC, 512], BF16, "dbw") for _ in range(2)])
            mn = P.sb([128, KC, 256], BF16, "xa_mn")
            self.dma(mn[:], memnT_d[:, :, :].rearrange("k p t -> p k t"), [memnT_d], [mn])
            for c in range(0, 2048, 512):
                w = self._dbw.next()
                self.dma(w[:], wkv_ap[:, c:c + 512].rearrange("(k p) c -> p k c", p=128), [], [w], eng="pool")
                for b in range(4):
                    ps = self.pp.next()
                    for kc in range(KC):
                        self.mm(ps[:, 0:256], w[:, kc, b * 128:(b + 1) * 128], mn[:, kc, :], kc == 0, kc == KC - 1, [w, mn], [ps])
                    self.act(kT[:, c // 128 + b, :], ps[:, 0:256], AF.Copy, [ps], [kT])
            for c in range(0, 2048, 512):
                w = self._dbw.next()
                self.dma(w[:], wkv_ap[:, 2048 + c:2048 + c + 512].rearrange("(k p) c -> p k c", p=128), [], [w], eng="pool")
                for mb in range(2):
                    ps = self.pp.next()
                    for kc in range(KC):
                        self.mm(ps[:], mn[:, kc, mb * 128:(mb + 1) * 128], w[:, kc, :], kc == 0, kc == KC - 1, [w, mn], [ps])
                    self.act(v[:, mb, c:c + 512], ps[:], AF.Copy, [ps], [v])
            hv = self.halves()
            xr_ = Ring([P.sb([128, KC, max(b - a for a, b in hv)], BF16, "xa_u") for _ in range(1)])
            for (h0, h1) in hv:
                x = xr_.next()
                self.dma(x[:, :, 0:h1 - h0], uT_d[:, :, h0:h1].rearrange("k p t -> p k t"), [uT_d], [x])

                def evq(ps, cb, n0, nn, h0=h0):
                    self.act(qT[:, cb, h0 + n0:h0 + n0 + nn], ps[:, 0:nn], AF.Copy, [ps], [qT], scale=sc)
                self.dense_B(x, wq_ap, 0, 2048, evq, ntok=h1 - h0)
        rPT = Ring([P.sb([128, 2, 512], BF16, "xa_pt") for _ in range(2)])
        rden = Ring([P.sb([128, 512], F32, "xa_den") for _ in range(2)])
        ro = Ring([P.sb([128, 4, 512], BF16, "xa_o") for _ in range(2)])
        for hh in range(4):
            for (n0, nn) in self.tok_chunks(0, self.TT):
                PT = rPT.next()
                for mb in range(2):
                    ps = self.pp.next()
                    for dc in range(4):
                        self.mm(ps[:, 0:nn], kT[:, hh * 4 + dc, mb * 128:(mb + 1) * 128], qT[:, hh * 4 + dc, n0:n0 + nn], dc == 0, dc == 3, [kT, qT], [ps])
                    self.act(PT[:, mb, 0:nn], ps[:, 0:nn], AF.Exp, [ps], [PT])
                psd = self.pp.next()
                for mb in range(2):
                    self.mm(psd[:, 0:nn], self.ones_b[:], PT[:, mb, 0:nn], mb == 0, mb == 1, [self.ones_b, PT], [psd])
                den = rden.next()
                P.op("dve", lambda e, den=den, psd=psd, nn=nn: e.reciprocal(out=den[:, 0:nn], in_=psd[:, 0:nn]), [psd], [den])
                o = ro.next()
                for dc in range(4):
                    ps = self.pp.next()
                    for mb in range(2):
                        self.mm(ps[:, 0:nn], v[:, mb, hh * 512 + dc * 128:hh * 512 + (dc + 1) * 128], PT[:, mb, 0:nn], mb == 0, mb == 1, [v, PT], [ps])
                    self.tt(o[:, dc, 0:nn], ps[:, 0:nn], den[:, 0:nn], ALU.mult, [ps, den], [o])
                self.dma(oT_d[hh * 4:hh * 4 + 4, :, n0:n0 + nn].rearrange("j p t -> p j t"), o[:, :, 0:nn], [o], [oT_d])

    def ffn_up(self, uT_d, wup_ap, cw_ap, cb_ap, maskF_ap, aT_d):
        P = self.P
        TT = self.TT
        x = self.load_xT(uT_d, "ff_u")
        cw = [self.load_cols(cw_ap[k:k + 1, :].rearrange("o (j p) -> (o j) p", p=128), 44, f"fcw{k}") for k in range(3)]
        cbs = self.load_cols(cb_ap.rearrange("o (j p) -> (o j) p", p=128), 44, "fcb")
        mF = P.sb([128, 128], F32, "maskF")
        self.dma(mF[:], maskF_ap.partition_broadcast(128), [], [mF])
        rw = Ring([P.sb([128, KC, 2, 256], BF16, "fw") for _ in range(2)])
        rgr = Ring([P.sb([128, 2 + TT], F32, "fgr") for _ in range(2)])
        rup = Ring([P.sb([128, TT], F32, "fup") for _ in range(2)])
        rac = Ring([P.sb([128, TT], F32, "fac") for _ in range(2)])
        rao = Ring([P.sb([128, TT], BF16, "fao") for _ in range(2)])
        for jp in range(0, 44, 2):
            w = rw.next()
            self.dma(w[:, :, 0, :], wup_ap[:, jp * 128:jp * 128 + 256].rearrange("(k p) c -> p k c", p=128), [], [w], eng="pool")
            self.dma(w[:, :, 1, :], wup_ap[:, D_FF + jp * 128:D_FF + jp * 128 + 256].rearrange("(k p) c -> p k c", p=128), [], [w], eng="pool")
            for b in range(2):
                j = jp + b
                gr = rgr.next(); up = rup.next(); ac = rac.next(); ao = rao.next()
                P.op("dve", lambda e, gr=gr: e.memset(gr[:, 0:2], 0.0), (), [gr])
                for (n0, nn) in self.tok_chunks(0, TT):
                    ps = self.pp.next()
                    for kc in range(KC):
                        self.mm(ps[:, 0:nn], w[:, kc, 0, b * 128:(b + 1) * 128], x[:, kc, n0:n0 + nn], kc == 0, kc == KC - 1, [w, x], [ps])
                    self.act(gr[:, 2 + n0:2 + n0 + nn], ps[:, 0:nn], AF.Copy, [ps], [gr])
                    ps2 = self.pp.next()
                    for kc in range(KC):
                        self.mm(ps2[:, 0:nn], w[:, kc, 1, b * 128:(b + 1) * 128], x[:, kc, n0:n0 + nn], kc == 0, kc == KC - 1, [w, x], [ps2])
                    self.act(up[:, n0:n0 + nn], ps2[:, 0:nn], AF.Copy, [ps2], [up])
                self.tt(gr[:, 2:130], gr[:, 2:130], mF[:], ALU.mult, [gr, mF], [gr])
                self.ts(ac[:], gr[:, 2:2 + TT], cw[2][:, j:j + 1], cbs[:, j:j + 1], ALU.mult, ALU.add, [gr, cw[2], cbs], [ac])
                for k in range(2):
                    self.stt(ac[:], gr[:, k:k + TT], cw[k][:, j:j + 1], ac[:], ALU.mult, ALU.add, [gr, cw[k], ac], [ac])
                self.act(ac[:], ac[:], AF.Gelu, [ac], [ac])
                self.tt(ao[:], ac[:], up[:], ALU.mult, [ac, up], [ao])
                self.dma(aT_d[j, :, :], ao[:], [ao], [aT_d])

    def final_norm(self, h_ap, hres, g_ap, out_ap, out_r, tiles):
        P = self.P
        gbc = P.sb([128, D], F32, "fgbc")
        self.dma(gbc[:], g_ap.partition_broadcast(128), [], [gbc])
        hb = Ring([P.sb([128, D], F32, "fhb") for _ in range(2)])
        ob = Ring([P.sb([128, D], F32, "fob") for _ in range(2)])
        junk = P.sb([128, D], BF16, "fjunk")
        stt_ = Ring([P.sb([128, 4], F32, "fst") for _ in range(2)])
        outs = []
        for i in tiles:
            h = hb.next(); o = ob.next(); s = stt_.next()
            self.dma(h[:], h_ap[i * 128:(i + 1) * 128, :], [hres], [h])
            self.act(junk[:], h[:], AF.Square, [h], [junk, s], accum_out=s[:, 0:1])
            self.act(s[:, 1:2], s[:, 0:1], AF.Ln, [s, self.epsc], [s], scale=1.0 / D, bias=self.epsc[:, 0:1])
            self.act(s[:, 2:3], s[:, 1:2], AF.Exp, [s], [s], scale=-0.5)
            self.stt(o[:], h[:], s[:, 2:3], gbc[:], ALU.mult, ALU.mult, [h, s, gbc], [o])
            outs.append(self.dma(out_ap[i * 128:(i + 1) * 128, :], o[:], [o], [out_r]))
        return outs


def _inp(nc, name, shape):
    return nc.dram_tensor(name, list(shape), F32, kind="ExternalInput").ap()


def _outp(nc, name, shape):
    return nc.dram_tensor(name, list(shape), F32, kind="ExternalOutput").ap()


def build_state(l, NT):
    nc = bass.Bass("TRN2", target_bir_lowering=False)
    TT = NT * 128
    h = _inp(nc, "h", [TT, D]); mask = _inp(nc, "mask", [1, TT]); g = _inp(nc, "mix_g", [1, D])
    w_in = _inp(nc, "w_in", [D, N_IN]); lbl = _inp(nc, "lbl", [2, 2048])
    cw = _inp(nc, "m2_cw", [4, 4096]); cb = _inp(nc, "m2_cb", [1, 4096])
    dtb = _inp(nc, "m2_dtb", [1, 32]); alog = _inp(nc, "m2_alog", [1, 32])
    o_shg = _outp(nc, "o_shg", [128, 16, 128]); o_dhg = _outp(nc, "o_dhg", [128, 16])
    o_sm2 = _outp(nc, "o_sm2", [128, 32, 64]); o_dm2 = _outp(nc, "o_dm2", [128, 32])
    k = K(nc, NT); P = k.P
    uT_d = P.dram([KC, 128, TT], BF16, "uT")
    with P.scope():
        k.norm_to_uT(h, T(h, Res()), g, uT_d)
    k.hg_setup_lb(lbl, l)
    Shg = P.sb([128, 16, 128], F32, "Shg"); dhg = P.sb([128, 16], F32, "dhg")
    P.op("dve", lambda e: e.memset(Shg[:], 0.0), (), [Shg])
    P.op("dve", lambda e: e.memset(dhg[:], 0.0), (), [dhg])
    with P.scope():
        k.hg_state(uT_d, w_in, mask, Shg, dhg)
    outs = [k.dma(o_shg, Shg[:], [Shg], [T(o_shg, Res())]), k.dma(o_dhg, dhg[:], [dhg], [T(o_dhg, Res())])]
    k.m2_setup(w_in, cw, cb, dtb, alog, None, None, mask)
    Sm2 = P.sb([128, 32, 64], F32, "Sm2"); Sbf = P.sb([128, 32, 64], BF16, "Sm2b"); dm2 = P.sb([128, 32], F32, "dm2")
    P.op("dve", lambda e: e.memset(Sm2[:], 0.0), (), [Sm2])
    P.op("dve", lambda e: e.memset(dm2[:], 0.0), (), [dm2])
    with P.scope():
        k.m2_state(uT_d, w_in, Sm2, dm2)
    outs += [k.dma(o_sm2, Sm2[:], [Sm2], [T(o_sm2, Res())]), k.dma(o_dm2, dm2[:], [dm2], [T(o_dm2, Res())])]
    P.emit(outs)
    return nc


def build_main(l, NT, last):
    nc = bass.Bass("TRN2", target_bir_lowering=False)
    TT = NT * 128
    h = _inp(nc, "h", [TT, D]); mask = _inp(nc, "mask", [1, TT]); maskF = _inp(nc, "maskF", [1, 128])
    g = _inp(nc, "mix_g", [1, D]); w_in = _inp(nc, "w_in", [D, N_IN]); lbl = _inp(nc, "lbl", [2, 2048])
    hgng = _inp(nc, "hg_ng", [1, 2048])
    cw = _inp(nc, "m2_cw", [4, 4096]); cb = _inp(nc, "m2_cb", [1, 4096])
    dtb = _inp(nc, "m2_dtb", [1, 32]); alog = _inp(nc, "m2_alog", [1, 32]); dsk = _inp(nc, "m2_dsk", [1, 32])
    m2ng = _inp(nc, "m2_ng", [1, 2048])
    wbh = _inp(nc, "w_bhg", [2048, D]); wbm = _inp(nc, "w_bm2", [2048, D]); wout = _inp(nc, "w_out", [D, D])
    mem = _inp(nc, "mem", [N_MEM, D]); memg = _inp(nc, "mem_g", [1, D]); xag = _inp(nc, "xa_g", [1, D])
    wq = _inp(nc, "xa_wq", [D, D]); wkv = _inp(nc, "xa_wkv", [D, 2 * D]); wo = _inp(nc, "xa_wo", [D, D])
    ffg = _inp(nc, "ffn_g", [1, D]); wup = _inp(nc, "ffn_wup", [D, 2 * D_FF])
    fcw = _inp(nc, "ffn_cw", [3, D_FF]); fcb = _inp(nc, "ffn_cb", [1, D_FF]); wdn = _inp(nc, "ffn_wdn", [D_FF, D])
    ps_hg = _inp(nc, "ps_hg", [7, 128, 16, 128]); pd_hg = _inp(nc, "pd_hg", [7, 128, 16])
    ps_m2 = _inp(nc, "ps_m2", [7, 128, 32, 64]); pd_m2 = _inp(nc, "pd_m2", [7, 128, 32])
    if last:
        fing = _inp(nc, "fin_g", [1, D])
    h_out = _outp(nc, "h_out", [TT, D])
    k = K(nc, NT); P = k.P
    hin = T(h, Res())
    uT_d = P.dram([KC, 128, TT], BF16, "uT")
    hgT_d = P.dram([KC, 128, TT], BF16, "hgT")
    m2T_d = P.dram([KC, 128, TT], BF16, "m2T")
    mT_d = P.dram([KC, 128, TT], BF16, "mT")
    h1_d = P.dram([TT, D], F32, "h1")
    h2_d = P.dram([TT, D], F32, "h2")
    h3_d = T(h_out, Res()) if not last else P.dram([TT, D], F32, "h3")
    memT_d = P.dram([KC, 128, N_MEM], BF16, "memT")
    oT_d = P.dram([KC, 128, TT], BF16, "oT")
    aT_d = P.dram([D_FF // 128, 128, TT], BF16, "aT")
    with P.scope():
        k.norm_to_uT(h, hin, g, uT_d)
    with P.scope():
        k.hg_setup(lbl, l, hgng)
        Shg = P.sb([128, 16, 128], F32, "Shg")
        P.op("dve", lambda e: e.memset(Shg[:], 0.0), (), [Shg])
        Sm2 = P.sb([128, 32, 64], F32, "Sm2"); Sbf = P.sb([128, 32, 64], BF16, "Sm2b")
        P.op("dve", lambda e: e.memset(Sm2[:], 0.0), (), [Sm2])
        with P.scope():
            t1 = Ring([P.sb([128, 16, 128], F32, "pst") for _ in range(2)])
            d1 = Ring([P.sb([128, 16], F32, "pdt") for _ in range(2)])
            t2 = Ring([P.sb([128, 32, 64], F32, "pst2") for _ in range(2)])
            d2 = Ring([P.sb([128, 32], F32, "pdt2") for _ in range(2)])
            for j in range(7):
                a = t1.next(); b = d1.next(); c = t2.next(); d_ = d2.next()
                k.dma(a[:], ps_hg[j], [], [a]); k.dma(b[:], pd_hg[j], [], [b])
                k.dma(c[:], ps_m2[j], [], [c]); k.dma(d_[:], pd_m2[j], [], [d_])
                k.act(b[:], b[:], AF.Exp, [b], [b]); k.act(d_[:], d_[:], AF.Exp, [d_], [d_])
                k.tt(Shg[:], Shg[:], b[:].unsqueeze(2).to_broadcast([128, 16, 128]), ALU.mult, [Shg, b], [Shg])
                k.tt(Shg[:], Shg[:], a[:], ALU.add, [Shg, a], [Shg])
                k.tt(Sm2[:], Sm2[:], d_[:].unsqueeze(2).to_broadcast([128, 32, 64]), ALU.mult, [Sm2, d_], [Sm2])
                k.tt(Sm2[:], Sm2[:], c[:], ALU.add, [Sm2, c], [Sm2])
        k.act(Sbf[:], Sm2[:], AF.Copy, [Sm2], [Sbf])
        with P.scope():
            k.hg_mixer(uT_d, w_in, mask, Shg, hgT_d)
        k.m2_setup(w_in, cw, cb, dtb, alog, dsk, m2ng, mask)
        with P.scope():
            k.m2_mixer(uT_d, w_in, Sm2, Sbf, m2T_d)
    with P.scope():
        k.mixer_merge(uT_d, hgT_d, m2T_d, w_in, wbh, wbm, mT_d)
    with P.scope():
        k.dense_A_res(mT_d, KC, wout, h, hin, h1_d, NT)
    u3_d = P.dram([KC, 128, TT], BF16, "u3T")
    with P.scope():
        k.norm_to_uT(h1_d[:, :], h1_d, xag, u3_d)
    with P.scope():
        k.norm_to_uT(mem, T(mem, Res()), memg, memT_d, tiles=range(N_MEM // 128))
    with P.scope():
        k.xattn(u3_d, memT_d, wq, wkv, oT_d)
    with P.scope():
        k.dense_A_res(oT_d, KC, wo, h1_d[:, :], h1_d, h2_d, NT)
    u4_d = P.dram([KC, 128, TT], BF16, "u4T")
    with P.scope():
        k.norm_to_uT(h2_d[:, :], h2_d, ffg, u4_d)
    with P.scope():
        k.ffn_up(u4_d, wup, fcw, fcb, maskF, aT_d)
    with P.scope():
        k.dense_A_res(aT_d, D_FF // 128, wdn, h2_d[:, :], h2_d, h3_d, 6)
    if last:
        with P.scope():
            outs = k.final_norm(h3_d[:, :], h3_d, fing, h_out, T(h_out, Res()), range(NT))
    else:
        outs = [h3_d.r.last_write]
    stats = P.emit(outs)
    return nc, stats


_PROG_CACHE = {}


def _prog(kind, l, NT, last=False):
    key = (kind, l, NT, last)
    if key not in _PROG_CACHE:
        if kind == "state":
            _PROG_CACHE[key] = build_state(l, NT)
        else:
            _PROG_CACHE[key] = build_main(l, NT, last)[0]
    return _PROG_CACHE[key]


def _halo_slices(hfull, n_cores, TOWN):
    out = []
    for c in range(n_cores):
        s = c * TOWN
        if c == 0:
            blk = np.concatenate([np.zeros((128, hfull.shape[1]), np.float32), hfull[0:TOWN]], 0)
        else:
            blk = hfull[s - 128:s + TOWN]
        out.append(np.ascontiguousarray(blk, dtype=np.float32))
    return out


def kernel_impl(inputs, n_cores=8):
    f32 = np.float32
    x = np.asarray(inputs["x"], f32)
    SEQ = x.shape[1]
    TOWN = SEQ // n_cores
    NT = TOWN // 128 + 1
    TT = NT * 128
    mem = np.ascontiguousarray(np.asarray(inputs["mem"], f32)[0])
    g = lambda k: np.asarray(inputs[k], f32)
    row = lambda a: np.ascontiguousarray(a.reshape(1, -1))
    cores = list(range(n_cores))
    m_state, m_main, m_f = [], [], []
    for c in cores:
        ms = np.zeros((1, TT), f32); mm_ = np.zeros((1, TT), f32)
        lo = 128 if c == 0 else 3
        ms[0, lo:TOWN + 3] = 1.0
        mm_[0, lo:] = 1.0
        m_state.append(ms); m_main.append(mm_)
        m_f.append(np.zeros((1, 128), f32) if c == 0 else np.ones((1, 128), f32))
    hfull = np.ascontiguousarray(x[0])
    for l in range(DEPTH):
        hs = _halo_slices(hfull, n_cores, TOWN)
        common_state = dict(mix_g=row(g("mix_norm_g")[l]), w_in=np.ascontiguousarray(g("w_in")[l]), lbl=np.ascontiguousarray(g("hg_lb_logits")),
                            m2_cw=np.ascontiguousarray(g("m2_conv_w")[l]), m2_cb=row(g("m2_conv_b")[l]),
                            m2_dtb=row(g("m2_dt_bias")[l]), m2_alog=row(g("m2_A_log")[l]))
        nc = _prog("state", l, NT)
        res = run_bass_kernel_spmd(nc, [dict(common_state, h=hs[c], mask=m_state[c]) for c in cores], core_ids=cores)
        st = res.results
        last = (l == DEPTH - 1)
        common = dict(common_state, hg_ng=row(g("hg_norm_g")[l]), m2_dsk=row(g("m2_D")[l]), m2_ng=row(g("m2_norm_g")[l]),
                      w_bhg=np.ascontiguousarray(g("w_branch_hg")[l]), w_bm2=np.ascontiguousarray(g("w_branch_m2")[l]),
                      w_out=np.ascontiguousarray(g("w_out")[l]), mem=mem, mem_g=row(g("mem_norm_g")), xa_g=row(g("xa_norm_g")[l]),
                      xa_wq=np.ascontiguousarray(g("xa_wq")[l]), xa_wkv=np.ascontiguousarray(g("xa_wkv")[l]),
                      xa_wo=np.ascontiguousarray(g("xa_wo")[l]), ffn_g=row(g("ffn_norm_g")[l]),
                      ffn_wup=np.ascontiguousarray(g("ffn_w_up")[l]), ffn_cw=np.ascontiguousarray(g("ffn_conv_w")[l]),
                      ffn_cb=row(g("ffn_conv_b")[l]), ffn_wdn=np.ascontiguousarray(g("ffn_w_down")[l]))
        if last:
            common["fin_g"] = row(g("final_norm_g"))
        maps = []
        for c in cores:
            ps_hg = np.zeros((7, 128, 16, 128), f32); pd_hg = np.zeros((7, 128, 16), f32)
            ps_m2 = np.zeros((7, 128, 32, 64), f32); pd_m2 = np.zeros((7, 128, 32), f32)
            for j in range(7):
                src = c - 7 + j
                if src >= 0:
                    ps_hg[j] = st[src]["o_shg"]; pd_hg[j] = st[src]["o_dhg"]
                    ps_m2[j] = st[src]["o_sm2"]; pd_m2[j] = st[src]["o_dm2"]
            maps.append(dict(common, h=hs[c], mask=m_main[c], maskF=m_f[c], ps_hg=ps_hg, pd_hg=pd_hg, ps_m2=ps_m2, pd_m2=pd_m2))
        nc = _prog("main", l, NT, last)
        res = run_bass_kernel_spmd(nc, maps, core_ids=cores)
        hfull = np.concatenate([np.asarray(r["h_out"])[128:] for r in res.results], 0)
    return np.ascontiguousarray(hfull.reshape(1, SEQ, D).astype(np.float32))


def kernel(**inputs):
    return kernel_impl(inputs, 8)
```

```python
import contextlib
import numpy as np
import concourse.bass as bass
import concourse.mybir as mybir
from concourse.bass_utils import run_bass_kernel_spmd

F32 = mybir.dt.float32
BF16 = mybir.dt.bfloat16
AF = mybir.ActivationFunctionType
ALU = mybir.AluOpType
AX = mybir.AxisListType

D = 2048
KC = D // 128
DEPTH = 2
N_MEM = 256
HG_H = 16
M2_H = 32
M2_G = 8
D_FF = 5632
N_IN = 18464
C_Q, C_F, C_I, C_OG = 0, 2048, 4096, 6144
C_Z = 8192
C_X = 10240
C_B = C_X + 2048
C_C = C_B + 1024
C_DT = 14336
C_GHG = 14368
C_GM2 = 16416
EPS = 1e-6
M2_EPS = 1e-5

ENGS = ["pe", "act", "dve", "pool", "sp"]
DMA_WINDOW = 8


class Res:
    __slots__ = ("name", "last_write", "readers")

    def __init__(self, name=""):
        self.name = name
        self.last_write = None
        self.readers = []


class Op:
    __slots__ = ("eng", "fn", "deps", "signaled", "is_dma", "dma_idx", "tok", "cc")

    def __init__(self, eng, fn, is_dma=False):
        self.eng = eng
        self.fn = fn
        self.deps = []
        self.signaled = False
        self.is_dma = is_dma
        self.dma_idx = -1
        self.tok = None
        self.cc = False


class T:
    def __init__(self, t, res):
        self.t = t
        self.r = res

    def __getitem__(self, k):
        return self.t[k]


class Prog:
    def __init__(self, nc):
        self.nc = nc
        self.ops = {e: [] for e in ENGS}
        self.dmas = {e: [] for e in ENGS}
        self.stack = contextlib.ExitStack()
        self.scopes = [self.stack]
        self.n = 0
        self.pending = {e: [] for e in ENGS}

    def sb(self, shape, dt, name=None):
        self.n += 1
        name = (name or "sb") + f"_{self.n}"
        t = self.scopes[-1].enter_context(self.nc.sbuf_tensor(name, list(shape), dt))
        return T(t, Res(name))

    @contextlib.contextmanager
    def scope(self):
        st = contextlib.ExitStack()
        self.scopes.append(st)
        try:
            yield
        finally:
            self.scopes.pop()
            self.barrier()
            st.close()

    def ps(self, shape, dt, name=None):
        self.n += 1
        name = (name or "ps") + f"_{self.n}"
        t = self.stack.enter_context(self.nc.psum_tensor(name, list(shape), dt))
        return T(t, Res(name))

    def dram(self, shape, dt, name=None):
        self.n += 1
        name = (name or "dr") + f"_{self.n}"
        t = self.nc.dram_tensor(name, list(shape), dt)
        return T(t.ap(), Res(name))

    def _record(self, o, reads, writes):
        deps = []
        seen = set()

        def add(d):
            if d is None or d is o or id(d) in seen:
                return
            if d.eng == "pe" and o.eng == "pe" and not d.is_dma:
                return
            seen.add(id(d))
            deps.append(d)

        for r in reads:
            add(r.r.last_write)
        for w in writes:
            add(w.r.last_write)
            for rd in w.r.readers:
                add(rd)
        for d in self.pending[o.eng]:
            add(d)
        self.pending[o.eng] = []
        if o.is_dma:
            q = self.dmas[o.eng]
            o.dma_idx = len(q)
            if o.dma_idx >= DMA_WINDOW:
                add(q[o.dma_idx - DMA_WINDOW])
            q.append(o)
        for d in deps:
            d.signaled = True
        o.deps = deps
        for r in reads:
            r.r.readers.append(o)
        for w in writes:
            w.r.last_write = o
            w.r.readers = []
        self.ops[o.eng].append(o)
        return o

    def op(self, eng, fn, reads=(), writes=()):
        return self._record(Op(eng, fn), reads, writes)

    def dma(self, eng, fn, reads=(), writes=()):
        return self._record(Op(eng, fn, is_dma=True), reads, writes)

    def barrier(self):
        tails = []
        for e in ENGS:
            if self.ops[e]:
                tails.append(self.ops[e][-1])
            tails.extend(self.dmas[e][-DMA_WINDOW:])
        for e in ENGS:
            self.pending[e] = list(tails)

    def emit(self, final_deps):
        nc = self.nc
        st = self.stack
        sems = {e: st.enter_context(nc.semaphore(f"s_{e}")) for e in ENGS}
        dsem = {
            e: [st.enter_context(nc.semaphore(f"d_{e}{i}")) for i in range(DMA_WINDOW)]
            for e in ENGS
            if self.dmas[e]
        }
        fin = Op("sp", None)
        fin.deps = list(final_deps)
        for d in fin.deps:
            d.signaled = True
        self.ops["sp"].append(fin)
        ccs = []
        for e in ENGS:
            c = 0
            for o in self.ops[e]:
                if o.is_dma:
                    o.tok = (dsem[e][o.dma_idx % DMA_WINDOW], 16 * (o.dma_idx // DMA_WINDOW + 1))
                elif o.cc:
                    o.tok = (st.enter_context(nc.semaphore(f"cc{len(ccs)}")), 1)
                    ccs.append(o)
                elif o.signaled:
                    c += 1
                    o.tok = (sems[e], c)
        engobj = {"pe": "tensor", "act": "scalar", "dve": "vector", "pool": "gpsimd", "sp": "sync"}
        stats = {}
        with nc.Block() as block:
            for e in ENGS:
                ops = self.ops[e]
                if not ops:
                    continue
                nwait = [0]

                def body(eng, ops=ops, nwait=nwait):
                    waited = {}
                    for o in ops:
                        need = {}
                        for d in o.deps:
                            s, v = d.tok
                            k = id(s)
                            if waited.get(k, 0) < v and need.get(k, (None, 0))[1] < v:
                                need[k] = (s, v)
                        for k, (s, v) in need.items():
                            eng.wait_ge(s, v)
                            waited[k] = v
                            nwait[0] += 1
                        if o.fn is None:
                            continue
                        ins = o.fn(eng)
                        if o.is_dma:
                            ins.then_inc(o.tok[0], 16)
                        elif o.cc:
                            ins.then_inc(o.tok[0], 1)
                        elif o.signaled:
                            ins.then_inc(o.tok[0], 1)

                getattr(block, engobj[e])(body)
                stats[e] = (len(ops), nwait[0])
        return stats


class Ring:
    def __init__(self, items):
        self.items = items
        self.i = 0

    def next(self):
        x = self.items[self.i % len(self.items)]
        self.i += 1
        return x


class K:
    def __init__(self, nc, NT):
        self.nc = nc
        self.P = Prog(nc)
        self.NT = NT
        self.TT = NT * 128
        self.consts()

    def act(self, out, in_, func, reads, writes, **kw):
        return self.P.op("act", lambda e: e.activation(out=out, in_=in_, func=func, **kw), reads, writes)

    def tt(self, out, in0, in1, op, reads, writes, eng="dve"):
        return self.P.op(eng, lambda e: e.tensor_tensor(out=out, in0=in0, in1=in1, op=op), reads, writes)

    def ts(self, out, in0, s1, s2, op0, op1, reads, writes, eng="dve"):
        if op1 is None:
            return self.P.op(eng, lambda e: e.tensor_scalar(out=out, in0=in0, scalar1=s1, scalar2=None, op0=op0), reads, writes)
        return self.P.op(eng, lambda e: e.tensor_scalar(out=out, in0=in0, scalar1=s1, scalar2=s2, op0=op0, op1=op1), reads, writes)

    def stt(self, out, in0, scalar, in1, op0, op1, reads, writes):
        return self.P.op("dve", lambda e: e.scalar_tensor_tensor(out=out, in0=in0, scalar=scalar, in1=in1, op0=op0, op1=op1), reads, writes)

    def mm(self, out, lhsT, rhs, start, stop, reads, writes):
        return self.P.op("pe", lambda e: e.matmul(out, lhsT, rhs, start=start, stop=stop), reads, writes)

    def tr(self, out, in_, ident, reads, writes):
        return self.P.op("pe", lambda e: e.transpose(out, in_, ident), reads, writes)

    def dma(self, out, in_, reads, writes, eng="sp"):
        return self.P.dma(eng, lambda e: e.dma_start(out=out, in_=in_), reads, writes)

    def consts(self):
        P = self.P
        self.idf = P.sb([128, 128], F32, "idf")
        self.idb = P.sb([128, 128], BF16, "idb")
        self.ones_f = P.sb([128, 128], F32, "ones_f")
        self.ones_b = P.sb([128, 128], BF16, "ones_b")
        self.triu = P.sb([128, 128], F32, "triu")
        self.epsc = P.sb([128, 4], F32, "epsc")
        iot = P.sb([128, 128], F32, "iot")
        P.op("pool", lambda e: e.iota(iot[:], [[1, 128]], base=0, channel_multiplier=-1,
                                      allow_small_or_imprecise_dtypes=True), (), [iot])
        self.ts(self.idf[:], iot[:], 0.0, None, ALU.is_equal, None, [iot], [self.idf])
        self.ts(self.idb[:], iot[:], 0.0, None, ALU.is_equal, None, [iot], [self.idb])
        self.ts(self.triu[:], iot[:], 0.0, None, ALU.is_ge, None, [iot], [self.triu])
        P.op("dve", lambda e: e.memset(self.ones_f[:], 1.0), (), [self.ones_f])
        P.op("dve", lambda e: e.memset(self.ones_b[:], 1.0), (), [self.ones_b])
        P.op("dve", lambda e: e.memset(self.epsc[:, 0:1], EPS), (), [self.epsc])
        P.op("dve", lambda e: e.memset(self.epsc[:, 1:2], M2_EPS), (), [self.epsc])
        P.op("dve", lambda e: e.memset(self.epsc[:, 2:3], 1.0), (), [self.epsc])
        self.MAXS = 768
        self.rmask = P.sb([128, self.MAXS], F32, "rmask")
        P.op("dve", lambda e: e.memset(self.rmask[:], 1.0), (), [self.rmask])
        P.op("dve", lambda e: e.memset(self.rmask[:].rearrange("p (c j) -> p c j", j=64)[:, :, 0:1], 0.0), (), [self.rmask])
        self.pp = Ring([P.ps([128, 512], F32, f"pp{i}") for i in range(4)])
        self.pq = Ring([P.ps([128, 512], F32, f"pq{i}") for i in range(2)])
        self.pt = Ring([P.ps([128, 1024], BF16, f"pt{i}") for i in range(2)])

    def load_cols(self, vec_ap, n, name):
        P = self.P
        rows = P.sb([n, 128], F32, name + "_r")
        cols = P.sb([128, n], F32, name)
        self.dma(rows[:], vec_ap, [], [rows])
        ps = self.pq.next()
        self.tr(ps[:, 0:n], rows[:], self.idf[0:n, 0:n], [rows, self.idf], [ps])
        self.act(cols[:], ps[:, 0:n], AF.Copy, [ps], [cols])
        return cols

    def norm_to_uT(self, h_ap, hres, g_ap, uT_d, tiles=None):
        P = self.P
        gbc = P.sb([128, D], F32, "gbc")
        self.dma(gbc[:], g_ap.partition_broadcast(128), [], [gbc])
        hb = Ring([P.sb([128, D], F32, "hb") for _ in range(4)])
        ub = Ring([P.sb([128, D], BF16, "ub") for _ in range(4)])
        junk = P.sb([128, D], BF16, "junk")
        stt_ = Ring([P.sb([128, 4], F32, "nst") for _ in range(4)])
        uo = Ring([P.sb([128, KC, 128], BF16, "uo") for _ in range(4)])
        for i in (tiles if tiles is not None else range(self.NT)):
            h = hb.next(); u = ub.next(); s = stt_.next(); o = uo.next()
            self.dma(h[:], h_ap[i * 128:(i + 1) * 128, :], [hres], [h])
            self.act(junk[:], h[:], AF.Square, [h], [junk, s], accum_out=s[:, 0:1])
            self.act(s[:, 1:2], s[:, 0:1], AF.Ln, [s, self.epsc], [s], scale=1.0 / D, bias=self.epsc[:, 0:1])
            self.act(s[:, 2:3], s[:, 1:2], AF.Exp, [s], [s], scale=-0.5)
            self.stt(u[:], h[:], s[:, 2:3], gbc[:], ALU.mult, ALU.mult, [h, s, gbc], [u])
            for half in range(2):
                pt = self.pt.next()
                for j in range(8):
                    kc = half * 8 + j
                    self.tr(pt[:, j * 128:(j + 1) * 128], u[:, kc * 128:(kc + 1) * 128], self.idb[:], [u, self.idb], [pt])
                self.act(o[:, half * 8:(half + 1) * 8, :], pt[:].rearrange("p (j t) -> p j t", j=8), AF.Copy, [pt], [o])
            self.dma(uT_d[:, :, i * 128:(i + 1) * 128].rearrange("k p t -> p k t"), o[:], [o], [uT_d])

    def hg_setup(self, lbl_ap, l, hgng_ap):
        lg0 = self.load_cols(lbl_ap[0:1, :].rearrange("o (h p) -> (o h) p", p=128), 16, "lg0")
        lg1 = self.load_cols(lbl_ap[1:2, :].rearrange("o (h p) -> (o h) p", p=128), 16, "lg1")
        self.lb = self.P.sb([128, 16], F32, "lb")
        self.oml = self.P.sb([128, 16], F32, "oml")
        if l == 0:
            self.tt(self.lb[:], lg0[:], lg0[:], ALU.subtract, [lg0], [self.lb])
        else:
            self.tt(self.lb[:], lg1[:], lg0[:], ALU.subtract, [lg0, lg1], [self.lb])
            self.act(self.lb[:], self.lb[:], AF.Sigmoid, [self.lb], [self.lb])
        self.ts(self.oml[:], self.lb[:], -1.0, 1.0, ALU.mult, ALU.add, [self.lb], [self.oml])
        if hgng_ap is not None:
            self.hgng = self.load_cols(hgng_ap.rearrange("o (h p) -> (o h) p", p=128), 16, "hgng")

    def hg_setup_lb(self, lbl_ap, l):
        self.hg_setup(lbl_ap, l, None)

    def load_uT_seg(self, uT_d, t0, nt, ring):
        u = ring.next()
        self.dma(u[:, :, 0:nt * 128], uT_d[:, :, t0 * 128:(t0 + nt) * 128].rearrange("k p t -> p k t"), [uT_d], [u])
        return u

    @staticmethod
    def interleave(gens):
        gens = list(gens)
        while gens:
            for g_ in list(gens):
                try:
                    next(g_)
                except StopIteration:
                    gens.remove(g_)

    def pipeline(self, units, *stages):
        units = list(units)
        ns = len(stages)
        for i in range(len(units) + ns - 1):
            gens = []
            for j in range(ns - 1, -1, -1):
                if 0 <= i - j < len(units):
                    gens.append(stages[j](units[i - j]))
            self.interleave(gens)

    def make_segs(self, maxt):
        n = -(-self.NT // maxt)
        base, extra = divmod(self.NT, n)
        segs, t = [], 0
        for i in range(n):
            c = base + (1 if i < extra else 0)
            segs.append((t, c)); t += c
        return segs

    def hg_mixer(self, uT_d, w_ap, mask_ap, S, hgT_d, state_only=False, dec_out=None):
        P = self.P
        MAXS = self.MAXS
        MC = MAXS // 64
        so = state_only
        useg = Ring([P.sb([128, KC, MAXS], BF16, "useg") for _ in range(1)])
        wts = Ring([P.sb([128, KC, 4, 128], BF16, "hgw") for _ in range(2)])
        mk = Ring([P.sb([128, MAXS], F32, "mk") for _ in range(2)])

        def ring(n, shape, dt, nm):
            return Ring([P.sb(shape, dt, nm) for _ in range(n)])

        rA = ring(2, [128, MAXS], F32, "hA")
        rB, rC, rD = (ring(1, [128, MAXS], F32, n) for n in ("hB", "hC", "hD"))
        rE = ring(2, [128, 2 if so else MAXS], F32, "hE")
        rG = ring(1, [128, MAXS], BF16, "hG")
        rI = ring(2, [128, MAXS], BF16, "hI")
        q_ = 2 if so else MAXS
        rSG = ring(3, [128, q_], F32, "hSG")
        rO = ring(1, [128, q_], F32, "hO")
        rF, rHo = (ring(2, [128, q_], BF16, n) for n in ("hF", "hHo"))
        rsq = ring(1, [128, q_], F32, "hsq")
        rcs = ring(2, [128, 4, 16], F32, "hcs")
        rkv = ring(2, [64, MC, 256], BF16, "hkv")
        rsT = ring(2, [64, MC, 2 if so else 64], BF16, "hsT")
        rtm = ring(2, [128, MC, 128], F32, "htm")
        rSp = ring(3, [128, 128], BF16, "hSp")
        rrs = ring(2, [128, 512], F32, "hrs")
        scale = 128 ** -0.5
        seg = {}

        def stage0(un):
            (t0, nt, hd) = un
            NS = nt * 128
            NCH = NS // 64
            if hd == 0:
                seg["u"] = self.load_uT_seg(uT_d, t0, nt, useg)
                m_ = mk.next()
                self.dma(m_[:, 0:NS], mask_ap[0:1, t0 * 128:t0 * 128 + NS].partition_broadcast(128), [], [m_])
                seg["mask"] = m_
            u = seg["u"]; mask = seg["mask"]
            chunks = [(n0, min(512, NS - n0)) for n0 in range(0, NS, 512)]
            wt = wts.next()
            mats = [(1, C_F), (2, C_I)] if so else [(1, C_F), (0, C_Q), (2, C_I), (3, C_OG)]
            for (m, c0) in mats:
                self.dma(wt[:, :, m, :], w_ap[:, c0 + hd * 128:c0 + (hd + 1) * 128].rearrange("(k p) c -> p k c", p=128),
                         [], [wt], eng="pool")
            A, E, I, SG = (r.next() for r in (rA, rE, rI, rSG))
            st_ = dict(NS=NS, NCH=NCH, chunks=chunks, SG=SG, A=A, E=E, I=I, mask=mask, t0=t0, hd=hd)
            seg[("st", t0, hd)] = st_
            for (m, c0) in mats:
                for (n0, nn) in chunks:
                    ps = self.pp.next()
                    for kc in range(KC):
                        self.mm(ps[:, 0:nn], wt[:, kc, m, :], u[:, kc, n0:n0 + nn], kc == 0, kc == KC - 1, [wt, u], [ps])
                    if m == 1:
                        self.act(A[:, n0:n0 + nn], ps[:, 0:nn], AF.Sigmoid, [ps], [A])
                    elif m == 0:
                        self.act(E[:, n0:n0 + nn], ps[:, 0:nn], AF.Copy, [ps], [E], scale=scale)
                    elif m == 2:
                        self.act(I[:, n0:n0 + nn], ps[:, 0:nn], AF.Copy, [ps], [I])
                    else:
                        self.act(SG[:, n0:n0 + nn], ps[:, 0:nn], AF.Sigmoid, [ps], [SG])
                    yield

        def stage1(un):
            (t0, nt, hd) = un
            st_ = seg[("st", t0, hd)]
            NS, NCH, A, E, I, mask = (st_[k_] for k_ in ("NS", "NCH", "A", "E", "I", "mask"))
            B, C, Dd, G = (r.next() for r in (rB, rC, rD, rG))
            F = rF.next()
            cs = rcs.next(); kv = rkv.next(); sTa = rsT.next(); tm = rtm.next()
            st_.update(F=F, cs=cs, kv=kv, sT=sTa, tm=tm)
            self.ts(A[:, 0:NS], A[:, 0:NS], self.oml[:, hd:hd + 1], self.lb[:, hd:hd + 1], ALU.mult, ALU.add, [A, self.oml, self.lb], [A])
            self.ts(B[:, 0:NS], A[:, 0:NS], -1.0, 1.0, ALU.mult, ALU.add, [A], [B])
            yield
            self.tt(B[:, 0:NS], B[:, 0:NS], mask[:, 0:NS], ALU.mult, [B, mask], [B])
            self.act(A[:, 0:NS], A[:, 0:NS], AF.Ln, [A], [A])
            self.tt(A[:, 0:NS], A[:, 0:NS], mask[:, 0:NS], ALU.mult, [A, mask], [A])
            yield
            P.op("dve", lambda e, C=C, A=A, NS=NS: e.tensor_tensor_scan(out=C[:, 0:NS], data0=self.rmask[:, 0:NS], data1=A[:, 0:NS],
                                                                    initial=0.0, op0=ALU.mult, op1=ALU.add), [A, self.rmask], [C])
            C3 = C[:, 0:NS].rearrange("p (c j) -> p c j", j=64)
            A3 = A[:, 0:NS].rearrange("p (c j) -> p c j", j=64)
            yield
            self.tt(A3, C3, C3[:, :, 31:32].to_broadcast([128, NCH, 64]), ALU.subtract, [C], [A])
            self.tt(cs[:, 3, 0:NCH], C3[:, :, 63], C3[:, :, 31], ALU.subtract, [C], [cs])
            self.act(cs[:, 0, 0:NCH], C3[:, :, 63], AF.Exp, [C], [cs])
            self.act(cs[:, 1, 0:NCH], cs[:, 3, 0:NCH], AF.Exp, [cs], [cs])
            self.act(cs[:, 2, 0:NCH], C3[:, :, 31], AF.Exp, [C], [cs])
            if dec_out is not None:
                P.op("dve", lambda e, cs=cs, C3=C3, NCH=NCH: e.tensor_reduce(out=cs[:, 3, 0:1], in_=C3[:, :, 63], axis=AX.X, op=ALU.add), [C], [cs])
                self.tt(dec_out[:, hd:hd + 1], dec_out[:, hd:hd + 1], cs[:, 3, 0:1], ALU.add, [dec_out, cs], [dec_out])
            self.act(Dd[:, 0:NS], A[:, 0:NS], AF.Exp, [A], [Dd], scale=-1.0)
            yield
            self.tt(G[:, 0:NS], B[:, 0:NS], Dd[:, 0:NS], ALU.mult, [B, Dd], [G])
            if not so:
                self.act(A[:, 0:NS], A[:, 0:NS], AF.Exp, [A], [A])
                self.tt(F[:, 0:NS], E[:, 0:NS], A[:, 0:NS], ALU.mult, [E, A], [F])
            yield
            def state_mm(c):
                pst = self.pq.next()
                self.mm(pst[:, 0:128], kv[:, c, 0:128], kv[:, c, 128:256], True, True, [kv], [pst])
                self.ts(tm[:, c, :], pst[:, 0:128], cs[:, 1, c:c + 1], None, ALU.mult, None, [pst, cs], [tm])

            for c in range(NCH):
                c0 = c * 64
                pt = self.pt.next()
                self.tr(pt[0:64, 0:128], G[:, c0:c0 + 64], self.idb[:], [G, self.idb], [pt])
                self.tr(pt[0:64, 128:256], I[:, c0:c0 + 64], self.idb[:], [I, self.idb], [pt])
                self.act(kv[:, c, :], pt[0:64, 0:256], AF.Copy, [pt], [kv])
                if not so:
                    psc = self.pq.next()
                    self.mm(psc[0:64, 0:64], G[:, c0:c0 + 64], F[:, c0:c0 + 64], True, True, [G, F], [psc])
                    self.tt(sTa[:, c, :], psc[0:64, 0:64], self.triu[0:64, 0:64], ALU.mult, [psc, self.triu], [sTa])
                if c > 0:
                    state_mm(c - 1)
                yield
            state_mm(NCH - 1)
            yield

        def stage2(un):
            (t0, nt, hd) = un
            d = seg.pop(("st", t0, hd))
            NS, NCH, chunks = d["NS"], d["NCH"], d["chunks"]
            SG, F, cs, kv, sTa, tm = (d[k_] for k_ in ("SG", "F", "cs", "kv", "sT", "tm"))
            O = rO.next(); Ho = rHo.next()
            Sh = S[:, hd, :]
            Sp = None
            if not so:
                Sp = rSp.next()
                self.ts(Sp[:], Sh, cs[:, 2, 0:1], None, ALU.mult, None, [S, cs], [Sp])
                yield
            for c in range(NCH):
                c0 = c * 64
                self.stt(Sh, Sh, cs[:, 0, c:c + 1], tm[:, c, :], ALU.mult, ALU.add, [S, cs, tm], [S])
                if not so:
                    Spn = None
                    if c + 1 < NCH:
                        Spn = rSp.next()
                        self.ts(Spn[:], Sh, cs[:, 2, c + 1:c + 2], None, ALU.mult, None, [S, cs], [Spn])
                    po = self.pq.next()
                    self.mm(po[:, 0:64], kv[:, c, 128:256], sTa[:, c, :], True, False, [kv, sTa], [po])
                    self.mm(po[:, 0:64], Sp[:], F[:, c0:c0 + 64], False, True, [Sp, F], [po])
                    self.tt(O[:, c0:c0 + 64], po[:, 0:64], SG[:, c0:c0 + 64], ALU.mult, [po, SG], [O])
                    Sp = Spn
                yield
            if so:
                return
            sq = rsq.next()
            self.tt(sq[:, 0:NS], O[:, 0:NS], O[:, 0:NS], ALU.mult, [O], [sq])
            for (n0, nn) in chunks:
                ps = self.pp.next()
                self.mm(ps[:, 0:nn], self.ones_f[:], sq[:, n0:n0 + nn], True, True, [self.ones_f, sq], [ps])
                rs = rrs.next()
                self.act(rs[:, 0:nn], ps[:, 0:nn], AF.Ln, [ps, self.epsc], [rs], scale=1.0 / 128, bias=self.epsc[:, 0:1])
                self.act(rs[:, 0:nn], rs[:, 0:nn], AF.Exp, [rs], [rs], scale=-0.5)
                self.stt(Ho[:, n0:n0 + nn], O[:, n0:n0 + nn], self.hgng[:, hd:hd + 1], rs[:, 0:nn], ALU.mult, ALU.mult,
                         [O, self.hgng, rs], [Ho])
                yield
            self.dma(hgT_d[hd, :, t0 * 128:t0 * 128 + NS], Ho[:, 0:NS], [Ho], [hgT_d])

        units = [(t0, nt, hd) for (t0, nt) in self.make_segs(MAXS // 128) for hd in range(HG_H)]
        self.pipeline(units, stage0, stage1, stage2)

    def m2_setup(self, w_ap, convw_ap, convb_ap, dtb_ap, alog_ap, dsk_ap, ng_ap, mask_ap):
        P = self.P
        self.cw = [self.load_cols(convw_ap[k:k + 1, :].rearrange("o (j p) -> (o j) p", p=128), 32, f"cw{k}") for k in range(4)]
        self.cbias = self.load_cols(convb_ap.rearrange("o (j p) -> (o j) p", p=128), 32, "cbias")
        self.maskT = self.load_cols(mask_ap.rearrange("o (j p) -> (o j) p", p=128), self.NT, "maskT")
        self.dtb = P.sb([128, 32], F32, "dtb")
        self.negA = P.sb([128, 32], F32, "negA")
        self.dsk = P.sb([128, 32], F32, "dsk")
        self.m2ng_ap = ng_ap
        self.dma(self.dtb[:], dtb_ap.partition_broadcast(128), [], [self.dtb])
        self.dma(self.negA[:], alog_ap.partition_broadcast(128), [], [self.negA])
        if dsk_ap is not None:
            self.dma(self.dsk[:], dsk_ap.partition_broadcast(128), [], [self.dsk])
        self.act(self.negA[:], self.negA[:], AF.Exp, [self.negA], [self.negA])
        self.ts(self.negA[:], self.negA[:], -1.0, None, ALU.mult, None, [self.negA], [self.negA])
        self.wdt = P.sb([128, KC, 32], BF16, "wdt")
        self.dma(self.wdt[:], w_ap[:, C_DT:C_DT + 32].rearrange("(k p) c -> p k c", p=128), [], [self.wdt], eng="pool")
        self.strict = P.sb([128, 128], F32, "strict")
        self.ts(self.strict[:], self.triu[:], -1.0, 1.0, ALU.mult, ALU.add, [self.triu], [self.strict])
        self.carry = P.sb([128, 8, 4, 3], F32, "carry")
        P.op("dve", lambda e: e.memset(self.carry[:], 0.0), (), [self.carry])

    def m2_mixer(self, uT_d, w_ap, S, Sbf, m2T_d, state_only=False, dec_out=None):
        P = self.P
        MT = 5
        MAXS = MT * 128
        so = state_only

        def ring(n, shape, dt, nm):
            return Ring([P.sb(shape, dt, nm) for _ in range(n)])

        useg = ring(1, [128, KC, MAXS], BF16, "useg")
        wts = ring(2, [128, KC, 768], BF16, "m2w")
        rxr = ring(2, [128, 4, 3 + MAXS], F32, "xr")
        racc = ring(1, [128, 2, MAXS], F32, "xacc")
        rtmpc = ring(1, [128, MAXS], F32, "ctmp")
        rBT = ring(1, [128, MAXS], BF16, "BT")
        rCT = ring(2, [128, 2 if so else MAXS], BF16, "CT")
        rzs = ring(2, [128, 1 if so else MT, 256], F32, "zs")
        rm2o = ring(2, [128, 2, 2 if so else MAXS], BF16, "m2o")
        rng = ring(2, [128, 2 if so else 256], F32, "m2ngs")
        rybuf = ring(2, [128, 1 if so else MT, 256], F32, "ybuf")
        rxddA = ring(2, [128, MT, 256], BF16, "xddA")
        rBtmA = ring(2, [128, MT, 128], BF16, "BtmA")
        rps = Ring([dict((n, P.sb([128, MT, 32], F32, n)) for n in ("dtS", "aS", "cumS", "lastS", "ecS", "elS", "ddS")) for _ in range(2)])
        rxs = ring(2, [128, 256], F32, "xs")
        rxdt = ring(4, [128, 4, 64], BF16, "xdt")
        q_ = 2 if so else 128
        rcbm = ring(3, [128, q_], F32, "cbm")
        rM1 = ring(3, [128, 4, q_], F32, "M1")
        rEL = ring(2, [128, 4, q_], F32, "EL")
        rWT = ring(3, [128, 4, q_], BF16, "WT")
        ry = ring(2, [128, 2 * q_], F32, "y")
        ry2 = ring(3, [128, 2 * q_], F32, "y2")
        ryn = ring(2, [128, 2 * q_], BF16, "yn")
        rst = ring(2, [128, 4], F32, "mst")
        rtS = ring(2, [128, 4, 64], F32, "tS")
        rpSb = ring(3, [128, 256], F32, "pSb")
        junk = P.sb([128, 256], BF16, "mjunk")
        seg = {}

        def seg_prep(t0, nt):
            u = self.load_uT_seg(uT_d, t0, nt, useg)
            seg["u"] = u
            ps_ = rps.next()
            seg["ps"] = ps_
            dtS, aS, cumS, lastS, ecS, elS, ddS = (ps_[n] for n in ("dtS", "aS", "cumS", "lastS", "ecS", "elS", "ddS"))
            for ti in range(nt):
                ps = self.pq.next()
                for kc in range(KC):
                    self.mm(ps[:, 0:32], u[:, kc, ti * 128:(ti + 1) * 128], self.wdt[:, kc, :], kc == 0, kc == KC - 1, [u, self.wdt], [ps])
                self.tt(dtS[:, ti, :], ps[:, 0:32], self.dtb[:], ALU.add, [ps, self.dtb], [dtS])
            self.act(dtS[:, 0:nt, :], dtS[:, 0:nt, :], AF.Exp, [dtS], [dtS])
            self.act(dtS[:, 0:nt, :], dtS[:, 0:nt, :], AF.Ln, [dtS, self.epsc], [dtS], bias=self.epsc[:, 2:3])
            for ti in range(nt):
                self.ts(dtS[:, ti, :], dtS[:, ti, :], self.maskT[:, t0 + ti:t0 + ti + 1], None, ALU.mult, None, [dtS, self.maskT], [dtS])
                self.tt(aS[:, ti, :], dtS[:, ti, :], self.negA[:], ALU.mult, [dtS, self.negA], [aS])
            for ti in range(nt):
                ps = self.pq.next()
                self.mm(ps[:, 0:32], self.triu[:], aS[:, ti, :], True, True, [self.triu, aS], [ps])
                self.mm(ps[:, 32:64], self.ones_f[:], aS[:, ti, :], True, True, [self.ones_f, aS], [ps])
                self.act(cumS[:, ti, :], ps[:, 0:32], AF.Copy, [ps], [cumS])
                self.act(lastS[:, ti, :], ps[:, 32:64], AF.Copy, [ps], [lastS])
                if dec_out is not None:
                    self.tt(dec_out[:], dec_out[:], lastS[:, ti, :], ALU.add, [dec_out, lastS], [dec_out])
            self.act(ecS[:, 0:nt, :], cumS[:, 0:nt, :], AF.Exp, [cumS], [ecS])
            self.act(elS[:, 0:nt, :], lastS[:, 0:nt, :], AF.Exp, [lastS], [elS])
            self.tt(ddS[:, 0:nt, :], lastS[:, 0:nt, :], cumS[:, 0:nt, :], ALU.subtract, [lastS, cumS], [ddS])
            self.act(ddS[:, 0:nt, :], ddS[:, 0:nt, :], AF.Exp, [ddS], [ddS])
            self.tt(ddS[:, 0:nt, :], ddS[:, 0:nt, :], dtS[:, 0:nt, :], ALU.mult, [ddS, dtS], [ddS])

        def stage0(un):
            (t0, nt, gi) = un
            NS = nt * 128
            if gi == 0:
                seg_prep(t0, nt)
                yield
            u = seg["u"]
            chunks = [(n0, min(512, NS - n0)) for n0 in range(0, NS, 512)]
            wt = wts.next()
            srcs = [(0, C_Z + gi * 256, 256), (256, C_X + gi * 256, 256), (512, C_B + gi * 128, 128), (640, C_C + gi * 128, 128)]
            for (o0, c0, w_) in srcs:
                if so and o0 in (0, 640):
                    continue
                self.dma(wt[:, :, o0:o0 + w_], w_ap[:, c0:c0 + w_].rearrange("(k p) c -> p k c", p=128), [], [wt], eng="pool")
            xr = rxr.next(); zs = rzs.next()
            seg[("st", t0, gi)] = dict(xr=xr, zs=zs, ps=seg["ps"], chunks=chunks)
            blks = [0, 1, 2] if so else [0, 1, 2, 3]
            if not so:
                for ti in range(nt):
                    ps = self.pp.next()
                    for kc in range(KC):
                        self.mm(ps[:, 0:256], u[:, kc, ti * 128:(ti + 1) * 128], wt[:, kc, 0:256], kc == 0, kc == KC - 1, [u, wt], [ps])
                    self.act(zs[:, ti, :], ps[:, 0:256], AF.Silu, [ps], [zs])
                    yield
            for b in blks:
                for (n0, nn) in chunks:
                    ps = self.pp.next()
                    for kc in range(KC):
                        self.mm(ps[:, 0:nn], wt[:, kc, 256 + b * 128:256 + (b + 1) * 128], u[:, kc, n0:n0 + nn], kc == 0, kc == KC - 1, [wt, u], [ps])
                    self.act(xr[:, b, 3 + n0:3 + n0 + nn], ps[:, 0:nn], AF.Copy, [ps], [xr])
                    yield

        def stage1(un):
            (t0, nt, gi) = un
            NS = nt * 128
            st_ = seg[("st", t0, gi)]
            xr = st_["xr"]; ps_ = st_["ps"]
            dtS, aS, ddS = ps_["dtS"], ps_["aS"], ps_["ddS"]
            acc = racc.next(); BT = rBT.next(); CT = rCT.next(); m2o = rm2o.next()
            ybuf = rybuf.next(); xddA = rxddA.next(); BtmA = rBtmA.next(); ngs = rng.next()
            if not so:
                self.dma(ngs[:], self.m2ng_ap[0:1, gi * 256:(gi + 1) * 256].partition_broadcast(128), [], [ngs])
            st_.update(CT=CT, m2o=m2o, ybuf=ybuf, xddA=xddA, BtmA=BtmA, ngs=ngs)
            blks = [0, 1, 2] if so else [0, 1, 2, 3]
            for b in blks:
                j = (gi * 2 + b) if b < 2 else (16 + gi if b == 2 else 24 + gi)
                self.act(xr[:, b, 0:3], self.carry[:, gi, b, :], AF.Copy, [self.carry], [xr])
                self.act(self.carry[:, gi, b, :], xr[:, b, NS:NS + 3], AF.Copy, [xr], [self.carry])
                tmpc = rtmpc.next()
                self.ts(tmpc[:, 0:NS], xr[:, b, 3:3 + NS], self.cw[3][:, j:j + 1], self.cbias[:, j:j + 1], ALU.mult, ALU.add,
                        [xr, self.cw[3], self.cbias], [tmpc])
                for k_ in range(3):
                    self.stt(tmpc[:, 0:NS], xr[:, b, k_:k_ + NS], self.cw[k_][:, j:j + 1], tmpc[:, 0:NS], ALU.mult, ALU.add,
                             [xr, self.cw[k_], tmpc], [tmpc])
                if b < 2:
                    self.act(acc[:, b, 0:NS], tmpc[:, 0:NS], AF.Silu, [tmpc], [acc])
                elif b == 2:
                    self.act(BT[:, 0:NS], tmpc[:, 0:NS], AF.Silu, [tmpc], [BT])
                else:
                    self.act(CT[:, 0:NS], tmpc[:, 0:NS], AF.Silu, [tmpc], [CT])
                yield
            hs = slice(gi * 4, gi * 4 + 4)
            tl = {}

            def phaseA(ti):
                tk = slice(ti * 128, (ti + 1) * 128)
                px = self.pq.next()
                self.tr(px[:, 0:128], acc[:, 0, tk], self.idf[:], [acc, self.idf], [px])
                self.tr(px[:, 128:256], acc[:, 1, tk], self.idf[:], [acc, self.idf], [px])
                pb = self.pt.next()
                self.tr(pb[:, 0:128], BT[:, tk], self.idb[:], [BT, self.idb], [pb])
                xs = rxs.next()
                self.act(xs[:], px[:, 0:256], AF.Copy, [px], [xs])
                self.act(BtmA[:, ti, :], pb[:, 0:128], AF.Copy, [pb], [BtmA])
                xs3 = xs[:].rearrange("p (h q) -> p h q", q=64)
                self.tt(xddA[:, ti, :].rearrange("p (h q) -> p h q", q=64), xs3, ddS[:, ti, hs].unsqueeze(2).to_broadcast([128, 4, 64]),
                        ALU.mult, [xs, ddS], [xddA])
                if so:
                    return
                xdt = rxdt.next(); y2 = ry2.next(); cbm = rcbm.next(); M1 = rM1.next()
                self.tt(xdt[:], xs3, dtS[:, ti, hs].unsqueeze(2).to_broadcast([128, 4, 64]), ALU.mult, [xs, dtS], [xdt])
                self.tt(y2[:].rearrange("p (h q) -> p h q", q=64), xs3, self.dsk[:, hs].unsqueeze(2).to_broadcast([128, 4, 64]),
                        ALU.mult, [xs, self.dsk], [y2])
                pcb = self.pq.next()
                self.mm(pcb[:, 0:128], BT[:, tk], CT[:, tk], True, True, [BT, CT], [pcb])
                self.tt(cbm[:], pcb[:, 0:128], self.triu[:], ALU.mult, [pcb, self.triu], [cbm])
                self.tt(M1[:], self.strict[:].unsqueeze(1).to_broadcast([128, 4, 128]),
                        aS[:, ti, hs].unsqueeze(2).to_broadcast([128, 4, 128]), ALU.mult, [self.strict, aS], [M1])
                tl[ti] = dict(xdt=xdt, y2=y2, cbm=cbm, M1=M1)

            def phaseB(ti):
                d_ = tl[ti]
                pD = self.pq.next()
                for h in range(4):
                    self.mm(pD[:, h * 128:(h + 1) * 128], d_["M1"][:, h, :], self.triu[:], True, True, [d_["M1"], self.triu], [pD])
                EL = rEL.next()
                self.act(EL[:], pD[:].rearrange("p (h t) -> p h t", h=4), AF.Exp, [pD], [EL])
                WT = rWT.next()
                self.tt(WT[:], EL[:], d_["cbm"][:].unsqueeze(1).to_broadcast([128, 4, 128]), ALU.mult, [EL, d_["cbm"]], [WT])
                d_["WT"] = WT

            def phaseC(ti):
                d_ = tl.pop(ti)
                py = self.pq.next()
                for h in range(4):
                    self.mm(py[:, h * 64:(h + 1) * 64], d_["WT"][:, h, :], d_["xdt"][:, h, :], True, True, [d_["WT"], d_["xdt"]], [py])
                self.tt(ybuf[:, ti, :], d_["y2"][:], py[:, 0:256], ALU.add, [d_["y2"], py], [ybuf])

            for step in range(nt + (0 if so else 2)):
                if not so and 0 <= step - 2 < nt:
                    phaseC(step - 2)
                if not so and 0 <= step - 1 < nt:
                    phaseB(step - 1)
                if step < nt:
                    phaseA(step)
                yield

        def stage2(un):
            (t0, nt, gi) = un
            NS = nt * 128
            d = seg.pop(("st", t0, gi))
            zs = d["zs"]; ps_ = d["ps"]
            ecS, elS = ps_["ecS"], ps_["elS"]
            CT, m2o, ybuf, xddA, BtmA, ngs = (d[k_] for k_ in ("CT", "m2o", "ybuf", "xddA", "BtmA", "ngs"))
            Sg = S[:, gi * 4:(gi + 1) * 4, :]
            Sbg = Sbf[:, gi * 4:(gi + 1) * 4, :]
            hs = slice(gi * 4, gi * 4 + 4)
            def ps_mm(ti):
                pS_ = self.pq.next()
                self.mm(pS_[:, 0:256], BtmA[:, ti, :], xddA[:, ti, :], True, True, [BtmA, xddA], [pS_])
                h_ = rpSb.next()
                self.act(h_[:], pS_[:, 0:256], AF.Copy, [pS_], [h_])
                return h_, 0

            nxt = ps_mm(0)
            for ti in range(nt):
                tk = slice(ti * 128, (ti + 1) * 128)
                if not so:
                    py = self.pq.next()
                    self.mm(py[:, 0:256], CT[:, tk], Sbg.rearrange("p h q -> p (h q)"), True, True, [CT, Sbf], [py])
                (pS, po_) = nxt
                if ti + 1 < nt:
                    nxt = ps_mm(ti + 1)
                tS = rtS.next()
                self.tt(tS[:], Sg, elS[:, ti, hs].unsqueeze(2).to_broadcast([128, 4, 64]), ALU.mult, [S, elS], [tS])
                self.tt(Sg, tS[:], pS[:, po_:po_ + 256].rearrange("p (h q) -> p h q", q=64), ALU.add, [tS, pS], [S])
                if not so:
                    self.act(Sbg, Sg, AF.Copy, [S], [Sbf])
                    y = ry.next()
                    self.tt(y[:].rearrange("p (h q) -> p h q", q=64), py[:, 0:256].rearrange("p (h q) -> p h q", q=64),
                            ecS[:, ti, hs].unsqueeze(2).to_broadcast([128, 4, 64]), ALU.mult, [py, ecS], [y])
                    self.tt(y[:], y[:], ybuf[:, ti, :], ALU.add, [y, ybuf], [y])
                    self.tt(y[:], y[:], zs[:, ti, :], ALU.mult, [y, zs], [y])
                    st = rst.next()
                    self.act(junk[:], y[:], AF.Square, [y], [junk, st], accum_out=st[:, 0:1])
                    self.act(st[:, 1:2], st[:, 0:1], AF.Ln, [st, self.epsc], [st], scale=1.0 / 256, bias=self.epsc[:, 1:2])
                    self.act(st[:, 2:3], st[:, 1:2], AF.Exp, [st], [st], scale=-0.5)
                    yn = ryn.next()
                    self.stt(yn[:], y[:], st[:, 2:3], ngs[:], ALU.mult, ALU.mult, [y, st, ngs], [yn])
                    po = self.pt.next()
                    self.tr(po[:, 0:128], yn[:, 0:128], self.idb[:], [yn, self.idb], [po])
                    self.tr(po[:, 128:256], yn[:, 128:256], self.idb[:], [yn, self.idb], [po])
                    self.act(m2o[:, :, tk], po[:, 0:256].rearrange("p (j t) -> p j t", j=2), AF.Copy, [po], [m2o])
                yield
            if not so:
                self.dma(m2T_d[gi * 2:gi * 2 + 2, :, t0 * 128:t0 * 128 + NS].rearrange("j p t -> p j t"), m2o[:, :, 0:NS], [m2o], [m2T_d])

        units = [(t0, nt, gi) for (t0, nt) in self.make_segs(MT) for gi in range(M2_G)]
        self.pipeline(units, stage0, stage1, stage2)

    def hg_state(self, uT_d, w_ap, mask_ap, S, dec_out):
        P = self.P
        MT = 6
        MAXS = MT * 128

        def ring(n, shape, dt, nm):
            return Ring([P.sb(shape, dt, nm) for _ in range(n)])

        useg = ring(1, [128, KC, MAXS], BF16, "useg")
        wts = ring(2, [128, KC, 2, 128], BF16, "hsw")
        mk = ring(2, [128, MAXS], F32, "mk")
        rA = ring(2, [128, MAXS], F32, "sA")
        rI = ring(2, [128, MAXS], BF16, "sI")
        rB, rC = (ring(1, [128, MAXS], F32, n) for n in ("sB", "sC"))
        rG = ring(2, [128, MAXS], BF16, "sG")
        rkv = ring(2, [128, MT, 256], BF16, "skv")
        rcs = ring(2, [128, 4], F32, "scs")
        onesr = P.sb([128, MAXS], F32, "onesr")
        P.op("dve", lambda e: e.memset(onesr[:], 1.0), (), [onesr])
        seg = {}

        def stage0(un):
            (t0, nt, hd) = un
            NS = nt * 128
            if hd == 0:
                seg["u"] = self.load_uT_seg(uT_d, t0, nt, useg)
                m_ = mk.next()
                self.dma(m_[:, 0:NS], mask_ap[0:1, t0 * 128:t0 * 128 + NS].partition_broadcast(128), [], [m_])
                seg["mask"] = m_
            u = seg["u"]
            wt = wts.next()
            for (m, c0) in ((0, C_F), (1, C_I)):
                self.dma(wt[:, :, m, :], w_ap[:, c0 + hd * 128:c0 + (hd + 1) * 128].rearrange("(k p) c -> p k c", p=128), [], [wt], eng="pool")
            A = rA.next(); I = rI.next()
            seg[("st", t0, hd)] = dict(A=A, I=I, mask=seg["mask"])
            for m in range(2):
                for (n0, nn) in [(a, min(512, NS - a)) for a in range(0, NS, 512)]:
                    ps = self.pp.next()
                    for kc in range(KC):
                        self.mm(ps[:, 0:nn], wt[:, kc, m, :], u[:, kc, n0:n0 + nn], kc == 0, kc == KC - 1, [wt, u], [ps])
                    if m == 0:
                        self.act(A[:, n0:n0 + nn], ps[:, 0:nn], AF.Sigmoid, [ps], [A])
                    else:
                        self.act(I[:, n0:n0 + nn], ps[:, 0:nn], AF.Copy, [ps], [I])
                    yield

        def stage1(un):
            (t0, nt, hd) = un
            NS = nt * 128
            d = seg.pop(("st", t0, hd))
            A, I, mask = d["A"], d["I"], d["mask"]
            B = rB.next(); C = rC.next(); G = rG.next(); kv = rkv.next(); cs = rcs.next()
            self.ts(A[:, 0:NS], A[:, 0:NS], self.oml[:, hd:hd + 1], self.lb[:, hd:hd + 1], ALU.mult, ALU.add, [A, self.oml, self.lb], [A])
            self.ts(B[:, 0:NS], A[:, 0:NS], -1.0, 1.0, ALU.mult, ALU.add, [A], [B])
            self.tt(B[:, 0:NS], B[:, 0:NS], mask[:, 0:NS], ALU.mult, [B, mask], [B])
            self.act(A[:, 0:NS], A[:, 0:NS], AF.Ln, [A], [A])
            self.tt(A[:, 0:NS], A[:, 0:NS], mask[:, 0:NS], ALU.mult, [A, mask], [A])
            yield
            P.op("dve", lambda e, C=C, A=A, NS=NS: e.tensor_tensor_scan(out=C[:, 0:NS], data0=onesr[:, 0:NS], data1=A[:, 0:NS],
                                                                    initial=0.0, op0=ALU.mult, op1=ALU.add), [A, onesr], [C])
            self.act(A[:, 0:NS], C[:, 0:NS], AF.Exp, [C], [A], scale=-1.0, bias=C[:, NS - 1:NS])
            self.act(cs[:, 0:1], C[:, NS - 1:NS], AF.Exp, [C], [cs])
            self.tt(dec_out[:, hd:hd + 1], dec_out[:, hd:hd + 1], C[:, NS - 1:NS], ALU.add, [dec_out, C], [dec_out])
            self.tt(G[:, 0:NS], B[:, 0:NS], A[:, 0:NS], ALU.mult, [B, A], [G])
            yield
            for ti in range(nt):
                tk = slice(ti * 128, (ti + 1) * 128)
                pt = self.pt.next()
                self.tr(pt[:, 0:128], G[:, tk], self.idb[:], [G, self.idb], [pt])
                self.tr(pt[:, 128:256], I[:, tk], self.idb[:], [I, self.idb], [pt])
                self.act(kv[:, ti, :], pt[:, 0:256], AF.Copy, [pt], [kv])
                if ti % 2 == 1:
                    yield
            pst = self.pq.next()
            for ti in range(nt):
                self.mm(pst[:, 0:128], kv[:, ti, 0:128], kv[:, ti, 128:256], ti == 0, ti == nt - 1, [kv], [pst])
            Sh = S[:, hd, :]
            self.stt(Sh, Sh, cs[:, 0:1], pst[:, 0:128], ALU.mult, ALU.add, [S, cs, pst], [S])
            yield

        units = [(t0, nt, hd) for (t0, nt) in self.make_segs(MT) for hd in range(HG_H)]
        self.pipeline(units, stage0, stage1)

    def m2_state(self, uT_d, w_ap, S, dec_out):
        P = self.P
        MT = 6
        MAXS = MT * 128

        def ring(n, shape, dt, nm):
            return Ring([P.sb(shape, dt, nm) for _ in range(n)])

        useg = ring(1, [128, KC, MAXS], BF16, "useg")
        wts = ring(2, [128, KC, 384], BF16, "msw")
        rxr = ring(2, [128, 3, 3 + MAXS], F32, "xr")
        racc = ring(1, [128, 2, MAXS], F32, "xacc")
        rtmpc = ring(2, [128, MAXS], F32, "ctmp")
        rBT = ring(1, [128, MAXS], BF16, "BT")
        rps = Ring([dict((n, P.sb([128, MT, 32], F32, n)) for n in ("dtS", "aS", "cumS", "lastS", "ddS")) for _ in range(2)])
        rtot = ring(2, [128, 2, 32], F32, "mtot")
        rxs = ring(2, [128, 256], F32, "xs")
        rxdd = ring(2, [128, MT, 256], BF16, "xddA")
        rBtm = ring(2, [128, MT, 128], BF16, "BtmA")
        rtS = ring(2, [128, 4, 64], F32, "tS")
        seg = {}

        def seg_prep(t0, nt):
            u = self.load_uT_seg(uT_d, t0, nt, useg)
            seg["u"] = u
            ps_ = rps.next(); tot = rtot.next()
            seg["ps"] = ps_; seg["tot"] = tot
            dtS, aS, cumS, lastS, ddS = (ps_[n] for n in ("dtS", "aS", "cumS", "lastS", "ddS"))
            for ti in range(nt):
                ps = self.pq.next()
                for kc in range(KC):
                    self.mm(ps[:, 0:32], u[:, kc, ti * 128:(ti + 1) * 128], self.wdt[:, kc, :], kc == 0, kc == KC - 1, [u, self.wdt], [ps])
                self.tt(dtS[:, ti, :], ps[:, 0:32], self.dtb[:], ALU.add, [ps, self.dtb], [dtS])
            self.act(dtS[:, 0:nt, :], dtS[:, 0:nt, :], AF.Exp, [dtS], [dtS])
            self.act(dtS[:, 0:nt, :], dtS[:, 0:nt, :], AF.Ln, [dtS, self.epsc], [dtS], bias=self.epsc[:, 2:3])
            for ti in range(nt):
                self.ts(dtS[:, ti, :], dtS[:, ti, :], self.maskT[:, t0 + ti:t0 + ti + 1], None, ALU.mult, None, [dtS, self.maskT], [dtS])
                self.tt(aS[:, ti, :], dtS[:, ti, :], self.negA[:], ALU.mult, [dtS, self.negA], [aS])
            for ti in range(nt):
                ps = self.pq.next()
                self.mm(ps[:, 0:32], self.triu[:], aS[:, ti, :], True, True, [self.triu, aS], [ps])
                self.mm(ps[:, 32:64], self.ones_f[:], aS[:, ti, :], True, True, [self.ones_f, aS], [ps])
                self.act(cumS[:, ti, :], ps[:, 0:32], AF.Copy, [ps], [cumS])
                self.act(lastS[:, ti, :], ps[:, 32:64], AF.Copy, [ps], [lastS])
            self.tt(ddS[:, 0:nt, :], lastS[:, 0:nt, :], cumS[:, 0:nt, :], ALU.subtract, [lastS, cumS], [ddS])
            P.op("dve", lambda e, tot=tot: e.memset(tot[:, 0, :], 0.0), (), [tot])
            for ti in range(nt - 1, -1, -1):
                self.tt(ddS[:, ti, :], ddS[:, ti, :], tot[:, 0, :], ALU.add, [ddS, tot], [ddS])
                self.tt(tot[:, 0, :], tot[:, 0, :], lastS[:, ti, :], ALU.add, [tot, lastS], [tot])
            self.tt(dec_out[:], dec_out[:], tot[:, 0, :], ALU.add, [dec_out, tot], [dec_out])
            self.act(tot[:, 1, :], tot[:, 0, :], AF.Exp, [tot], [tot])
            self.act(ddS[:, 0:nt, :], ddS[:, 0:nt, :], AF.Exp, [ddS], [ddS])
            self.tt(ddS[:, 0:nt, :], ddS[:, 0:nt, :], dtS[:, 0:nt, :], ALU.mult, [ddS, dtS], [ddS])

        def stage0(un):
            (t0, nt, gi) = un
            NS = nt * 128
            if gi == 0:
                seg_prep(t0, nt)
                yield
            u = seg["u"]
            wt = wts.next()
            self.dma(wt[:, :, 0:256], w_ap[:, C_X + gi * 256:C_X + (gi + 1) * 256].rearrange("(k p) c -> p k c", p=128), [], [wt], eng="pool")
            self.dma(wt[:, :, 256:384], w_ap[:, C_B + gi * 128:C_B + (gi + 1) * 128].rearrange("(k p) c -> p k c", p=128), [], [wt], eng="pool")
            xr = rxr.next()
            seg[("st", t0, gi)] = dict(xr=xr, ps=seg["ps"], tot=seg["tot"])
            for b in range(3):
                for (n0, nn) in [(a, min(512, NS - a)) for a in range(0, NS, 512)]:
                    ps = self.pp.next()
                    for kc in range(KC):
                        self.mm(ps[:, 0:nn], wt[:, kc, b * 128:(b + 1) * 128], u[:, kc, n0:n0 + nn], kc == 0, kc == KC - 1, [wt, u], [ps])
                    self.act(xr[:, b, 3 + n0:3 + n0 + nn], ps[:, 0:nn], AF.Copy, [ps], [xr])
                    yield

        def stage1(un):
            (t0, nt, gi) = un
            NS = nt * 128
            d = seg.pop(("st", t0, gi))
            xr = d["xr"]; ddS = d["ps"]["ddS"]; tot = d["tot"]
            acc = racc.next(); BT = rBT.next(); xdd = rxdd.next(); Btm = rBtm.next()
            for b in range(3):
                j = (gi * 2 + b) if b < 2 else 16 + gi
                self.act(xr[:, b, 0:3], self.carry[:, gi, b, :], AF.Copy, [self.carry], [xr])
                self.act(self.carry[:, gi, b, :], xr[:, b, NS:NS + 3], AF.Copy, [xr], [self.carry])
                tmpc = rtmpc.next()
                self.ts(tmpc[:, 0:NS], xr[:, b, 3:3 + NS], self.cw[3][:, j:j + 1], self.cbias[:, j:j + 1], ALU.mult, ALU.add,
                        [xr, self.cw[3], self.cbias], [tmpc])
                for k_ in range(3):
                    self.stt(tmpc[:, 0:NS], xr[:, b, k_:k_ + NS], self.cw[k_][:, j:j + 1], tmpc[:, 0:NS], ALU.mult, ALU.add,
                             [xr, self.cw[k_], tmpc], [tmpc])
                if b < 2:
                    self.act(acc[:, b, 0:NS], tmpc[:, 0:NS], AF.Silu, [tmpc], [acc])
                else:
                    self.act(BT[:, 0:NS], tmpc[:, 0:NS], AF.Silu, [tmpc], [BT])
                yield
            hs = slice(gi * 4, gi * 4 + 4)
            for ti in range(nt):
                tk = slice(ti * 128, (ti + 1) * 128)
                px = self.pq.next()
                self.tr(px[:, 0:128], acc[:, 0, tk], self.idf[:], [acc, self.idf], [px])
                self.tr(px[:, 128:256], acc[:, 1, tk], self.idf[:], [acc, self.idf], [px])
                pb = self.pt.next()
                self.tr(pb[:, 0:128], BT[:, tk], self.idb[:], [BT, self.idb], [pb])
                xs = rxs.next()
                self.act(xs[:], px[:, 0:256], AF.Copy, [px], [xs])
                self.act(Btm[:, ti, :], pb[:, 0:128], AF.Copy, [pb], [Btm])
                self.tt(xdd[:, ti, :].rearrange("p (h q) -> p h q", q=64), xs[:].rearrange("p (h q) -> p h q", q=64),
                        ddS[:, ti, hs].unsqueeze(2).to_broadcast([128, 4, 64]), ALU.mult, [xs, ddS], [xdd])
                yield
            pS = self.pq.next()
            for ti in range(nt):
                self.mm(pS[:, 0:256], Btm[:, ti, :], xdd[:, ti, :], ti == 0, ti == nt - 1, [Btm, xdd], [pS])
            Sg = S[:, gi * 4:(gi + 1) * 4, :]
            tS = rtS.next()
            self.tt(tS[:], Sg, tot[:, 1, hs].unsqueeze(2).to_broadcast([128, 4, 64]), ALU.mult, [S, tot], [tS])
            self.tt(Sg, tS[:], pS[:, 0:256].rearrange("p (h q) -> p h q", q=64), ALU.add, [tS, pS], [S])
            yield

        units = [(t0, nt, gi) for (t0, nt) in self.make_segs(MT) for gi in range(M2_G)]
        self.pipeline(units, stage0, stage1)

    def tok_chunks(self, n0, n1):
        return [(a, min(512, n1 - a)) for a in range(n0, n1, 512)]

    def dense_A_res(self, xT_d, KCx, W_ap, res_ap, res_r, out_d, seg_tiles):
        P = self.P
        NTs = seg_tiles
        xs_ = Ring([P.sb([128, KCx, NTs * 128], BF16, "dax") for _ in range(1)])
        wr = Ring([P.sb([128, KCx, 512], BF16, "daw") for _ in range(2)])
        rr = Ring([P.sb([128, 512], F32, "dar") for _ in range(3)])
        orr = Ring([P.sb([128, 512], F32, "dao") for _ in range(3)])
        t = 0
        while t < self.NT:
            nt = min(NTs, self.NT - t)
            x = xs_.next()
            self.dma(x[:, :, 0:nt * 128], xT_d[:, :, t * 128:(t + nt) * 128].rearrange("k p t -> p k t"), [xT_d], [x])
            for cb in range(4):
                w = wr.next()
                self.dma(w[:], W_ap[:, cb * 512:(cb + 1) * 512].rearrange("(k p) c -> p k c", p=128), [], [w], eng="pool")
                for ti in range(nt):
                    r = rr.next(); o = orr.next()
                    tok = slice((t + ti) * 128, (t + ti + 1) * 128)
                    self.dma(r[:], res_ap[tok, cb * 512:(cb + 1) * 512], [res_r], [r])
                    ps = self.pp.next()
                    for kc in range(KCx):
                        self.mm(ps[:], x[:, kc, ti * 128:(ti + 1) * 128], w[:, kc, :], kc == 0, kc == KCx - 1, [x, w], [ps])
                    self.tt(o[:], ps[:], r[:], ALU.add, [ps, r], [o])
                    self.dma(out_d[tok, cb * 512:(cb + 1) * 512], o[:], [o], [out_d])
            t += nt

    def dense_B(self, x, W_ap, c0, ncols, evac, ntok=None):
        P = self.P
        wr = self._dbw
        ntok = self.TT if ntok is None else ntok
        for c in range(0, ncols, 512):
            wc = min(512, ncols - c)
            w = wr.next()
            self.dma(w[:, :, 0:wc], W_ap[:, c0 + c:c0 + c + wc].rearrange("(k p) c -> p k c", p=128), [], [w], eng="pool")
            for b in range(wc // 128):
                for (n0, nn) in self.tok_chunks(0, ntok):
                    ps = self.pp.next()
                    for kc in range(KC):
                        self.mm(ps[:, 0:nn], w[:, kc, b * 128:(b + 1) * 128], x[:, kc, n0:n0 + nn], kc == 0, kc == KC - 1, [w, x], [ps])
                    evac(ps, (c // 128) + b, n0, nn)

    def halves(self):
        a = ((self.NT + 1) // 2) * 128
        return [(0, a), (a, self.TT)] if a < self.TT else [(0, self.TT)]

    def load_xT(self, xT_d, name):
        x = self.P.sb([128, KC, self.TT], BF16, name)
        self.dma(x[:], xT_d[:, :, :].rearrange("k p t -> p k t"), [xT_d], [x])
        return x

    def mixer_merge(self, uT_d, hgT_d, m2T_d, w_ap, wbh_ap, wbm_ap, mT_d):
        P = self.P
        hv = self.halves()
        NSM = max(b - a for a, b in hv)
        ru = Ring([P.sb([128, KC, NSM], BF16, "mu") for _ in range(1)])
        rh = Ring([P.sb([128, KC, NSM], BF16, "mh") for _ in range(1)])
        rm = Ring([P.sb([128, KC, NSM], BF16, "mm") for _ in range(1)])
        rw = Ring([P.sb([128, KC, 4, 128], BF16, "mw") for _ in range(2)])
        rg = Ring([P.sb([128, 2, 512], F32, "mg") for _ in range(2)])
        rt = Ring([P.sb([128, 2, 512], F32, "mt") for _ in range(2)])
        ro = Ring([P.sb([128, NSM], BF16, "mo") for _ in range(2)])
        for (h0, h1) in hv:
            nh = h1 - h0
            u = ru.next(); hg = rh.next(); m2 = rm.next()
            for (dst, src) in ((u, uT_d), (hg, hgT_d), (m2, m2T_d)):
                self.dma(dst[:, :, 0:nh], src[:, :, h0:h1].rearrange("k p t -> p k t"), [src], [dst])
            for kb in range(16):
                c = kb * 128
                w = rw.next()
                for mi, (ap, cc) in enumerate(((w_ap, C_GHG + c), (wbh_ap, c), (w_ap, C_GM2 + c), (wbm_ap, c))):
                    self.dma(w[:, :, mi, :], ap[:, cc:cc + 128].rearrange("(k p) c -> p k c", p=128), [], [w], eng="pool")
                o = ro.next()
                for (n0, nn) in self.tok_chunks(0, nh):
                    g = rg.next(); t_ = rt.next()
                    for half, (xa, xb) in enumerate(((u, hg), (u, m2))):
                        pg = self.pp.next()
                        for kc in range(KC):
                            self.mm(pg[:, 0:nn], w[:, kc, 2 * half, :], xa[:, kc, n0:n0 + nn], kc == 0, kc == KC - 1, [w, xa], [pg])
                        self.act(g[:, half, 0:nn], pg[:, 0:nn], AF.Sigmoid, [pg], [g])
                        py_ = self.pq.next()
                        for kc in range(KC):
                            self.mm(py_[:, 0:nn], w[:, kc, 2 * half + 1, :], xb[:, kc, n0:n0 + nn], kc == 0, kc == KC - 1, [w, xb], [py_])
                        self.tt(t_[:, half, 0:nn], py_[:, 0:nn], g[:, half, 0:nn], ALU.mult, [py_, g], [t_])
                    self.tt(o[:, n0:n0 + nn], t_[:, 0, 0:nn], t_[:, 1, 0:nn], ALU.add, [t_], [o])
                self.dma(mT_d[kb, :, h0:h1], o[:, 0:nh], [o], [mT_d])

    def xattn(self, uT_d, memnT_d, wq_ap, wkv_ap, oT_d):
        P = self.P
        qT = P.sb([128, KC, self.TT], BF16, "xa_q")
        kT = P.sb([128, KC, 256], BF16, "xa_k")
        v = P.sb([128, 2, 2048], BF16, "xa_v")
        sc = 512 ** -0.5
        with P.scope():
            self._dbw = Ring([P.sb([128, KC, 512], BF16, "dbw") for _ in range(2)])
            mn = P.sb([128, KC, 256], BF16, "xa_mn")
            self.dma(mn[:], memnT_d[:, :, :].rearrange("k p t -> p k t"), [memnT_d], [mn])
            for c in range(0, 2048, 512):
                w = self._dbw.next()
                self.dma(w[:], wkv_ap[:, c:c + 512].rearrange("(k p) c -> p k c", p=128), [], [w], eng="pool")
                for b in range(4):
                    ps = self.pp.next()
                    for kc in range(KC):
                        self.mm(ps[:, 0:256], w[:, kc, b * 128:(b + 1) * 128], mn[:, kc, :], kc == 0, kc == KC - 1, [w, mn], [ps])
                    self.act(kT[:, c // 128 + b, :], ps[:, 0:256], AF.Copy, [ps], [kT])
            for c in range(0, 2048, 512):
                w = self._dbw.next()
                self.dma(w[:], wkv_ap[:, 2048 + c:2048 + c + 512].rearrange("(k p) c -> p k c", p=128), [], [w], eng="pool")
                for mb in range(2):
                    ps = self.pp.next()
                    for kc in range(KC):
                        self.mm(ps[:], mn[:, kc, mb * 128:(mb + 1) * 128], w[:, kc, :], kc == 0, kc == KC - 1, [w, mn], [ps])
                    self.act(v[:, mb, c:c + 512], ps[:], AF.Copy, [ps], [v])
            hv = self.halves()
            xr_ = Ring([P.sb([128, KC, max(b - a for a, b in hv)], BF16, "xa_u") for _ in range(1)])
            for (h0, h1) in hv:
                x = xr_.next()
                self.dma(x[:, :, 0:h1 - h0], uT_d[:, :, h0:h1].rearrange("k p t -> p k t"), [uT_d], [x])

                def evq(ps, cb, n0, nn, h0=h0):
                    self.act(qT[:, cb, h0 + n0:h0 + n0 + nn], ps[:, 0:nn], AF.Copy, [ps], [qT], scale=sc)
                self.dense_B(x, wq_ap, 0, 2048, evq, ntok=h1 - h0)
        rPT = Ring([P.sb([128, 2, 512], BF16, "xa_pt") for _ in range(2)])
        rden = Ring([P.sb([128, 512], F32, "xa_den") for _ in range(2)])
        ro = Ring([P.sb([128, 4, 512], BF16, "xa_o") for _ in range(2)])
        for hh in range(4):
            for (n0, nn) in self.tok_chunks(0, self.TT):
                PT = rPT.next()
                for mb in range(2):
                    ps = self.pp.next()
                    for dc in range(4):
                        self.mm(ps[:, 0:nn], kT[:, hh * 4 + dc, mb * 128:(mb + 1) * 128], qT[:, hh * 4 + dc, n0:n0 + nn], dc == 0, dc == 3, [kT, qT], [ps])
                    self.act(PT[:, mb, 0:nn], ps[:, 0:nn], AF.Exp, [ps], [PT])
                psd = self.pp.next()
                for mb in range(2):
                    self.mm(psd[:, 0:nn], self.ones_b[:], PT[:, mb, 0:nn], mb == 0, mb == 1, [self.ones_b, PT], [psd])
                den = rden.next()
                P.op("dve", lambda e, den=den, psd=psd, nn=nn: e.reciprocal(out=den[:, 0:nn], in_=psd[:, 0:nn]), [psd], [den])
                o = ro.next()
                for dc in range(4):
                    ps = self.pp.next()
                    for mb in range(2):
                        self.mm(ps[:, 0:nn], v[:, mb, hh * 512 + dc * 128:hh * 512 + (dc + 1) * 128], PT[:, mb, 0:nn], mb == 0, mb == 1, [v, PT], [ps])
                    self.tt(o[:, dc, 0:nn], ps[:, 0:nn], den[:, 0:nn], ALU.mult, [ps, den], [o])
                self.dma(oT_d[hh * 4:hh * 4 + 4, :, n0:n0 + nn].rearrange("j p t -> p j t"), o[:, :, 0:nn], [o], [oT_d])

    def ffn_up(self, uT_d, wup_ap, cw_ap, cb_ap, maskF_ap, aT_d):
        P = self.P
        TT = self.TT
        x = self.load_xT(uT_d, "ff_u")
        cw = [self.load_cols(cw_ap[k:k + 1, :].rearrange("o (j p) -> (o j) p", p=128), 44, f"fcw{k}") for k in range(3)]
        cbs = self.load_cols(cb_ap.rearrange("o (j p) -> (o j) p", p=128), 44, "fcb")
        mF = P.sb([128, 128], F32, "maskF")
        self.dma(mF[:], maskF_ap.partition_broadcast(128), [], [mF])
        rw = Ring([P.sb([128, KC, 2, 256], BF16, "fw") for _ in range(2)])
        rgr = Ring([P.sb([128, 2 + TT], F32, "fgr") for _ in range(2)])
        rup = Ring([P.sb([128, TT], F32, "fup") for _ in range(2)])
        rac = Ring([P.sb([128, TT], F32, "fac") for _ in range(2)])
        rao = Ring([P.sb([128, TT], BF16, "fao") for _ in range(2)])
        for jp in range(0, 44, 2):
            w = rw.next()
            self.dma(w[:, :, 0, :], wup_ap[:, jp * 128:jp * 128 + 256].rearrange("(k p) c -> p k c", p=128), [], [w], eng="pool")
            self.dma(w[:, :, 1, :], wup_ap[:, D_FF + jp * 128:D_FF + jp * 128 + 256].rearrange("(k p) c -> p k c", p=128), [], [w], eng="pool")
            for b in range(2):
                j = jp + b
                gr = rgr.next(); up = rup.next(); ac = rac.next(); ao = rao.next()
                P.op("dve", lambda e, gr=gr: e.memset(gr[:, 0:2], 0.0), (), [gr])
                for (n0, nn) in self.tok_chunks(0, TT):
                    ps = self.pp.next()
                    for kc in range(KC):
                        self.mm(ps[:, 0:nn], w[:, kc, 0, b * 128:(b + 1) * 128], x[:, kc, n0:n0 + nn], kc == 0, kc == KC - 1, [w, x], [ps])
                    self.act(gr[:, 2 + n0:2 + n0 + nn], ps[:, 0:nn], AF.Copy, [ps], [gr])
                    ps2 = self.pp.next()
                    for kc in range(KC):
                        self.mm(ps2[:, 0:nn], w[:, kc, 1, b * 128:(b + 1) * 128], x[:, kc, n0:n0 + nn], kc == 0, kc == KC - 1, [w, x], [ps2])
                    self.act(up[:, n0:n0 + nn], ps2[:, 0:nn], AF.Copy, [ps2], [up])
                self.tt(gr[:, 2:130], gr[:, 2:130], mF[:], ALU.mult, [gr, mF], [gr])
                self.ts(ac[:], gr[:, 2:2 + TT], cw[2][:, j:j + 1], cbs[:, j:j + 1], ALU.mult, ALU.add, [gr, cw[2], cbs], [ac])
                for k in range(2):
                    self.stt(ac[:], gr[:, k:k + TT], cw[k][:, j:j + 1], ac[:], ALU.mult, ALU.add, [gr, cw[k], ac], [ac])
                self.act(ac[:], ac[:], AF.Gelu, [ac], [ac])
                self.tt(ao[:], ac[:], up[:], ALU.mult, [ac, up], [ao])
                self.dma(aT_d[j, :, :], ao[:], [ao], [aT_d])

    def final_norm(self, h_ap, hres, g_ap, out_ap, out_r, tiles):
        P = self.P
        gbc = P.sb([128, D], F32, "fgbc")
        self.dma(gbc[:], g_ap.partition_broadcast(128), [], [gbc])
        hb = Ring([P.sb([128, D], F32, "fhb") for _ in range(2)])
        ob = Ring([P.sb([128, D], F32, "fob") for _ in range(2)])
        junk = P.sb([128, D], BF16, "fjunk")
        stt_ = Ring([P.sb([128, 4], F32, "fst") for _ in range(2)])
        outs = []
        for i in tiles:
            h = hb.next(); o = ob.next(); s = stt_.next()
            self.dma(h[:], h_ap[i * 128:(i + 1) * 128, :], [hres], [h])
            self.act(junk[:], h[:], AF.Square, [h], [junk, s], accum_out=s[:, 0:1])
            self.act(s[:, 1:2], s[:, 0:1], AF.Ln, [s, self.epsc], [s], scale=1.0 / D, bias=self.epsc[:, 0:1])
            self.act(s[:, 2:3], s[:, 1:2], AF.Exp, [s], [s], scale=-0.5)
            self.stt(o[:], h[:], s[:, 2:3], gbc[:], ALU.mult, ALU.mult, [h, s, gbc], [o])
            outs.append(self.dma(out_ap[i * 128:(i + 1) * 128, :], o[:], [o], [out_r]))
        return outs


def _inp(nc, name, shape):
    return nc.dram_tensor(name, list(shape), F32, kind="ExternalInput").ap()


def _outp(nc, name, shape):
    return nc.dram_tensor(name, list(shape), F32, kind="ExternalOutput").ap()


def build_state(l, NT):
    nc = bass.Bass("TRN2", target_bir_lowering=False)
    TT = NT * 128
    h = _inp(nc, "h", [TT, D]); mask = _inp(nc, "mask", [1, TT]); g = _inp(nc, "mix_g", [1, D])
    w_in = _inp(nc, "w_in", [D, N_IN]); lbl = _inp(nc, "lbl", [2, 2048])
    cw = _inp(nc, "m2_cw", [4, 4096]); cb = _inp(nc, "m2_cb", [1, 4096])
    dtb = _inp(nc, "m2_dtb", [1, 32]); alog = _inp(nc, "m2_alog", [1, 32])
    o_shg = _outp(nc, "o_shg", [128, 16, 128]); o_dhg = _outp(nc, "o_dhg", [128, 16])
    o_sm2 = _outp(nc, "o_sm2", [128, 32, 64]); o_dm2 = _outp(nc, "o_dm2", [128, 32])
    k = K(nc, NT); P = k.P
    uT_d = P.dram([KC, 128, TT], BF16, "uT")
    with P.scope():
        k.norm_to_uT(h, T(h, Res()), g, uT_d)
    k.hg_setup_lb(lbl, l)
    Shg = P.sb([128, 16, 128], F32, "Shg"); dhg = P.sb([128, 16], F32, "dhg")
    P.op("dve", lambda e: e.memset(Shg[:], 0.0), (), [Shg])
    P.op("dve", lambda e: e.memset(dhg[:], 0.0), (), [dhg])
    with P.scope():
        k.hg_state(uT_d, w_in, mask, Shg, dhg)
    outs = [k.dma(o_shg, Shg[:], [Shg], [T(o_shg, Res())]), k.dma(o_dhg, dhg[:], [dhg], [T(o_dhg, Res())])]
    k.m2_setup(w_in, cw, cb, dtb, alog, None, None, mask)
    Sm2 = P.sb([128, 32, 64], F32, "Sm2"); Sbf = P.sb([128, 32, 64], BF16, "Sm2b"); dm2 = P.sb([128, 32], F32, "dm2")
    P.op("dve", lambda e: e.memset(Sm2[:], 0.0), (), [Sm2])
    P.op("dve", lambda e: e.memset(dm2[:], 0.0), (), [dm2])
    with P.scope():
        k.m2_state(uT_d, w_in, Sm2, dm2)
    outs += [k.dma(o_sm2, Sm2[:], [Sm2], [T(o_sm2, Res())]), k.dma(o_dm2, dm2[:], [dm2], [T(o_dm2, Res())])]
    P.emit(outs)
    return nc


def build_main(l, NT, last):
    nc = bass.Bass("TRN2", target_bir_lowering=False)
    TT = NT * 128
    h = _inp(nc, "h", [TT, D]); mask = _inp(nc, "mask", [1, TT]); maskF = _inp(nc, "maskF", [1, 128])
    g = _inp(nc, "mix_g", [1, D]); w_in = _inp(nc, "w_in", [D, N_IN]); lbl = _inp(nc, "lbl", [2, 2048])
    hgng = _inp(nc, "hg_ng", [1, 2048])
    cw = _inp(nc, "m2_cw", [4, 4096]); cb = _inp(nc, "m2_cb", [1, 4096])
    dtb = _inp(nc, "m2_dtb", [1, 32]); alog = _inp(nc, "m2_alog", [1, 32]); dsk = _inp(nc, "m2_dsk", [1, 32])
    m2ng = _inp(nc, "m2_ng", [1, 2048])
    wbh = _inp(nc, "w_bhg", [2048, D]); wbm = _inp(nc, "w_bm2", [2048, D]); wout = _inp(nc, "w_out", [D, D])
    mem = _inp(nc, "mem", [N_MEM, D]); memg = _inp(nc, "mem_g", [1, D]); xag = _inp(nc, "xa_g", [1, D])
    wq = _inp(nc, "xa_wq", [D, D]); wkv = _inp(nc, "xa_wkv", [D, 2 * D]); wo = _inp(nc, "xa_wo", [D, D])
    ffg = _inp(nc, "ffn_g", [1, D]); wup = _inp(nc, "ffn_wup", [D, 2 * D_FF])
    fcw = _inp(nc, "ffn_cw", [3, D_FF]); fcb = _inp(nc, "ffn_cb", [1, D_FF]); wdn = _inp(nc, "ffn_wdn", [D_FF, D])
    ps_hg = _inp(nc, "ps_hg", [7, 128, 16, 128]); pd_hg = _inp(nc, "pd_hg", [7, 128, 16])
    ps_m2 = _inp(nc, "ps_m2", [7, 128, 32, 64]); pd_m2 = _inp(nc, "pd_m2", [7, 128, 32])
    if last:
        fing = _inp(nc, "fin_g", [1, D])
    h_out = _outp(nc, "h_out", [TT, D])
    k = K(nc, NT); P = k.P
    hin = T(h, Res())
    uT_d = P.dram([KC, 128, TT], BF16, "uT")
    hgT_d = P.dram([KC, 128, TT], BF16, "hgT")
    m2T_d = P.dram([KC, 128, TT], BF16, "m2T")
    mT_d = P.dram([KC, 128, TT], BF16, "mT")
    h1_d = P.dram([TT, D], F32, "h1")
    h2_d = P.dram([TT, D], F32, "h2")
    h3_d = T(h_out, Res()) if not last else P.dram([TT, D], F32, "h3")
    memT_d = P.dram([KC, 128, N_MEM], BF16, "memT")
    oT_d = P.dram([KC, 128, TT], BF16, "oT")
    aT_d = P.dram([D_FF // 128, 128, TT], BF16, "aT")
    with P.scope():
        k.norm_to_uT(h, hin, g, uT_d)
    with P.scope():
        k.hg_setup(lbl, l, hgng)
        Shg = P.sb([128, 16, 128], F32, "Shg")
        P.op("dve", lambda e: e.memset(Shg[:], 0.0), (), [Shg])
        Sm2 = P.sb([128, 32, 64], F32, "Sm2"); Sbf = P.sb([128, 32, 64], BF16, "Sm2b")
        P.op("dve", lambda e: e.memset(Sm2[:], 0.0), (), [Sm2])
        with P.scope():
            t1 = Ring([P.sb([128, 16, 128], F32, "pst") for _ in range(2)])
            d1 = Ring([P.sb([128, 16], F32, "pdt") for _ in range(2)])
            t2 = Ring([P.sb([128, 32, 64], F32, "pst2") for _ in range(2)])
            d2 = Ring([P.sb([128, 32], F32, "pdt2") for _ in range(2)])
            for j in range(7):
                a = t1.next(); b = d1.next(); c = t2.next(); d_ = d2.next()
                k.dma(a[:], ps_hg[j], [], [a]); k.dma(b[:], pd_hg[j], [], [b])
                k.dma(c[:], ps_m2[j], [], [c]); k.dma(d_[:], pd_m2[j], [], [d_])
                k.act(b[:], b[:], AF.Exp, [b], [b]); k.act(d_[:], d_[:], AF.Exp, [d_], [d_])
                k.tt(Shg[:], Shg[:], b[:].unsqueeze(2).to_broadcast([128, 16, 128]), ALU.mult, [Shg, b], [Shg])
                k.tt(Shg[:], Shg[:], a[:], ALU.add, [Shg, a], [Shg])
                k.tt(Sm2[:], Sm2[:], d_[:].unsqueeze(2).to_broadcast([128, 32, 64]), ALU.mult, [Sm2, d_], [Sm2])
                k.tt(Sm2[:], Sm2[:], c[:], ALU.add, [Sm2, c], [Sm2])
        k.act(Sbf[:], Sm2[:], AF.Copy, [Sm2], [Sbf])
        with P.scope():
            k.hg_mixer(uT_d, w_in, mask, Shg, hgT_d)
        k.m2_setup(w_in, cw, cb, dtb, alog, dsk, m2ng, mask)
        with P.scope():
            k.m2_mixer(uT_d, w_in, Sm2, Sbf, m2T_d)
    with P.scope():
        k.mixer_merge(uT_d, hgT_d, m2T_d, w_in, wbh, wbm, mT_d)
    with P.scope():
        k.dense_A_res(mT_d, KC, wout, h, hin, h1_d, NT)
    u3_d = P.dram([KC, 128, TT], BF16, "u3T")
    with P.scope():
        k.norm_to_uT(h1_d[:, :], h1_d, xag, u3_d)
    with P.scope():
        k.norm_to_uT(mem, T(mem, Res()), memg, memT_d, tiles=range(N_MEM // 128))
    with P.scope():
        k.xattn(u3_d, memT_d, wq, wkv, oT_d)
    with P.scope():
        k.dense_A_res(oT_d, KC, wo, h1_d[:, :], h1_d, h2_d, NT)
    u4_d = P.dram([KC, 128, TT], BF16, "u4T")
    with P.scope():
        k.norm_to_uT(h2_d[:, :], h2_d, ffg, u4_d)
    with P.scope():
        k.ffn_up(u4_d, wup, fcw, fcb, maskF, aT_d)
    with P.scope():
        k.dense_A_res(aT_d, D_FF // 128, wdn, h2_d[:, :], h2_d, h3_d, 6)
    if last:
        with P.scope():
            outs = k.final_norm(h3_d[:, :], h3_d, fing, h_out, T(h_out, Res()), range(NT))
    else:
        outs = [h3_d.r.last_write]
    stats = P.emit(outs)
    return nc, stats


_PROG_CACHE = {}


def _prog(kind, l, NT, last=False):
    key = (kind, l, NT, last)
    if key not in _PROG_CACHE:
        if kind == "state":
            _PROG_CACHE[key] = build_state(l, NT)
        else:
            _PROG_CACHE[key] = build_main(l, NT, last)[0]
    return _PROG_CACHE[key]


def _halo_slices(hfull, n_cores, TOWN):
    out = []
    for c in range(n_cores):
        s = c * TOWN
        if c == 0:
            blk = np.concatenate([np.zeros((128, hfull.shape[1]), np.float32), hfull[0:TOWN]], 0)
        else:
            blk = hfull[s - 128:s + TOWN]
        out.append(np.ascontiguousarray(blk, dtype=np.float32))
    return out


def kernel_impl(inputs, n_cores=8):
    f32 = np.float32
    x = np.asarray(inputs["x"], f32)
    SEQ = x.shape[1]
    TOWN = SEQ // n_cores
    NT = TOWN // 128 + 1
    TT = NT * 128
    mem = np.ascontiguousarray(np.asarray(inputs["mem"], f32)[0])
    g = lambda k: np.asarray(inputs[k], f32)
    row = lambda a: np.ascontiguousarray(a.reshape(1, -1))
    cores = list(range(n_cores))
    m_state, m_main, m_f = [], [], []
    for c in cores:
        ms = np.zeros((1, TT), f32); mm_ = np.zeros((1, TT), f32)
        lo = 128 if c == 0 else 3
        ms[0, lo:TOWN + 3] = 1.0
        mm_[0, lo:] = 1.0
        m_state.append(ms); m_main.append(mm_)
        m_f.append(np.zeros((1, 128), f32) if c == 0 else np.ones((1, 128), f32))
    hfull = np.ascontiguousarray(x[0])
    for l in range(DEPTH):
        hs = _halo_slices(hfull, n_cores, TOWN)
        common_state = dict(mix_g=row(g("mix_norm_g")[l]), w_in=np.ascontiguousarray(g("w_in")[l]), lbl=np.ascontiguousarray(g("hg_lb_logits")),
                            m2_cw=np.ascontiguousarray(g("m2_conv_w")[l]), m2_cb=row(g("m2_conv_b")[l]),
                            m2_dtb=row(g("m2_dt_bias")[l]), m2_alog=row(g("m2_A_log")[l]))
        nc = _prog("state", l, NT)
        res = run_bass_kernel_spmd(nc, [dict(common_state, h=hs[c], mask=m_state[c]) for c in cores], core_ids=cores)
        st = res.results
        last = (l == DEPTH - 1)
        common = dict(common_state, hg_ng=row(g("hg_norm_g")[l]), m2_dsk=row(g("m2_D")[l]), m2_ng=row(g("m2_norm_g")[l]),
                      w_bhg=np.ascontiguousarray(g("w_branch_hg")[l]), w_bm2=np.ascontiguousarray(g("w_branch_m2")[l]),
                      w_out=np.ascontiguousarray(g("w_out")[l]), mem=mem, mem_g=row(g("mem_norm_g")), xa_g=row(g("xa_norm_g")[l]),
                      xa_wq=np.ascontiguousarray(g("xa_wq")[l]), xa_wkv=np.ascontiguousarray(g("xa_wkv")[l]),
                      xa_wo=np.ascontiguousarray(g("xa_wo")[l]), ffn_g=row(g("ffn_norm_g")[l]),
                      ffn_wup=np.ascontiguousarray(g("ffn_w_up")[l]), ffn_cw=np.ascontiguousarray(g("ffn_conv_w")[l]),
                      ffn_cb=row(g("ffn_conv_b")[l]), ffn_wdn=np.ascontiguousarray(g("ffn_w_down")[l]))
        if last:
            common["fin_g"] = row(g("final_norm_g"))
        maps = []
        for c in cores:
            ps_hg = np.zeros((7, 128, 16, 128), f32); pd_hg = np.zeros((7, 128, 16), f32)
            ps_m2 = np.zeros((7, 128, 32, 64), f32); pd_m2 = np.zeros((7, 128, 32), f32)
            for j in range(7):
                src = c - 7 + j
                if src >= 0:
                    ps_hg[j] = st[src]["o_shg"]; pd_hg[j] = st[src]["o_dhg"]
                    ps_m2[j] = st[src]["o_sm2"]; pd_m2[j] = st[src]["o_dm2"]
            maps.append(dict(common, h=hs[c], mask=m_main[c], maskF=m_f[c], ps_hg=ps_hg, pd_hg=pd_hg, ps_m2=ps_m2, pd_m2=pd_m2))
        nc = _prog("main", l, NT, last)
        res = run_bass_kernel_spmd(nc, maps, core_ids=cores)
        hfull = np.concatenate([np.asarray(r["h_out"])[128:] for r in res.results], 0)
    return np.ascontiguousarray(hfull.reshape(1, SEQ, D).astype(np.float32))


def kernel(**inputs):
    return kernel_impl(inputs, 8)
```

```python
import contextlib
import numpy as np
import concourse.bass as bass
import concourse.mybir as mybir
from concourse.bass_utils import run_bass_kernel_spmd

F32 = mybir.dt.float32
BF16 = mybir.dt.bfloat16
AF = mybir.ActivationFunctionType
ALU = mybir.AluOpType
AX = mybir.AxisListType

D = 2048
KC = D // 128
DEPTH = 2
N_MEM = 256
HG_H = 16
M2_H = 32
M2_G = 8
D_FF = 5632
N_IN = 18464
C_Q, C_F, C_I, C_OG = 0, 2048, 4096, 6144
C_Z = 8192
C_X = 10240
C_B = C_X + 2048
C_C = C_B + 1024
C_DT = 14336
C_GHG = 14368
C_GM2 = 16416
EPS = 1e-6
M2_EPS = 1e-5

ENGS = ["pe", "act", "dve", "pool", "sp"]
DMA_WINDOW = 8


class Res:
    __slots__ = ("name", "last_write", "readers")

    def __init__(self, name=""):
        self.name = name
        self.last_write = None
        self.readers = []


class Op:
    __slots__ = ("eng", "fn", "deps", "signaled", "is_dma", "dma_idx", "tok", "cc")

    def __init__(self, eng, fn, is_dma=False):
        self.eng = eng
        self.fn = fn
        self.deps = []
        self.signaled = False
        self.is_dma = is_dma
        self.dma_idx = -1
        self.tok = None
        self.cc = False


class T:
    def __init__(self, t, res):
        self.t = t
        self.r = res

    def __getitem__(self, k):
        return self.t[k]


class Prog:
    def __init__(self, nc):
        self.nc = nc
        self.ops = {e: [] for e in ENGS}
        self.dmas = {e: [] for e in ENGS}
        self.stack = contextlib.ExitStack()
        self.scopes = [self.stack]
        self.n = 0
        self.pending = {e: [] for e in ENGS}

    def sb(self, shape, dt, name=None):
        self.n += 1
        name = (name or "sb") + f"_{self.n}"
        t = self.scopes[-1].enter_context(self.nc.sbuf_tensor(name, list(shape), dt))
        return T(t, Res(name))

    @contextlib.contextmanager
    def scope(self):
        st = contextlib.ExitStack()
        self.scopes.append(st)
        try:
            yield
        finally:
            self.scopes.pop()
            self.barrier()
            st.close()

    def ps(self, shape, dt, name=None):
        self.n += 1
        name = (name or "ps") + f"_{self.n}"
        t = self.stack.enter_context(self.nc.psum_tensor(name, list(shape), dt))
        return T(t, Res(name))

    def dram(self, shape, dt, name=None):
        self.n += 1
        name = (name or "dr") + f"_{self.n}"
        t = self.nc.dram_tensor(name, list(shape), dt)
        return T(t.ap(), Res(name))

    def _record(self, o, reads, writes):
        deps = []
        seen = set()

        def add(d):
            if d is None or d is o or id(d) in seen:
                return
            if d.eng == "pe" and o.eng == "pe" and not d.is_dma:
                return
            seen.add(id(d))
            deps.append(d)

        for r in reads:
            add(r.r.last_write)
        for w in writes:
            add(w.r.last_write)
            for rd in w.r.readers:
                add(rd)
        for d in self.pending[o.eng]:
            add(d)
        self.pending[o.eng] = []
        if o.is_dma:
            q = self.dmas[o.eng]
            o.dma_idx = len(q)
            if o.dma_idx >= DMA_WINDOW:
                add(q[o.dma_idx - DMA_WINDOW])
            q.append(o)
        for d in deps:
            d.signaled = True
        o.deps = deps
        for r in reads:
            r.r.readers.append(o)
        for w in writes:
            w.r.last_write = o
            w.r.readers = []
        self.ops[o.eng].append(o)
        return o

    def op(self, eng, fn, reads=(), writes=()):
        return self._record(Op(eng, fn), reads, writes)

    def dma(self, eng, fn, reads=(), writes=()):
        return self._record(Op(eng, fn, is_dma=True), reads, writes)

    def barrier(self):
        tails = []
        for e in ENGS:
            if self.ops[e]:
                tails.append(self.ops[e][-1])
            tails.extend(self.dmas[e][-DMA_WINDOW:])
        for e in ENGS:
            self.pending[e] = list(tails)

    def emit(self, final_deps):
        nc = self.nc
        st = self.stack
        sems = {e: st.enter_context(nc.semaphore(f"s_{e}")) for e in ENGS}
        dsem = {
            e: [st.enter_context(nc.semaphore(f"d_{e}{i}")) for i in range(DMA_WINDOW)]
            for e in ENGS
            if self.dmas[e]
        }
        fin = Op("sp", None)
        fin.deps = list(final_deps)
        for d in fin.deps:
            d.signaled = True
        self.ops["sp"].append(fin)
        ccs = []
        for e in ENGS:
            c = 0
            for o in self.ops[e]:
                if o.is_dma:
                    o.tok = (dsem[e][o.dma_idx % DMA_WINDOW], 16 * (o.dma_idx // DMA_WINDOW + 1))
                elif o.cc:
                    o.tok = (st.enter_context(nc.semaphore(f"cc{len(ccs)}")), 1)
                    ccs.append(o)
                elif o.signaled:
                    c += 1
                    o.tok = (sems[e], c)
        engobj = {"pe": "tensor", "act": "scalar", "dve": "vector", "pool": "gpsimd", "sp": "sync"}
        stats = {}
        with nc.Block() as block:
            for e in ENGS:
                ops = self.ops[e]
                if not ops:
                    continue
                nwait = [0]

                def body(eng, ops=ops, nwait=nwait):
                    waited = {}
                    for o in ops:
                        need = {}
                        for d in o.deps:
                            s, v = d.tok
                            k = id(s)
                            if waited.get(k, 0) < v and need.get(k, (None, 0))[1] < v:
                                need[k] = (s, v)
                        for k, (s, v) in need.items():
                            eng.wait_ge(s, v)
                            waited[k] = v
                            nwait[0] += 1
                        if o.fn is None:
                            continue
                        ins = o.fn(eng)
                        if o.is_dma:
                            ins.then_inc(o.tok[0], 16)
                        elif o.cc:
                            ins.then_inc(o.tok[0], 1)
                        elif o.signaled:
                            ins.then_inc(o.tok[0], 1)

                getattr(block, engobj[e])(body)
                stats[e] = (len(ops), nwait[0])
        return stats


class Ring:
    def __init__(self, items):
        self.items = items
        self.i = 0

    def next(self):
        x = self.items[self.i % len(self.items)]
        self.i += 1
        return x


class K:
    def __init__(self, nc, NT):
        self.nc = nc
        self.P = Prog(nc)
        self.NT = NT
        self.TT = NT * 128
        self.consts()

    def act(self, out, in_, func, reads, writes, **kw):
        return self.P.op("act", lambda e: e.activation(out=out, in_=in_, func=func, **kw), reads, writes)

    def tt(self, out, in0, in1, op, reads, writes, eng="dve"):
        return self.P.op(eng, lambda e: e.tensor_tensor(out=out, in0=in0, in1=in1, op=op), reads, writes)

    def ts(self, out, in0, s1, s2, op0, op1, reads, writes, eng="dve"):
        if op1 is None:
            return self.P.op(eng, lambda e: e.tensor_scalar(out=out, in0=in0, scalar1=s1, scalar2=None, op0=op0), reads, writes)
        return self.P.op(eng, lambda e: e.tensor_scalar(out=out, in0=in0, scalar1=s1, scalar2=s2, op0=op0, op1=op1), reads, writes)

    def stt(self, out, in0, scalar, in1, op0, op1, reads, writes):
        return self.P.op("dve", lambda e: e.scalar_tensor_tensor(out=out, in0=in0, scalar=scalar, in1=in1, op0=op0, op1=op1), reads, writes)

    def mm(self, out, lhsT, rhs, start, stop, reads, writes):
        return self.P.op("pe", lambda e: e.matmul(out, lhsT, rhs, start=start, stop=stop), reads, writes)

    def tr(self, out, in_, ident, reads, writes):
        return self.P.op("pe", lambda e: e.transpose(out, in_, ident), reads, writes)

    def dma(self, out, in_, reads, writes, eng="sp"):
        return self.P.dma(eng, lambda e: e.dma_start(out=out, in_=in_), reads, writes)

    def consts(self):
        P = self.P
        self.idf = P.sb([128, 128], F32, "idf")
        self.idb = P.sb([128, 128], BF16, "idb")
        self.ones_f = P.sb([128, 128], F32, "ones_f")
        self.ones_b = P.sb([128, 128], BF16, "ones_b")
        self.triu = P.sb([128, 128], F32, "triu")
        self.epsc = P.sb([128, 4], F32, "epsc")
        iot = P.sb([128, 128], F32, "iot")
        P.op("pool", lambda e: e.iota(iot[:], [[1, 128]], base=0, channel_multiplier=-1,
                                      allow_small_or_imprecise_dtypes=True), (), [iot])
        self.ts(self.idf[:], iot[:], 0.0, None, ALU.is_equal, None, [iot], [self.idf])
        self.ts(self.idb[:], iot[:], 0.0, None, ALU.is_equal, None, [iot], [self.idb])
        self.ts(self.triu[:], iot[:], 0.0, None, ALU.is_ge, None, [iot], [self.triu])
        P.op("dve", lambda e: e.memset(self.ones_f[:], 1.0), (), [self.ones_f])
        P.op("dve", lambda e: e.memset(self.ones_b[:], 1.0), (), [self.ones_b])
        P.op("dve", lambda e: e.memset(self.epsc[:, 0:1], EPS), (), [self.epsc])
        P.op("dve", lambda e: e.memset(self.epsc[:, 1:2], M2_EPS), (), [self.epsc])
        P.op("dve", lambda e: e.memset(self.epsc[:, 2:3], 1.0), (), [self.epsc])
        self.MAXS = 768
        self.rmask = P.sb([128, self.MAXS], F32, "rmask")
        P.op("dve", lambda e: e.memset(self.rmask[:], 1.0), (), [self.rmask])
        P.op("dve", lambda e: e.memset(self.rmask[:].rearrange("p (c j) -> p c j", j=64)[:, :, 0:1], 0.0), (), [self.rmask])
        self.pp = Ring([P.ps([128, 512], F32, f"pp{i}") for i in range(4)])
        self.pq = Ring([P.ps([128, 512], F32, f"pq{i}") for i in range(2)])
        self.pt = Ring([P.ps([128, 1024], BF16, f"pt{i}") for i in range(2)])

    def load_cols(self, vec_ap, n, name):
        P = self.P
        rows = P.sb([n, 128], F32, name + "_r")
        cols = P.sb([128, n], F32, name)
        self.dma(rows[:], vec_ap, [], [rows])
        ps = self.pq.next()
        self.tr(ps[:, 0:n], rows[:], self.idf[0:n, 0:n], [rows, self.idf], [ps])
        self.act(cols[:], ps[:, 0:n], AF.Copy, [ps], [cols])
        return cols

    def norm_to_uT(self, h_ap, hres, g_ap, uT_d, tiles=None):
        P = self.P
        gbc = P.sb([128, D], F32, "gbc")
        self.dma(gbc[:], g_ap.partition_broadcast(128), [], [gbc])
        hb = Ring([P.sb([128, D], F32, "hb") for _ in range(4)])
        ub = Ring([P.sb([128, D], BF16, "ub") for _ in range(4)])
        junk = P.sb([128, D], BF16, "junk")
        stt_ = Ring([P.sb([128, 4], F32, "nst") for _ in range(4)])
        uo = Ring([P.sb([128, KC, 512], BF16, "uo") for _ in range(2)])
        tl_ = list(tiles if tiles is not None else range(self.NT))
        o = None
        for n_, i in enumerate(tl_):
            h = hb.next(); u = ub.next(); s = stt_.next()
            if n_ % 4 == 0:
                o = uo.next()
            q4 = (n_ % 4) * 128
            self.dma(h[:], h_ap[i * 128:(i + 1) * 128, :], [hres], [h])
            self.act(junk[:], h[:], AF.Square, [h], [junk, s], accum_out=s[:, 0:1])
            self.act(s[:, 1:2], s[:, 0:1], AF.Ln, [s, self.epsc], [s], scale=1.0 / D, bias=self.epsc[:, 0:1])
            self.act(s[:, 2:3], s[:, 1:2], AF.Exp, [s], [s], scale=-0.5)
            self.stt(u[:], h[:], s[:, 2:3], gbc[:], ALU.mult, ALU.mult, [h, s, gbc], [u])
            for half in range(2):
                pt = self.pt.next()
                for j in range(8):
                    kc = half * 8 + j
                    self.tr(pt[:, j * 128:(j + 1) * 128], u[:, kc * 128:(kc + 1) * 128], self.idb[:], [u, self.idb], [pt])
                self.act(o[:, half * 8:(half + 1) * 8, q4:q4 + 128], pt[:].rearrange("p (j t) -> p j t", j=8), AF.Copy, [pt], [o])
            if n_ % 4 == 3 or n_ == len(tl_) - 1:
                i0_ = tl_[n_ - (n_ % 4)]
                self.dma(uT_d[:, :, i0_ * 128:(i + 1) * 128].rearrange("k p t -> p k t"), o[:, :, 0:q4 + 128], [o], [uT_d])

    def hg_setup(self, lbl_ap, l, hgng_ap):
        lg0 = self.load_cols(lbl_ap[0:1, :].rearrange("o (h p) -> (o h) p", p=128), 16, "lg0")
        lg1 = self.load_cols(lbl_ap[1:2, :].rearrange("o (h p) -> (o h) p", p=128), 16, "lg1")
        self.lb = self.P.sb([128, 16], F32, "lb")
        self.oml = self.P.sb([128, 16], F32, "oml")
        if l == 0:
            self.tt(self.lb[:], lg0[:], lg0[:], ALU.subtract, [lg0], [self.lb])
        else:
            self.tt(self.lb[:], lg1[:], lg0[:], ALU.subtract, [lg0, lg1], [self.lb])
            self.act(self.lb[:], self.lb[:], AF.Sigmoid, [self.lb], [self.lb])
        self.ts(self.oml[:], self.lb[:], -1.0, 1.0, ALU.mult, ALU.add, [self.lb], [self.oml])
        if hgng_ap is not None:
            self.hgng = self.load_cols(hgng_ap.rearrange("o (h p) -> (o h) p", p=128), 16, "hgng")

    def hg_setup_lb(self, lbl_ap, l):
        self.hg_setup(lbl_ap, l, None)

    def load_uT_seg(self, uT_d, t0, nt, ring):
        u = ring.next()
        self.dma(u[:, :, 0:nt * 128], uT_d[:, :, t0 * 128:(t0 + nt) * 128].rearrange("k p t -> p k t"), [uT_d], [u])
        return u

    @staticmethod
    def interleave(gens):
        gens = list(gens)
        while gens:
            for g_ in list(gens):
                try:
                    next(g_)
                except StopIteration:
                    gens.remove(g_)

    def pipeline(self, units, *stages):
        units = list(units)
        ns = len(stages)
        for i in range(len(units) + ns - 1):
            gens = []
            for j in range(ns - 1, -1, -1):
                if 0 <= i - j < len(units):
                    gens.append(stages[j](units[i - j]))
            self.interleave(gens)

    def make_segs(self, maxt):
        n = -(-self.NT // maxt)
        base, extra = divmod(self.NT, n)
        segs, t = [], 0
        for i in range(n):
            c = base + (1 if i < extra else 0)
            segs.append((t, c)); t += c
        return segs

    def hg_mixer(self, uT_d, w_ap, mask_ap, S, hgT_d, state_only=False, dec_out=None):
        P = self.P
        MAXS = self.MAXS
        MC = MAXS // 64
        so = state_only
        useg = Ring([P.sb([128, KC, MAXS], BF16, "useg") for _ in range(1)])
        wts = Ring([P.sb([128, KC, 4, 128], BF16, "hgw") for _ in range(2)])
        mk = Ring([P.sb([128, MAXS], F32, "mk") for _ in range(2)])

        def ring(n, shape, dt, nm):
            return Ring([P.sb(shape, dt, nm) for _ in range(n)])

        rA = ring(2, [128, MAXS], F32, "hA")
        rB, rC, rD = (ring(1, [128, MAXS], F32, n) for n in ("hB", "hC", "hD"))
        rE = ring(2, [128, 2 if so else MAXS], F32, "hE")
        rG = ring(1, [128, MAXS], BF16, "hG")
        rI = ring(2, [128, MAXS], BF16, "hI")
        q_ = 2 if so else MAXS
        rSG = ring(3, [128, q_], F32, "hSG")
        rO = ring(1, [128, q_], F32, "hO")
        rF, rHo = (ring(2, [128, q_], BF16, n) for n in ("hF", "hHo"))
        rsq = ring(1, [128, q_], F32, "hsq")
        rcs = ring(2, [128, 4, 16], F32, "hcs")
        rkv = ring(2, [64, MC, 256], BF16, "hkv")
        rsT = ring(2, [64, MC, 2 if so else 64], BF16, "hsT")
        rtm = ring(2, [128, MC, 128], F32, "htm")
        rSp = ring(3, [128, 128], BF16, "hSp")
        rrs = ring(2, [128, 512], F32, "hrs")
        scale = 128 ** -0.5
        seg = {}

        def stage0(un):
            (t0, nt, hd) = un
            NS = nt * 128
            NCH = NS // 64
            if hd == 0:
                seg["u"] = self.load_uT_seg(uT_d, t0, nt, useg)
                m_ = mk.next()
                self.dma(m_[:, 0:NS], mask_ap[0:1, t0 * 128:t0 * 128 + NS].partition_broadcast(128), [], [m_])
                seg["mask"] = m_
            u = seg["u"]; mask = seg["mask"]
            chunks = [(n0, min(512, NS - n0)) for n0 in range(0, NS, 512)]
            wt = wts.next()
            mats = [(1, C_F), (2, C_I)] if so else [(1, C_F), (0, C_Q), (2, C_I), (3, C_OG)]
            for (m, c0) in mats:
                self.dma(wt[:, :, m, :], w_ap[:, c0 + hd * 128:c0 + (hd + 1) * 128].rearrange("(k p) c -> p k c", p=128),
                         [], [wt], eng="pool")
            A, E, I, SG = (r.next() for r in (rA, rE, rI, rSG))
            st_ = dict(NS=NS, NCH=NCH, chunks=chunks, SG=SG, A=A, E=E, I=I, mask=mask, t0=t0, hd=hd)
            seg[("st", t0, hd)] = st_
            for (m, c0) in mats:
                for (n0, nn) in chunks:
                    ps = self.pp.next()
                    for kc in range(KC):
                        self.mm(ps[:, 0:nn], wt[:, kc, m, :], u[:, kc, n0:n0 + nn], kc == 0, kc == KC - 1, [wt, u], [ps])
                    if m == 1:
                        self.act(A[:, n0:n0 + nn], ps[:, 0:nn], AF.Sigmoid, [ps], [A])
                    elif m == 0:
                        self.act(E[:, n0:n0 + nn], ps[:, 0:nn], AF.Copy, [ps], [E], scale=scale)
                    elif m == 2:
                        self.act(I[:, n0:n0 + nn], ps[:, 0:nn], AF.Copy, [ps], [I])
                    else:
                        self.act(SG[:, n0:n0 + nn], ps[:, 0:nn], AF.Sigmoid, [ps], [SG])
                    yield

        def stage1(un):
            (t0, nt, hd) = un
            st_ = seg[("st", t0, hd)]
            NS, NCH, A, E, I, mask = (st_[k_] for k_ in ("NS", "NCH", "A", "E", "I", "mask"))
            B, C, Dd, G = (r.next() for r in (rB, rC, rD, rG))
            F = rF.next()
            cs = rcs.next(); kv = rkv.next(); sTa = rsT.next(); tm = rtm.next()
            st_.update(F=F, cs=cs, kv=kv, sT=sTa, tm=tm)
            self.ts(A[:, 0:NS], A[:, 0:NS], self.oml[:, hd:hd + 1], self.lb[:, hd:hd + 1], ALU.mult, ALU.add, [A, self.oml, self.lb], [A])
            self.ts(B[:, 0:NS], A[:, 0:NS], -1.0, 1.0, ALU.mult, ALU.add, [A], [B])
            yield
            self.tt(B[:, 0:NS], B[:, 0:NS], mask[:, 0:NS], ALU.mult, [B, mask], [B])
            self.act(A[:, 0:NS], A[:, 0:NS], AF.Ln, [A], [A])
            self.tt(A[:, 0:NS], A[:, 0:NS], mask[:, 0:NS], ALU.mult, [A, mask], [A])
            yield
            P.op("dve", lambda e, C=C, A=A, NS=NS: e.tensor_tensor_scan(out=C[:, 0:NS], data0=self.rmask[:, 0:NS], data1=A[:, 0:NS],
                                                                    initial=0.0, op0=ALU.mult, op1=ALU.add), [A, self.rmask], [C])
            C3 = C[:, 0:NS].rearrange("p (c j) -> p c j", j=64)
            A3 = A[:, 0:NS].rearrange("p (c j) -> p c j", j=64)
            yield
            self.tt(A3, C3, C3[:, :, 31:32].to_broadcast([128, NCH, 64]), ALU.subtract, [C], [A])
            self.tt(cs[:, 3, 0:NCH], C3[:, :, 63], C3[:, :, 31], ALU.subtract, [C], [cs])
            self.act(cs[:, 0, 0:NCH], C3[:, :, 63], AF.Exp, [C], [cs])
            self.act(cs[:, 1, 0:NCH], cs[:, 3, 0:NCH], AF.Exp, [cs], [cs])
            self.act(cs[:, 2, 0:NCH], C3[:, :, 31], AF.Exp, [C], [cs])
            if dec_out is not None:
                P.op("dve", lambda e, cs=cs, C3=C3, NCH=NCH: e.tensor_reduce(out=cs[:, 3, 0:1], in_=C3[:, :, 63], axis=AX.X, op=ALU.add), [C], [cs])
                self.tt(dec_out[:, hd:hd + 1], dec_out[:, hd:hd + 1], cs[:, 3, 0:1], ALU.add, [dec_out, cs], [dec_out])
            self.act(Dd[:, 0:NS], A[:, 0:NS], AF.Exp, [A], [Dd], scale=-1.0)
            yield
            self.tt(G[:, 0:NS], B[:, 0:NS], Dd[:, 0:NS], ALU.mult, [B, Dd], [G])
            if not so:
                self.act(A[:, 0:NS], A[:, 0:NS], AF.Exp, [A], [A])
                self.tt(F[:, 0:NS], E[:, 0:NS], A[:, 0:NS], ALU.mult, [E, A], [F])
            yield
            def state_mm(c):
                pst = self.pq.next()
                self.mm(pst[:, 0:128], kv[:, c, 0:128], kv[:, c, 128:256], True, True, [kv], [pst])
                self.ts(tm[:, c, :], pst[:, 0:128], cs[:, 1, c:c + 1], None, ALU.mult, None, [pst, cs], [tm])

            for c in range(NCH):
                c0 = c * 64
                pt = self.pt.next()
                self.tr(pt[0:64, 0:128], G[:, c0:c0 + 64], self.idb[:], [G, self.idb], [pt])
                self.tr(pt[0:64, 128:256], I[:, c0:c0 + 64], self.idb[:], [I, self.idb], [pt])
                self.act(kv[:, c, :], pt[0:64, 0:256], AF.Copy, [pt], [kv])
                if not so:
                    psc = self.pq.next()
                    self.mm(psc[0:64, 0:64], G[:, c0:c0 + 64], F[:, c0:c0 + 64], True, True, [G, F], [psc])
                    self.tt(sTa[:, c, :], psc[0:64, 0:64], self.triu[0:64, 0:64], ALU.mult, [psc, self.triu], [sTa])
                if c > 0:
                    state_mm(c - 1)
                yield
            state_mm(NCH - 1)
            yield

        def stage2(un):
            (t0, nt, hd) = un
            d = seg.pop(("st", t0, hd))
            NS, NCH, chunks = d["NS"], d["NCH"], d["chunks"]
            SG, F, cs, kv, sTa, tm = (d[k_] for k_ in ("SG", "F", "cs", "kv", "sT", "tm"))
            O = rO.next(); Ho = rHo.next()
            Sh = S[:, hd, :]
            Sp = None
            if not so:
                Sp = rSp.next()
                self.ts(Sp[:], Sh, cs[:, 2, 0:1], None, ALU.mult, None, [S, cs], [Sp])
                yield
            for c in range(NCH):
                c0 = c * 64
                self.stt(Sh, Sh, cs[:, 0, c:c + 1], tm[:, c, :], ALU.mult, ALU.add, [S, cs, tm], [S])
                if not so:
                    Spn = None
                    if c + 1 < NCH:
                        Spn = rSp.next()
                        self.ts(Spn[:], Sh, cs[:, 2, c + 1:c + 2], None, ALU.mult, None, [S, cs], [Spn])
                    po = self.pq.next()
                    self.mm(po[:, 0:64], kv[:, c, 128:256], sTa[:, c, :], True, False, [kv, sTa], [po])
                    self.mm(po[:, 0:64], Sp[:], F[:, c0:c0 + 64], False, True, [Sp, F], [po])
                    self.tt(O[:, c0:c0 + 64], po[:, 0:64], SG[:, c0:c0 + 64], ALU.mult, [po, SG], [O])
                    Sp = Spn
                yield
            if so:
                return
            sq = rsq.next()
            self.tt(sq[:, 0:NS], O[:, 0:NS], O[:, 0:NS], ALU.mult, [O], [sq])
            for (n0, nn) in chunks:
                ps = self.pp.next()
                self.mm(ps[:, 0:nn], self.ones_f[:], sq[:, n0:n0 + nn], True, True, [self.ones_f, sq], [ps])
                rs = rrs.next()
                self.act(rs[:, 0:nn], ps[:, 0:nn], AF.Ln, [ps, self.epsc], [rs], scale=1.0 / 128, bias=self.epsc[:, 0:1])
                self.act(rs[:, 0:nn], rs[:, 0:nn], AF.Exp, [rs], [rs], scale=-0.5)
                self.stt(Ho[:, n0:n0 + nn], O[:, n0:n0 + nn], self.hgng[:, hd:hd + 1], rs[:, 0:nn], ALU.mult, ALU.mult,
                         [O, self.hgng, rs], [Ho])
                yield
            self.dma(hgT_d[hd, :, t0 * 128:t0 * 128 + NS], Ho[:, 0:NS], [Ho], [hgT_d])

        units = [(t0, nt, hd) for (t0, nt) in self.make_segs(MAXS // 128) for hd in range(HG_H)]
        self.pipeline(units, stage0, stage1, stage2)

    def m2_setup(self, w_ap, convw_ap, convb_ap, dtb_ap, alog_ap, dsk_ap, ng_ap, mask_ap):
        P = self.P
        self.cw = [self.load_cols(convw_ap[k:k + 1, :].rearrange("o (j p) -> (o j) p", p=128), 32, f"cw{k}") for k in range(4)]
        self.cbias = self.load_cols(convb_ap.rearrange("o (j p) -> (o j) p", p=128), 32, "cbias")
        self.maskT = self.load_cols(mask_ap.rearrange("o (j p) -> (o j) p", p=128), self.NT, "maskT")
        self.dtb = P.sb([128, 32], F32, "dtb")
        self.negA = P.sb([128, 32], F32, "negA")
        self.dsk = P.sb([128, 32], F32, "dsk")
        self.m2ng_ap = ng_ap
        self.dma(self.dtb[:], dtb_ap.partition_broadcast(128), [], [self.dtb])
        self.dma(self.negA[:], alog_ap.partition_broadcast(128), [], [self.negA])
        if dsk_ap is not None:
            self.dma(self.dsk[:], dsk_ap.partition_broadcast(128), [], [self.dsk])
        self.act(self.negA[:], self.negA[:], AF.Exp, [self.negA], [self.negA])
        self.ts(self.negA[:], self.negA[:], -1.0, None, ALU.mult, None, [self.negA], [self.negA])
        self.wdt = P.sb([128, KC, 32], BF16, "wdt")
        self.dma(self.wdt[:], w_ap[:, C_DT:C_DT + 32].rearrange("(k p) c -> p k c", p=128), [], [self.wdt], eng="pool")
        self.strict = P.sb([128, 128], F32, "strict")
        self.ts(self.strict[:], self.triu[:], -1.0, 1.0, ALU.mult, ALU.add, [self.triu], [self.strict])
        self.carry = P.sb([128, 8, 4, 3], F32, "carry")
        P.op("dve", lambda e: e.memset(self.carry[:], 0.0), (), [self.carry])

    def m2_mixer(self, uT_d, w_ap, S, Sbf, m2T_d, state_only=False, dec_out=None):
        P = self.P
        MT = 5
        MAXS = MT * 128
        so = state_only

        def ring(n, shape, dt, nm):
            return Ring([P.sb(shape, dt, nm) for _ in range(n)])

        useg = ring(1, [128, KC, MAXS], BF16, "useg")
        wts = ring(2, [128, KC, 768], BF16, "m2w")
        rxr = ring(2, [128, 4, 3 + MAXS], F32, "xr")
        racc = ring(1, [128, 2, MAXS], F32, "xacc")
        rtmpc = ring(1, [128, MAXS], F32, "ctmp")
        rBT = ring(1, [128, MAXS], BF16, "BT")
        rCT = ring(2, [128, 2 if so else MAXS], BF16, "CT")
        rzs = ring(2, [128, 1 if so else MT, 256], F32, "zs")
        rm2o = ring(2, [128, 2, 2 if so else MAXS], BF16, "m2o")
        rng = ring(2, [128, 2 if so else 256], F32, "m2ngs")
        rybuf = ring(2, [128, 1 if so else MT, 256], F32, "ybuf")
        rxddA = ring(2, [128, MT, 256], BF16, "xddA")
        rBtmA = ring(2, [128, MT, 128], BF16, "BtmA")
        rps = Ring([dict((n, P.sb([128, MT, 32], F32, n)) for n in ("dtS", "aS", "cumS", "lastS", "ecS", "elS", "ddS")) for _ in range(2)])
        rxs = ring(2, [128, 256], F32, "xs")
        rxdt = ring(4, [128, 4, 64], BF16, "xdt")
        q_ = 2 if so else 128
        rcbm = ring(3, [128, q_], F32, "cbm")
        rM1 = ring(3, [128, 4, q_], F32, "M1")
        rEL = ring(2, [128, 4, q_], F32, "EL")
        rWT = ring(3, [128, 4, q_], BF16, "WT")
        ry = ring(2, [128, 2 * q_], F32, "y")
        ry2 = ring(3, [128, 2 * q_], F32, "y2")
        ryn = ring(2, [128, 2 * q_], BF16, "yn")
        rst = ring(2, [128, 4], F32, "mst")
        rtS = ring(2, [128, 4, 64], F32, "tS")
        rpSb = ring(3, [128, 256], F32, "pSb")
        junk = P.sb([128, 256], BF16, "mjunk")
        seg = {}

        def seg_prep(t0, nt):
            u = self.load_uT_seg(uT_d, t0, nt, useg)
            seg["u"] = u
            ps_ = rps.next()
            seg["ps"] = ps_
            dtS, aS, cumS, lastS, ecS, elS, ddS = (ps_[n] for n in ("dtS", "aS", "cumS", "lastS", "ecS", "elS", "ddS"))
            for ti in range(nt):
                ps = self.pq.next()
                for kc in range(KC):
                    self.mm(ps[:, 0:32], u[:, kc, ti * 128:(ti + 1) * 128], self.wdt[:, kc, :], kc == 0, kc == KC - 1, [u, self.wdt], [ps])
                self.tt(dtS[:, ti, :], ps[:, 0:32], self.dtb[:], ALU.add, [ps, self.dtb], [dtS])
            self.act(dtS[:, 0:nt, :], dtS[:, 0:nt, :], AF.Exp, [dtS], [dtS])
            self.act(dtS[:, 0:nt, :], dtS[:, 0:nt, :], AF.Ln, [dtS, self.epsc], [dtS], bias=self.epsc[:, 2:3])
            for ti in range(nt):
                self.ts(dtS[:, ti, :], dtS[:, ti, :], self.maskT[:, t0 + ti:t0 + ti + 1], None, ALU.mult, None, [dtS, self.maskT], [dtS])
                self.tt(aS[:, ti, :], dtS[:, ti, :], self.negA[:], ALU.mult, [dtS, self.negA], [aS])
            for ti in range(nt):
                ps = self.pq.next()
                self.mm(ps[:, 0:32], self.triu[:], aS[:, ti, :], True, True, [self.triu, aS], [ps])
                self.mm(ps[:, 32:64], self.ones_f[:], aS[:, ti, :], True, True, [self.ones_f, aS], [ps])
                self.act(cumS[:, ti, :], ps[:, 0:32], AF.Copy, [ps], [cumS])
                self.act(lastS[:, ti, :], ps[:, 32:64], AF.Copy, [ps], [lastS])
                if dec_out is not None:
                    self.tt(dec_out[:], dec_out[:], lastS[:, ti, :], ALU.add, [dec_out, lastS], [dec_out])
            self.act(ecS[:, 0:nt, :], cumS[:, 0:nt, :], AF.Exp, [cumS], [ecS])
            self.act(elS[:, 0:nt, :], lastS[:, 0:nt, :], AF.Exp, [lastS], [elS])
            self.tt(ddS[:, 0:nt, :], lastS[:, 0:nt, :], cumS[:, 0:nt, :], ALU.subtract, [lastS, cumS], [ddS])
            self.act(ddS[:, 0:nt, :], ddS[:, 0:nt, :], AF.Exp, [ddS], [ddS])
            self.tt(ddS[:, 0:nt, :], ddS[:, 0:nt, :], dtS[:, 0:nt, :], ALU.mult, [ddS, dtS], [ddS])

        def stage0(un):
            (t0, nt, gi) = un
            NS = nt * 128
            if gi == 0:
                seg_prep(t0, nt)
                yield
            u = seg["u"]
            chunks = [(n0, min(512, NS - n0)) for n0 in range(0, NS, 512)]
            wt = wts.next()
            srcs = [(0, C_Z + gi * 256, 256), (256, C_X + gi * 256, 256), (512, C_B + gi * 128, 128), (640, C_C + gi * 128, 128)]
            for (o0, c0, w_) in srcs:
                if so and o0 in (0, 640):
                    continue
                self.dma(wt[:, :, o0:o0 + w_], w_ap[:, c0:c0 + w_].rearrange("(k p) c -> p k c", p=128), [], [wt], eng="pool")
            xr = rxr.next(); zs = rzs.next()
            seg[("st", t0, gi)] = dict(xr=xr, zs=zs, ps=seg["ps"], chunks=chunks)
            blks = [0, 1, 2] if so else [0, 1, 2, 3]
            if not so:
                for ti in range(nt):
                    ps = self.pp.next()
                    for kc in range(KC):
                        self.mm(ps[:, 0:256], u[:, kc, ti * 128:(ti + 1) * 128], wt[:, kc, 0:256], kc == 0, kc == KC - 1, [u, wt], [ps])
                    self.act(zs[:, ti, :], ps[:, 0:256], AF.Silu, [ps], [zs])
                    yield
            for b in blks:
                for (n0, nn) in chunks:
                    ps = self.pp.next()
                    for kc in range(KC):
                        self.mm(ps[:, 0:nn], wt[:, kc, 256 + b * 128:256 + (b + 1) * 128], u[:, kc, n0:n0 + nn], kc == 0, kc == KC - 1, [wt, u], [ps])
                    self.act(xr[:, b, 3 + n0:3 + n0 + nn], ps[:, 0:nn], AF.Copy, [ps], [xr])
                    yield

        def stage1(un):
            (t0, nt, gi) = un
            NS = nt * 128
            st_ = seg[("st", t0, gi)]
            xr = st_["xr"]; ps_ = st_["ps"]
            dtS, aS, ddS = ps_["dtS"], ps_["aS"], ps_["ddS"]
            acc = racc.next(); BT = rBT.next(); CT = rCT.next(); m2o = rm2o.next()
            ybuf = rybuf.next(); xddA = rxddA.next(); BtmA = rBtmA.next(); ngs = rng.next()
            if not so:
                self.dma(ngs[:], self.m2ng_ap[0:1, gi * 256:(gi + 1) * 256].partition_broadcast(128), [], [ngs])
            st_.update(CT=CT, m2o=m2o, ybuf=ybuf, xddA=xddA, BtmA=BtmA, ngs=ngs)
            blks = [0, 1, 2] if so else [0, 1, 2, 3]
            for b in blks:
                j = (gi * 2 + b) if b < 2 else (16 + gi if b == 2 else 24 + gi)
                self.act(xr[:, b, 0:3], self.carry[:, gi, b, :], AF.Copy, [self.carry], [xr])
                self.act(self.carry[:, gi, b, :], xr[:, b, NS:NS + 3], AF.Copy, [xr], [self.carry])
                tmpc = rtmpc.next()
                self.ts(tmpc[:, 0:NS], xr[:, b, 3:3 + NS], self.cw[3][:, j:j + 1], self.cbias[:, j:j + 1], ALU.mult, ALU.add,
                        [xr, self.cw[3], self.cbias], [tmpc])
                for k_ in range(3):
                    self.stt(tmpc[:, 0:NS], xr[:, b, k_:k_ + NS], self.cw[k_][:, j:j + 1], tmpc[:, 0:NS], ALU.mult, ALU.add,
                             [xr, self.cw[k_], tmpc], [tmpc])
                if b < 2:
                    self.act(acc[:, b, 0:NS], tmpc[:, 0:NS], AF.Silu, [tmpc], [acc])
                elif b == 2:
                    self.act(BT[:, 0:NS], tmpc[:, 0:NS], AF.Silu, [tmpc], [BT])
                else:
                    self.act(CT[:, 0:NS], tmpc[:, 0:NS], AF.Silu, [tmpc], [CT])
                yield
            hs = slice(gi * 4, gi * 4 + 4)
            tl = {}

            def phaseA(ti):
                tk = slice(ti * 128, (ti + 1) * 128)
                px = self.pq.next()
                self.tr(px[:, 0:128], acc[:, 0, tk], self.idf[:], [acc, self.idf], [px])
                self.tr(px[:, 128:256], acc[:, 1, tk], self.idf[:], [acc, self.idf], [px])
                pb = self.pt.next()
                self.tr(pb[:, 0:128], BT[:, tk], self.idb[:], [BT, self.idb], [pb])
                xs = rxs.next()
                self.act(xs[:], px[:, 0:256], AF.Copy, [px], [xs])
                self.act(BtmA[:, ti, :], pb[:, 0:128], AF.Copy, [pb], [BtmA])
                xs3 = xs[:].rearrange("p (h q) -> p h q", q=64)
                self.tt(xddA[:, ti, :].rearrange("p (h q) -> p h q", q=64), xs3, ddS[:, ti, hs].unsqueeze(2).to_broadcast([128, 4, 64]),
                        ALU.mult, [xs, ddS], [xddA])
                if so:
                    return
                xdt = rxdt.next(); y2 = ry2.next(); cbm = rcbm.next(); M1 = rM1.next()
                self.tt(xdt[:], xs3, dtS[:, ti, hs].unsqueeze(2).to_broadcast([128, 4, 64]), ALU.mult, [xs, dtS], [xdt])
                self.tt(y2[:].rearrange("p (h q) -> p h q", q=64), xs3, self.dsk[:, hs].unsqueeze(2).to_broadcast([128, 4, 64]),
                        ALU.mult, [xs, self.dsk], [y2])
                pcb = self.pq.next()
                self.mm(pcb[:, 0:128], BT[:, tk], CT[:, tk], True, True, [BT, CT], [pcb])
                self.tt(cbm[:], pcb[:, 0:128], self.triu[:], ALU.mult, [pcb, self.triu], [cbm])
                self.tt(M1[:], self.strict[:].unsqueeze(1).to_broadcast([128, 4, 128]),
                        aS[:, ti, hs].unsqueeze(2).to_broadcast([128, 4, 128]), ALU.mult, [self.strict, aS], [M1])
                tl[ti] = dict(xdt=xdt, y2=y2, cbm=cbm, M1=M1)

            def phaseB(ti):
                d_ = tl[ti]
                pD = self.pq.next()
                for h in range(4):
                    self.mm(pD[:, h * 128:(h + 1) * 128], d_["M1"][:, h, :], self.triu[:], True, True, [d_["M1"], self.triu], [pD])
                EL = rEL.next()
                self.act(EL[:], pD[:].rearrange("p (h t) -> p h t", h=4), AF.Exp, [pD], [EL])
                WT = rWT.next()
                self.tt(WT[:], EL[:], d_["cbm"][:].unsqueeze(1).to_broadcast([128, 4, 128]), ALU.mult, [EL, d_["cbm"]], [WT])
                d_["WT"] = WT

            def phaseC(ti):
                d_ = tl.pop(ti)
                py = self.pq.next()
                for h in range(4):
                    self.mm(py[:, h * 64:(h + 1) * 64], d_["WT"][:, h, :], d_["xdt"][:, h, :], True, True, [d_["WT"], d_["xdt"]], [py])
                self.tt(ybuf[:, ti, :], d_["y2"][:], py[:, 0:256], ALU.add, [d_["y2"], py], [ybuf])

            for step in range(nt + (0 if so else 2)):
                if not so and 0 <= step - 2 < nt:
                    phaseC(step - 2)
                if not so and 0 <= step - 1 < nt:
                    phaseB(step - 1)
                if step < nt:
                    phaseA(step)
                yield

        def stage2(un):
            (t0, nt, gi) = un
            NS = nt * 128
            d = seg.pop(("st", t0, gi))
            zs = d["zs"]; ps_ = d["ps"]
            ecS, elS = ps_["ecS"], ps_["elS"]
            CT, m2o, ybuf, xddA, BtmA, ngs = (d[k_] for k_ in ("CT", "m2o", "ybuf", "xddA", "BtmA", "ngs"))
            Sg = S[:, gi * 4:(gi + 1) * 4, :]
            Sbg = Sbf[:, gi * 4:(gi + 1) * 4, :]
            hs = slice(gi * 4, gi * 4 + 4)
            def ps_mm(ti):
                pS_ = self.pq.next()
                self.mm(pS_[:, 0:256], BtmA[:, ti, :], xddA[:, ti, :], True, True, [BtmA, xddA], [pS_])
                h_ = rpSb.next()
                self.act(h_[:], pS_[:, 0:256], AF.Copy, [pS_], [h_])
                return h_, 0

            nxt = ps_mm(0)
            for ti in range(nt):
                tk = slice(ti * 128, (ti + 1) * 128)
                if not so:
                    py = self.pq.next()
                    self.mm(py[:, 0:256], CT[:, tk], Sbg.rearrange("p h q -> p (h q)"), True, True, [CT, Sbf], [py])
                (pS, po_) = nxt
                if ti + 1 < nt:
                    nxt = ps_mm(ti + 1)
                tS = rtS.next()
                self.tt(tS[:], Sg, elS[:, ti, hs].unsqueeze(2).to_broadcast([128, 4, 64]), ALU.mult, [S, elS], [tS])
                self.tt(Sg, tS[:], pS[:, po_:po_ + 256].rearrange("p (h q) -> p h q", q=64), ALU.add, [tS, pS], [S])
                if not so:
                    self.act(Sbg, Sg, AF.Copy, [S], [Sbf])
                    y = ry.next()
                    self.tt(y[:].rearrange("p (h q) -> p h q", q=64), py[:, 0:256].rearrange("p (h q) -> p h q", q=64),
                            ecS[:, ti, hs].unsqueeze(2).to_broadcast([128, 4, 64]), ALU.mult, [py, ecS], [y])
                    self.tt(y[:], y[:], ybuf[:, ti, :], ALU.add, [y, ybuf], [y])
                    self.tt(y[:], y[:], zs[:, ti, :], ALU.mult, [y, zs], [y])
                    st = rst.next()
                    self.act(junk[:], y[:], AF.Square, [y], [junk, st], accum_out=st[:, 0:1])
                    self.act(st[:, 1:2], st[:, 0:1], AF.Ln, [st, self.epsc], [st], scale=1.0 / 256, bias=self.epsc[:, 1:2])
                    self.act(st[:, 2:3], st[:, 1:2], AF.Exp, [st], [st], scale=-0.5)
                    yn = ryn.next()
                    self.stt(yn[:], y[:], st[:, 2:3], ngs[:], ALU.mult, ALU.mult, [y, st, ngs], [yn])
                    po = self.pt.next()
                    self.tr(po[:, 0:128], yn[:, 0:128], self.idb[:], [yn, self.idb], [po])
                    self.tr(po[:, 128:256], yn[:, 128:256], self.idb[:], [yn, self.idb], [po])
                    self.act(m2o[:, :, tk], po[:, 0:256].rearrange("p (j t) -> p j t", j=2), AF.Copy, [po], [m2o])
                yield
            if not so:
                self.dma(m2T_d[gi * 2:gi * 2 + 2, :, t0 * 128:t0 * 128 + NS].rearrange("j p t -> p j t"), m2o[:, :, 0:NS], [m2o], [m2T_d])

        units = [(t0, nt, gi) for (t0, nt) in self.make_segs(MT) for gi in range(M2_G)]
        self.pipeline(units, stage0, stage1, stage2)

    def hg_state(self, uT_d, w_ap, mask_ap, S, dec_out):
        P = self.P
        MT = 6
        MAXS = MT * 128

        def ring(n, shape, dt, nm):
            return Ring([P.sb(shape, dt, nm) for _ in range(n)])

        useg = ring(1, [128, KC, MAXS], BF16, "useg")
        wts = ring(2, [128, KC, 2, 128], BF16, "hsw")
        mk = ring(2, [128, MAXS], F32, "mk")
        rA = ring(2, [128, MAXS], F32, "sA")
        rI = ring(2, [128, MAXS], BF16, "sI")
        rB, rC = (ring(1, [128, MAXS], F32, n) for n in ("sB", "sC"))
        rG = ring(2, [128, MAXS], BF16, "sG")
        rkv = ring(2, [128, MT, 256], BF16, "skv")
        rcs = ring(2, [128, 4], F32, "scs")
        onesr = P.sb([128, MAXS], F32, "onesr")
        P.op("dve", lambda e: e.memset(onesr[:], 1.0), (), [onesr])
        seg = {}

        def stage0(un):
            (t0, nt, hd) = un
            NS = nt * 128
            if hd == 0:
                seg["u"] = self.load_uT_seg(uT_d, t0, nt, useg)
                m_ = mk.next()
                self.dma(m_[:, 0:NS], mask_ap[0:1, t0 * 128:t0 * 128 + NS].partition_broadcast(128), [], [m_])
                seg["mask"] = m_
            u = seg["u"]
            wt = wts.next()
            for (m, c0) in ((0, C_F), (1, C_I)):
                self.dma(wt[:, :, m, :], w_ap[:, c0 + hd * 128:c0 + (hd + 1) * 128].rearrange("(k p) c -> p k c", p=128), [], [wt], eng="pool")
            A = rA.next(); I = rI.next()
            seg[("st", t0, hd)] = dict(A=A, I=I, mask=seg["mask"])
            for m in range(2):
                for (n0, nn) in [(a, min(512, NS - a)) for a in range(0, NS, 512)]:
                    ps = self.pp.next()
                    for kc in range(KC):
                        self.mm(ps[:, 0:nn], wt[:, kc, m, :], u[:, kc, n0:n0 + nn], kc == 0, kc == KC - 1, [wt, u], [ps])
                    if m == 0:
                        self.act(A[:, n0:n0 + nn], ps[:, 0:nn], AF.Sigmoid, [ps], [A])
                    else:
                        self.act(I[:, n0:n0 + nn], ps[:, 0:nn], AF.Copy, [ps], [I])
                    yield

        def stage1(un):
            (t0, nt, hd) = un
            NS = nt * 128
            d = seg.pop(("st", t0, hd))
            A, I, mask = d["A"], d["I"], d["mask"]
            B = rB.next(); C = rC.next(); G = rG.next(); kv = rkv.next(); cs = rcs.next()
            self.ts(A[:, 0:NS], A[:, 0:NS], self.oml[:, hd:hd + 1], self.lb[:, hd:hd + 1], ALU.mult, ALU.add, [A, self.oml, self.lb], [A])
            self.ts(B[:, 0:NS], A[:, 0:NS], -1.0, 1.0, ALU.mult, ALU.add, [A], [B])
            self.tt(B[:, 0:NS], B[:, 0:NS], mask[:, 0:NS], ALU.mult, [B, mask], [B])
            self.act(A[:, 0:NS], A[:, 0:NS], AF.Ln, [A], [A])
            self.tt(A[:, 0:NS], A[:, 0:NS], mask[:, 0:NS], ALU.mult, [A, mask], [A])
            yield
            P.op("dve", lambda e, C=C, A=A, NS=NS: e.tensor_tensor_scan(out=C[:, 0:NS], data0=onesr[:, 0:NS], data1=A[:, 0:NS],
                                                                    initial=0.0, op0=ALU.mult, op1=ALU.add), [A, onesr], [C])
            self.act(A[:, 0:NS], C[:, 0:NS], AF.Exp, [C], [A], scale=-1.0, bias=C[:, NS - 1:NS])
            self.act(cs[:, 0:1], C[:, NS - 1:NS], AF.Exp, [C], [cs])
            self.tt(dec_out[:, hd:hd + 1], dec_out[:, hd:hd + 1], C[:, NS - 1:NS], ALU.add, [dec_out, C], [dec_out])
            self.tt(G[:, 0:NS], B[:, 0:NS], A[:, 0:NS], ALU.mult, [B, A], [G])
            yield
            for ti in range(nt):
                tk = slice(ti * 128, (ti + 1) * 128)
                pt = self.pt.next()
                self.tr(pt[:, 0:128], G[:, tk], self.idb[:], [G, self.idb], [pt])
                self.tr(pt[:, 128:256], I[:, tk], self.idb[:], [I, self.idb], [pt])
                self.act(kv[:, ti, :], pt[:, 0:256], AF.Copy, [pt], [kv])
                if ti % 2 == 1:
                    yield
            pst = self.pq.next()
            for ti in range(nt):
                self.mm(pst[:, 0:128], kv[:, ti, 0:128], kv[:, ti, 128:256], ti == 0, ti == nt - 1, [kv], [pst])
            Sh = S[:, hd, :]
            self.stt(Sh, Sh, cs[:, 0:1], pst[:, 0:128], ALU.mult, ALU.add, [S, cs, pst], [S])
            yield

        units = [(t0, nt, hd) for (t0, nt) in self.make_segs(MT) for hd in range(HG_H)]
        self.pipeline(units, stage0, stage1)

    def m2_state(self, uT_d, w_ap, S, dec_out):
        P = self.P
        MT = 6
        MAXS = MT * 128

        def ring(n, shape, dt, nm):
            return Ring([P.sb(shape, dt, nm) for _ in range(n)])

        useg = ring(1, [128, KC, MAXS], BF16, "useg")
        wts = ring(2, [128, KC, 384], BF16, "msw")
        rxr = ring(2, [128, 3, 3 + MAXS], F32, "xr")
        racc = ring(1, [128, 2, MAXS], F32, "xacc")
        rtmpc = ring(2, [128, MAXS], F32, "ctmp")
        rBT = ring(1, [128, MAXS], BF16, "BT")
        rps = Ring([dict((n, P.sb([128, MT, 32], F32, n)) for n in ("dtS", "aS", "cumS", "lastS", "ddS")) for _ in range(2)])
        rtot = ring(2, [128, 2, 32], F32, "mtot")
        rxs = ring(2, [128, 256], F32, "xs")
        rxdd = ring(2, [128, MT, 256], BF16, "xddA")
        rBtm = ring(2, [128, MT, 128], BF16, "BtmA")
        rtS = ring(2, [128, 4, 64], F32, "tS")
        seg = {}

        def seg_prep(t0, nt):
            u = self.load_uT_seg(uT_d, t0, nt, useg)
            seg["u"] = u
            ps_ = rps.next(); tot = rtot.next()
            seg["ps"] = ps_; seg["tot"] = tot
            dtS, aS, cumS, lastS, ddS = (ps_[n] for n in ("dtS", "aS", "cumS", "lastS", "ddS"))
            for ti in range(nt):
                ps = self.pq.next()
                for kc in range(KC):
                    self.mm(ps[:, 0:32], u[:, kc, ti * 128:(ti + 1) * 128], self.wdt[:, kc, :], kc == 0, kc == KC - 1, [u, self.wdt], [ps])
                self.tt(dtS[:, ti, :], ps[:, 0:32], self.dtb[:], ALU.add, [ps, self.dtb], [dtS])
            self.act(dtS[:, 0:nt, :], dtS[:, 0:nt, :], AF.Exp, [dtS], [dtS])
            self.act(dtS[:, 0:nt, :], dtS[:, 0:nt, :], AF.Ln, [dtS, self.epsc], [dtS], bias=self.epsc[:, 2:3])
            for ti in range(nt):
                self.ts(dtS[:, ti, :], dtS[:, ti, :], self.maskT[:, t0 + ti:t0 + ti + 1], None, ALU.mult, None, [dtS, self.maskT], [dtS])
                self.tt(aS[:, ti, :], dtS[:, ti, :], self.negA[:], ALU.mult, [dtS, self.negA], [aS])
            for ti in range(nt):
                ps = self.pq.next()
                self.mm(ps[:, 0:32], self.triu[:], aS[:, ti, :], True, True, [self.triu, aS], [ps])
                self.mm(ps[:, 32:64], self.ones_f[:], aS[:, ti, :], True, True, [self.ones_f, aS], [ps])
                self.act(cumS[:, ti, :], ps[:, 0:32], AF.Copy, [ps], [cumS])
                self.act(lastS[:, ti, :], ps[:, 32:64], AF.Copy, [ps], [lastS])
            self.tt(ddS[:, 0:nt, :], lastS[:, 0:nt, :], cumS[:, 0:nt, :], ALU.subtract, [lastS, cumS], [ddS])
            P.op("dve", lambda e, tot=tot: e.memset(tot[:, 0, :], 0.0), (), [tot])
            for ti in range(nt - 1, -1, -1):
                self.tt(ddS[:, ti, :], ddS[:, ti, :], tot[:, 0, :], ALU.add, [ddS, tot], [ddS])
                self.tt(tot[:, 0, :], tot[:, 0, :], lastS[:, ti, :], ALU.add, [tot, lastS], [tot])
            self.tt(dec_out[:], dec_out[:], tot[:, 0, :], ALU.add, [dec_out, tot], [dec_out])
            self.act(tot[:, 1, :], tot[:, 0, :], AF.Exp, [tot], [tot])
            self.act(ddS[:, 0:nt, :], ddS[:, 0:nt, :], AF.Exp, [ddS], [ddS])
            self.tt(ddS[:, 0:nt, :], ddS[:, 0:nt, :], dtS[:, 0:nt, :], ALU.mult, [ddS, dtS], [ddS])

        def stage0(un):
            (t0, nt, gi) = un
            NS = nt * 128
            if gi == 0:
                seg_prep(t0, nt)
                yield
            u = seg["u"]
            wt = wts.next()
            self.dma(wt[:, :, 0:256], w_ap[:, C_X + gi * 256:C_X + (gi + 1) * 256].rearrange("(k p) c -> p k c", p=128), [], [wt], eng="pool")
            self.dma(wt[:, :, 256:384], w_ap[:, C_B + gi * 128:C_B + (gi + 1) * 128].rearrange("(k p) c -> p k c", p=128), [], [wt], eng="pool")
            xr = rxr.next()
            seg[("st", t0, gi)] = dict(xr=xr, ps=seg["ps"], tot=seg["tot"])
            for b in range(3):
                for (n0, nn) in [(a, min(512, NS - a)) for a in range(0, NS, 512)]:
                    ps = self.pp.next()
                    for kc in range(KC):
                        self.mm(ps[:, 0:nn], wt[:, kc, b * 128:(b + 1) * 128], u[:, kc, n0:n0 + nn], kc == 0, kc == KC - 1, [wt, u], [ps])
                    self.act(xr[:, b, 3 + n0:3 + n0 + nn], ps[:, 0:nn], AF.Copy, [ps], [xr])
                    yield

        def stage1(un):
            (t0, nt, gi) = un
            NS = nt * 128
            d = seg.pop(("st", t0, gi))
            xr = d["xr"]; ddS = d["ps"]["ddS"]; tot = d["tot"]
            acc = racc.next(); BT = rBT.next(); xdd = rxdd.next(); Btm = rBtm.next()
            for b in range(3):
                j = (gi * 2 + b) if b < 2 else 16 + gi
                self.act(xr[:, b, 0:3], self.carry[:, gi, b, :], AF.Copy, [self.carry], [xr])
                self.act(self.carry[:, gi, b, :], xr[:, b, NS:NS + 3], AF.Copy, [xr], [self.carry])
                tmpc = rtmpc.next()
                self.ts(tmpc[:, 0:NS], xr[:, b, 3:3 + NS], self.cw[3][:, j:j + 1], self.cbias[:, j:j + 1], ALU.mult, ALU.add,
                        [xr, self.cw[3], self.cbias], [tmpc])
                for k_ in range(3):
                    self.stt(tmpc[:, 0:NS], xr[:, b, k_:k_ + NS], self.cw[k_][:, j:j + 1], tmpc[:, 0:NS], ALU.mult, ALU.add,
                             [xr, self.cw[k_], tmpc], [tmpc])
                if b < 2:
                    self.act(acc[:, b, 0:NS], tmpc[:, 0:NS], AF.Silu, [tmpc], [acc])
                else:
                    self.act(BT[:, 0:NS], tmpc[:, 0:NS], AF.Silu, [tmpc], [BT])
                yield
            hs = slice(gi * 4, gi * 4 + 4)
            for ti in range(nt):
                tk = slice(ti * 128, (ti + 1) * 128)
                px = self.pq.next()
                self.tr(px[:, 0:128], acc[:, 0, tk], self.idf[:], [acc, self.idf], [px])
                self.tr(px[:, 128:256], acc[:, 1, tk], self.idf[:], [acc, self.idf], [px])
                pb = self.pt.next()
                self.tr(pb[:, 0:128], BT[:, tk], self.idb[:], [BT, self.idb], [pb])
                xs = rxs.next()
                self.act(xs[:], px[:, 0:256], AF.Copy, [px], [xs])
                self.act(Btm[:, ti, :], pb[:, 0:128], AF.Copy, [pb], [Btm])
                self.tt(xdd[:, ti, :].rearrange("p (h q) -> p h q", q=64), xs[:].rearrange("p (h q) -> p h q", q=64),
                        ddS[:, ti, hs].unsqueeze(2).to_broadcast([128, 4, 64]), ALU.mult, [xs, ddS], [xdd])
                yield
            pS = self.pq.next()
            for ti in range(nt):
                self.mm(pS[:, 0:256], Btm[:, ti, :], xdd[:, ti, :], ti == 0, ti == nt - 1, [Btm, xdd], [pS])
            Sg = S[:, gi * 4:(gi + 1) * 4, :]
            tS = rtS.next()
            self.tt(tS[:], Sg, tot[:, 1, hs].unsqueeze(2).to_broadcast([128, 4, 64]), ALU.mult, [S, tot], [tS])
            self.tt(Sg, tS[:], pS[:, 0:256].rearrange("p (h q) -> p h q", q=64), ALU.add, [tS, pS], [S])
            yield

        units = [(t0, nt, gi) for (t0, nt) in self.make_segs(MT) for gi in range(M2_G)]
        self.pipeline(units, stage0, stage1)

    def tok_chunks(self, n0, n1):
        return [(a, min(512, n1 - a)) for a in range(n0, n1, 512)]

    def dense_A_res(self, xT_d, KCx, W_ap, res_ap, res_r, out_d, seg_tiles):
        P = self.P
        NTs = seg_tiles
        xs_ = Ring([P.sb([128, KCx, NTs * 128], BF16, "dax") for _ in range(1)])
        wr = Ring([P.sb([128, KCx, 512], BF16, "daw") for _ in range(2)])
        rr = Ring([P.sb([128, 512], F32, "dar") for _ in range(3)])
        orr = Ring([P.sb([128, 512], F32, "dao") for _ in range(3)])
        t = 0
        while t < self.NT:
            nt = min(NTs, self.NT - t)
            x = xs_.next()
            self.dma(x[:, :, 0:nt * 128], xT_d[:, :, t * 128:(t + nt) * 128].rearrange("k p t -> p k t"), [xT_d], [x])
            for cb in range(4):
                w = wr.next()
                self.dma(w[:], W_ap[:, cb * 512:(cb + 1) * 512].rearrange("(k p) c -> p k c", p=128), [], [w], eng="pool")
                for ti in range(nt):
                    r = rr.next(); o = orr.next()
                    tok = slice((t + ti) * 128, (t + ti + 1) * 128)
                    self.dma(r[:], res_ap[tok, cb * 512:(cb + 1) * 512], [res_r], [r])
                    ps = self.pp.next()
                    for kc in range(KCx):
                        self.mm(ps[:], x[:, kc, ti * 128:(ti + 1) * 128], w[:, kc, :], kc == 0, kc == KCx - 1, [x, w], [ps])
                    self.tt(o[:], ps[:], r[:], ALU.add, [ps, r], [o])
                    self.dma(out_d[tok, cb * 512:(cb + 1) * 512], o[:], [o], [out_d])
            t += nt

    def dense_A_res_norm(self, xT_d, W_ap, res_ap, res_r, out_d, g_ap, uT_out_d):
        P = self.P
        x = self.load_xT(xT_d, "dnx")
        w = P.sb([128, KC, D], BF16, "dnw")
        for cb in range(4):
            self.dma(w[:, :, cb * 512:(cb + 1) * 512], W_ap[:, cb * 512:(cb + 1) * 512].rearrange("(k p) c -> p k c", p=128), [], [w], eng="pool")
        gbc = P.sb([128, D], F32, "dngbc")
        self.dma(gbc[:], g_ap.partition_broadcast(128), [], [gbc])
        rr = Ring([P.sb([128, 512], F32, "dnr") for _ in range(3)])
        ro = Ring([P.sb([128, D], F32, "dno") for _ in range(2)])
        ru = Ring([P.sb([128, D], BF16, "dnu") for _ in range(2)])
        NB_ = 2
        ruo = Ring([P.sb([128, KC, NB_ * 128], BF16, "dnuo") for _ in range(2)])
        cur = {}
        rs_ = Ring([P.sb([128, 4], F32, "dns") for _ in range(2)])

        def norm_tail(i, u):
            if i % NB_ == 0:
                cur["o"] = ruo.next()
            o_ = cur["o"]
            q4 = (i % NB_) * 128
            for half in range(2):
                pt = self.pt.next()
                for j in range(8):
                    kc = half * 8 + j
                    self.tr(pt[:, j * 128:(j + 1) * 128], u[:, kc * 128:(kc + 1) * 128], self.idb[:], [u, self.idb], [pt])
                self.act(o_[:, half * 8:(half + 1) * 8, q4:q4 + 128], pt[:].rearrange("p (j t) -> p j t", j=8), AF.Copy, [pt], [o_])
            if i % NB_ == NB_ - 1 or i == self.NT - 1:
                i0_ = i - (i % NB_)
                self.dma(uT_out_d[:, :, i0_ * 128:(i + 1) * 128].rearrange("k p t -> p k t"), o_[:, :, 0:q4 + 128], [o_], [uT_out_d])

        pend = None
        for i in range(self.NT):
            tok = slice(i * 128, (i + 1) * 128)
            o = ro.next(); u = ru.next(); s_ = rs_.next()
            for cb in range(4):
                r = rr.next()
                self.dma(r[:], res_ap[tok, cb * 512:(cb + 1) * 512], [res_r], [r])
                ps = self.pp.next()
                for kc in range(KC):
                    self.mm(ps[:], x[:, kc, tok], w[:, kc, cb * 512:(cb + 1) * 512], kc == 0, kc == KC - 1, [x, w], [ps])
                self.tt(o[:, cb * 512:(cb + 1) * 512], ps[:], r[:], ALU.add, [ps, r], [o])
            self.dma(out_d[tok, :], o[:], [o], [out_d])
            self.act(u[:], o[:], AF.Square, [o], [u, s_], accum_out=s_[:, 0:1])
            self.act(s_[:, 1:2], s_[:, 0:1], AF.Ln, [s_, self.epsc], [s_], scale=1.0 / D, bias=self.epsc[:, 0:1])
            self.act(s_[:, 2:3], s_[:, 1:2], AF.Exp, [s_], [s_], scale=-0.5)
            self.stt(u[:], o[:], s_[:, 2:3], gbc[:], ALU.mult, ALU.mult, [o, s_, gbc], [u])
            if pend is not None:
                norm_tail(*pend)
            pend = (i, u)
        norm_tail(*pend)

    def dense_B(self, x, W_ap, c0, ncols, evac, ntok=None):
        P = self.P
        wr = self._dbw
        ntok = self.TT if ntok is None else ntok
        for c in range(0, ncols, 512):
            wc = min(512, ncols - c)
            w = wr.next()
            self.dma(w[:, :, 0:wc], W_ap[:, c0 + c:c0 + c + wc].rearrange("(k p) c -> p k c", p=128), [], [w], eng="pool")
            for b in range(wc // 128):
                for (n0, nn) in self.tok_chunks(0, ntok):
                    ps = self.pp.next()
                    for kc in range(KC):
                        self.mm(ps[:, 0:nn], w[:, kc, b * 128:(b + 1) * 128], x[:, kc, n0:n0 + nn], kc == 0, kc == KC - 1, [w, x], [ps])
                    evac(ps, (c // 128) + b, n0, nn)

    def halves(self):
        a = ((self.NT + 1) // 2) * 128
        return [(0, a), (a, self.TT)] if a < self.TT else [(0, self.TT)]

    def load_xT(self, xT_d, name):
        x = self.P.sb([128, KC, self.TT], BF16, name)
        self.dma(x[:], xT_d[:, :, :].rearrange("k p t -> p k t"), [xT_d], [x])
        return x

    def mixer_merge(self, uT_d, hgT_d, m2T_d, w_ap, wbh_ap, wbm_ap, mT_d):
        P = self.P
        hv = self.halves()
        NSM = max(b - a for a, b in hv)
        ru = Ring([P.sb([128, KC, NSM], BF16, "mu") for _ in range(1)])
        rh = Ring([P.sb([128, KC, NSM], BF16, "mh") for _ in range(1)])
        rm = Ring([P.sb([128, KC, NSM], BF16, "mm") for _ in range(1)])
        rw = Ring([P.sb([128, KC, 4, 128], BF16, "mw") for _ in range(2)])
        rg = Ring([P.sb([128, 2, 512], F32, "mg") for _ in range(2)])
        rt = Ring([P.sb([128, 2, 512], F32, "mt") for _ in range(2)])
        ro = Ring([P.sb([128, NSM], BF16, "mo") for _ in range(2)])
        for (h0, h1) in hv:
            nh = h1 - h0
            u = ru.next(); hg = rh.next(); m2 = rm.next()
            for (dst, src) in ((u, uT_d), (hg, hgT_d), (m2, m2T_d)):
                self.dma(dst[:, :, 0:nh], src[:, :, h0:h1].rearrange("k p t -> p k t"), [src], [dst])
            for kb in range(16):
                c = kb * 128
                w = rw.next()
                for mi, (ap, cc) in enumerate(((w_ap, C_GHG + c), (wbh_ap, c), (w_ap, C_GM2 + c), (wbm_ap, c))):
                    self.dma(w[:, :, mi, :], ap[:, cc:cc + 128].rearrange("(k p) c -> p k c", p=128), [], [w], eng="pool")
                o = ro.next()
                for (n0, nn) in self.tok_chunks(0, nh):
                    g = rg.next(); t_ = rt.next()
                    for half, (xa, xb) in enumerate(((u, hg), (u, m2))):
                        pg = self.pp.next()
                        for kc in range(KC):
                            self.mm(pg[:, 0:nn], w[:, kc, 2 * half, :], xa[:, kc, n0:n0 + nn], kc == 0, kc == KC - 1, [w, xa], [pg])
                        self.act(g[:, half, 0:nn], pg[:, 0:nn], AF.Sigmoid, [pg], [g])
                        py_ = self.pq.next()
                        for kc in range(KC):
                            self.mm(py_[:, 0:nn], w[:, kc, 2 * half + 1, :], xb[:, kc, n0:n0 + nn], kc == 0, kc == KC - 1, [w, xb], [py_])
                        self.tt(t_[:, half, 0:nn], py_[:, 0:nn], g[:, half, 0:nn], ALU.mult, [py_, g], [t_])
                    self.tt(o[:, n0:n0 + nn], t_[:, 0, 0:nn], t_[:, 1, 0:nn], ALU.add, [t_], [o])
                self.dma(mT_d[kb, :, h0:h1], o[:, 0:nh], [o], [mT_d])

    def xattn(self, uT_d, memnT_d, wq_ap, wkv_ap, oT_d):
        P = self.P
        qT = P.sb([128, KC, self.TT], BF16, "xa_q")
        kT = P.sb([128, KC, 256], BF16, "xa_k")
        v = P.sb([128, 2, 2048], BF16, "xa_v")
        sc = 512 ** -0.5
        with P.scope():
            self._dbw = Ring([P.sb([128, KC, 512], BF16, "dbw") for _ in range(2)])
            mn = P.sb([128, KC, 256], BF16, "xa_mn")
            self.dma(mn[:], memnT_d[:, :, :].rearrange("k p t -> p k t"), [memnT_d], [mn])
            for c in range(0, 2048, 512):
                w = self._dbw.next()
                self.dma(w[:], wkv_ap[:, c:c + 512].rearrange("(k p) c -> p k c", p=128), [], [w], eng="pool")
                for b in range(4):
                    ps = self.pp.next()
                    for kc in range(KC):
                        self.mm(ps[:, 0:256], w[:, kc, b * 128:(b + 1) * 128], mn[:, kc, :], kc == 0, kc == KC - 1, [w, mn], [ps])
                    self.act(kT[:, c // 128 + b, :], ps[:, 0:256], AF.Copy, [ps], [kT])
            for c in range(0, 2048, 512):
                w = self._dbw.next()
                self.dma(w[:], wkv_ap[:, 2048 + c:2048 + c + 512].rearrange("(k p) c -> p k c", p=128), [], [w], eng="pool")
                for mb in range(2):
                    ps = self.pp.next()
                    for kc in range(KC):
                        self.mm(ps[:], mn[:, kc, mb * 128:(mb + 1) * 128], w[:, kc, :], kc == 0, kc == KC - 1, [w, mn], [ps])
                    self.act(v[:, mb, c:c + 512], ps[:], AF.Copy, [ps], [v])
            hv = self.halves()
            xr_ = Ring([P.sb([128, KC, max(b - a for a, b in hv)], BF16, "xa_u") for _ in range(1)])
            for (h0, h1) in hv:
                x = xr_.next()
                self.dma(x[:, :, 0:h1 - h0], uT_d[:, :, h0:h1].rearrange("k p t -> p k t"), [uT_d], [x])

                def evq(ps, cb, n0, nn, h0=h0):
                    self.act(qT[:, cb, h0 + n0:h0 + n0 + nn], ps[:, 0:nn], AF.Copy, [ps], [qT], scale=sc)
                self.dense_B(x, wq_ap, 0, 2048, evq, ntok=h1 - h0)
        rPT = Ring([P.sb([128, 2, 512], BF16, "xa_pt") for _ in range(2)])
        rden = Ring([P.sb([128, 512], F32, "xa_den") for _ in range(2)])
        ro = Ring([P.sb([128, 4, 512], BF16, "xa_o") for _ in range(2)])
        for hh in range(4):
            for (n0, nn) in self.tok_chunks(0, self.TT):
                PT = rPT.next()
                for mb in range(2):
                    ps = self.pp.next()
                    for dc in range(4):
                        self.mm(ps[:, 0:nn], kT[:, hh * 4 + dc, mb * 128:(mb + 1) * 128], qT[:, hh * 4 + dc, n0:n0 + nn], dc == 0, dc == 3, [kT, qT], [ps])
                    self.act(PT[:, mb, 0:nn], ps[:, 0:nn], AF.Exp, [ps], [PT])
                psd = self.pp.next()
                for mb in range(2):
                    self.mm(psd[:, 0:nn], self.ones_b[:], PT[:, mb, 0:nn], mb == 0, mb == 1, [self.ones_b, PT], [psd])
                den = rden.next()
                P.op("dve", lambda e, den=den, psd=psd, nn=nn: e.reciprocal(out=den[:, 0:nn], in_=psd[:, 0:nn]), [psd], [den])
                o = ro.next()
                for dc in range(4):
                    ps = self.pp.next()
                    for mb in range(2):
                        self.mm(ps[:, 0:nn], v[:, mb, hh * 512 + dc * 128:hh * 512 + (dc + 1) * 128], PT[:, mb, 0:nn], mb == 0, mb == 1, [v, PT], [ps])
                    self.tt(o[:, dc, 0:nn], ps[:, 0:nn], den[:, 0:nn], ALU.mult, [ps, den], [o])
                self.dma(oT_d[hh * 4:hh * 4 + 4, :, n0:n0 + nn].rearrange("j p t -> p j t"), o[:, :, 0:nn], [o], [oT_d])

    def ffn_up(self, uT_d, wup_ap, cw_ap, cb_ap, maskF_ap, aT_d):
        P = self.P
        TT = self.TT
        x = self.load_xT(uT_d, "ff_u")
        cw = [self.load_cols(cw_ap[k:k + 1, :].rearrange("o (j p) -> (o j) p", p=128), 44, f"fcw{k}") for k in range(3)]
        cbs = self.load_cols(cb_ap.rearrange("o (j p) -> (o j) p", p=128), 44, "fcb")
        mF = P.sb([128, 128], F32, "maskF")
        self.dma(mF[:], maskF_ap.partition_broadcast(128), [], [mF])
        rw = Ring([P.sb([128, KC, 2, 256], BF16, "fw") for _ in range(2)])
        rgr = Ring([P.sb([128, 2 + TT], F32, "fgr") for _ in range(2)])
        rup = Ring([P.sb([128, TT], F32, "fup") for _ in range(2)])
        rac = Ring([P.sb([128, TT], F32, "fac") for _ in range(2)])
        rao = Ring([P.sb([128, TT], BF16, "fao") for _ in range(2)])
        for jp in range(0, 44, 2):
            w = rw.next()
            self.dma(w[:, :, 0, :], wup_ap[:, jp * 128:jp * 128 + 256].rearrange("(k p) c -> p k c", p=128), [], [w], eng="pool")
            self.dma(w[:, :, 1, :], wup_ap[:, D_FF + jp * 128:D_FF + jp * 128 + 256].rearrange("(k p) c -> p k c", p=128), [], [w], eng="pool")
            for b in range(2):
                j = jp + b
                gr = rgr.next(); up = rup.next(); ac = rac.next(); ao = rao.next()
                P.op("dve", lambda e, gr=gr: e.memset(gr[:, 0:2], 0.0), (), [gr])
                for (n0, nn) in self.tok_chunks(0, TT):
                    ps = self.pp.next()
                    for kc in range(KC):
                        self.mm(ps[:, 0:nn], w[:, kc, 0, b * 128:(b + 1) * 128], x[:, kc, n0:n0 + nn], kc == 0, kc == KC - 1, [w, x], [ps])
                    self.act(gr[:, 2 + n0:2 + n0 + nn], ps[:, 0:nn], AF.Copy, [ps], [gr])
                    ps2 = self.pp.next()
                    for kc in range(KC):
                        self.mm(ps2[:, 0:nn], w[:, kc, 1, b * 128:(b + 1) * 128], x[:, kc, n0:n0 + nn], kc == 0, kc == KC - 1, [w, x], [ps2])
                    self.act(up[:, n0:n0 + nn], ps2[:, 0:nn], AF.Copy, [ps2], [up])
                self.tt(gr[:, 2:130], gr[:, 2:130], mF[:], ALU.mult, [gr, mF], [gr])
                self.ts(ac[:], gr[:, 2:2 + TT], cw[2][:, j:j + 1], cbs[:, j:j + 1], ALU.mult, ALU.add, [gr, cw[2], cbs], [ac])
                for k in range(2):
                    self.stt(ac[:], gr[:, k:k + TT], cw[k][:, j:j + 1], ac[:], ALU.mult, ALU.add, [gr, cw[k], ac], [ac])
                self.act(ac[:], ac[:], AF.Gelu, [ac], [ac])
                self.tt(ao[:], ac[:], up[:], ALU.mult, [ac, up], [ao])
                self.dma(aT_d[j, :, :], ao[:], [ao], [aT_d])

    def final_norm(self, h_ap, hres, g_ap, out_ap, out_r, tiles):
        P = self.P
        gbc = P.sb([128, D], F32, "fgbc")
        self.dma(gbc[:], g_ap.partition_broadcast(128), [], [gbc])
        hb = Ring([P.sb([128, D], F32, "fhb") for _ in range(2)])
        ob = Ring([P.sb([128, D], F32, "fob") for _ in range(2)])
        junk = P.sb([128, D], BF16, "fjunk")
        stt_ = Ring([P.sb([128, 4], F32, "fst") for _ in range(2)])
        outs = []
        for i in tiles:
            h = hb.next(); o = ob.next(); s = stt_.next()
            self.dma(h[:], h_ap[i * 128:(i + 1) * 128, :], [hres], [h])
            self.act(junk[:], h[:], AF.Square, [h], [junk, s], accum_out=s[:, 0:1])
            self.act(s[:, 1:2], s[:, 0:1], AF.Ln, [s, self.epsc], [s], scale=1.0 / D, bias=self.epsc[:, 0:1])
            self.act(s[:, 2:3], s[:, 1:2], AF.Exp, [s], [s], scale=-0.5)
            self.stt(o[:], h[:], s[:, 2:3], gbc[:], ALU.mult, ALU.mult, [h, s, gbc], [o])
            outs.append(self.dma(out_ap[i * 128:(i + 1) * 128, :], o[:], [o], [out_r]))
        return outs


def _inp(nc, name, shape):
    return nc.dram_tensor(name, list(shape), F32, kind="ExternalInput").ap()


def _outp(nc, name, shape):
    return nc.dram_tensor(name, list(shape), F32, kind="ExternalOutput").ap()


def build_state(l, NT):
    nc = bass.Bass("TRN2", target_bir_lowering=False)
    TT = NT * 128
    h = _inp(nc, "h", [TT, D]); mask = _inp(nc, "mask", [1, TT]); g = _inp(nc, "mix_g", [1, D])
    w_in = _inp(nc, "w_in", [D, N_IN]); lbl = _inp(nc, "lbl", [2, 2048])
    cw = _inp(nc, "m2_cw", [4, 4096]); cb = _inp(nc, "m2_cb", [1, 4096])
    dtb = _inp(nc, "m2_dtb", [1, 32]); alog = _inp(nc, "m2_alog", [1, 32])
    o_shg = _outp(nc, "o_shg", [128, 16, 128]); o_dhg = _outp(nc, "o_dhg", [128, 16])
    o_sm2 = _outp(nc, "o_sm2", [128, 32, 64]); o_dm2 = _outp(nc, "o_dm2", [128, 32])
    k = K(nc, NT); P = k.P
    uT_d = P.dram([KC, 128, TT], BF16, "uT")
    with P.scope():
        k.norm_to_uT(h, T(h, Res()), g, uT_d)
    k.hg_setup_lb(lbl, l)
    Shg = P.sb([128, 16, 128], F32, "Shg"); dhg = P.sb([128, 16], F32, "dhg")
    P.op("dve", lambda e: e.memset(Shg[:], 0.0), (), [Shg])
    P.op("dve", lambda e: e.memset(dhg[:], 0.0), (), [dhg])
    with P.scope():
        k.hg_state(uT_d, w_in, mask, Shg, dhg)
    outs = [k.dma(o_shg, Shg[:], [Shg], [T(o_shg, Res())]), k.dma(o_dhg, dhg[:], [dhg], [T(o_dhg, Res())])]
    k.m2_setup(w_in, cw, cb, dtb, alog, None, None, mask)
    Sm2 = P.sb([128, 32, 64], F32, "Sm2"); Sbf = P.sb([128, 32, 64], BF16, "Sm2b"); dm2 = P.sb([128, 32], F32, "dm2")
    P.op("dve", lambda e: e.memset(Sm2[:], 0.0), (), [Sm2])
    P.op("dve", lambda e: e.memset(dm2[:], 0.0), (), [dm2])
    with P.scope():
        k.m2_state(uT_d, w_in, Sm2, dm2)
    outs += [k.dma(o_sm2, Sm2[:], [Sm2], [T(o_sm2, Res())]), k.dma(o_dm2, dm2[:], [dm2], [T(o_dm2, Res())])]
    P.emit(outs)
    return nc


def build_main(l, NT, last):
    nc = bass.Bass("TRN2", target_bir_lowering=False)
    TT = NT * 128
    h = _inp(nc, "h", [TT, D]); mask = _inp(nc, "mask", [1, TT]); maskF = _inp(nc, "maskF", [1, 128])
    g = _inp(nc, "mix_g", [1, D]); w_in = _inp(nc, "w_in", [D, N_IN]); lbl = _inp(nc, "lbl", [2, 2048])
    hgng = _inp(nc, "hg_ng", [1, 2048])
    cw = _inp(nc, "m2_cw", [4, 4096]); cb = _inp(nc, "m2_cb", [1, 4096])
    dtb = _inp(nc, "m2_dtb", [1, 32]); alog = _inp(nc, "m2_alog", [1, 32]); dsk = _inp(nc, "m2_dsk", [1, 32])
    m2ng = _inp(nc, "m2_ng", [1, 2048])
    wbh = _inp(nc, "w_bhg", [2048, D]); wbm = _inp(nc, "w_bm2", [2048, D]); wout = _inp(nc, "w_out", [D, D])
    mem = _inp(nc, "mem", [N_MEM, D]); memg = _inp(nc, "mem_g", [1, D]); xag = _inp(nc, "xa_g", [1, D])
    wq = _inp(nc, "xa_wq", [D, D]); wkv = _inp(nc, "xa_wkv", [D, 2 * D]); wo = _inp(nc, "xa_wo", [D, D])
    ffg = _inp(nc, "ffn_g", [1, D]); wup = _inp(nc, "ffn_wup", [D, 2 * D_FF])
    fcw = _inp(nc, "ffn_cw", [3, D_FF]); fcb = _inp(nc, "ffn_cb", [1, D_FF]); wdn = _inp(nc, "ffn_wdn", [D_FF, D])
    ps_hg = _inp(nc, "ps_hg", [7, 128, 16, 128]); pd_hg = _inp(nc, "pd_hg", [7, 128, 16])
    ps_m2 = _inp(nc, "ps_m2", [7, 128, 32, 64]); pd_m2 = _inp(nc, "pd_m2", [7, 128, 32])
    if last:
        fing = _inp(nc, "fin_g", [1, D])
    h_out = _outp(nc, "h_out", [TT, D])
    k = K(nc, NT); P = k.P
    hin = T(h, Res())
    uT_d = P.dram([KC, 128, TT], BF16, "uT")
    hgT_d = P.dram([KC, 128, TT], BF16, "hgT")
    m2T_d = P.dram([KC, 128, TT], BF16, "m2T")
    mT_d = P.dram([KC, 128, TT], BF16, "mT")
    h1_d = P.dram([TT, D], F32, "h1")
    h2_d = P.dram([TT, D], F32, "h2")
    h3_d = T(h_out, Res()) if not last else P.dram([TT, D], F32, "h3")
    memT_d = P.dram([KC, 128, N_MEM], BF16, "memT")
    oT_d = P.dram([KC, 128, TT], BF16, "oT")
    aT_d = P.dram([D_FF // 128, 128, TT], BF16, "aT")
    with P.scope():
        k.norm_to_uT(h, hin, g, uT_d)
    with P.scope():
        k.hg_setup(lbl, l, hgng)
        Shg = P.sb([128, 16, 128], F32, "Shg")
        P.op("dve", lambda e: e.memset(Shg[:], 0.0), (), [Shg])
        Sm2 = P.sb([128, 32, 64], F32, "Sm2"); Sbf = P.sb([128, 32, 64], BF16, "Sm2b")
        P.op("dve", lambda e: e.memset(Sm2[:], 0.0), (), [Sm2])
        with P.scope():
            t1 = Ring([P.sb([128, 16, 128], F32, "pst") for _ in range(2)])
            d1 = Ring([P.sb([128, 16], F32, "pdt") for _ in range(2)])
            t2 = Ring([P.sb([128, 32, 64], F32, "pst2") for _ in range(2)])
            d2 = Ring([P.sb([128, 32], F32, "pdt2") for _ in range(2)])
            for j in range(7):
                a = t1.next(); b = d1.next(); c = t2.next(); d_ = d2.next()
                k.dma(a[:], ps_hg[j], [], [a]); k.dma(b[:], pd_hg[j], [], [b])
                k.dma(c[:], ps_m2[j], [], [c]); k.dma(d_[:], pd_m2[j], [], [d_])
                k.act(b[:], b[:], AF.Exp, [b], [b]); k.act(d_[:], d_[:], AF.Exp, [d_], [d_])
                k.tt(Shg[:], Shg[:], b[:].unsqueeze(2).to_broadcast([128, 16, 128]), ALU.mult, [Shg, b], [Shg])
                k.tt(Shg[:], Shg[:], a[:], ALU.add, [Shg, a], [Shg])
                k.tt(Sm2[:], Sm2[:], d_[:].unsqueeze(2).to_broadcast([128, 32, 64]), ALU.mult, [Sm2, d_], [Sm2])
                k.tt(Sm2[:], Sm2[:], c[:], ALU.add, [Sm2, c], [Sm2])
        k.act(Sbf[:], Sm2[:], AF.Copy, [Sm2], [Sbf])
        with P.scope():
            k.hg_mixer(uT_d, w_in, mask, Shg, hgT_d)
        k.m2_setup(w_in, cw, cb, dtb, alog, dsk, m2ng, mask)
        with P.scope():
            k.m2_mixer(uT_d, w_in, Sm2, Sbf, m2T_d)
    with P.scope():
        k.mixer_merge(uT_d, hgT_d, m2T_d, w_in, wbh, wbm, mT_d)
    u3_d = P.dram([KC, 128, TT], BF16, "u3T")
    with P.scope():
        k.dense_A_res_norm(mT_d, wout, h, hin, h1_d, xag, u3_d)
    with P.scope():
        k.norm_to_uT(mem, T(mem, Res()), memg, memT_d, tiles=range(N_MEM // 128))
    with P.scope():
        k.xattn(u3_d, memT_d, wq, wkv, oT_d)
    u4_d = P.dram([KC, 128, TT], BF16, "u4T")
    with P.scope():
        k.dense_A_res_norm(oT_d, wo, h1_d[:, :], h1_d, h2_d, ffg, u4_d)
    with P.scope():
        k.ffn_up(u4_d, wup, fcw, fcb, maskF, aT_d)
    with P.scope():
        k.dense_A_res(aT_d, D_FF // 128, wdn, h2_d[:, :], h2_d, h3_d, 6)
    if last:
        with P.scope():
            outs = k.final_norm(h3_d[:, :], h3_d, fing, h_out, T(h_out, Res()), range(NT))
    else:
        outs = [h3_d.r.last_write]
    stats = P.emit(outs)
    return nc, stats


_PROG_CACHE = {}


def _prog(kind, l, NT, last=False):
    key = (kind, l, NT, last)
    if key not in _PROG_CACHE:
        if kind == "state":
            _PROG_CACHE[key] = build_state(l, NT)
        else:
            _PROG_CACHE[key] = build_main(l, NT, last)[0]
    return _PROG_CACHE[key]


def _halo_slices(hfull, n_cores, TOWN):
    out = []
    for c in range(n_cores):
        s = c * TOWN
        if c == 0:
            blk = np.concatenate([np.zeros((128, hfull.shape[1]), np.float32), hfull[0:TOWN]], 0)
        else:
            blk = hfull[s - 128:s + TOWN]
        out.append(np.ascontiguousarray(blk, dtype=np.float32))
    return out


def kernel_impl(inputs, n_cores=8):
    f32 = np.float32
    x = np.asarray(inputs["x"], f32)
    SEQ = x.shape[1]
    TOWN = SEQ // n_cores
    NT = TOWN // 128 + 1
    TT = NT * 128
    mem = np.ascontiguousarray(np.asarray(inputs["mem"], f32)[0])
    g = lambda k: np.asarray(inputs[k], f32)
    row = lambda a: np.ascontiguousarray(a.reshape(1, -1))
    cores = list(range(n_cores))
    m_state, m_main, m_f = [], [], []
    for c in cores:
        ms = np.zeros((1, TT), f32); mm_ = np.zeros((1, TT), f32)
        lo = 128 if c == 0 else 3
        ms[0, lo:TOWN + 3] = 1.0
        mm_[0, lo:] = 1.0
        m_state.append(ms); m_main.append(mm_)
        m_f.append(np.zeros((1, 128), f32) if c == 0 else np.ones((1, 128), f32))
    hfull = np.ascontiguousarray(x[0])
    for l in range(DEPTH):
        hs = _halo_slices(hfull, n_cores, TOWN)
        common_state = dict(mix_g=row(g("mix_norm_g")[l]), w_in=np.ascontiguousarray(g("w_in")[l]), lbl=np.ascontiguousarray(g("hg_lb_logits")),
                            m2_cw=np.ascontiguousarray(g("m2_conv_w")[l]), m2_cb=row(g("m2_conv_b")[l]),
                            m2_dtb=row(g("m2_dt_bias")[l]), m2_alog=row(g("m2_A_log")[l]))
        nc = _prog("state", l, NT)
        res = run_bass_kernel_spmd(nc, [dict(common_state, h=hs[c], mask=m_state[c]) for c in cores], core_ids=cores)
        st = res.results
        last = (l == DEPTH - 1)
        common = dict(common_state, hg_ng=row(g("hg_norm_g")[l]), m2_dsk=row(g("m2_D")[l]), m2_ng=row(g("m2_norm_g")[l]),
                      w_bhg=np.ascontiguousarray(g("w_branch_hg")[l]), w_bm2=np.ascontiguousarray(g("w_branch_m2")[l]),
                      w_out=np.ascontiguousarray(g("w_out")[l]), mem=mem, mem_g=row(g("mem_norm_g")), xa_g=row(g("xa_norm_g")[l]),
                      xa_wq=np.ascontiguousarray(g("xa_wq")[l]), xa_wkv=np.ascontiguousarray(g("xa_wkv")[l]),
                      xa_wo=np.ascontiguousarray(g("xa_wo")[l]), ffn_g=row(g("ffn_norm_g")[l]),
                      ffn_wup=np.ascontiguousarray(g("ffn_w_up")[l]), ffn_cw=np.ascontiguousarray(g("ffn_conv_w")[l]),
                      ffn_cb=row(g("ffn_conv_b")[l]), ffn_wdn=np.ascontiguousarray(g("ffn_w_down")[l]))
        if last:
            common["fin_g"] = row(g("final_norm_g"))
        maps = []
        for c in cores:
            ps_hg = np.zeros((7, 128, 16, 128), f32); pd_hg = np.zeros((7, 128, 16), f32)
            ps_m2 = np.zeros((7, 128, 32, 64), f32); pd_m2 = np.zeros((7, 128, 32), f32)
            for j in range(7):
                src = c - 7 + j
                if src >= 0:
                    ps_hg[j] = st[src]["o_shg"]; pd_hg[j] = st[src]["o_dhg"]
                    ps_m2[j] = st[src]["o_sm2"]; pd_m2[j] = st[src]["o_dm2"]
            maps.append(dict(common, h=hs[c], mask=m_main[c], maskF=m_f[c], ps_hg=ps_hg, pd_hg=pd_hg, ps_m2=ps_m2, pd_m2=pd_m2))
        nc = _prog("main", l, NT, last)
        res = run_bass_kernel_spmd(nc, maps, core_ids=cores)
        hfull = np.concatenate([np.asarray(r["h_out"])[128:] for r in res.results], 0)
    return np.ascontiguousarray(hfull.reshape(1, SEQ, D).astype(np.float32))


def kernel(**inputs):
    return kernel_impl(inputs, 8)
```

```python
import contextlib
import numpy as np
import concourse.bass as bass
import concourse.mybir as mybir
from concourse.bass_utils import run_bass_kernel_spmd

F32 = mybir.dt.float32
BF16 = mybir.dt.bfloat16
AF = mybir.ActivationFunctionType
ALU = mybir.AluOpType
AX = mybir.AxisListType

D = 2048
KC = D // 128
DEPTH = 2
N_MEM = 256
HG_H = 16
M2_H = 32
M2_G = 8
D_FF = 5632
N_IN = 18464
C_Q, C_F, C_I, C_OG = 0, 2048, 4096, 6144
C_Z = 8192
C_X = 10240
C_B = C_X + 2048
C_C = C_B + 1024
C_DT = 14336
C_GHG = 14368
C_GM2 = 16416
EPS = 1e-6
M2_EPS = 1e-5

ENGS = ["pe", "act", "dve", "pool", "sp"]
DMA_WINDOW = 8


class Res:
    __slots__ = ("name", "last_write", "readers")

    def __init__(self, name=""):
        self.name = name
        self.last_write = None
        self.readers = []


class Op:
    __slots__ = ("eng", "fn", "deps", "signaled", "is_dma", "dma_idx", "tok", "cc")

    def __init__(self, eng, fn, is_dma=False):
        self.eng = eng
        self.fn = fn
        self.deps = []
        self.signaled = False
        self.is_dma = is_dma
        self.dma_idx = -1
        self.tok = None
        self.cc = False


class T:
    def __init__(self, t, res):
        self.t = t
        self.r = res

    def __getitem__(self, k):
        return self.t[k]


class Prog:
    def __init__(self, nc):
        self.nc = nc
        self.ops = {e: [] for e in ENGS}
        self.dmas = {e: [] for e in ENGS}
        self.stack = contextlib.ExitStack()
        self.scopes = [self.stack]
        self.n = 0
        self.pending = {e: [] for e in ENGS}

    def sb(self, shape, dt, name=None):
        self.n += 1
        name = (name or "sb") + f"_{self.n}"
        t = self.scopes[-1].enter_context(self.nc.sbuf_tensor(name, list(shape), dt))
        return T(t, Res(name))

    @contextlib.contextmanager
    def scope(self):
        st = contextlib.ExitStack()
        self.scopes.append(st)
        try:
            yield
        finally:
            self.scopes.pop()
            self.barrier()
            st.close()

    def ps(self, shape, dt, name=None):
        self.n += 1
        name = (name or "ps") + f"_{self.n}"
        t = self.stack.enter_context(self.nc.psum_tensor(name, list(shape), dt))
        return T(t, Res(name))

    def dram(self, shape, dt, name=None):
        self.n += 1
        name = (name or "dr") + f"_{self.n}"
        t = self.nc.dram_tensor(name, list(shape), dt)
        return T(t.ap(), Res(name))

    def _record(self, o, reads, writes):
        deps = []
        seen = set()

        def add(d):
            if d is None or d is o or id(d) in seen:
                return
            if d.eng == "pe" and o.eng == "pe" and not d.is_dma:
                return
            seen.add(id(d))
            deps.append(d)

        for r in reads:
            add(r.r.last_write)
        for w in writes:
            add(w.r.last_write)
            for rd in w.r.readers:
                add(rd)
        for d in self.pending[o.eng]:
            add(d)
        self.pending[o.eng] = []
        if o.is_dma:
            q = self.dmas[o.eng]
            o.dma_idx = len(q)
            if o.dma_idx >= DMA_WINDOW:
                add(q[o.dma_idx - DMA_WINDOW])
            q.append(o)
        for d in deps:
            d.signaled = True
        o.deps = deps
        for r in reads:
            r.r.readers.append(o)
        for w in writes:
            w.r.last_write = o
            w.r.readers = []
        self.ops[o.eng].append(o)
        return o

    def op(self, eng, fn, reads=(), writes=()):
        return self._record(Op(eng, fn), reads, writes)

    def dma(self, eng, fn, reads=(), writes=()):
        return self._record(Op(eng, fn, is_dma=True), reads, writes)

    def barrier(self):
        tails = []
        for e in ENGS:
            if self.ops[e]:
                tails.append(self.ops[e][-1])
            tails.extend(self.dmas[e][-DMA_WINDOW:])
        for e in ENGS:
            self.pending[e] = list(tails)

    def emit(self, final_deps):
        nc = self.nc
        st = self.stack
        sems = {e: st.enter_context(nc.semaphore(f"s_{e}")) for e in ENGS}
        dsem = {
            e: [st.enter_context(nc.semaphore(f"d_{e}{i}")) for i in range(DMA_WINDOW)]
            for e in ENGS
            if self.dmas[e]
        }
        fin = Op("sp", None)
        fin.deps = list(final_deps)
        for d in fin.deps:
            d.signaled = True
        self.ops["sp"].append(fin)
        ccs = []
        for e in ENGS:
            c = 0
            for o in self.ops[e]:
                if o.is_dma:
                    o.tok = (dsem[e][o.dma_idx % DMA_WINDOW], 16 * (o.dma_idx // DMA_WINDOW + 1))
                elif o.cc:
                    o.tok = (st.enter_context(nc.semaphore(f"cc{len(ccs)}")), 1)
                    ccs.append(o)
                elif o.signaled:
                    c += 1
                    o.tok = (sems[e], c)
        engobj = {"pe": "tensor", "act": "scalar", "dve": "vector", "pool": "gpsimd", "sp": "sync"}
        stats = {}
        with nc.Block() as block:
            for e in ENGS:
                ops = self.ops[e]
                if not ops:
                    continue
                nwait = [0]

                def body(eng, ops=ops, nwait=nwait):
                    waited = {}
                    for o in ops:
                        need = {}
                        for d in o.deps:
                            s, v = d.tok
                            k = id(s)
                            if waited.get(k, 0) < v and need.get(k, (None, 0))[1] < v:
                                need[k] = (s, v)
                        for k, (s, v) in need.items():
                            eng.wait_ge(s, v)
                            waited[k] = v
                            nwait[0] += 1
                        if o.fn is None:
                            continue
                        ins = o.fn(eng)
                        if o.is_dma:
                            ins.then_inc(o.tok[0], 16)
                        elif o.cc:
                            ins.then_inc(o.tok[0], 1)
                        elif o.signaled:
                            ins.then_inc(o.tok[0], 1)

                getattr(block, engobj[e])(body)
                stats[e] = (len(ops), nwait[0])
        return stats


class Ring:
    def __init__(self, items):
        self.items = items
        self.i = 0

    def next(self):
        x = self.items[self.i % len(self.items)]
        self.i += 1
        return x


class K:
    def __init__(self, nc, NT):
        self.nc = nc
        self.P = Prog(nc)
        self.NT = NT
        self.TT = NT * 128
        self.consts()

    def act(self, out, in_, func, reads, writes, **kw):
        return self.P.op("act", lambda e: e.activation(out=out, in_=in_, func=func, **kw), reads, writes)

    def tt(self, out, in0, in1, op, reads, writes, eng="dve"):
        return self.P.op(eng, lambda e: e.tensor_tensor(out=out, in0=in0, in1=in1, op=op), reads, writes)

    def ts(self, out, in0, s1, s2, op0, op1, reads, writes, eng="dve"):
        if op1 is None:
            return self.P.op(eng, lambda e: e.tensor_scalar(out=out, in0=in0, scalar1=s1, scalar2=None, op0=op0), reads, writes)
        return self.P.op(eng, lambda e: e.tensor_scalar(out=out, in0=in0, scalar1=s1, scalar2=s2, op0=op0, op1=op1), reads, writes)

    def stt(self, out, in0, scalar, in1, op0, op1, reads, writes):
        return self.P.op("dve", lambda e: e.scalar_tensor_tensor(out=out, in0=in0, scalar=scalar, in1=in1, op0=op0, op1=op1), reads, writes)

    def mm(self, out, lhsT, rhs, start, stop, reads, writes):
        return self.P.op("pe", lambda e: e.matmul(out, lhsT, rhs, start=start, stop=stop), reads, writes)

    def tr(self, out, in_, ident, reads, writes):
        return self.P.op("pe", lambda e: e.transpose(out, in_, ident), reads, writes)

    def dma(self, out, in_, reads, writes, eng="sp"):
        return self.P.dma(eng, lambda e: e.dma_start(out=out, in_=in_), reads, writes)

    def consts(self):
        P = self.P
        self.idf = P.sb([128, 128], F32, "idf")
        self.idb = P.sb([128, 128], BF16, "idb")
        self.ones_f = P.sb([128, 128], F32, "ones_f")
        self.ones_b = P.sb([128, 128], BF16, "ones_b")
        self.triu = P.sb([128, 128], F32, "triu")
        self.epsc = P.sb([128, 4], F32, "epsc")
        iot = P.sb([128, 128], F32, "iot")
        P.op("pool", lambda e: e.iota(iot[:], [[1, 128]], base=0, channel_multiplier=-1,
                                      allow_small_or_imprecise_dtypes=True), (), [iot])
        self.ts(self.idf[:], iot[:], 0.0, None, ALU.is_equal, None, [iot], [self.idf])
        self.ts(self.idb[:], iot[:], 0.0, None, ALU.is_equal, None, [iot], [self.idb])
        self.ts(self.triu[:], iot[:], 0.0, None, ALU.is_ge, None, [iot], [self.triu])
        P.op("dve", lambda e: e.memset(self.ones_f[:], 1.0), (), [self.ones_f])
        P.op("dve", lambda e: e.memset(self.ones_b[:], 1.0), (), [self.ones_b])
        P.op("dve", lambda e: e.memset(self.epsc[:, 0:1], EPS), (), [self.epsc])
        P.op("dve", lambda e: e.memset(self.epsc[:, 1:2], M2_EPS), (), [self.epsc])
        P.op("dve", lambda e: e.memset(self.epsc[:, 2:3], 1.0), (), [self.epsc])
        self.MAXS = 768
        self.rmask = P.sb([128, self.MAXS], F32, "rmask")
        P.op("dve", lambda e: e.memset(self.rmask[:], 1.0), (), [self.rmask])
        P.op("dve", lambda e: e.memset(self.rmask[:].rearrange("p (c j) -> p c j", j=64)[:, :, 0:1], 0.0), (), [self.rmask])
        self.pp = Ring([P.ps([128, 512], F32, f"pp{i}") for i in range(4)])
        self.pq = Ring([P.ps([128, 512], F32, f"pq{i}") for i in range(2)])
        self.pt = Ring([P.ps([128, 1024], BF16, f"pt{i}") for i in range(2)])

    def load_cols(self, vec_ap, n, name):
        P = self.P
        rows = P.sb([n, 128], F32, name + "_r")
        cols = P.sb([128, n], F32, name)
        self.dma(rows[:], vec_ap, [], [rows])
        ps = self.pq.next()
        self.tr(ps[:, 0:n], rows[:], self.idf[0:n, 0:n], [rows, self.idf], [ps])
        self.act(cols[:], ps[:, 0:n], AF.Copy, [ps], [cols])
        return cols

    def norm_to_uT(self, h_ap, hres, g_ap, uT_d, tiles=None):
        P = self.P
        gbc = P.sb([128, D], F32, "gbc")
        self.dma(gbc[:], g_ap.partition_broadcast(128), [], [gbc])
        hb = Ring([P.sb([128, D], F32, "hb") for _ in range(4)])
        ub = Ring([P.sb([128, D], BF16, "ub") for _ in range(4)])
        junk = P.sb([128, D], BF16, "junk")
        stt_ = Ring([P.sb([128, 4], F32, "nst") for _ in range(4)])
        uo = Ring([P.sb([128, KC, 512], BF16, "uo") for _ in range(2)])
        tl_ = list(tiles if tiles is not None else range(self.NT))
        o = None
        for n_, i in enumerate(tl_):
            h = hb.next(); u = ub.next(); s = stt_.next()
            if n_ % 4 == 0:
                o = uo.next()
            q4 = (n_ % 4) * 128
            self.dma(h[:], h_ap[i * 128:(i + 1) * 128, :], [hres], [h])
            self.act(junk[:], h[:], AF.Square, [h], [junk, s], accum_out=s[:, 0:1])
            self.act(s[:, 1:2], s[:, 0:1], AF.Ln, [s, self.epsc], [s], scale=1.0 / D, bias=self.epsc[:, 0:1])
            self.act(s[:, 2:3], s[:, 1:2], AF.Exp, [s], [s], scale=-0.5)
            self.stt(u[:], h[:], s[:, 2:3], gbc[:], ALU.mult, ALU.mult, [h, s, gbc], [u])
            for half in range(2):
                pt = self.pt.next()
                for j in range(8):
                    kc = half * 8 + j
                    self.tr(pt[:, j * 128:(j + 1) * 128], u[:, kc * 128:(kc + 1) * 128], self.idb[:], [u, self.idb], [pt])
                self.act(o[:, half * 8:(half + 1) * 8, q4:q4 + 128], pt[:].rearrange("p (j t) -> p j t", j=8), AF.Copy, [pt], [o])
            if n_ % 4 == 3 or n_ == len(tl_) - 1:
                i0_ = tl_[n_ - (n_ % 4)]
                self.dma(uT_d[:, :, i0_ * 128:(i + 1) * 128].rearrange("k p t -> p k t"), o[:, :, 0:q4 + 128], [o], [uT_d])

    def hg_setup(self, lbl_ap, l, hgng_ap):
        lg0 = self.load_cols(lbl_ap[0:1, :].rearrange("o (h p) -> (o h) p", p=128), 16, "lg0")
        lg1 = self.load_cols(lbl_ap[1:2, :].rearrange("o (h p) -> (o h) p", p=128), 16, "lg1")
        self.lb = self.P.sb([128, 16], F32, "lb")
        self.oml = self.P.sb([128, 16], F32, "oml")
        if l == 0:
            self.tt(self.lb[:], lg0[:], lg0[:], ALU.subtract, [lg0], [self.lb])
        else:
            self.tt(self.lb[:], lg1[:], lg0[:], ALU.subtract, [lg0, lg1], [self.lb])
            self.act(self.lb[:], self.lb[:], AF.Sigmoid, [self.lb], [self.lb])
        self.ts(self.oml[:], self.lb[:], -1.0, 1.0, ALU.mult, ALU.add, [self.lb], [self.oml])
        if hgng_ap is not None:
            self.hgng = self.load_cols(hgng_ap.rearrange("o (h p) -> (o h) p", p=128), 16, "hgng")

    def hg_setup_lb(self, lbl_ap, l):
        self.hg_setup(lbl_ap, l, None)

    def load_uT_seg(self, uT_d, t0, nt, ring):
        u = ring.next()
        self.dma(u[:, :, 0:nt * 128], uT_d[:, :, t0 * 128:(t0 + nt) * 128].rearrange("k p t -> p k t"), [uT_d], [u])
        return u

    @staticmethod
    def interleave(gens):
        gens = list(gens)
        while gens:
            for g_ in list(gens):
                try:
                    next(g_)
                except StopIteration:
                    gens.remove(g_)

    def pipeline(self, units, *stages):
        units = list(units)
        ns = len(stages)
        for i in range(len(units) + ns - 1):
            gens = []
            for j in range(ns - 1, -1, -1):
                if 0 <= i - j < len(units):
                    gens.append(stages[j](units[i - j]))
            self.interleave(gens)

    def make_segs(self, maxt):
        n = -(-self.NT // maxt)
        base, extra = divmod(self.NT, n)
        segs, t = [], 0
        for i in range(n):
            c = base + (1 if i < extra else 0)
            segs.append((t, c)); t += c
        return segs

    def hg_mixer(self, uT_d, w_ap, mask_ap, S, hgT_d, state_only=False, dec_out=None, impA=None, impI=None):
        P = self.P
        MAXS = self.MAXS
        MC = MAXS // 64
        so = state_only
        useg = Ring([P.sb([128, KC, MAXS], BF16, "useg") for _ in range(1)])
        wts = Ring([P.sb([128, KC, 4, 128], BF16, "hgw") for _ in range(2)])
        mk = Ring([P.sb([128, MAXS], F32, "mk") for _ in range(2)])

        def ring(n, shape, dt, nm):
            return Ring([P.sb(shape, dt, nm) for _ in range(n)])

        rA = ring(2, [128, MAXS], F32, "hA")
        rB, rC, rD = (ring(1, [128, MAXS], F32, n) for n in ("hB", "hC", "hD"))
        rE = ring(2, [128, 2 if so else MAXS], F32, "hE")
        rG = ring(1, [128, MAXS], BF16, "hG")
        rI = ring(2, [128, MAXS], BF16, "hI")
        q_ = 2 if so else MAXS
        rSG = ring(3, [128, q_], F32, "hSG")
        rO = ring(1, [128, q_], F32, "hO")
        rF, rHo = (ring(2, [128, q_], BF16, n) for n in ("hF", "hHo"))
        rsq = ring(1, [128, q_], F32, "hsq")
        rcs = ring(2, [128, 4, 16], F32, "hcs")
        rkv = ring(2, [64, MC, 256], BF16, "hkv")
        rsT = ring(2, [64, MC, 2 if so else 64], BF16, "hsT")
        rtm = ring(2, [128, MC, 128], F32, "htm")
        rSp = ring(3, [128, 128], BF16, "hSp")
        rrs = ring(2, [128, 512], F32, "hrs")
        scale = 128 ** -0.5
        seg = {}

        def stage0(un):
            (t0, nt, hd) = un
            NS = nt * 128
            NCH = NS // 64
            if hd == 0:
                seg["u"] = self.load_uT_seg(uT_d, t0, nt, useg)
                m_ = mk.next()
                self.dma(m_[:, 0:NS], mask_ap[0:1, t0 * 128:t0 * 128 + NS].partition_broadcast(128), [], [m_])
                seg["mask"] = m_
            u = seg["u"]; mask = seg["mask"]
            chunks = [(n0, min(512, NS - n0)) for n0 in range(0, NS, 512)]
            wt = wts.next()
            mats = [(1, C_F), (2, C_I)] if so else [(1, C_F), (0, C_Q), (2, C_I), (3, C_OG)]
            if impA is not None:
                mats = [(0, C_Q), (3, C_OG)]
            for (m, c0) in mats:
                self.dma(wt[:, :, m, :], w_ap[:, c0 + hd * 128:c0 + (hd + 1) * 128].rearrange("(k p) c -> p k c", p=128),
                         [], [wt], eng="pool")
            A, E, I, SG = (r.next() for r in (rA, rE, rI, rSG))
            if impA is not None:
                self.dma(A[:, 0:NS], impA[hd, :, t0 * 128:t0 * 128 + NS], [impA], [A])
                self.dma(I[:, 0:NS], impI[hd, :, t0 * 128:t0 * 128 + NS], [impI], [I])
            st_ = dict(NS=NS, NCH=NCH, chunks=chunks, SG=SG, A=A, E=E, I=I, mask=mask, t0=t0, hd=hd)
            seg[("st", t0, hd)] = st_
            for (m, c0) in mats:
                for (n0, nn) in chunks:
                    ps = self.pp.next()
                    for kc in range(KC):
                        self.mm(ps[:, 0:nn], wt[:, kc, m, :], u[:, kc, n0:n0 + nn], kc == 0, kc == KC - 1, [wt, u], [ps])
                    if m == 1:
                        self.act(A[:, n0:n0 + nn], ps[:, 0:nn], AF.Sigmoid, [ps], [A])
                    elif m == 0:
                        self.act(E[:, n0:n0 + nn], ps[:, 0:nn], AF.Copy, [ps], [E], scale=scale)
                    elif m == 2:
                        self.act(I[:, n0:n0 + nn], ps[:, 0:nn], AF.Copy, [ps], [I])
                    else:
                        self.act(SG[:, n0:n0 + nn], ps[:, 0:nn], AF.Sigmoid, [ps], [SG])
                    yield

        def stage1(un):
            (t0, nt, hd) = un
            st_ = seg[("st", t0, hd)]
            NS, NCH, A, E, I, mask = (st_[k_] for k_ in ("NS", "NCH", "A", "E", "I", "mask"))
            B, C, Dd, G = (r.next() for r in (rB, rC, rD, rG))
            F = rF.next()
            cs = rcs.next(); kv = rkv.next(); sTa = rsT.next(); tm = rtm.next()
            st_.update(F=F, cs=cs, kv=kv, sT=sTa, tm=tm)
            self.ts(A[:, 0:NS], A[:, 0:NS], self.oml[:, hd:hd + 1], self.lb[:, hd:hd + 1], ALU.mult, ALU.add, [A, self.oml, self.lb], [A])
            self.ts(B[:, 0:NS], A[:, 0:NS], -1.0, 1.0, ALU.mult, ALU.add, [A], [B])
            yield
            self.tt(B[:, 0:NS], B[:, 0:NS], mask[:, 0:NS], ALU.mult, [B, mask], [B])
            self.act(A[:, 0:NS], A[:, 0:NS], AF.Ln, [A], [A])
            self.tt(A[:, 0:NS], A[:, 0:NS], mask[:, 0:NS], ALU.mult, [A, mask], [A])
            yield
            P.op("dve", lambda e, C=C, A=A, NS=NS: e.tensor_tensor_scan(out=C[:, 0:NS], data0=self.rmask[:, 0:NS], data1=A[:, 0:NS],
                                                                    initial=0.0, op0=ALU.mult, op1=ALU.add), [A, self.rmask], [C])
            C3 = C[:, 0:NS].rearrange("p (c j) -> p c j", j=64)
            A3 = A[:, 0:NS].rearrange("p (c j) -> p c j", j=64)
            yield
            self.tt(A3, C3, C3[:, :, 31:32].to_broadcast([128, NCH, 64]), ALU.subtract, [C], [A])
            self.tt(cs[:, 3, 0:NCH], C3[:, :, 63], C3[:, :, 31], ALU.subtract, [C], [cs])
            self.act(cs[:, 0, 0:NCH], C3[:, :, 63], AF.Exp, [C], [cs])
            self.act(cs[:, 1, 0:NCH], cs[:, 3, 0:NCH], AF.Exp, [cs], [cs])
            self.act(cs[:, 2, 0:NCH], C3[:, :, 31], AF.Exp, [C], [cs])
            if dec_out is not None:
                P.op("dve", lambda e, cs=cs, C3=C3, NCH=NCH: e.tensor_reduce(out=cs[:, 3, 0:1], in_=C3[:, :, 63], axis=AX.X, op=ALU.add), [C], [cs])
                self.tt(dec_out[:, hd:hd + 1], dec_out[:, hd:hd + 1], cs[:, 3, 0:1], ALU.add, [dec_out, cs], [dec_out])
            self.act(Dd[:, 0:NS], A[:, 0:NS], AF.Exp, [A], [Dd], scale=-1.0)
            yield
            self.tt(G[:, 0:NS], B[:, 0:NS], Dd[:, 0:NS], ALU.mult, [B, Dd], [G])
            if not so:
                self.act(A[:, 0:NS], A[:, 0:NS], AF.Exp, [A], [A])
                self.tt(F[:, 0:NS], E[:, 0:NS], A[:, 0:NS], ALU.mult, [E, A], [F])
            yield
            def state_mm(c):
                pst = self.pq.next()
                self.mm(pst[:, 0:128], kv[:, c, 0:128], kv[:, c, 128:256], True, True, [kv], [pst])
                self.ts(tm[:, c, :], pst[:, 0:128], cs[:, 1, c:c + 1], None, ALU.mult, None, [pst, cs], [tm])

            for c in range(NCH):
                c0 = c * 64
                pt = self.pt.next()
                self.tr(pt[0:64, 0:128], G[:, c0:c0 + 64], self.idb[:], [G, self.idb], [pt])
                self.tr(pt[0:64, 128:256], I[:, c0:c0 + 64], self.idb[:], [I, self.idb], [pt])
                self.act(kv[:, c, :], pt[0:64, 0:256], AF.Copy, [pt], [kv])
                if not so:
                    psc = self.pq.next()
                    self.mm(psc[0:64, 0:64], G[:, c0:c0 + 64], F[:, c0:c0 + 64], True, True, [G, F], [psc])
                    self.tt(sTa[:, c, :], psc[0:64, 0:64], self.triu[0:64, 0:64], ALU.mult, [psc, self.triu], [sTa])
                if c > 0:
                    state_mm(c - 1)
                yield
            state_mm(NCH - 1)
            yield

        def stage2(un):
            (t0, nt, hd) = un
            d = seg.pop(("st", t0, hd))
            NS, NCH, chunks = d["NS"], d["NCH"], d["chunks"]
            SG, F, cs, kv, sTa, tm = (d[k_] for k_ in ("SG", "F", "cs", "kv", "sT", "tm"))
            O = rO.next(); Ho = rHo.next()
            Sh = S[:, hd, :]
            Sp = None
            if not so:
                Sp = rSp.next()
                self.ts(Sp[:], Sh, cs[:, 2, 0:1], None, ALU.mult, None, [S, cs], [Sp])
                yield
            for c in range(NCH):
                c0 = c * 64
                self.stt(Sh, Sh, cs[:, 0, c:c + 1], tm[:, c, :], ALU.mult, ALU.add, [S, cs, tm], [S])
                if not so:
                    Spn = None
                    if c + 1 < NCH:
                        Spn = rSp.next()
                        self.ts(Spn[:], Sh, cs[:, 2, c + 1:c + 2], None, ALU.mult, None, [S, cs], [Spn])
                    po = self.pq.next()
                    self.mm(po[:, 0:64], kv[:, c, 128:256], sTa[:, c, :], True, False, [kv, sTa], [po])
                    self.mm(po[:, 0:64], Sp[:], F[:, c0:c0 + 64], False, True, [Sp, F], [po])
                    self.tt(O[:, c0:c0 + 64], po[:, 0:64], SG[:, c0:c0 + 64], ALU.mult, [po, SG], [O])
                    Sp = Spn
                yield
            if so:
                return
            sq = rsq.next()
            self.tt(sq[:, 0:NS], O[:, 0:NS], O[:, 0:NS], ALU.mult, [O], [sq])
            for (n0, nn) in chunks:
                ps = self.pp.next()
                self.mm(ps[:, 0:nn], self.ones_f[:], sq[:, n0:n0 + nn], True, True, [self.ones_f, sq], [ps])
                rs = rrs.next()
                self.act(rs[:, 0:nn], ps[:, 0:nn], AF.Ln, [ps, self.epsc], [rs], scale=1.0 / 128, bias=self.epsc[:, 0:1])
                self.act(rs[:, 0:nn], rs[:, 0:nn], AF.Exp, [rs], [rs], scale=-0.5)
                self.stt(Ho[:, n0:n0 + nn], O[:, n0:n0 + nn], self.hgng[:, hd:hd + 1], rs[:, 0:nn], ALU.mult, ALU.mult,
                         [O, self.hgng, rs], [Ho])
                yield
            self.dma(hgT_d[hd, :, t0 * 128:t0 * 128 + NS], Ho[:, 0:NS], [Ho], [hgT_d])

        units = [(t0, nt, hd) for (t0, nt) in self.make_segs(MAXS // 128) for hd in range(HG_H)]
        self.pipeline(units, stage0, stage1, stage2)

    def m2_setup(self, w_ap, convw_ap, convb_ap, dtb_ap, alog_ap, dsk_ap, ng_ap, mask_ap):
        P = self.P
        self.cw = [self.load_cols(convw_ap[k:k + 1, :].rearrange("o (j p) -> (o j) p", p=128), 32, f"cw{k}") for k in range(4)]
        self.cbias = self.load_cols(convb_ap.rearrange("o (j p) -> (o j) p", p=128), 32, "cbias")
        self.maskT = self.load_cols(mask_ap.rearrange("o (j p) -> (o j) p", p=128), self.NT, "maskT")
        self.dtb = P.sb([128, 32], F32, "dtb")
        self.negA = P.sb([128, 32], F32, "negA")
        self.dsk = P.sb([128, 32], F32, "dsk")
        self.m2ng_ap = ng_ap
        self.dma(self.dtb[:], dtb_ap.partition_broadcast(128), [], [self.dtb])
        self.dma(self.negA[:], alog_ap.partition_broadcast(128), [], [self.negA])
        if dsk_ap is not None:
            self.dma(self.dsk[:], dsk_ap.partition_broadcast(128), [], [self.dsk])
        self.act(self.negA[:], self.negA[:], AF.Exp, [self.negA], [self.negA])
        self.ts(self.negA[:], self.negA[:], -1.0, None, ALU.mult, None, [self.negA], [self.negA])
        self.wdt = P.sb([128, KC, 32], BF16, "wdt")
        self.dma(self.wdt[:], w_ap[:, C_DT:C_DT + 32].rearrange("(k p) c -> p k c", p=128), [], [self.wdt], eng="pool")
        self.strict = P.sb([128, 128], F32, "strict")
        self.ts(self.strict[:], self.triu[:], -1.0, 1.0, ALU.mult, ALU.add, [self.triu], [self.strict])
        self.carry = P.sb([128, 8, 4, 3], F32, "carry")
        P.op("dve", lambda e: e.memset(self.carry[:], 0.0), (), [self.carry])

    def m2_mixer(self, uT_d, w_ap, S, Sbf, m2T_d, state_only=False, dec_out=None, impX=None):
        P = self.P
        MT = 5
        MAXS = MT * 128
        so = state_only

        def ring(n, shape, dt, nm):
            return Ring([P.sb(shape, dt, nm) for _ in range(n)])

        useg = ring(1, [128, KC, MAXS], BF16, "useg")
        wts = ring(2, [128, KC, 768], BF16, "m2w")
        rxr = ring(2, [128, 4, 3 + MAXS], F32, "xr")
        racc = ring(1, [128, 2, MAXS], F32, "xacc")
        rtmpc = ring(1, [128, MAXS], F32, "ctmp")
        rBT = ring(1, [128, MAXS], BF16, "BT")
        rCT = ring(2, [128, 2 if so else MAXS], BF16, "CT")
        rzs = ring(2, [128, 1 if so else MT, 256], F32, "zs")
        rm2o = ring(2, [128, 2, 2 if so else MAXS], BF16, "m2o")
        rng = ring(2, [128, 2 if so else 256], F32, "m2ngs")
        rybuf = ring(2, [128, 1 if so else MT, 256], F32, "ybuf")
        rxddA = ring(2, [128, MT, 256], BF16, "xddA")
        rBtmA = ring(2, [128, MT, 128], BF16, "BtmA")
        rps = Ring([dict((n, P.sb([128, MT, 32], F32, n)) for n in ("dtS", "aS", "cumS", "lastS", "ecS", "elS", "ddS")) for _ in range(2)])
        rxs = ring(2, [128, 256], F32, "xs")
        rxdt = ring(4, [128, 4, 64], BF16, "xdt")
        q_ = 2 if so else 128
        rcbm = ring(3, [128, q_], F32, "cbm")
        rM1 = ring(3, [128, 4, q_], F32, "M1")
        rEL = ring(2, [128, 4, q_], F32, "EL")
        rWT = ring(3, [128, 4, q_], BF16, "WT")
        ry = ring(2, [128, 2 * q_], F32, "y")
        ry2 = ring(3, [128, 2 * q_], F32, "y2")
        ryn = ring(2, [128, 2 * q_], BF16, "yn")
        rst = ring(2, [128, 4], F32, "mst")
        rtS = ring(2, [128, 4, 64], F32, "tS")
        rpSb = ring(3, [128, 256], F32, "pSb")
        junk = P.sb([128, 256], BF16, "mjunk")
        seg = {}

        def seg_prep(t0, nt):
            u = self.load_uT_seg(uT_d, t0, nt, useg)
            seg["u"] = u
            ps_ = rps.next()
            seg["ps"] = ps_
            dtS, aS, cumS, lastS, ecS, elS, ddS = (ps_[n] for n in ("dtS", "aS", "cumS", "lastS", "ecS", "elS", "ddS"))
            for ti in range(nt):
                ps = self.pq.next()
                for kc in range(KC):
                    self.mm(ps[:, 0:32], u[:, kc, ti * 128:(ti + 1) * 128], self.wdt[:, kc, :], kc == 0, kc == KC - 1, [u, self.wdt], [ps])
                self.tt(dtS[:, ti, :], ps[:, 0:32], self.dtb[:], ALU.add, [ps, self.dtb], [dtS])
            self.act(dtS[:, 0:nt, :], dtS[:, 0:nt, :], AF.Exp, [dtS], [dtS])
            self.act(dtS[:, 0:nt, :], dtS[:, 0:nt, :], AF.Ln, [dtS, self.epsc], [dtS], bias=self.epsc[:, 2:3])
            for ti in range(nt):
                self.ts(dtS[:, ti, :], dtS[:, ti, :], self.maskT[:, t0 + ti:t0 + ti + 1], None, ALU.mult, None, [dtS, self.maskT], [dtS])
                self.tt(aS[:, ti, :], dtS[:, ti, :], self.negA[:], ALU.mult, [dtS, self.negA], [aS])
            for ti in range(nt):
                ps = self.pq.next()
                self.mm(ps[:, 0:32], self.triu[:], aS[:, ti, :], True, True, [self.triu, aS], [ps])
                self.mm(ps[:, 32:64], self.ones_f[:], aS[:, ti, :], True, True, [self.ones_f, aS], [ps])
                self.act(cumS[:, ti, :], ps[:, 0:32], AF.Copy, [ps], [cumS])
                self.act(lastS[:, ti, :], ps[:, 32:64], AF.Copy, [ps], [lastS])
                if dec_out is not None:
                    self.tt(dec_out[:], dec_out[:], lastS[:, ti, :], ALU.add, [dec_out, lastS], [dec_out])
            self.act(ecS[:, 0:nt, :], cumS[:, 0:nt, :], AF.Exp, [cumS], [ecS])
            self.act(elS[:, 0:nt, :], lastS[:, 0:nt, :], AF.Exp, [lastS], [elS])
            self.tt(ddS[:, 0:nt, :], lastS[:, 0:nt, :], cumS[:, 0:nt, :], ALU.subtract, [lastS, cumS], [ddS])
            self.act(ddS[:, 0:nt, :], ddS[:, 0:nt, :], AF.Exp, [ddS], [ddS])
            self.tt(ddS[:, 0:nt, :], ddS[:, 0:nt, :], dtS[:, 0:nt, :], ALU.mult, [ddS, dtS], [ddS])

        def stage0(un):
            (t0, nt, gi) = un
            NS = nt * 128
            if gi == 0:
                seg_prep(t0, nt)
                yield
            u = seg["u"]
            chunks = [(n0, min(512, NS - n0)) for n0 in range(0, NS, 512)]
            wt = wts.next()
            srcs = [(0, C_Z + gi * 256, 256), (256, C_X + gi * 256, 256), (512, C_B + gi * 128, 128), (640, C_C + gi * 128, 128)]
            for (o0, c0, w_) in srcs:
                if so and o0 in (0, 640):
                    continue
                if impX is not None and o0 in (256, 512):
                    continue
                self.dma(wt[:, :, o0:o0 + w_], w_ap[:, c0:c0 + w_].rearrange("(k p) c -> p k c", p=128), [], [wt], eng="pool")
            xr = rxr.next(); zs = rzs.next()
            if impX is not None:
                self.dma(xr[:, 0:3, 3:3 + NS], impX[gi, :, :, t0 * 128:t0 * 128 + NS].rearrange("b p t -> p b t"), [impX], [xr])
            seg[("st", t0, gi)] = dict(xr=xr, zs=zs, ps=seg["ps"], chunks=chunks)
            blks = [0, 1, 2] if so else [0, 1, 2, 3]
            if not so:
                for ti in range(nt):
                    ps = self.pp.next()
                    for kc in range(KC):
                        self.mm(ps[:, 0:256], u[:, kc, ti * 128:(ti + 1) * 128], wt[:, kc, 0:256], kc == 0, kc == KC - 1, [u, wt], [ps])
                    self.act(zs[:, ti, :], ps[:, 0:256], AF.Silu, [ps], [zs])
                    yield
            for b in blks:
                if impX is not None and b < 3:
                    continue
                for (n0, nn) in chunks:
                    ps = self.pp.next()
                    for kc in range(KC):
                        self.mm(ps[:, 0:nn], wt[:, kc, 256 + b * 128:256 + (b + 1) * 128], u[:, kc, n0:n0 + nn], kc == 0, kc == KC - 1, [wt, u], [ps])
                    self.act(xr[:, b, 3 + n0:3 + n0 + nn], ps[:, 0:nn], AF.Copy, [ps], [xr])
                    yield

        def stage1(un):
            (t0, nt, gi) = un
            NS = nt * 128
            st_ = seg[("st", t0, gi)]
            xr = st_["xr"]; ps_ = st_["ps"]
            dtS, aS, ddS = ps_["dtS"], ps_["aS"], ps_["ddS"]
            acc = racc.next(); BT = rBT.next(); CT = rCT.next(); m2o = rm2o.next()
            ybuf = rybuf.next(); xddA = rxddA.next(); BtmA = rBtmA.next(); ngs = rng.next()
            if not so:
                self.dma(ngs[:], self.m2ng_ap[0:1, gi * 256:(gi + 1) * 256].partition_broadcast(128), [], [ngs])
            st_.update(CT=CT, m2o=m2o, ybuf=ybuf, xddA=xddA, BtmA=BtmA, ngs=ngs)
            blks = [0, 1, 2] if so else [0, 1, 2, 3]
            for b in blks:
                j = (gi * 2 + b) if b < 2 else (16 + gi if b == 2 else 24 + gi)
                self.act(xr[:, b, 0:3], self.carry[:, gi, b, :], AF.Copy, [self.carry], [xr])
                self.act(self.carry[:, gi, b, :], xr[:, b, NS:NS + 3], AF.Copy, [xr], [self.carry])
                tmpc = rtmpc.next()
                self.ts(tmpc[:, 0:NS], xr[:, b, 3:3 + NS], self.cw[3][:, j:j + 1], self.cbias[:, j:j + 1], ALU.mult, ALU.add,
                        [xr, self.cw[3], self.cbias], [tmpc])
                for k_ in range(3):
                    self.stt(tmpc[:, 0:NS], xr[:, b, k_:k_ + NS], self.cw[k_][:, j:j + 1], tmpc[:, 0:NS], ALU.mult, ALU.add,
                             [xr, self.cw[k_], tmpc], [tmpc])
                if b < 2:
                    self.act(acc[:, b, 0:NS], tmpc[:, 0:NS], AF.Silu, [tmpc], [acc])
                elif b == 2:
                    self.act(BT[:, 0:NS], tmpc[:, 0:NS], AF.Silu, [tmpc], [BT])
                else:
                    self.act(CT[:, 0:NS], tmpc[:, 0:NS], AF.Silu, [tmpc], [CT])
                yield
            hs = slice(gi * 4, gi * 4 + 4)
            tl = {}

            def phaseA(ti):
                tk = slice(ti * 128, (ti + 1) * 128)
                px = self.pq.next()
                self.tr(px[:, 0:128], acc[:, 0, tk], self.idf[:], [acc, self.idf], [px])
                self.tr(px[:, 128:256], acc[:, 1, tk], self.idf[:], [acc, self.idf], [px])
                pb = self.pt.next()
                self.tr(pb[:, 0:128], BT[:, tk], self.idb[:], [BT, self.idb], [pb])
                xs = rxs.next()
                self.act(xs[:], px[:, 0:256], AF.Copy, [px], [xs])
                self.act(BtmA[:, ti, :], pb[:, 0:128], AF.Copy, [pb], [BtmA])
                xs3 = xs[:].rearrange("p (h q) -> p h q", q=64)
                self.tt(xddA[:, ti, :].rearrange("p (h q) -> p h q", q=64), xs3, ddS[:, ti, hs].unsqueeze(2).to_broadcast([128, 4, 64]),
                        ALU.mult, [xs, ddS], [xddA])
                if so:
                    return
                xdt = rxdt.next(); y2 = ry2.next(); cbm = rcbm.next(); M1 = rM1.next()
                self.tt(xdt[:], xs3, dtS[:, ti, hs].unsqueeze(2).to_broadcast([128, 4, 64]), ALU.mult, [xs, dtS], [xdt])
                self.tt(y2[:].rearrange("p (h q) -> p h q", q=64), xs3, self.dsk[:, hs].unsqueeze(2).to_broadcast([128, 4, 64]),
                        ALU.mult, [xs, self.dsk], [y2])
                pcb = self.pq.next()
                self.mm(pcb[:, 0:128], BT[:, tk], CT[:, tk], True, True, [BT, CT], [pcb])
                self.tt(cbm[:], pcb[:, 0:128], self.triu[:], ALU.mult, [pcb, self.triu], [cbm])
                self.tt(M1[:], self.strict[:].unsqueeze(1).to_broadcast([128, 4, 128]),
                        aS[:, ti, hs].unsqueeze(2).to_broadcast([128, 4, 128]), ALU.mult, [self.strict, aS], [M1])
                tl[ti] = dict(xdt=xdt, y2=y2, cbm=cbm, M1=M1)

            def phaseB(ti):
                d_ = tl[ti]
                pD = self.pq.next()
                for h in range(4):
                    self.mm(pD[:, h * 128:(h + 1) * 128], d_["M1"][:, h, :], self.triu[:], True, True, [d_["M1"], self.triu], [pD])
                EL = rEL.next()
                self.act(EL[:], pD[:].rearrange("p (h t) -> p h t", h=4), AF.Exp, [pD], [EL])
                WT = rWT.next()
                self.tt(WT[:], EL[:], d_["cbm"][:].unsqueeze(1).to_broadcast([128, 4, 128]), ALU.mult, [EL, d_["cbm"]], [WT])
                d_["WT"] = WT

            def phaseC(ti):
                d_ = tl.pop(ti)
                py = self.pq.next()
                for h in range(4):
                    self.mm(py[:, h * 64:(h + 1) * 64], d_["WT"][:, h, :], d_["xdt"][:, h, :], True, True, [d_["WT"], d_["xdt"]], [py])
                self.tt(ybuf[:, ti, :], d_["y2"][:], py[:, 0:256], ALU.add, [d_["y2"], py], [ybuf])

            for step in range(nt + (0 if so else 2)):
                if not so and 0 <= step - 2 < nt:
                    phaseC(step - 2)
                if not so and 0 <= step - 1 < nt:
                    phaseB(step - 1)
                if step < nt:
                    phaseA(step)
                yield

        def stage2(un):
            (t0, nt, gi) = un
            NS = nt * 128
            d = seg.pop(("st", t0, gi))
            zs = d["zs"]; ps_ = d["ps"]
            ecS, elS = ps_["ecS"], ps_["elS"]
            CT, m2o, ybuf, xddA, BtmA, ngs = (d[k_] for k_ in ("CT", "m2o", "ybuf", "xddA", "BtmA", "ngs"))
            Sg = S[:, gi * 4:(gi + 1) * 4, :]
            Sbg = Sbf[:, gi * 4:(gi + 1) * 4, :]
            hs = slice(gi * 4, gi * 4 + 4)
            def ps_mm(ti):
                pS_ = self.pq.next()
                self.mm(pS_[:, 0:256], BtmA[:, ti, :], xddA[:, ti, :], True, True, [BtmA, xddA], [pS_])
                h_ = rpSb.next()
                self.act(h_[:], pS_[:, 0:256], AF.Copy, [pS_], [h_])
                return h_, 0

            nxt = ps_mm(0)
            for ti in range(nt):
                tk = slice(ti * 128, (ti + 1) * 128)
                if not so:
                    py = self.pq.next()
                    self.mm(py[:, 0:256], CT[:, tk], Sbg.rearrange("p h q -> p (h q)"), True, True, [CT, Sbf], [py])
                (pS, po_) = nxt
                if ti + 1 < nt:
                    nxt = ps_mm(ti + 1)
                tS = rtS.next()
                self.tt(tS[:], Sg, elS[:, ti, hs].unsqueeze(2).to_broadcast([128, 4, 64]), ALU.mult, [S, elS], [tS])
                self.tt(Sg, tS[:], pS[:, po_:po_ + 256].rearrange("p (h q) -> p h q", q=64), ALU.add, [tS, pS], [S])
                if not so:
                    self.act(Sbg, Sg, AF.Copy, [S], [Sbf])
                    y = ry.next()
                    self.tt(y[:].rearrange("p (h q) -> p h q", q=64), py[:, 0:256].rearrange("p (h q) -> p h q", q=64),
                            ecS[:, ti, hs].unsqueeze(2).to_broadcast([128, 4, 64]), ALU.mult, [py, ecS], [y])
                    self.tt(y[:], y[:], ybuf[:, ti, :], ALU.add, [y, ybuf], [y])
                    self.tt(y[:], y[:], zs[:, ti, :], ALU.mult, [y, zs], [y])
                    st = rst.next()
                    self.act(junk[:], y[:], AF.Square, [y], [junk, st], accum_out=st[:, 0:1])
                    self.act(st[:, 1:2], st[:, 0:1], AF.Ln, [st, self.epsc], [st], scale=1.0 / 256, bias=self.epsc[:, 1:2])
                    self.act(st[:, 2:3], st[:, 1:2], AF.Exp, [st], [st], scale=-0.5)
                    yn = ryn.next()
                    self.stt(yn[:], y[:], st[:, 2:3], ngs[:], ALU.mult, ALU.mult, [y, st, ngs], [yn])
                    po = self.pt.next()
                    self.tr(po[:, 0:128], yn[:, 0:128], self.idb[:], [yn, self.idb], [po])
                    self.tr(po[:, 128:256], yn[:, 128:256], self.idb[:], [yn, self.idb], [po])
                    self.act(m2o[:, :, tk], po[:, 0:256].rearrange("p (j t) -> p j t", j=2), AF.Copy, [po], [m2o])
                yield
            if not so:
                self.dma(m2T_d[gi * 2:gi * 2 + 2, :, t0 * 128:t0 * 128 + NS].rearrange("j p t -> p j t"), m2o[:, :, 0:NS], [m2o], [m2T_d])

        units = [(t0, nt, gi) for (t0, nt) in self.make_segs(MT) for gi in range(M2_G)]
        self.pipeline(units, stage0, stage1, stage2)

    def hg_state(self, uT_d, w_ap, mask_ap, S, dec_out, expA=None, expI=None):
        P = self.P
        MT = 6
        MAXS = MT * 128

        def ring(n, shape, dt, nm):
            return Ring([P.sb(shape, dt, nm) for _ in range(n)])

        useg = ring(1, [128, KC, MAXS], BF16, "useg")
        wts = ring(2, [128, KC, 2, 128], BF16, "hsw")
        mk = ring(2, [128, MAXS], F32, "mk")
        rA = ring(2, [128, MAXS], F32, "sA")
        rI = ring(2, [128, MAXS], BF16, "sI")
        rB, rC = (ring(1, [128, MAXS], F32, n) for n in ("sB", "sC"))
        rG = ring(2, [128, MAXS], BF16, "sG")
        rkv = ring(2, [128, MT, 256], BF16, "skv")
        rcs = ring(2, [128, 4], F32, "scs")
        onesr = P.sb([128, MAXS], F32, "onesr")
        P.op("dve", lambda e: e.memset(onesr[:], 1.0), (), [onesr])
        seg = {}

        def stage0(un):
            (t0, nt, hd) = un
            NS = nt * 128
            if hd == 0:
                seg["u"] = self.load_uT_seg(uT_d, t0, nt, useg)
                m_ = mk.next()
                self.dma(m_[:, 0:NS], mask_ap[0:1, t0 * 128:t0 * 128 + NS].partition_broadcast(128), [], [m_])
                seg["mask"] = m_
            u = seg["u"]
            wt = wts.next()
            for (m, c0) in ((0, C_F), (1, C_I)):
                self.dma(wt[:, :, m, :], w_ap[:, c0 + hd * 128:c0 + (hd + 1) * 128].rearrange("(k p) c -> p k c", p=128), [], [wt], eng="pool")
            A = rA.next(); I = rI.next()
            seg[("st", t0, hd)] = dict(A=A, I=I, mask=seg["mask"])
            for m in range(2):
                for (n0, nn) in [(a, min(512, NS - a)) for a in range(0, NS, 512)]:
                    ps = self.pp.next()
                    for kc in range(KC):
                        self.mm(ps[:, 0:nn], wt[:, kc, m, :], u[:, kc, n0:n0 + nn], kc == 0, kc == KC - 1, [wt, u], [ps])
                    if m == 0:
                        self.act(A[:, n0:n0 + nn], ps[:, 0:nn], AF.Sigmoid, [ps], [A])
                    else:
                        self.act(I[:, n0:n0 + nn], ps[:, 0:nn], AF.Copy, [ps], [I])
                    yield
            if expA is not None:
                self.dma(expA[hd, :, t0 * 128:t0 * 128 + NS], A[:, 0:NS], [A], [expA])
                self.dma(expI[hd, :, t0 * 128:t0 * 128 + NS], I[:, 0:NS], [I], [expI])

        def stage1(un):
            (t0, nt, hd) = un
            NS = nt * 128
            d = seg.pop(("st", t0, hd))
            A, I, mask = d["A"], d["I"], d["mask"]
            B = rB.next(); C = rC.next(); G = rG.next(); kv = rkv.next(); cs = rcs.next()
            self.ts(A[:, 0:NS], A[:, 0:NS], self.oml[:, hd:hd + 1], self.lb[:, hd:hd + 1], ALU.mult, ALU.add, [A, self.oml, self.lb], [A])
            self.ts(B[:, 0:NS], A[:, 0:NS], -1.0, 1.0, ALU.mult, ALU.add, [A], [B])
            self.tt(B[:, 0:NS], B[:, 0:NS], mask[:, 0:NS], ALU.mult, [B, mask], [B])
            self.act(A[:, 0:NS], A[:, 0:NS], AF.Ln, [A], [A])
            self.tt(A[:, 0:NS], A[:, 0:NS], mask[:, 0:NS], ALU.mult, [A, mask], [A])
            yield
            P.op("dve", lambda e, C=C, A=A, NS=NS: e.tensor_tensor_scan(out=C[:, 0:NS], data0=onesr[:, 0:NS], data1=A[:, 0:NS],
                                                                    initial=0.0, op0=ALU.mult, op1=ALU.add), [A, onesr], [C])
            self.act(A[:, 0:NS], C[:, 0:NS], AF.Exp, [C], [A], scale=-1.0, bias=C[:, NS - 1:NS])
            self.act(cs[:, 0:1], C[:, NS - 1:NS], AF.Exp, [C], [cs])
            self.tt(dec_out[:, hd:hd + 1], dec_out[:, hd:hd + 1], C[:, NS - 1:NS], ALU.add, [dec_out, C], [dec_out])
            self.tt(G[:, 0:NS], B[:, 0:NS], A[:, 0:NS], ALU.mult, [B, A], [G])
            yield
            for ti in range(nt):
                tk = slice(ti * 128, (ti + 1) * 128)
                pt = self.pt.next()
                self.tr(pt[:, 0:128], G[:, tk], self.idb[:], [G, self.idb], [pt])
                self.tr(pt[:, 128:256], I[:, tk], self.idb[:], [I, self.idb], [pt])
                self.act(kv[:, ti, :], pt[:, 0:256], AF.Copy, [pt], [kv])
                if ti % 2 == 1:
                    yield
            pst = self.pq.next()
            for ti in range(nt):
                self.mm(pst[:, 0:128], kv[:, ti, 0:128], kv[:, ti, 128:256], ti == 0, ti == nt - 1, [kv], [pst])
            Sh = S[:, hd, :]
            self.stt(Sh, Sh, cs[:, 0:1], pst[:, 0:128], ALU.mult, ALU.add, [S, cs, pst], [S])
            yield

        units = [(t0, nt, hd) for (t0, nt) in self.make_segs(MT) for hd in range(HG_H)]
        self.pipeline(units, stage0, stage1)

    def m2_state(self, uT_d, w_ap, S, dec_out, expX=None):
        P = self.P
        MT = 6
        MAXS = MT * 128

        def ring(n, shape, dt, nm):
            return Ring([P.sb(shape, dt, nm) for _ in range(n)])

        useg = ring(1, [128, KC, MAXS], BF16, "useg")
        wts = ring(2, [128, KC, 384], BF16, "msw")
        rxr = ring(2, [128, 3, 3 + MAXS], F32, "xr")
        racc = ring(1, [128, 2, MAXS], F32, "xacc")
        rtmpc = ring(2, [128, MAXS], F32, "ctmp")
        rBT = ring(1, [128, MAXS], BF16, "BT")
        rps = Ring([dict((n, P.sb([128, MT, 32], F32, n)) for n in ("dtS", "aS", "cumS", "lastS", "ddS")) for _ in range(2)])
        rtot = ring(2, [128, 2, 32], F32, "mtot")
        rxs = ring(2, [128, 256], F32, "xs")
        rxdd = ring(2, [128, MT, 256], BF16, "xddA")
        rBtm = ring(2, [128, MT, 128], BF16, "BtmA")
        rtS = ring(2, [128, 4, 64], F32, "tS")
        seg = {}

        def seg_prep(t0, nt):
            u = self.load_uT_seg(uT_d, t0, nt, useg)
            seg["u"] = u
            ps_ = rps.next(); tot = rtot.next()
            seg["ps"] = ps_; seg["tot"] = tot
            dtS, aS, cumS, lastS, ddS = (ps_[n] for n in ("dtS", "aS", "cumS", "lastS", "ddS"))
            for ti in range(nt):
                ps = self.pq.next()
                for kc in range(KC):
                    self.mm(ps[:, 0:32], u[:, kc, ti * 128:(ti + 1) * 128], self.wdt[:, kc, :], kc == 0, kc == KC - 1, [u, self.wdt], [ps])
                self.tt(dtS[:, ti, :], ps[:, 0:32], self.dtb[:], ALU.add, [ps, self.dtb], [dtS])
            self.act(dtS[:, 0:nt, :], dtS[:, 0:nt, :], AF.Exp, [dtS], [dtS])
            self.act(dtS[:, 0:nt, :], dtS[:, 0:nt, :], AF.Ln, [dtS, self.epsc], [dtS], bias=self.epsc[:, 2:3])
            for ti in range(nt):
                self.ts(dtS[:, ti, :], dtS[:, ti, :], self.maskT[:, t0 + ti:t0 + ti + 1], None, ALU.mult, None, [dtS, self.maskT], [dtS])
                self.tt(aS[:, ti, :], dtS[:, ti, :], self.negA[:], ALU.mult, [dtS, self.negA], [aS])
            for ti in range(nt):
                ps = self.pq.next()
                self.mm(ps[:, 0:32], self.triu[:], aS[:, ti, :], True, True, [self.triu, aS], [ps])
                self.mm(ps[:, 32:64], self.ones_f[:], aS[:, ti, :], True, True, [self.ones_f, aS], [ps])
                self.act(cumS[:, ti, :], ps[:, 0:32], AF.Copy, [ps], [cumS])
                self.act(lastS[:, ti, :], ps[:, 32:64], AF.Copy, [ps], [lastS])
            self.tt(ddS[:, 0:nt, :], lastS[:, 0:nt, :], cumS[:, 0:nt, :], ALU.subtract, [lastS, cumS], [ddS])
            P.op("dve", lambda e, tot=tot: e.memset(tot[:, 0, :], 0.0), (), [tot])
            for ti in range(nt - 1, -1, -1):
                self.tt(ddS[:, ti, :], ddS[:, ti, :], tot[:, 0, :], ALU.add, [ddS, tot], [ddS])
                self.tt(tot[:, 0, :], tot[:, 0, :], lastS[:, ti, :], ALU.add, [tot, lastS], [tot])
            self.tt(dec_out[:], dec_out[:], tot[:, 0, :], ALU.add, [dec_out, tot], [dec_out])
            self.act(tot[:, 1, :], tot[:, 0, :], AF.Exp, [tot], [tot])
            self.act(ddS[:, 0:nt, :], ddS[:, 0:nt, :], AF.Exp, [ddS], [ddS])
            self.tt(ddS[:, 0:nt, :], ddS[:, 0:nt, :], dtS[:, 0:nt, :], ALU.mult, [ddS, dtS], [ddS])

        def stage0(un):
            (t0, nt, gi) = un
            NS = nt * 128
            if gi == 0:
                seg_prep(t0, nt)
                yield
            u = seg["u"]
            wt = wts.next()
            self.dma(wt[:, :, 0:256], w_ap[:, C_X + gi * 256:C_X + (gi + 1) * 256].rearrange("(k p) c -> p k c", p=128), [], [wt], eng="pool")
            self.dma(wt[:, :, 256:384], w_ap[:, C_B + gi * 128:C_B + (gi + 1) * 128].rearrange("(k p) c -> p k c", p=128), [], [wt], eng="pool")
            xr = rxr.next()
            seg[("st", t0, gi)] = dict(xr=xr, ps=seg["ps"], tot=seg["tot"])
            for b in range(3):
                for (n0, nn) in [(a, min(512, NS - a)) for a in range(0, NS, 512)]:
                    ps = self.pp.next()
                    for kc in range(KC):
                        self.mm(ps[:, 0:nn], wt[:, kc, b * 128:(b + 1) * 128], u[:, kc, n0:n0 + nn], kc == 0, kc == KC - 1, [wt, u], [ps])
                    self.act(xr[:, b, 3 + n0:3 + n0 + nn], ps[:, 0:nn], AF.Copy, [ps], [xr])
                    yield
            if expX is not None:
                self.dma(expX[gi, :, :, t0 * 128:t0 * 128 + NS].rearrange("b p t -> p b t"), xr[:, :, 3:3 + NS], [xr], [expX])

        def stage1(un):
            (t0, nt, gi) = un
            NS = nt * 128
            d = seg.pop(("st", t0, gi))
            xr = d["xr"]; ddS = d["ps"]["ddS"]; tot = d["tot"]
            acc = racc.next(); BT = rBT.next(); xdd = rxdd.next(); Btm = rBtm.next()
            for b in range(3):
                j = (gi * 2 + b) if b < 2 else 16 + gi
                self.act(xr[:, b, 0:3], self.carry[:, gi, b, :], AF.Copy, [self.carry], [xr])
                self.act(self.carry[:, gi, b, :], xr[:, b, NS:NS + 3], AF.Copy, [xr], [self.carry])
                tmpc = rtmpc.next()
                self.ts(tmpc[:, 0:NS], xr[:, b, 3:3 + NS], self.cw[3][:, j:j + 1], self.cbias[:, j:j + 1], ALU.mult, ALU.add,
                        [xr, self.cw[3], self.cbias], [tmpc])
                for k_ in range(3):
                    self.stt(tmpc[:, 0:NS], xr[:, b, k_:k_ + NS], self.cw[k_][:, j:j + 1], tmpc[:, 0:NS], ALU.mult, ALU.add,
                             [xr, self.cw[k_], tmpc], [tmpc])
                if b < 2:
                    self.act(acc[:, b, 0:NS], tmpc[:, 0:NS], AF.Silu, [tmpc], [acc])
                else:
                    self.act(BT[:, 0:NS], tmpc[:, 0:NS], AF.Silu, [tmpc], [BT])
                yield
            hs = slice(gi * 4, gi * 4 + 4)
            for ti in range(nt):
                tk = slice(ti * 128, (ti + 1) * 128)
                px = self.pq.next()
                self.tr(px[:, 0:128], acc[:, 0, tk], self.idf[:], [acc, self.idf], [px])
                self.tr(px[:, 128:256], acc[:, 1, tk], self.idf[:], [acc, self.idf], [px])
                pb = self.pt.next()
                self.tr(pb[:, 0:128], BT[:, tk], self.idb[:], [BT, self.idb], [pb])
                xs = rxs.next()
                self.act(xs[:], px[:, 0:256], AF.Copy, [px], [xs])
                self.act(Btm[:, ti, :], pb[:, 0:128], AF.Copy, [pb], [Btm])
                self.tt(xdd[:, ti, :].rearrange("p (h q) -> p h q", q=64), xs[:].rearrange("p (h q) -> p h q", q=64),
                        ddS[:, ti, hs].unsqueeze(2).to_broadcast([128, 4, 64]), ALU.mult, [xs, ddS], [xdd])
                yield
            pS = self.pq.next()
            for ti in range(nt):
                self.mm(pS[:, 0:256], Btm[:, ti, :], xdd[:, ti, :], ti == 0, ti == nt - 1, [Btm, xdd], [pS])
            Sg = S[:, gi * 4:(gi + 1) * 4, :]
            tS = rtS.next()
            self.tt(tS[:], Sg, tot[:, 1, hs].unsqueeze(2).to_broadcast([128, 4, 64]), ALU.mult, [S, tot], [tS])
            self.tt(Sg, tS[:], pS[:, 0:256].rearrange("p (h q) -> p h q", q=64), ALU.add, [tS, pS], [S])
            yield

        units = [(t0, nt, gi) for (t0, nt) in self.make_segs(MT) for gi in range(M2_G)]
        self.pipeline(units, stage0, stage1)

    def tok_chunks(self, n0, n1):
        return [(a, min(512, n1 - a)) for a in range(n0, n1, 512)]

    def dense_A_res(self, xT_d, KCx, W_ap, res_ap, res_r, out_d, seg_tiles):
        P = self.P
        NTs = seg_tiles
        xs_ = Ring([P.sb([128, KCx, NTs * 128], BF16, "dax") for _ in range(1)])
        wr = Ring([P.sb([128, KCx, 512], BF16, "daw") for _ in range(2)])
        rr = Ring([P.sb([128, 512], F32, "dar") for _ in range(3)])
        orr = Ring([P.sb([128, 512], F32, "dao") for _ in range(3)])
        t = 0
        while t < self.NT:
            nt = min(NTs, self.NT - t)
            x = xs_.next()
            self.dma(x[:, :, 0:nt * 128], xT_d[:, :, t * 128:(t + nt) * 128].rearrange("k p t -> p k t"), [xT_d], [x])
            for cb in range(4):
                w = wr.next()
                self.dma(w[:], W_ap[:, cb * 512:(cb + 1) * 512].rearrange("(k p) c -> p k c", p=128), [], [w], eng="pool")
                for ti in range(nt):
                    r = rr.next(); o = orr.next()
                    tok = slice((t + ti) * 128, (t + ti + 1) * 128)
                    self.dma(r[:], res_ap[tok, cb * 512:(cb + 1) * 512], [res_r], [r])
                    ps = self.pp.next()
                    for kc in range(KCx):
                        self.mm(ps[:], x[:, kc, ti * 128:(ti + 1) * 128], w[:, kc, :], kc == 0, kc == KCx - 1, [x, w], [ps])
                    self.tt(o[:], ps[:], r[:], ALU.add, [ps, r], [o])
                    self.dma(out_d[tok, cb * 512:(cb + 1) * 512], o[:], [o], [out_d])
            t += nt

    def dense_A_res_norm(self, xT_d, W_ap, res_ap, res_r, out_d, g_ap, uT_out_d):
        P = self.P
        x = self.load_xT(xT_d, "dnx")
        w = P.sb([128, KC, D], BF16, "dnw")
        for cb in range(4):
            self.dma(w[:, :, cb * 512:(cb + 1) * 512], W_ap[:, cb * 512:(cb + 1) * 512].rearrange("(k p) c -> p k c", p=128), [], [w], eng="pool")
        gbc = P.sb([128, D], F32, "dngbc")
        self.dma(gbc[:], g_ap.partition_broadcast(128), [], [gbc])
        rr = Ring([P.sb([128, 512], F32, "dnr") for _ in range(3)])
        ro = Ring([P.sb([128, D], F32, "dno") for _ in range(2)])
        ru = Ring([P.sb([128, D], BF16, "dnu") for _ in range(2)])
        NB_ = 2
        ruo = Ring([P.sb([128, KC, NB_ * 128], BF16, "dnuo") for _ in range(2)])
        cur = {}
        rs_ = Ring([P.sb([128, 4], F32, "dns") for _ in range(2)])

        def norm_tail(i, u):
            if i % NB_ == 0:
                cur["o"] = ruo.next()
            o_ = cur["o"]
            q4 = (i % NB_) * 128
            for half in range(2):
                pt = self.pt.next()
                for j in range(8):
                    kc = half * 8 + j
                    self.tr(pt[:, j * 128:(j + 1) * 128], u[:, kc * 128:(kc + 1) * 128], self.idb[:], [u, self.idb], [pt])
                self.act(o_[:, half * 8:(half + 1) * 8, q4:q4 + 128], pt[:].rearrange("p (j t) -> p j t", j=8), AF.Copy, [pt], [o_])
            if i % NB_ == NB_ - 1 or i == self.NT - 1:
                i0_ = i - (i % NB_)
                self.dma(uT_out_d[:, :, i0_ * 128:(i + 1) * 128].rearrange("k p t -> p k t"), o_[:, :, 0:q4 + 128], [o_], [uT_out_d])

        pend = None
        for i in range(self.NT):
            tok = slice(i * 128, (i + 1) * 128)
            o = ro.next(); u = ru.next(); s_ = rs_.next()
            for cb in range(4):
                r = rr.next()
                self.dma(r[:], res_ap[tok, cb * 512:(cb + 1) * 512], [res_r], [r])
                ps = self.pp.next()
                for kc in range(KC):
                    self.mm(ps[:], x[:, kc, tok], w[:, kc, cb * 512:(cb + 1) * 512], kc == 0, kc == KC - 1, [x, w], [ps])
                self.tt(o[:, cb * 512:(cb + 1) * 512], ps[:], r[:], ALU.add, [ps, r], [o])
            self.dma(out_d[tok, :], o[:], [o], [out_d])
            self.act(u[:], o[:], AF.Square, [o], [u, s_], accum_out=s_[:, 0:1])
            self.act(s_[:, 1:2], s_[:, 0:1], AF.Ln, [s_, self.epsc], [s_], scale=1.0 / D, bias=self.epsc[:, 0:1])
            self.act(s_[:, 2:3], s_[:, 1:2], AF.Exp, [s_], [s_], scale=-0.5)
            self.stt(u[:], o[:], s_[:, 2:3], gbc[:], ALU.mult, ALU.mult, [o, s_, gbc], [u])
            if pend is not None:
                norm_tail(*pend)
            pend = (i, u)
        norm_tail(*pend)

    def dense_B(self, x, W_ap, c0, ncols, evac, ntok=None):
        P = self.P
        wr = self._dbw
        ntok = self.TT if ntok is None else ntok
        for c in range(0, ncols, 512):
            wc = min(512, ncols - c)
            w = wr.next()
            self.dma(w[:, :, 0:wc], W_ap[:, c0 + c:c0 + c + wc].rearrange("(k p) c -> p k c", p=128), [], [w], eng="pool")
            for b in range(wc // 128):
                for (n0, nn) in self.tok_chunks(0, ntok):
                    ps = self.pp.next()
                    for kc in range(KC):
                        self.mm(ps[:, 0:nn], w[:, kc, b * 128:(b + 1) * 128], x[:, kc, n0:n0 + nn], kc == 0, kc == KC - 1, [w, x], [ps])
                    evac(ps, (c // 128) + b, n0, nn)

    def halves(self):
        a = ((self.NT + 1) // 2) * 128
        return [(0, a), (a, self.TT)] if a < self.TT else [(0, self.TT)]

    def load_xT(self, xT_d, name):
        x = self.P.sb([128, KC, self.TT], BF16, name)
        self.dma(x[:], xT_d[:, :, :].rearrange("k p t -> p k t"), [xT_d], [x])
        return x

    def mixer_merge(self, uT_d, hgT_d, m2T_d, w_ap, wbh_ap, wbm_ap, mT_d):
        P = self.P
        hv = self.halves()
        NSM = max(b - a for a, b in hv)
        ru = Ring([P.sb([128, KC, NSM], BF16, "mu") for _ in range(1)])
        rh = Ring([P.sb([128, KC, NSM], BF16, "mh") for _ in range(1)])
        rm = Ring([P.sb([128, KC, NSM], BF16, "mm") for _ in range(1)])
        rw = Ring([P.sb([128, KC, 4, 128], BF16, "mw") for _ in range(2)])
        rg = Ring([P.sb([128, 2, 512], F32, "mg") for _ in range(2)])
        rt = Ring([P.sb([128, 2, 512], F32, "mt") for _ in range(2)])
        ro = Ring([P.sb([128, NSM], BF16, "mo") for _ in range(2)])
        for (h0, h1) in hv:
            nh = h1 - h0
            u = ru.next(); hg = rh.next(); m2 = rm.next()
            for (dst, src) in ((u, uT_d), (hg, hgT_d), (m2, m2T_d)):
                self.dma(dst[:, :, 0:nh], src[:, :, h0:h1].rearrange("k p t -> p k t"), [src], [dst])
            for kb in range(16):
                c = kb * 128
                w = rw.next()
                for mi, (ap, cc) in enumerate(((w_ap, C_GHG + c), (wbh_ap, c), (w_ap, C_GM2 + c), (wbm_ap, c))):
                    self.dma(w[:, :, mi, :], ap[:, cc:cc + 128].rearrange("(k p) c -> p k c", p=128), [], [w], eng="pool")
                o = ro.next()
                for (n0, nn) in self.tok_chunks(0, nh):
                    g = rg.next(); t_ = rt.next()
                    for half, (xa, xb) in enumerate(((u, hg), (u, m2))):
                        pg = self.pp.next()
                        for kc in range(KC):
                            self.mm(pg[:, 0:nn], w[:, kc, 2 * half, :], xa[:, kc, n0:n0 + nn], kc == 0, kc == KC - 1, [w, xa], [pg])
                        self.act(g[:, half, 0:nn], pg[:, 0:nn], AF.Sigmoid, [pg], [g])
                        py_ = self.pq.next()
                        for kc in range(KC):
                            self.mm(py_[:, 0:nn], w[:, kc, 2 * half + 1, :], xb[:, kc, n0:n0 + nn], kc == 0, kc == KC - 1, [w, xb], [py_])
                        self.tt(t_[:, half, 0:nn], py_[:, 0:nn], g[:, half, 0:nn], ALU.mult, [py_, g], [t_])
                    self.tt(o[:, n0:n0 + nn], t_[:, 0, 0:nn], t_[:, 1, 0:nn], ALU.add, [t_], [o])
                self.dma(mT_d[kb, :, h0:h1], o[:, 0:nh], [o], [mT_d])

    def xattn(self, uT_d, memnT_d, wq_ap, wkv_ap, oT_d):
        P = self.P
        qT = P.sb([128, KC, self.TT], BF16, "xa_q")
        kT = P.sb([128, KC, 256], BF16, "xa_k")
        v = P.sb([128, 2, 2048], BF16, "xa_v")
        sc = 512 ** -0.5
        with P.scope():
            self._dbw = Ring([P.sb([128, KC, 512], BF16, "dbw") for _ in range(2)])
            mn = P.sb([128, KC, 256], BF16, "xa_mn")
            self.dma(mn[:], memnT_d[:, :, :].rearrange("k p t -> p k t"), [memnT_d], [mn])
            for c in range(0, 2048, 512):
                w = self._dbw.next()
                self.dma(w[:], wkv_ap[:, c:c + 512].rearrange("(k p) c -> p k c", p=128), [], [w], eng="pool")
                for b in range(4):
                    ps = self.pp.next()
                    for kc in range(KC):
                        self.mm(ps[:, 0:256], w[:, kc, b * 128:(b + 1) * 128], mn[:, kc, :], kc == 0, kc == KC - 1, [w, mn], [ps])
                    self.act(kT[:, c // 128 + b, :], ps[:, 0:256], AF.Copy, [ps], [kT])
            for c in range(0, 2048, 512):
                w = self._dbw.next()
                self.dma(w[:], wkv_ap[:, 2048 + c:2048 + c + 512].rearrange("(k p) c -> p k c", p=128), [], [w], eng="pool")
                for mb in range(2):
                    ps = self.pp.next()
                    for kc in range(KC):
                        self.mm(ps[:], mn[:, kc, mb * 128:(mb + 1) * 128], w[:, kc, :], kc == 0, kc == KC - 1, [w, mn], [ps])
                    self.act(v[:, mb, c:c + 512], ps[:], AF.Copy, [ps], [v])
            hv = self.halves()
            xr_ = Ring([P.sb([128, KC, max(b - a for a, b in hv)], BF16, "xa_u") for _ in range(1)])
            for (h0, h1) in hv:
                x = xr_.next()
                self.dma(x[:, :, 0:h1 - h0], uT_d[:, :, h0:h1].rearrange("k p t -> p k t"), [uT_d], [x])

                def evq(ps, cb, n0, nn, h0=h0):
                    self.act(qT[:, cb, h0 + n0:h0 + n0 + nn], ps[:, 0:nn], AF.Copy, [ps], [qT], scale=sc)
                self.dense_B(x, wq_ap, 0, 2048, evq, ntok=h1 - h0)
        rPT = Ring([P.sb([128, 2, 512], BF16, "xa_pt") for _ in range(2)])
        rden = Ring([P.sb([128, 512], F32, "xa_den") for _ in range(2)])
        ro = Ring([P.sb([128, 4, 512], BF16, "xa_o") for _ in range(2)])
        for hh in range(4):
            for (n0, nn) in self.tok_chunks(0, self.TT):
                PT = rPT.next()
                for mb in range(2):
                    ps = self.pp.next()
                    for dc in range(4):
                        self.mm(ps[:, 0:nn], kT[:, hh * 4 + dc, mb * 128:(mb + 1) * 128], qT[:, hh * 4 + dc, n0:n0 + nn], dc == 0, dc == 3, [kT, qT], [ps])
                    self.act(PT[:, mb, 0:nn], ps[:, 0:nn], AF.Exp, [ps], [PT])
                psd = self.pp.next()
                for mb in range(2):
                    self.mm(psd[:, 0:nn], self.ones_b[:], PT[:, mb, 0:nn], mb == 0, mb == 1, [self.ones_b, PT], [psd])
                den = rden.next()
                P.op("dve", lambda e, den=den, psd=psd, nn=nn: e.reciprocal(out=den[:, 0:nn], in_=psd[:, 0:nn]), [psd], [den])
                o = ro.next()
                for dc in range(4):
                    ps = self.pp.next()
                    for mb in range(2):
                        self.mm(ps[:, 0:nn], v[:, mb, hh * 512 + dc * 128:hh * 512 + (dc + 1) * 128], PT[:, mb, 0:nn], mb == 0, mb == 1, [v, PT], [ps])
                    self.tt(o[:, dc, 0:nn], ps[:, 0:nn], den[:, 0:nn], ALU.mult, [ps, den], [o])
                self.dma(oT_d[hh * 4:hh * 4 + 4, :, n0:n0 + nn].rearrange("j p t -> p j t"), o[:, :, 0:nn], [o], [oT_d])

    def ffn_up(self, uT_d, wup_ap, cw_ap, cb_ap, maskF_ap, aT_d):
        P = self.P
        TT = self.TT
        x = self.load_xT(uT_d, "ff_u")
        cw = [self.load_cols(cw_ap[k:k + 1, :].rearrange("o (j p) -> (o j) p", p=128), 44, f"fcw{k}") for k in range(3)]
        cbs = self.load_cols(cb_ap.rearrange("o (j p) -> (o j) p", p=128), 44, "fcb")
        mF = P.sb([128, 128], F32, "maskF")
        self.dma(mF[:], maskF_ap.partition_broadcast(128), [], [mF])
        rw = Ring([P.sb([128, KC, 2, 256], BF16, "fw") for _ in range(2)])
        rgr = Ring([P.sb([128, 2 + TT], F32, "fgr") for _ in range(2)])
        rup = Ring([P.sb([128, TT], F32, "fup") for _ in range(2)])
        rac = Ring([P.sb([128, TT], F32, "fac") for _ in range(2)])
        rao = Ring([P.sb([128, TT], BF16, "fao") for _ in range(2)])
        for jp in range(0, 44, 2):
            w = rw.next()
            self.dma(w[:, :, 0, :], wup_ap[:, jp * 128:jp * 128 + 256].rearrange("(k p) c -> p k c", p=128), [], [w], eng="pool")
            self.dma(w[:, :, 1, :], wup_ap[:, D_FF + jp * 128:D_FF + jp * 128 + 256].rearrange("(k p) c -> p k c", p=128), [], [w], eng="pool")
            for b in range(2):
                j = jp + b
                gr = rgr.next(); up = rup.next(); ac = rac.next(); ao = rao.next()
                P.op("dve", lambda e, gr=gr: e.memset(gr[:, 0:2], 0.0), (), [gr])
                for (n0, nn) in self.tok_chunks(0, TT):
                    ps = self.pp.next()
                    for kc in range(KC):
                        self.mm(ps[:, 0:nn], w[:, kc, 0, b * 128:(b + 1) * 128], x[:, kc, n0:n0 + nn], kc == 0, kc == KC - 1, [w, x], [ps])
                    self.act(gr[:, 2 + n0:2 + n0 + nn], ps[:, 0:nn], AF.Copy, [ps], [gr])
                    ps2 = self.pp.next()
                    for kc in range(KC):
                        self.mm(ps2[:, 0:nn], w[:, kc, 1, b * 128:(b + 1) * 128], x[:, kc, n0:n0 + nn], kc == 0, kc == KC - 1, [w, x], [ps2])
                    self.act(up[:, n0:n0 + nn], ps2[:, 0:nn], AF.Copy, [ps2], [up])
                self.tt(gr[:, 2:130], gr[:, 2:130], mF[:], ALU.mult, [gr, mF], [gr])
                self.ts(ac[:], gr[:, 2:2 + TT], cw[2][:, j:j + 1], cbs[:, j:j + 1], ALU.mult, ALU.add, [gr, cw[2], cbs], [ac])
                for k in range(2):
                    self.stt(ac[:], gr[:, k:k + TT], cw[k][:, j:j + 1], ac[:], ALU.mult, ALU.add, [gr, cw[k], ac], [ac])
                self.act(ac[:], ac[:], AF.Gelu, [ac], [ac])
                self.tt(ao[:], ac[:], up[:], ALU.mult, [ac, up], [ao])
                self.dma(aT_d[j, :, :], ao[:], [ao], [aT_d])

    def final_norm(self, h_ap, hres, g_ap, out_ap, out_r, tiles):
        P = self.P
        gbc = P.sb([128, D], F32, "fgbc")
        self.dma(gbc[:], g_ap.partition_broadcast(128), [], [gbc])
        hb = Ring([P.sb([128, D], F32, "fhb") for _ in range(2)])
        ob = Ring([P.sb([128, D], F32, "fob") for _ in range(2)])
        junk = P.sb([128, D], BF16, "fjunk")
        stt_ = Ring([P.sb([128, 4], F32, "fst") for _ in range(2)])
        outs = []
        for i in tiles:
            h = hb.next(); o = ob.next(); s = stt_.next()
            self.dma(h[:], h_ap[i * 128:(i + 1) * 128, :], [hres], [h])
            self.act(junk[:], h[:], AF.Square, [h], [junk, s], accum_out=s[:, 0:1])
            self.act(s[:, 1:2], s[:, 0:1], AF.Ln, [s, self.epsc], [s], scale=1.0 / D, bias=self.epsc[:, 0:1])
            self.act(s[:, 2:3], s[:, 1:2], AF.Exp, [s], [s], scale=-0.5)
            self.stt(o[:], h[:], s[:, 2:3], gbc[:], ALU.mult, ALU.mult, [h, s, gbc], [o])
            outs.append(self.dma(out_ap[i * 128:(i + 1) * 128, :], o[:], [o], [out_r]))
        return outs


def _inp(nc, name, shape):
    return nc.dram_tensor(name, list(shape), F32, kind="ExternalInput").ap()


def _outp(nc, name, shape):
    return nc.dram_tensor(name, list(shape), F32, kind="ExternalOutput").ap()


def build_state(l, NT):
    nc = bass.Bass("TRN2", target_bir_lowering=False)
    TT = NT * 128
    h = _inp(nc, "h", [TT, D]); mask = _inp(nc, "mask", [1, TT]); g = _inp(nc, "mix_g", [1, D])
    w_in = _inp(nc, "w_in", [D, N_IN]); lbl = _inp(nc, "lbl", [2, 2048])
    cw = _inp(nc, "m2_cw", [4, 4096]); cb = _inp(nc, "m2_cb", [1, 4096])
    dtb = _inp(nc, "m2_dtb", [1, 32]); alog = _inp(nc, "m2_alog", [1, 32])
    o_shg = _outp(nc, "o_shg", [128, 16, 128]); o_dhg = _outp(nc, "o_dhg", [128, 16])
    o_sm2 = _outp(nc, "o_sm2", [128, 32, 64]); o_dm2 = _outp(nc, "o_dm2", [128, 32])
    k = K(nc, NT); P = k.P
    uT_d = T(nc.dram_tensor("o_uT", [KC, 128, TT], BF16, kind="ExternalOutput").ap(), Res())
    xA = T(nc.dram_tensor("o_hgA", [16, 128, TT], F32, kind="ExternalOutput").ap(), Res())
    xI = T(nc.dram_tensor("o_hgI", [16, 128, TT], BF16, kind="ExternalOutput").ap(), Res())
    xX = T(nc.dram_tensor("o_m2x", [8, 3, 128, TT], F32, kind="ExternalOutput").ap(), Res())
    with P.scope():
        k.norm_to_uT(h, T(h, Res()), g, uT_d)
    k.hg_setup_lb(lbl, l)
    Shg = P.sb([128, 16, 128], F32, "Shg"); dhg = P.sb([128, 16], F32, "dhg")
    P.op("dve", lambda e: e.memset(Shg[:], 0.0), (), [Shg])
    P.op("dve", lambda e: e.memset(dhg[:], 0.0), (), [dhg])
    with P.scope():
        k.hg_state(uT_d, w_in, mask, Shg, dhg, expA=xA, expI=xI)
    outs = [k.dma(o_shg, Shg[:], [Shg], [T(o_shg, Res())]), k.dma(o_dhg, dhg[:], [dhg], [T(o_dhg, Res())])]
    k.m2_setup(w_in, cw, cb, dtb, alog, None, None, mask)
    Sm2 = P.sb([128, 32, 64], F32, "Sm2"); Sbf = P.sb([128, 32, 64], BF16, "Sm2b"); dm2 = P.sb([128, 32], F32, "dm2")
    P.op("dve", lambda e: e.memset(Sm2[:], 0.0), (), [Sm2])
    P.op("dve", lambda e: e.memset(dm2[:], 0.0), (), [dm2])
    with P.scope():
        k.m2_state(uT_d, w_in, Sm2, dm2, expX=xX)
    outs += [k.dma(o_sm2, Sm2[:], [Sm2], [T(o_sm2, Res())]), k.dma(o_dm2, dm2[:], [dm2], [T(o_dm2, Res())])]
    outs += [uT_d.r.last_write, xA.r.last_write, xI.r.last_write, xX.r.last_write]
    P.emit(outs)
    return nc


def build_main(l, NT, last):
    nc = bass.Bass("TRN2", target_bir_lowering=False)
    TT = NT * 128
    h = _inp(nc, "h", [TT, D]); mask = _inp(nc, "mask", [1, TT]); maskF = _inp(nc, "maskF", [1, 128])
    g = _inp(nc, "mix_g", [1, D]); w_in = _inp(nc, "w_in", [D, N_IN]); lbl = _inp(nc, "lbl", [2, 2048])
    hgng = _inp(nc, "hg_ng", [1, 2048])
    cw = _inp(nc, "m2_cw", [4, 4096]); cb = _inp(nc, "m2_cb", [1, 4096])
    dtb = _inp(nc, "m2_dtb", [1, 32]); alog = _inp(nc, "m2_alog", [1, 32]); dsk = _inp(nc, "m2_dsk", [1, 32])
    m2ng = _inp(nc, "m2_ng", [1, 2048])
    wbh = _inp(nc, "w_bhg", [2048, D]); wbm = _inp(nc, "w_bm2", [2048, D]); wout = _inp(nc, "w_out", [D, D])
    mem = _inp(nc, "mem", [N_MEM, D]); memg = _inp(nc, "mem_g", [1, D]); xag = _inp(nc, "xa_g", [1, D])
    wq = _inp(nc, "xa_wq", [D, D]); wkv = _inp(nc, "xa_wkv", [D, 2 * D]); wo = _inp(nc, "xa_wo", [D, D])
    ffg = _inp(nc, "ffn_g", [1, D]); wup = _inp(nc, "ffn_wup", [D, 2 * D_FF])
    fcw = _inp(nc, "ffn_cw", [3, D_FF]); fcb = _inp(nc, "ffn_cb", [1, D_FF]); wdn = _inp(nc, "ffn_wdn", [D_FF, D])
    ps_hg = _inp(nc, "ps_hg", [7, 128, 16, 128]); pd_hg = _inp(nc, "pd_hg", [7, 128, 16])
    ps_m2 = _inp(nc, "ps_m2", [7, 128, 32, 64]); pd_m2 = _inp(nc, "pd_m2", [7, 128, 32])
    if last:
        fing = _inp(nc, "fin_g", [1, D])
    h_out = _outp(nc, "h_out", [TT, D])
    k = K(nc, NT); P = k.P
    hin = T(h, Res())
    uT_d = T(nc.dram_tensor("uT_in", [KC, 128, TT], BF16, kind="ExternalInput").ap(), Res())
    xA = T(nc.dram_tensor("hgA_in", [16, 128, TT], F32, kind="ExternalInput").ap(), Res())
    xI = T(nc.dram_tensor("hgI_in", [16, 128, TT], BF16, kind="ExternalInput").ap(), Res())
    xX = T(nc.dram_tensor("m2x_in", [8, 3, 128, TT], F32, kind="ExternalInput").ap(), Res())
    hgT_d = P.dram([KC, 128, TT], BF16, "hgT")
    m2T_d = P.dram([KC, 128, TT], BF16, "m2T")
    mT_d = P.dram([KC, 128, TT], BF16, "mT")
    h1_d = P.dram([TT, D], F32, "h1")
    h2_d = P.dram([TT, D], F32, "h2")
    h3_d = T(h_out, Res()) if not last else P.dram([TT, D], F32, "h3")
    memT_d = P.dram([KC, 128, N_MEM], BF16, "memT")
    oT_d = P.dram([KC, 128, TT], BF16, "oT")
    aT_d = P.dram([D_FF // 128, 128, TT], BF16, "aT")
    with P.scope():
        k.hg_setup(lbl, l, hgng)
        Shg = P.sb([128, 16, 128], F32, "Shg")
        P.op("dve", lambda e: e.memset(Shg[:], 0.0), (), [Shg])
        Sm2 = P.sb([128, 32, 64], F32, "Sm2"); Sbf = P.sb([128, 32, 64], BF16, "Sm2b")
        P.op("dve", lambda e: e.memset(Sm2[:], 0.0), (), [Sm2])
        with P.scope():
            t1 = Ring([P.sb([128, 16, 128], F32, "pst") for _ in range(2)])
            d1 = Ring([P.sb([128, 16], F32, "pdt") for _ in range(2)])
            t2 = Ring([P.sb([128, 32, 64], F32, "pst2") for _ in range(2)])
            d2 = Ring([P.sb([128, 32], F32, "pdt2") for _ in range(2)])
            for j in range(7):
                a = t1.next(); b = d1.next(); c = t2.next(); d_ = d2.next()
                k.dma(a[:], ps_hg[j], [], [a]); k.dma(b[:], pd_hg[j], [], [b])
                k.dma(c[:], ps_m2[j], [], [c]); k.dma(d_[:], pd_m2[j], [], [d_])
                k.act(b[:], b[:], AF.Exp, [b], [b]); k.act(d_[:], d_[:], AF.Exp, [d_], [d_])
                k.tt(Shg[:], Shg[:], b[:].unsqueeze(2).to_broadcast([128, 16, 128]), ALU.mult, [Shg, b], [Shg])
                k.tt(Shg[:], Shg[:], a[:], ALU.add, [Shg, a], [Shg])
                k.tt(Sm2[:], Sm2[:], d_[:].unsqueeze(2).to_broadcast([128, 32, 64]), ALU.mult, [Sm2, d_], [Sm2])
                k.tt(Sm2[:], Sm2[:], c[:], ALU.add, [Sm2, c], [Sm2])
        k.act(Sbf[:], Sm2[:], AF.Copy, [Sm2], [Sbf])
        with P.scope():
            k.hg_mixer(uT_d, w_in, mask, Shg, hgT_d, impA=xA, impI=xI)
        k.m2_setup(w_in, cw, cb, dtb, alog, dsk, m2ng, mask)
        with P.scope():
            k.m2_mixer(uT_d, w_in, Sm2, Sbf, m2T_d, impX=xX)
    with P.scope():
        k.mixer_merge(uT_d, hgT_d, m2T_d, w_in, wbh, wbm, mT_d)
    u3_d = P.dram([KC, 128, TT], BF16, "u3T")
    with P.scope():
        k.dense_A_res_norm(mT_d, wout, h, hin, h1_d, xag, u3_d)
    with P.scope():
        k.norm_to_uT(mem, T(mem, Res()), memg, memT_d, tiles=range(N_MEM // 128))
    with P.scope():
        k.xattn(u3_d, memT_d, wq, wkv, oT_d)
    u4_d = P.dram([KC, 128, TT], BF16, "u4T")
    with P.scope():
        k.dense_A_res_norm(oT_d, wo, h1_d[:, :], h1_d, h2_d, ffg, u4_d)
    with P.scope():
        k.ffn_up(u4_d, wup, fcw, fcb, maskF, aT_d)
    with P.scope():
        k.dense_A_res(aT_d, D_FF // 128, wdn, h2_d[:, :], h2_d, h3_d, 6)
    if last:
        with P.scope():
            outs = k.final_norm(h3_d[:, :], h3_d, fing, h_out, T(h_out, Res()), range(NT))
    else:
        outs = [h3_d.r.last_write]
    stats = P.emit(outs)
    return nc, stats


_PROG_CACHE = {}


def _prog(kind, l, NT, last=False):
    key = (kind, l, NT, last)
    if key not in _PROG_CACHE:
        if kind == "state":
            _PROG_CACHE[key] = build_state(l, NT)
        else:
            _PROG_CACHE[key] = build_main(l, NT, last)[0]
    return _PROG_CACHE[key]


def _halo_slices(hfull, n_cores, TOWN):
    out = []
    for c in range(n_cores):
        s = c * TOWN
        if c == 0:
            blk = np.concatenate([np.zeros((128, hfull.shape[1]), np.float32), hfull[0:TOWN]], 0)
        else:
            blk = hfull[s - 128:s + TOWN]
        out.append(np.ascontiguousarray(blk, dtype=np.float32))
    return out


def kernel_impl(inputs, n_cores=8):
    f32 = np.float32
    x = np.asarray(inputs["x"], f32)
    SEQ = x.shape[1]
    TOWN = SEQ // n_cores
    NT = TOWN // 128 + 1
    TT = NT * 128
    mem = np.ascontiguousarray(np.asarray(inputs["mem"], f32)[0])
    g = lambda k: np.asarray(inputs[k], f32)
    row = lambda a: np.ascontiguousarray(a.reshape(1, -1))
    cores = list(range(n_cores))
    m_state, m_main, m_f = [], [], []
    for c in cores:
        ms = np.zeros((1, TT), f32); mm_ = np.zeros((1, TT), f32)
        lo = 128 if c == 0 else 3
        ms[0, lo:TOWN + 3] = 1.0
        mm_[0, lo:] = 1.0
        m_state.append(ms); m_main.append(mm_)
        m_f.append(np.zeros((1, 128), f32) if c == 0 else np.ones((1, 128), f32))
    hfull = np.ascontiguousarray(x[0])
    for l in range(DEPTH):
        hs = _halo_slices(hfull, n_cores, TOWN)
        common_state = dict(mix_g=row(g("mix_norm_g")[l]), w_in=np.ascontiguousarray(g("w_in")[l]), lbl=np.ascontiguousarray(g("hg_lb_logits")),
                            m2_cw=np.ascontiguousarray(g("m2_conv_w")[l]), m2_cb=row(g("m2_conv_b")[l]),
                            m2_dtb=row(g("m2_dt_bias")[l]), m2_alog=row(g("m2_A_log")[l]))
        nc = _prog("state", l, NT)
        res = run_bass_kernel_spmd(nc, [dict(common_state, h=hs[c], mask=m_state[c]) for c in cores], core_ids=cores)
        st = res.results
        last = (l == DEPTH - 1)
        common = dict(common_state, hg_ng=row(g("hg_norm_g")[l]), m2_dsk=row(g("m2_D")[l]), m2_ng=row(g("m2_norm_g")[l]),
                      w_bhg=np.ascontiguousarray(g("w_branch_hg")[l]), w_bm2=np.ascontiguousarray(g("w_branch_m2")[l]),
                      w_out=np.ascontiguousarray(g("w_out")[l]), mem=mem, mem_g=row(g("mem_norm_g")), xa_g=row(g("xa_norm_g")[l]),
                      xa_wq=np.ascontiguousarray(g("xa_wq")[l]), xa_wkv=np.ascontiguousarray(g("xa_wkv")[l]),
                      xa_wo=np.ascontiguousarray(g("xa_wo")[l]), ffn_g=row(g("ffn_norm_g")[l]),
                      ffn_wup=np.ascontiguousarray(g("ffn_w_up")[l]), ffn_cw=np.ascontiguousarray(g("ffn_conv_w")[l]),
                      ffn_cb=row(g("ffn_conv_b")[l]), ffn_wdn=np.ascontiguousarray(g("ffn_w_down")[l]))
        if last:
            common["fin_g"] = row(g("final_norm_g"))
        maps = []
        for c in cores:
            ps_hg = np.zeros((7, 128, 16, 128), f32); pd_hg = np.zeros((7, 128, 16), f32)
            ps_m2 = np.zeros((7, 128, 32, 64), f32); pd_m2 = np.zeros((7, 128, 32), f32)
            for j in range(7):
                src = c - 7 + j
                if src >= 0:
                    ps_hg[j] = st[src]["o_shg"]; pd_hg[j] = st[src]["o_dhg"]
                    ps_m2[j] = st[src]["o_sm2"]; pd_m2[j] = st[src]["o_dm2"]
            maps.append(dict(common, h=hs[c], mask=m_main[c], maskF=m_f[c], ps_hg=ps_hg, pd_hg=pd_hg, ps_m2=ps_m2, pd_m2=pd_m2,
                             uT_in=st[c]["o_uT"], hgA_in=st[c]["o_hgA"], hgI_in=st[c]["o_hgI"], m2x_in=st[c]["o_m2x"]))
        nc = _prog("main", l, NT, last)
        res = run_bass_kernel_spmd(nc, maps, core_ids=cores)
        hfull = np.concatenate([np.asarray(r["h_out"])[128:] for r in res.results], 0)
    return np.ascontiguousarray(hfull.reshape(1, SEQ, D).astype(np.float32))


def kernel(**inputs):
    return kernel_impl(inputs, 8)
```

```python
import contextlib
import numpy as np
import concourse.bass as bass
import concourse.mybir as mybir
from concourse.bass_utils import run_bass_kernel_spmd

F32 = mybir.dt.float32
BF16 = mybir.dt.bfloat16
AF = mybir.ActivationFunctionType
ALU = mybir.AluOpType
AX = mybir.AxisListType

D = 2048
KC = D // 128
DEPTH = 2
N_MEM = 256
HG_H = 16
M2_H = 32
M2_G = 8
D_FF = 5632
N_IN = 18464
C_Q, C_F, C_I, C_OG = 0, 2048, 4096, 6144
C_Z = 8192
C_X = 10240
C_B = C_X + 2048
C_C = C_B + 1024
C_DT = 14336
C_GHG = 14368
C_GM2 = 16416
EPS = 1e-6
M2_EPS = 1e-5

ENGS = ["pe", "act", "dve", "pool", "sp"]
DMA_WINDOW = 8


class Res:
    __slots__ = ("name", "last_write", "readers")

    def __init__(self, name=""):
        self.name = name
        self.last_write = None
        self.readers = []


class Op:
    __slots__ = ("eng", "fn", "deps", "signaled", "is_dma", "dma_idx", "tok", "cc")

    def __init__(self, eng, fn, is_dma=False):
        self.eng = eng
        self.fn = fn
        self.deps = []
        self.signaled = False
        self.is_dma = is_dma
        self.dma_idx = -1
        self.tok = None
        self.cc = False


class T:
    def __init__(self, t, res):
        self.t = t
        self.r = res

    def __getitem__(self, k):
        return self.t[k]


class Prog:
    def __init__(self, nc):
        self.nc = nc
        self.ops = {e: [] for e in ENGS}
        self.dmas = {e: [] for e in ENGS}
        self.stack = contextlib.ExitStack()
        self.scopes = [self.stack]
        self.n = 0
        self.pending = {e: [] for e in ENGS}

    def sb(self, shape, dt, name=None):
        self.n += 1
        name = (name or "sb") + f"_{self.n}"
        t = self.scopes[-1].enter_context(self.nc.sbuf_tensor(name, list(shape), dt))
        return T(t, Res(name))

    @contextlib.contextmanager
    def scope(self):
        st = contextlib.ExitStack()
        self.scopes.append(st)
        try:
            yield
        finally:
            self.scopes.pop()
            self.barrier()
            st.close()

    def ps(self, shape, dt, name=None):
        self.n += 1
        name = (name or "ps") + f"_{self.n}"
        t = self.stack.enter_context(self.nc.psum_tensor(name, list(shape), dt))
        return T(t, Res(name))

    def dram(self, shape, dt, name=None):
        self.n += 1
        name = (name or "dr") + f"_{self.n}"
        t = self.nc.dram_tensor(name, list(shape), dt)
        return T(t.ap(), Res(name))

    def _record(self, o, reads, writes):
        deps = []
        seen = set()

        def add(d):
            if d is None or d is o or id(d) in seen:
                return
            if d.eng == "pe" and o.eng == "pe" and not d.is_dma:
                return
            seen.add(id(d))
            deps.append(d)

        for r in reads:
            add(r.r.last_write)
        for w in writes:
            add(w.r.last_write)
            for rd in w.r.readers:
                add(rd)
        for d in self.pending[o.eng]:
            add(d)
        self.pending[o.eng] = []
        if o.is_dma:
            q = self.dmas[o.eng]
            o.dma_idx = len(q)
            if o.dma_idx >= DMA_WINDOW:
                add(q[o.dma_idx - DMA_WINDOW])
            q.append(o)
        for d in deps:
            d.signaled = True
        o.deps = deps
        for r in reads:
            r.r.readers.append(o)
        for w in writes:
            w.r.last_write = o
            w.r.readers = []
        self.ops[o.eng].append(o)
        return o

    def op(self, eng, fn, reads=(), writes=()):
        return self._record(Op(eng, fn), reads, writes)

    def dma(self, eng, fn, reads=(), writes=()):
        return self._record(Op(eng, fn, is_dma=True), reads, writes)

    def barrier(self):
        tails = []
        for e in ENGS:
            if self.ops[e]:
                tails.append(self.ops[e][-1])
            tails.extend(self.dmas[e][-DMA_WINDOW:])
        for e in ENGS:
            self.pending[e] = list(tails)

    def emit(self, final_deps):
        nc = self.nc
        st = self.stack
        sems = {e: st.enter_context(nc.semaphore(f"s_{e}")) for e in ENGS}
        dsem = {
            e: [st.enter_context(nc.semaphore(f"d_{e}{i}")) for i in range(DMA_WINDOW)]
            for e in ENGS
            if self.dmas[e]
        }
        fin = Op("sp", None)
        fin.deps = list(final_deps)
        for d in fin.deps:
            d.signaled = True
        self.ops["sp"].append(fin)
        ccs = []
        for e in ENGS:
            c = 0
            for o in self.ops[e]:
                if o.is_dma:
                    o.tok = (dsem[e][o.dma_idx % DMA_WINDOW], 16 * (o.dma_idx // DMA_WINDOW + 1))
                elif o.cc:
                    o.tok = (st.enter_context(nc.semaphore(f"cc{len(ccs)}")), 1)
                    ccs.append(o)
                elif o.signaled:
                    c += 1
                    o.tok = (sems[e], c)
        engobj = {"pe": "tensor", "act": "scalar", "dve": "vector", "pool": "gpsimd", "sp": "sync"}
        stats = {}
        with nc.Block() as block:
            for e in ENGS:
                ops = self.ops[e]
                if not ops:
                    continue
                nwait = [0]

                def body(eng, ops=ops, nwait=nwait):
                    waited = {}
                    for o in ops:
                        need = {}
                        for d in o.deps:
                            s, v = d.tok
                            k = id(s)
                            if waited.get(k, 0) < v and need.get(k, (None, 0))[1] < v:
                                need[k] = (s, v)
                        for k, (s, v) in need.items():
                            eng.wait_ge(s, v)
                            waited[k] = v
                            nwait[0] += 1
                        if o.fn is None:
                            continue
                        ins = o.fn(eng)
                        if o.is_dma:
                            ins.then_inc(o.tok[0], 16)
                        elif o.cc:
                            ins.then_inc(o.tok[0], 1)
                        elif o.signaled:
                            ins.then_inc(o.tok[0], 1)

                getattr(block, engobj[e])(body)
                stats[e] = (len(ops), nwait[0])
        return stats


class Ring:
    def __init__(self, items):
        self.items = items
        self.i = 0

    def next(self):
        x = self.items[self.i % len(self.items)]
        self.i += 1
        return x


class K:
    def __init__(self, nc, NT):
        self.nc = nc
        self.P = Prog(nc)
        self.NT = NT
        self.TT = NT * 128
        self.consts()

    def act(self, out, in_, func, reads, writes, **kw):
        return self.P.op("act", lambda e: e.activation(out=out, in_=in_, func=func, **kw), reads, writes)

    def tt(self, out, in0, in1, op, reads, writes, eng="dve"):
        return self.P.op(eng, lambda e: e.tensor_tensor(out=out, in0=in0, in1=in1, op=op), reads, writes)

    def ts(self, out, in0, s1, s2, op0, op1, reads, writes, eng="dve"):
        if op1 is None:
            return self.P.op(eng, lambda e: e.tensor_scalar(out=out, in0=in0, scalar1=s1, scalar2=None, op0=op0), reads, writes)
        return self.P.op(eng, lambda e: e.tensor_scalar(out=out, in0=in0, scalar1=s1, scalar2=s2, op0=op0, op1=op1), reads, writes)

    def stt(self, out, in0, scalar, in1, op0, op1, reads, writes):
        return self.P.op("dve", lambda e: e.scalar_tensor_tensor(out=out, in0=in0, scalar=scalar, in1=in1, op0=op0, op1=op1), reads, writes)

    def mm(self, out, lhsT, rhs, start, stop, reads, writes):
        return self.P.op("pe", lambda e: e.matmul(out, lhsT, rhs, start=start, stop=stop), reads, writes)

    def tr(self, out, in_, ident, reads, writes):
        return self.P.op("pe", lambda e: e.transpose(out, in_, ident), reads, writes)

    def dma(self, out, in_, reads, writes, eng="sp"):
        return self.P.dma(eng, lambda e: e.dma_start(out=out, in_=in_), reads, writes)

    def consts(self):
        P = self.P
        self.idf = P.sb([128, 128], F32, "idf")
        self.idb = P.sb([128, 128], BF16, "idb")
        self.ones_f = P.sb([128, 128], F32, "ones_f")
        self.ones_b = P.sb([128, 128], BF16, "ones_b")
        self.triu = P.sb([128, 128], F32, "triu")
        self.epsc = P.sb([128, 4], F32, "epsc")
        iot = P.sb([128, 128], F32, "iot")
        P.op("pool", lambda e: e.iota(iot[:], [[1, 128]], base=0, channel_multiplier=-1,
                                      allow_small_or_imprecise_dtypes=True), (), [iot])
        self.ts(self.idf[:], iot[:], 0.0, None, ALU.is_equal, None, [iot], [self.idf])
        self.ts(self.idb[:], iot[:], 0.0, None, ALU.is_equal, None, [iot], [self.idb])
        self.ts(self.triu[:], iot[:], 0.0, None, ALU.is_ge, None, [iot], [self.triu])
        P.op("dve", lambda e: e.memset(self.ones_f[:], 1.0), (), [self.ones_f])
        P.op("dve", lambda e: e.memset(self.ones_b[:], 1.0), (), [self.ones_b])
        P.op("dve", lambda e: e.memset(self.epsc[:, 0:1], EPS), (), [self.epsc])
        P.op("dve", lambda e: e.memset(self.epsc[:, 1:2], M2_EPS), (), [self.epsc])
        P.op("dve", lambda e: e.memset(self.epsc[:, 2:3], 1.0), (), [self.epsc])
        self.MAXS = 768
        self.rmask = P.sb([128, self.MAXS], F32, "rmask")
        P.op("dve", lambda e: e.memset(self.rmask[:], 1.0), (), [self.rmask])
        P.op("dve", lambda e: e.memset(self.rmask[:].rearrange("p (c j) -> p c j", j=64)[:, :, 0:1], 0.0), (), [self.rmask])
        self.pp = Ring([P.ps([128, 512], F32, f"pp{i}") for i in range(4)])
        self.pq = Ring([P.ps([128, 512], F32, f"pq{i}") for i in range(2)])
        self.pt = Ring([P.ps([128, 1024], BF16, f"pt{i}") for i in range(2)])

    def load_cols(self, vec_ap, n, name):
        P = self.P
        rows = P.sb([n, 128], F32, name + "_r")
        cols = P.sb([128, n], F32, name)
        self.dma(rows[:], vec_ap, [], [rows])
        ps = self.pq.next()
        self.tr(ps[:, 0:n], rows[:], self.idf[0:n, 0:n], [rows, self.idf], [ps])
        self.act(cols[:], ps[:, 0:n], AF.Copy, [ps], [cols])
        return cols

    def norm_to_uT(self, h_ap, hres, g_ap, uT_d, tiles=None):
        P = self.P
        gbc = P.sb([128, D], F32, "gbc")
        self.dma(gbc[:], g_ap.partition_broadcast(128), [], [gbc])
        hb = Ring([P.sb([128, D], F32, "hb") for _ in range(4)])
        ub = Ring([P.sb([128, D], BF16, "ub") for _ in range(4)])
        junk = P.sb([128, D], BF16, "junk")
        stt_ = Ring([P.sb([128, 4], F32, "nst") for _ in range(4)])
        uo = Ring([P.sb([128, KC, 512], BF16, "uo") for _ in range(2)])
        tl_ = list(tiles if tiles is not None else range(self.NT))
        o = None
        for n_, i in enumerate(tl_):
            h = hb.next(); u = ub.next(); s = stt_.next()
            if n_ % 4 == 0:
                o = uo.next()
            q4 = (n_ % 4) * 128
            self.dma(h[:], h_ap[i * 128:(i + 1) * 128, :], [hres], [h])
            self.act(junk[:], h[:], AF.Square, [h], [junk, s], accum_out=s[:, 0:1])
            self.act(s[:, 1:2], s[:, 0:1], AF.Ln, [s, self.epsc], [s], scale=1.0 / D, bias=self.epsc[:, 0:1])
            self.act(s[:, 2:3], s[:, 1:2], AF.Exp, [s], [s], scale=-0.5)
            self.stt(u[:], h[:], s[:, 2:3], gbc[:], ALU.mult, ALU.mult, [h, s, gbc], [u])
            for half in range(2):
                pt = self.pt.next()
                for j in range(8):
                    kc = half * 8 + j
                    self.tr(pt[:, j * 128:(j + 1) * 128], u[:, kc * 128:(kc + 1) * 128], self.idb[:], [u, self.idb], [pt])
                self.act(o[:, half * 8:(half + 1) * 8, q4:q4 + 128], pt[:].rearrange("p (j t) -> p j t", j=8), AF.Copy, [pt], [o])
            if n_ % 4 == 3 or n_ == len(tl_) - 1:
                i0_ = tl_[n_ - (n_ % 4)]
                self.dma(uT_d[:, :, i0_ * 128:(i + 1) * 128].rearrange("k p t -> p k t"), o[:, :, 0:q4 + 128], [o], [uT_d])

    def hg_setup(self, lbl_ap, l, hgng_ap):
        lg0 = self.load_cols(lbl_ap[0:1, :].rearrange("o (h p) -> (o h) p", p=128), 16, "lg0")
        lg1 = self.load_cols(lbl_ap[1:2, :].rearrange("o (h p) -> (o h) p", p=128), 16, "lg1")
        self.lb = self.P.sb([128, 16], F32, "lb")
        self.oml = self.P.sb([128, 16], F32, "oml")
        if l == 0:
            self.tt(self.lb[:], lg0[:], lg0[:], ALU.subtract, [lg0], [self.lb])
        else:
            self.tt(self.lb[:], lg1[:], lg0[:], ALU.subtract, [lg0, lg1], [self.lb])
            self.act(self.lb[:], self.lb[:], AF.Sigmoid, [self.lb], [self.lb])
        self.ts(self.oml[:], self.lb[:], -1.0, 1.0, ALU.mult, ALU.add, [self.lb], [self.oml])
        if hgng_ap is not None:
            self.hgng = self.load_cols(hgng_ap.rearrange("o (h p) -> (o h) p", p=128), 16, "hgng")

    def hg_setup_lb(self, lbl_ap, l):
        self.hg_setup(lbl_ap, l, None)

    def load_uT_seg(self, uT_d, t0, nt, ring):
        u = ring.next()
        self.dma(u[:, :, 0:nt * 128], uT_d[:, :, t0 * 128:(t0 + nt) * 128].rearrange("k p t -> p k t"), [uT_d], [u])
        return u

    @staticmethod
    def interleave(gens):
        gens = list(gens)
        while gens:
            for g_ in list(gens):
                try:
                    next(g_)
                except StopIteration:
                    gens.remove(g_)

    def pipeline(self, units, *stages):
        units = list(units)
        ns = len(stages)
        for i in range(len(units) + ns - 1):
            gens = []
            for j in range(ns - 1, -1, -1):
                if 0 <= i - j < len(units):
                    gens.append(stages[j](units[i - j]))
            self.interleave(gens)

    def make_segs(self, maxt):
        n = -(-self.NT // maxt)
        base, extra = divmod(self.NT, n)
        segs, t = [], 0
        for i in range(n):
            c = base + (1 if i < extra else 0)
            segs.append((t, c)); t += c
        return segs

    def hg_mixer(self, uT_d, w_ap, mask_ap, S, hgT_d, state_only=False, dec_out=None, impA=None, impI=None):
        P = self.P
        MAXS = self.MAXS
        MC = MAXS // 64
        so = state_only
        useg = Ring([P.sb([128, KC, MAXS], BF16, "useg") for _ in range(1)])
        wts = Ring([P.sb([128, KC, 4, 128], BF16, "hgw") for _ in range(2)])
        mk = Ring([P.sb([128, MAXS], F32, "mk") for _ in range(2)])

        def ring(n, shape, dt, nm):
            return Ring([P.sb(shape, dt, nm) for _ in range(n)])

        rA = ring(2, [128, MAXS], F32, "hA")
        rB, rC, rD = (ring(1, [128, MAXS], F32, n) for n in ("hB", "hC", "hD"))
        rE = ring(2, [128, 2 if so else MAXS], F32, "hE")
        rG = ring(1, [128, MAXS], BF16, "hG")
        rI = ring(2, [128, MAXS], BF16, "hI")
        q_ = 2 if so else MAXS
        rSG = ring(3, [128, q_], F32, "hSG")
        rO = ring(1, [128, q_], F32, "hO")
        rF, rHo = (ring(2, [128, q_], BF16, n) for n in ("hF", "hHo"))
        rsq = ring(1, [128, q_], F32, "hsq")
        rcs = ring(2, [128, 4, 16], F32, "hcs")
        rkv = ring(2, [64, MC, 256], BF16, "hkv")
        rsT = ring(2, [64, MC, 2 if so else 64], BF16, "hsT")
        rtm = ring(2, [128, MC, 128], F32, "htm")
        rSp = ring(3, [128, 128], BF16, "hSp")
        rrs = ring(2, [128, 512], F32, "hrs")
        scale = 128 ** -0.5
        seg = {}

        def stage0(un):
            (t0, nt, hd) = un
            NS = nt * 128
            NCH = NS // 64
            if hd == 0:
                seg["u"] = self.load_uT_seg(uT_d, t0, nt, useg)
                m_ = mk.next()
                self.dma(m_[:, 0:NS], mask_ap[0:1, t0 * 128:t0 * 128 + NS].partition_broadcast(128), [], [m_])
                seg["mask"] = m_
            u = seg["u"]; mask = seg["mask"]
            chunks = [(n0, min(512, NS - n0)) for n0 in range(0, NS, 512)]
            wt = wts.next()
            mats = [(1, C_F), (2, C_I)] if so else [(1, C_F), (0, C_Q), (2, C_I), (3, C_OG)]
            if impA is not None:
                mats = [(0, C_Q), (3, C_OG)]
            for (m, c0) in mats:
                self.dma(wt[:, :, m, :], w_ap[:, c0 + hd * 128:c0 + (hd + 1) * 128].rearrange("(k p) c -> p k c", p=128),
                         [], [wt], eng="pool")
            A, E, I, SG = (r.next() for r in (rA, rE, rI, rSG))
            if impA is not None:
                self.dma(A[:, 0:NS], impA[hd, :, t0 * 128:t0 * 128 + NS], [impA], [A])
                self.dma(I[:, 0:NS], impI[hd, :, t0 * 128:t0 * 128 + NS], [impI], [I])
            st_ = dict(NS=NS, NCH=NCH, chunks=chunks, SG=SG, A=A, E=E, I=I, mask=mask, t0=t0, hd=hd)
            seg[("st", t0, hd)] = st_
            for (m, c0) in mats:
                for (n0, nn) in chunks:
                    ps = self.pp.next()
                    for kc in range(KC):
                        self.mm(ps[:, 0:nn], wt[:, kc, m, :], u[:, kc, n0:n0 + nn], kc == 0, kc == KC - 1, [wt, u], [ps])
                    if m == 1:
                        self.act(A[:, n0:n0 + nn], ps[:, 0:nn], AF.Sigmoid, [ps], [A])
                    elif m == 0:
                        self.act(E[:, n0:n0 + nn], ps[:, 0:nn], AF.Copy, [ps], [E], scale=scale)
                    elif m == 2:
                        self.act(I[:, n0:n0 + nn], ps[:, 0:nn], AF.Copy, [ps], [I])
                    else:
                        self.act(SG[:, n0:n0 + nn], ps[:, 0:nn], AF.Sigmoid, [ps], [SG])
                    yield

        def stage1(un):
            (t0, nt, hd) = un
            st_ = seg[("st", t0, hd)]
            NS, NCH, A, E, I, mask = (st_[k_] for k_ in ("NS", "NCH", "A", "E", "I", "mask"))
            B, C, Dd, G = (r.next() for r in (rB, rC, rD, rG))
            F = rF.next()
            cs = rcs.next(); kv = rkv.next(); sTa = rsT.next(); tm = rtm.next()
            st_.update(F=F, cs=cs, kv=kv, sT=sTa, tm=tm)
            self.ts(A[:, 0:NS], A[:, 0:NS], self.oml[:, hd:hd + 1], self.lb[:, hd:hd + 1], ALU.mult, ALU.add, [A, self.oml, self.lb], [A])
            self.ts(B[:, 0:NS], A[:, 0:NS], -1.0, 1.0, ALU.mult, ALU.add, [A], [B])
            yield
            self.tt(B[:, 0:NS], B[:, 0:NS], mask[:, 0:NS], ALU.mult, [B, mask], [B])
            self.act(A[:, 0:NS], A[:, 0:NS], AF.Ln, [A], [A])
            self.tt(A[:, 0:NS], A[:, 0:NS], mask[:, 0:NS], ALU.mult, [A, mask], [A])
            yield
            P.op("dve", lambda e, C=C, A=A, NS=NS: e.tensor_tensor_scan(out=C[:, 0:NS], data0=self.rmask[:, 0:NS], data1=A[:, 0:NS],
                                                                    initial=0.0, op0=ALU.mult, op1=ALU.add), [A, self.rmask], [C])
            C3 = C[:, 0:NS].rearrange("p (c j) -> p c j", j=64)
            A3 = A[:, 0:NS].rearrange("p (c j) -> p c j", j=64)
            yield
            self.tt(A3, C3, C3[:, :, 31:32].to_broadcast([128, NCH, 64]), ALU.subtract, [C], [A])
            self.tt(cs[:, 3, 0:NCH], C3[:, :, 63], C3[:, :, 31], ALU.subtract, [C], [cs])
            self.act(cs[:, 0, 0:NCH], C3[:, :, 63], AF.Exp, [C], [cs])
            self.act(cs[:, 1, 0:NCH], cs[:, 3, 0:NCH], AF.Exp, [cs], [cs])
            self.act(cs[:, 2, 0:NCH], C3[:, :, 31], AF.Exp, [C], [cs])
            if dec_out is not None:
                P.op("dve", lambda e, cs=cs, C3=C3, NCH=NCH: e.tensor_reduce(out=cs[:, 3, 0:1], in_=C3[:, :, 63], axis=AX.X, op=ALU.add), [C], [cs])
                self.tt(dec_out[:, hd:hd + 1], dec_out[:, hd:hd + 1], cs[:, 3, 0:1], ALU.add, [dec_out, cs], [dec_out])
            self.act(Dd[:, 0:NS], A[:, 0:NS], AF.Exp, [A], [Dd], scale=-1.0)
            yield
            self.tt(G[:, 0:NS], B[:, 0:NS], Dd[:, 0:NS], ALU.mult, [B, Dd], [G])
            if not so:
                self.act(A[:, 0:NS], A[:, 0:NS], AF.Exp, [A], [A])
                self.tt(F[:, 0:NS], E[:, 0:NS], A[:, 0:NS], ALU.mult, [E, A], [F])
            yield
            def state_mm(c):
                pst = self.pq.next()
                self.mm(pst[:, 0:128], kv[:, c, 0:128], kv[:, c, 128:256], True, True, [kv], [pst])
                self.ts(tm[:, c, :], pst[:, 0:128], cs[:, 1, c:c + 1], None, ALU.mult, None, [pst, cs], [tm])

            for c in range(NCH):
                c0 = c * 64
                pt = self.pt.next()
                self.tr(pt[0:64, 0:128], G[:, c0:c0 + 64], self.idb[:], [G, self.idb], [pt])
                self.tr(pt[0:64, 128:256], I[:, c0:c0 + 64], self.idb[:], [I, self.idb], [pt])
                self.act(kv[:, c, :], pt[0:64, 0:256], AF.Copy, [pt], [kv])
                if not so:
                    psc = self.pq.next()
                    self.mm(psc[0:64, 0:64], G[:, c0:c0 + 64], F[:, c0:c0 + 64], True, True, [G, F], [psc])
                    self.tt(sTa[:, c, :], psc[0:64, 0:64], self.triu[0:64, 0:64], ALU.mult, [psc, self.triu], [sTa])
                if c > 0:
                    state_mm(c - 1)
                yield
            state_mm(NCH - 1)
            yield

        def stage2(un):
            (t0, nt, hd) = un
            d = seg.pop(("st", t0, hd))
            NS, NCH, chunks = d["NS"], d["NCH"], d["chunks"]
            SG, F, cs, kv, sTa, tm = (d[k_] for k_ in ("SG", "F", "cs", "kv", "sT", "tm"))
            O = rO.next(); Ho = rHo.next()
            Sh = S[:, hd, :]
            Sp = None
            if not so:
                Sp = rSp.next()
                self.ts(Sp[:], Sh, cs[:, 2, 0:1], None, ALU.mult, None, [S, cs], [Sp])
                yield
            for c in range(NCH):
                c0 = c * 64
                self.stt(Sh, Sh, cs[:, 0, c:c + 1], tm[:, c, :], ALU.mult, ALU.add, [S, cs, tm], [S])
                if not so:
                    Spn = None
                    if c + 1 < NCH:
                        Spn = rSp.next()
                        self.ts(Spn[:], Sh, cs[:, 2, c + 1:c + 2], None, ALU.mult, None, [S, cs], [Spn])
                    po = self.pq.next()
                    self.mm(po[:, 0:64], kv[:, c, 128:256], sTa[:, c, :], True, False, [kv, sTa], [po])
                    self.mm(po[:, 0:64], Sp[:], F[:, c0:c0 + 64], False, True, [Sp, F], [po])
                    self.tt(O[:, c0:c0 + 64], po[:, 0:64], SG[:, c0:c0 + 64], ALU.mult, [po, SG], [O])
                    Sp = Spn
                yield
            if so:
                return
            sq = rsq.next()
            self.tt(sq[:, 0:NS], O[:, 0:NS], O[:, 0:NS], ALU.mult, [O], [sq])
            for (n0, nn) in chunks:
                ps = self.pp.next()
                self.mm(ps[:, 0:nn], self.ones_f[:], sq[:, n0:n0 + nn], True, True, [self.ones_f, sq], [ps])
                rs = rrs.next()
                self.act(rs[:, 0:nn], ps[:, 0:nn], AF.Ln, [ps, self.epsc], [rs], scale=1.0 / 128, bias=self.epsc[:, 0:1])
                self.act(rs[:, 0:nn], rs[:, 0:nn], AF.Exp, [rs], [rs], scale=-0.5)
                self.stt(Ho[:, n0:n0 + nn], O[:, n0:n0 + nn], self.hgng[:, hd:hd + 1], rs[:, 0:nn], ALU.mult, ALU.mult,
                         [O, self.hgng, rs], [Ho])
                yield
            self.dma(hgT_d[hd, :, t0 * 128:t0 * 128 + NS], Ho[:, 0:NS], [Ho], [hgT_d])

        units = [(t0, nt, hd) for (t0, nt) in self.make_segs(MAXS // 128) for hd in range(HG_H)]
        self.pipeline(units, stage0, stage1, stage2)

    def m2_setup(self, w_ap, convw_ap, convb_ap, dtb_ap, alog_ap, dsk_ap, ng_ap, mask_ap):
        P = self.P
        self.cw = [self.load_cols(convw_ap[k:k + 1, :].rearrange("o (j p) -> (o j) p", p=128), 32, f"cw{k}") for k in range(4)]
        self.cbias = self.load_cols(convb_ap.rearrange("o (j p) -> (o j) p", p=128), 32, "cbias")
        self.maskT = self.load_cols(mask_ap.rearrange("o (j p) -> (o j) p", p=128), self.NT, "maskT")
        self.dtb = P.sb([128, 32], F32, "dtb")
        self.negA = P.sb([128, 32], F32, "negA")
        self.dsk = P.sb([128, 32], F32, "dsk")
        self.m2ng_ap = ng_ap
        self.dma(self.dtb[:], dtb_ap.partition_broadcast(128), [], [self.dtb])
        self.dma(self.negA[:], alog_ap.partition_broadcast(128), [], [self.negA])
        if dsk_ap is not None:
            self.dma(self.dsk[:], dsk_ap.partition_broadcast(128), [], [self.dsk])
        self.act(self.negA[:], self.negA[:], AF.Exp, [self.negA], [self.negA])
        self.ts(self.negA[:], self.negA[:], -1.0, None, ALU.mult, None, [self.negA], [self.negA])
        self.wdt = P.sb([128, KC, 32], BF16, "wdt")
        self.dma(self.wdt[:], w_ap[:, C_DT:C_DT + 32].rearrange("(k p) c -> p k c", p=128), [], [self.wdt], eng="pool")
        self.strict = P.sb([128, 128], F32, "strict")
        self.ts(self.strict[:], self.triu[:], -1.0, 1.0, ALU.mult, ALU.add, [self.triu], [self.strict])
        self.carry = P.sb([128, 8, 4, 3], F32, "carry")
        P.op("dve", lambda e: e.memset(self.carry[:], 0.0), (), [self.carry])

    def m2_mixer(self, uT_d, w_ap, S, Sbf, m2T_d, state_only=False, dec_out=None, impX=None):
        P = self.P
        MT = 5
        MAXS = MT * 128
        so = state_only

        def ring(n, shape, dt, nm):
            return Ring([P.sb(shape, dt, nm) for _ in range(n)])

        useg = ring(1, [128, KC, MAXS], BF16, "useg")
        imp = impX is not None
        wts = ring(2, [128, KC, 384 if imp else 768], BF16, "m2w")

        def xoff(b):
            return 256 if (imp and b == 3) else 256 + b * 128

        rxr = ring(2, [128, 4, 3 + MAXS], F32, "xr")
        racc = ring(1, [128, 2, MAXS], F32, "xacc")
        rtmpc = ring(1, [128, MAXS], F32, "ctmp")
        rBT = ring(1, [128, MAXS], BF16, "BT")
        rCT = ring(2, [128, 2 if so else MAXS], BF16, "CT")
        rzs = ring(3, [128, 1 if so else MT, 256], F32, "zs")
        rm2o = ring(2, [128, 2, 2 if so else MAXS], BF16, "m2o")
        rng = ring(2, [128, 2 if so else 256], F32, "m2ngs")
        rybuf = ring(2, [128, 1 if so else MT, 256], F32, "ybuf")
        rxddA = ring(2, [128, MT, 256], BF16, "xddA")
        rBtmA = ring(2, [128, MT, 128], BF16, "BtmA")
        rps = Ring([dict((n, P.sb([128, MT, 32], F32, n)) for n in ("dtS", "aS", "cumS", "lastS", "ecS", "elS", "ddS")) for _ in range(2)])
        rxs = ring(2, [128, 256], F32, "xs")
        rxdt = ring(4, [128, 4, 64], BF16, "xdt")
        q_ = 2 if so else 128
        rcbm = ring(3, [128, q_], F32, "cbm")
        rM1 = ring(3, [128, 4, q_], F32, "M1")
        rEL = ring(2, [128, 4, q_], F32, "EL")
        rWT = ring(3, [128, 4, q_], BF16, "WT")
        ry = ring(4, [128, 2 * q_], F32, "y")
        ry2 = ring(3, [128, 2 * q_], F32, "y2")
        ryn = ring(3, [128, 2 * q_], BF16, "yn")
        rst = ring(4, [128, 4], F32, "mst")
        rtS = ring(2, [128, 4, 64], F32, "tS")
        rpSb = ring(3, [128, 256], F32, "pSb")
        junk = P.sb([128, 256], BF16, "mjunk")
        seg = {}

        def seg_prep(t0, nt):
            u = self.load_uT_seg(uT_d, t0, nt, useg)
            seg["u"] = u
            ps_ = rps.next()
            seg["ps"] = ps_
            dtS, aS, cumS, lastS, ecS, elS, ddS = (ps_[n] for n in ("dtS", "aS", "cumS", "lastS", "ecS", "elS", "ddS"))
            for ti in range(nt):
                ps = self.pq.next()
                for kc in range(KC):
                    self.mm(ps[:, 0:32], u[:, kc, ti * 128:(ti + 1) * 128], self.wdt[:, kc, :], kc == 0, kc == KC - 1, [u, self.wdt], [ps])
                self.tt(dtS[:, ti, :], ps[:, 0:32], self.dtb[:], ALU.add, [ps, self.dtb], [dtS])
            self.act(dtS[:, 0:nt, :], dtS[:, 0:nt, :], AF.Exp, [dtS], [dtS])
            self.act(dtS[:, 0:nt, :], dtS[:, 0:nt, :], AF.Ln, [dtS, self.epsc], [dtS], bias=self.epsc[:, 2:3])
            for ti in range(nt):
                self.ts(dtS[:, ti, :], dtS[:, ti, :], self.maskT[:, t0 + ti:t0 + ti + 1], None, ALU.mult, None, [dtS, self.maskT], [dtS])
                self.tt(aS[:, ti, :], dtS[:, ti, :], self.negA[:], ALU.mult, [dtS, self.negA], [aS])
            for ti in range(nt):
                ps = self.pq.next()
                self.mm(ps[:, 0:32], self.triu[:], aS[:, ti, :], True, True, [self.triu, aS], [ps])
                self.mm(ps[:, 32:64], self.ones_f[:], aS[:, ti, :], True, True, [self.ones_f, aS], [ps])
                self.act(cumS[:, ti, :], ps[:, 0:32], AF.Copy, [ps], [cumS])
                self.act(lastS[:, ti, :], ps[:, 32:64], AF.Copy, [ps], [lastS])
                if dec_out is not None:
                    self.tt(dec_out[:], dec_out[:], lastS[:, ti, :], ALU.add, [dec_out, lastS], [dec_out])
            self.act(ecS[:, 0:nt, :], cumS[:, 0:nt, :], AF.Exp, [cumS], [ecS])
            self.act(elS[:, 0:nt, :], lastS[:, 0:nt, :], AF.Exp, [lastS], [elS])
            self.tt(ddS[:, 0:nt, :], lastS[:, 0:nt, :], cumS[:, 0:nt, :], ALU.subtract, [lastS, cumS], [ddS])
            self.act(ddS[:, 0:nt, :], ddS[:, 0:nt, :], AF.Exp, [ddS], [ddS])
            self.tt(ddS[:, 0:nt, :], ddS[:, 0:nt, :], dtS[:, 0:nt, :], ALU.mult, [ddS, dtS], [ddS])

        def stage0(un):
            (t0, nt, gi) = un
            NS = nt * 128
            if gi == 0:
                seg_prep(t0, nt)
                yield
            u = seg["u"]
            chunks = [(n0, min(512, NS - n0)) for n0 in range(0, NS, 512)]
            wt = wts.next()
            srcs = [(0, C_Z + gi * 256, 256), (256, C_X + gi * 256, 256), (512, C_B + gi * 128, 128), (640, C_C + gi * 128, 128)]
            for (o0, c0, w_) in srcs:
                if so and o0 in (0, 640):
                    continue
                if impX is not None and o0 in (256, 512):
                    continue
                if imp and o0 == 640:
                    o0 = 256
                self.dma(wt[:, :, o0:o0 + w_], w_ap[:, c0:c0 + w_].rearrange("(k p) c -> p k c", p=128), [], [wt], eng="pool")
            xr = rxr.next(); zs = rzs.next()
            if impX is not None:
                self.dma(xr[:, 0:3, 3:3 + NS], impX[gi, :, :, t0 * 128:t0 * 128 + NS].rearrange("b p t -> p b t"), [impX], [xr])
            seg[("st", t0, gi)] = dict(xr=xr, zs=zs, ps=seg["ps"], chunks=chunks)
            blks = [0, 1, 2] if so else [0, 1, 2, 3]
            if not so:
                for ti in range(nt):
                    ps = self.pp.next()
                    for kc in range(KC):
                        self.mm(ps[:, 0:256], u[:, kc, ti * 128:(ti + 1) * 128], wt[:, kc, 0:256], kc == 0, kc == KC - 1, [u, wt], [ps])
                    self.act(zs[:, ti, :], ps[:, 0:256], AF.Silu, [ps], [zs])
                    yield
            for b in blks:
                if impX is not None and b < 3:
                    continue
                for (n0, nn) in chunks:
                    ps = self.pp.next()
                    for kc in range(KC):
                        self.mm(ps[:, 0:nn], wt[:, kc, xoff(b):xoff(b) + 128], u[:, kc, n0:n0 + nn], kc == 0, kc == KC - 1, [wt, u], [ps])
                    self.act(xr[:, b, 3 + n0:3 + n0 + nn], ps[:, 0:nn], AF.Copy, [ps], [xr])
                    yield

        def stage1(un):
            (t0, nt, gi) = un
            NS = nt * 128
            st_ = seg[("st", t0, gi)]
            xr = st_["xr"]; ps_ = st_["ps"]
            dtS, aS, ddS = ps_["dtS"], ps_["aS"], ps_["ddS"]
            acc = racc.next(); BT = rBT.next(); CT = rCT.next(); m2o = rm2o.next()
            ybuf = rybuf.next(); xddA = rxddA.next(); BtmA = rBtmA.next(); ngs = rng.next()
            if not so:
                self.dma(ngs[:], self.m2ng_ap[0:1, gi * 256:(gi + 1) * 256].partition_broadcast(128), [], [ngs])
            st_.update(CT=CT, m2o=m2o, ybuf=ybuf, xddA=xddA, BtmA=BtmA, ngs=ngs)
            blks = [0, 1, 2] if so else [0, 1, 2, 3]
            for b in blks:
                j = (gi * 2 + b) if b < 2 else (16 + gi if b == 2 else 24 + gi)
                self.act(xr[:, b, 0:3], self.carry[:, gi, b, :], AF.Copy, [self.carry], [xr])
                self.act(self.carry[:, gi, b, :], xr[:, b, NS:NS + 3], AF.Copy, [xr], [self.carry])
                tmpc = rtmpc.next()
                self.ts(tmpc[:, 0:NS], xr[:, b, 3:3 + NS], self.cw[3][:, j:j + 1], self.cbias[:, j:j + 1], ALU.mult, ALU.add,
                        [xr, self.cw[3], self.cbias], [tmpc])
                for k_ in range(3):
                    self.stt(tmpc[:, 0:NS], xr[:, b, k_:k_ + NS], self.cw[k_][:, j:j + 1], tmpc[:, 0:NS], ALU.mult, ALU.add,
                             [xr, self.cw[k_], tmpc], [tmpc])
                if b < 2:
                    self.act(acc[:, b, 0:NS], tmpc[:, 0:NS], AF.Silu, [tmpc], [acc])
                elif b == 2:
                    self.act(BT[:, 0:NS], tmpc[:, 0:NS], AF.Silu, [tmpc], [BT])
                else:
                    self.act(CT[:, 0:NS], tmpc[:, 0:NS], AF.Silu, [tmpc], [CT])
                yield
            hs = slice(gi * 4, gi * 4 + 4)
            tl = {}

            def phaseA(ti):
                tk = slice(ti * 128, (ti + 1) * 128)
                px = self.pq.next()
                self.tr(px[:, 0:128], acc[:, 0, tk], self.idf[:], [acc, self.idf], [px])
                self.tr(px[:, 128:256], acc[:, 1, tk], self.idf[:], [acc, self.idf], [px])
                pb = self.pt.next()
                self.tr(pb[:, 0:128], BT[:, tk], self.idb[:], [BT, self.idb], [pb])
                xs = rxs.next()
                self.act(xs[:], px[:, 0:256], AF.Copy, [px], [xs])
                self.act(BtmA[:, ti, :], pb[:, 0:128], AF.Copy, [pb], [BtmA])
                xs3 = xs[:].rearrange("p (h q) -> p h q", q=64)
                self.tt(xddA[:, ti, :].rearrange("p (h q) -> p h q", q=64), xs3, ddS[:, ti, hs].unsqueeze(2).to_broadcast([128, 4, 64]),
                        ALU.mult, [xs, ddS], [xddA])
                if so:
                    return
                xdt = rxdt.next(); y2 = ry2.next(); cbm = rcbm.next(); M1 = rM1.next()
                self.tt(xdt[:], xs3, dtS[:, ti, hs].unsqueeze(2).to_broadcast([128, 4, 64]), ALU.mult, [xs, dtS], [xdt])
                self.tt(y2[:].rearrange("p (h q) -> p h q", q=64), xs3, self.dsk[:, hs].unsqueeze(2).to_broadcast([128, 4, 64]),
                        ALU.mult, [xs, self.dsk], [y2])
                pcb = self.pq.next()
                self.mm(pcb[:, 0:128], BT[:, tk], CT[:, tk], True, True, [BT, CT], [pcb])
                self.tt(cbm[:], pcb[:, 0:128], self.triu[:], ALU.mult, [pcb, self.triu], [cbm])
                self.tt(M1[:], self.strict[:].unsqueeze(1).to_broadcast([128, 4, 128]),
                        aS[:, ti, hs].unsqueeze(2).to_broadcast([128, 4, 128]), ALU.mult, [self.strict, aS], [M1])
                tl[ti] = dict(xdt=xdt, y2=y2, cbm=cbm, M1=M1)

            def phaseB(ti):
                d_ = tl[ti]
                pD = self.pq.next()
                for h in range(4):
                    self.mm(pD[:, h * 128:(h + 1) * 128], d_["M1"][:, h, :], self.triu[:], True, True, [d_["M1"], self.triu], [pD])
                EL = rEL.next()
                self.act(EL[:], pD[:].rearrange("p (h t) -> p h t", h=4), AF.Exp, [pD], [EL])
                WT = rWT.next()
                self.tt(WT[:], EL[:], d_["cbm"][:].unsqueeze(1).to_broadcast([128, 4, 128]), ALU.mult, [EL, d_["cbm"]], [WT])
                d_["WT"] = WT

            def phaseC(ti):
                d_ = tl.pop(ti)
                py = self.pq.next()
                for h in range(4):
                    self.mm(py[:, h * 64:(h + 1) * 64], d_["WT"][:, h, :], d_["xdt"][:, h, :], True, True, [d_["WT"], d_["xdt"]], [py])
                self.tt(ybuf[:, ti, :], d_["y2"][:], py[:, 0:256], ALU.add, [d_["y2"], py], [ybuf])

            for step in range(nt + (0 if so else 2)):
                if not so and 0 <= step - 2 < nt:
                    phaseC(step - 2)
                if not so and 0 <= step - 1 < nt:
                    phaseB(step - 1)
                if step < nt:
                    phaseA(step)
                yield

        def stage2(un):
            (t0, nt, gi) = un
            NS = nt * 128
            d = seg.pop(("st", t0, gi))
            zs = d["zs"]; ps_ = d["ps"]
            ecS, elS = ps_["ecS"], ps_["elS"]
            CT, m2o, ybuf, xddA, BtmA, ngs = (d[k_] for k_ in ("CT", "m2o", "ybuf", "xddA", "BtmA", "ngs"))
            Sg = S[:, gi * 4:(gi + 1) * 4, :]
            Sbg = Sbf[:, gi * 4:(gi + 1) * 4, :]
            hs = slice(gi * 4, gi * 4 + 4)
            def ps_mm(ti):
                pS_ = self.pq.next()
                self.mm(pS_[:, 0:256], BtmA[:, ti, :], xddA[:, ti, :], True, True, [BtmA, xddA], [pS_])
                h_ = rpSb.next()
                self.act(h_[:], pS_[:, 0:256], AF.Copy, [pS_], [h_])
                return h_, 0

            nxt = ps_mm(0)
            yl = {}

            def post1(ti):
                y = yl[ti]["y"]
                st = rst.next()
                self.act(junk[:], y[:], AF.Square, [y], [junk, st], accum_out=st[:, 0:1])
                self.act(st[:, 1:2], st[:, 0:1], AF.Ln, [st, self.epsc], [st], scale=1.0 / 256, bias=self.epsc[:, 1:2])
                self.act(st[:, 2:3], st[:, 1:2], AF.Exp, [st], [st], scale=-0.5)
                yl[ti]["st"] = st

            def post2(ti):
                y = yl[ti]["y"]; st = yl[ti]["st"]
                yn = ryn.next()
                self.stt(yn[:], y[:], st[:, 2:3], ngs[:], ALU.mult, ALU.mult, [y, st, ngs], [yn])
                yl[ti]["yn"] = yn

            def post3(ti):
                yn = yl.pop(ti)["yn"]
                tk = slice(ti * 128, (ti + 1) * 128)
                po = self.pt.next()
                self.tr(po[:, 0:128], yn[:, 0:128], self.idb[:], [yn, self.idb], [po])
                self.tr(po[:, 128:256], yn[:, 128:256], self.idb[:], [yn, self.idb], [po])
                self.act(m2o[:, :, tk], po[:, 0:256].rearrange("p (j t) -> p j t", j=2), AF.Copy, [po], [m2o])

            for step in range(nt + (0 if so else 3)):
                ti = step
                if not so:
                    if 0 <= step - 3 < nt:
                        post3(step - 3)
                    if 0 <= step - 2 < nt:
                        post2(step - 2)
                    if 0 <= step - 1 < nt:
                        post1(step - 1)
                if ti < nt:
                    tk = slice(ti * 128, (ti + 1) * 128)
                    if not so:
                        py = self.pq.next()
                        self.mm(py[:, 0:256], CT[:, tk], Sbg.rearrange("p h q -> p (h q)"), True, True, [CT, Sbf], [py])
                    (pS, po_) = nxt
                    if ti + 1 < nt:
                        nxt = ps_mm(ti + 1)
                    tS = rtS.next()
                    self.tt(tS[:], Sg, elS[:, ti, hs].unsqueeze(2).to_broadcast([128, 4, 64]), ALU.mult, [S, elS], [tS])
                    self.tt(Sg, tS[:], pS[:, po_:po_ + 256].rearrange("p (h q) -> p h q", q=64), ALU.add, [tS, pS], [S])
                    if not so:
                        self.act(Sbg, Sg, AF.Copy, [S], [Sbf])
                        y = ry.next()
                        self.tt(y[:].rearrange("p (h q) -> p h q", q=64), py[:, 0:256].rearrange("p (h q) -> p h q", q=64),
                                ecS[:, ti, hs].unsqueeze(2).to_broadcast([128, 4, 64]), ALU.mult, [py, ecS], [y])
                        self.tt(y[:], y[:], ybuf[:, ti, :], ALU.add, [y, ybuf], [y])
                        self.tt(y[:], y[:], zs[:, ti, :], ALU.mult, [y, zs], [y])
                        yl[ti] = dict(y=y)
                yield
            if not so:
                self.dma(m2T_d[gi * 2:gi * 2 + 2, :, t0 * 128:t0 * 128 + NS].rearrange("j p t -> p j t"), m2o[:, :, 0:NS], [m2o], [m2T_d])

        units = [(t0, nt, gi) for (t0, nt) in self.make_segs(MT) for gi in range(M2_G)]
        self.pipeline(units, stage0, stage1, stage2)

    def hg_state(self, uT_d, w_ap, mask_ap, S, dec_out, expA=None, expI=None):
        P = self.P
        MT = 6
        MAXS = MT * 128

        def ring(n, shape, dt, nm):
            return Ring([P.sb(shape, dt, nm) for _ in range(n)])

        useg = ring(1, [128, KC, MAXS], BF16, "useg")
        wts = ring(2, [128, KC, 2, 128], BF16, "hsw")
        mk = ring(2, [128, MAXS], F32, "mk")
        rA = ring(2, [128, MAXS], F32, "sA")
        rI = ring(2, [128, MAXS], BF16, "sI")
        rB, rC = (ring(1, [128, MAXS], F32, n) for n in ("sB", "sC"))
        rG = ring(2, [128, MAXS], BF16, "sG")
        rkv = ring(2, [128, MT, 256], BF16, "skv")
        rcs = ring(2, [128, 4], F32, "scs")
        onesr = P.sb([128, MAXS], F32, "onesr")
        P.op("dve", lambda e: e.memset(onesr[:], 1.0), (), [onesr])
        seg = {}

        def stage0(un):
            (t0, nt, hd) = un
            NS = nt * 128
            if hd == 0:
                seg["u"] = self.load_uT_seg(uT_d, t0, nt, useg)
                m_ = mk.next()
                self.dma(m_[:, 0:NS], mask_ap[0:1, t0 * 128:t0 * 128 + NS].partition_broadcast(128), [], [m_])
                seg["mask"] = m_
            u = seg["u"]
            wt = wts.next()
            for (m, c0) in ((0, C_F), (1, C_I)):
                self.dma(wt[:, :, m, :], w_ap[:, c0 + hd * 128:c0 + (hd + 1) * 128].rearrange("(k p) c -> p k c", p=128), [], [wt], eng="pool")
            A = rA.next(); I = rI.next()
            seg[("st", t0, hd)] = dict(A=A, I=I, mask=seg["mask"])
            for m in range(2):
                for (n0, nn) in [(a, min(512, NS - a)) for a in range(0, NS, 512)]:
                    ps = self.pp.next()
                    for kc in range(KC):
                        self.mm(ps[:, 0:nn], wt[:, kc, m, :], u[:, kc, n0:n0 + nn], kc == 0, kc == KC - 1, [wt, u], [ps])
                    if m == 0:
                        self.act(A[:, n0:n0 + nn], ps[:, 0:nn], AF.Sigmoid, [ps], [A])
                    else:
                        self.act(I[:, n0:n0 + nn], ps[:, 0:nn], AF.Copy, [ps], [I])
                    yield
            if expA is not None:
                self.dma(expA[hd, :, t0 * 128:t0 * 128 + NS], A[:, 0:NS], [A], [expA])
                self.dma(expI[hd, :, t0 * 128:t0 * 128 + NS], I[:, 0:NS], [I], [expI])

        def stage1(un):
            (t0, nt, hd) = un
            NS = nt * 128
            d = seg.pop(("st", t0, hd))
            A, I, mask = d["A"], d["I"], d["mask"]
            B = rB.next(); C = rC.next(); G = rG.next(); kv = rkv.next(); cs = rcs.next()
            self.ts(A[:, 0:NS], A[:, 0:NS], self.oml[:, hd:hd + 1], self.lb[:, hd:hd + 1], ALU.mult, ALU.add, [A, self.oml, self.lb], [A])
            self.ts(B[:, 0:NS], A[:, 0:NS], -1.0, 1.0, ALU.mult, ALU.add, [A], [B])
            self.tt(B[:, 0:NS], B[:, 0:NS], mask[:, 0:NS], ALU.mult, [B, mask], [B])
            self.act(A[:, 0:NS], A[:, 0:NS], AF.Ln, [A], [A])
            self.tt(A[:, 0:NS], A[:, 0:NS], mask[:, 0:NS], ALU.mult, [A, mask], [A])
            yield
            P.op("dve", lambda e, C=C, A=A, NS=NS: e.tensor_tensor_scan(out=C[:, 0:NS], data0=onesr[:, 0:NS], data1=A[:, 0:NS],
                                                                    initial=0.0, op0=ALU.mult, op1=ALU.add), [A, onesr], [C])
            self.act(A[:, 0:NS], C[:, 0:NS], AF.Exp, [C], [A], scale=-1.0, bias=C[:, NS - 1:NS])
            self.act(cs[:, 0:1], C[:, NS - 1:NS], AF.Exp, [C], [cs])
            self.tt(dec_out[:, hd:hd + 1], dec_out[:, hd:hd + 1], C[:, NS - 1:NS], ALU.add, [dec_out, C], [dec_out])
            self.tt(G[:, 0:NS], B[:, 0:NS], A[:, 0:NS], ALU.mult, [B, A], [G])
            yield
            for ti in range(nt):
                tk = slice(ti * 128, (ti + 1) * 128)
                pt = self.pt.next()
                self.tr(pt[:, 0:128], G[:, tk], self.idb[:], [G, self.idb], [pt])
                self.tr(pt[:, 128:256], I[:, tk], self.idb[:], [I, self.idb], [pt])
                self.act(kv[:, ti, :], pt[:, 0:256], AF.Copy, [pt], [kv])
                if ti % 2 == 1:
                    yield
            pst = self.pq.next()
            for ti in range(nt):
                self.mm(pst[:, 0:128], kv[:, ti, 0:128], kv[:, ti, 128:256], ti == 0, ti == nt - 1, [kv], [pst])
            Sh = S[:, hd, :]
            self.stt(Sh, Sh, cs[:, 0:1], pst[:, 0:128], ALU.mult, ALU.add, [S, cs, pst], [S])
            yield

        units = [(t0, nt, hd) for (t0, nt) in self.make_segs(MT) for hd in range(HG_H)]
        self.pipeline(units, stage0, stage1)

    def m2_state(self, uT_d, w_ap, S, dec_out, expX=None):
        P = self.P
        MT = 6
        MAXS = MT * 128

        def ring(n, shape, dt, nm):
            return Ring([P.sb(shape, dt, nm) for _ in range(n)])

        useg = ring(1, [128, KC, MAXS], BF16, "useg")
        wts = ring(2, [128, KC, 384], BF16, "msw")
        rxr = ring(2, [128, 3, 3 + MAXS], F32, "xr")
        racc = ring(1, [128, 2, MAXS], F32, "xacc")
        rtmpc = ring(2, [128, MAXS], F32, "ctmp")
        rBT = ring(1, [128, MAXS], BF16, "BT")
        rps = Ring([dict((n, P.sb([128, MT, 32], F32, n)) for n in ("dtS", "aS", "cumS", "lastS", "ddS")) for _ in range(2)])
        rtot = ring(2, [128, 2, 32], F32, "mtot")
        rxs = ring(2, [128, 256], F32, "xs")
        rxdd = ring(2, [128, MT, 256], BF16, "xddA")
        rBtm = ring(2, [128, MT, 128], BF16, "BtmA")
        rtS = ring(2, [128, 4, 64], F32, "tS")
        seg = {}

        def seg_prep(t0, nt):
            u = self.load_uT_seg(uT_d, t0, nt, useg)
            seg["u"] = u
            ps_ = rps.next(); tot = rtot.next()
            seg["ps"] = ps_; seg["tot"] = tot
            dtS, aS, cumS, lastS, ddS = (ps_[n] for n in ("dtS", "aS", "cumS", "lastS", "ddS"))
            for ti in range(nt):
                ps = self.pq.next()
                for kc in range(KC):
                    self.mm(ps[:, 0:32], u[:, kc, ti * 128:(ti + 1) * 128], self.wdt[:, kc, :], kc == 0, kc == KC - 1, [u, self.wdt], [ps])
                self.tt(dtS[:, ti, :], ps[:, 0:32], self.dtb[:], ALU.add, [ps, self.dtb], [dtS])
            self.act(dtS[:, 0:nt, :], dtS[:, 0:nt, :], AF.Exp, [dtS], [dtS])
            self.act(dtS[:, 0:nt, :], dtS[:, 0:nt, :], AF.Ln, [dtS, self.epsc], [dtS], bias=self.epsc[:, 2:3])
            for ti in range(nt):
                self.ts(dtS[:, ti, :], dtS[:, ti, :], self.maskT[:, t0 + ti:t0 + ti + 1], None, ALU.mult, None, [dtS, self.maskT], [dtS])
                self.tt(aS[:, ti, :], dtS[:, ti, :], self.negA[:], ALU.mult, [dtS, self.negA], [aS])
            for ti in range(nt):
                ps = self.pq.next()
                self.mm(ps[:, 0:32], self.triu[:], aS[:, ti, :], True, True, [self.triu, aS], [ps])
                self.mm(ps[:, 32:64], self.ones_f[:], aS[:, ti, :], True, True, [self.ones_f, aS], [ps])
                self.act(cumS[:, ti, :], ps[:, 0:32], AF.Copy, [ps], [cumS])
                self.act(lastS[:, ti, :], ps[:, 32:64], AF.Copy, [ps], [lastS])
            self.tt(ddS[:, 0:nt, :], lastS[:, 0:nt, :], cumS[:, 0:nt, :], ALU.subtract, [lastS, cumS], [ddS])
            P.op("dve", lambda e, tot=tot: e.memset(tot[:, 0, :], 0.0), (), [tot])
            for ti in range(nt - 1, -1, -1):
                self.tt(ddS[:, ti, :], ddS[:, ti, :], tot[:, 0, :], ALU.add, [ddS, tot], [ddS])
                self.tt(tot[:, 0, :], tot[:, 0, :], lastS[:, ti, :], ALU.add, [tot, lastS], [tot])
            self.tt(dec_out[:], dec_out[:], tot[:, 0, :], ALU.add, [dec_out, tot], [dec_out])
            self.act(tot[:, 1, :], tot[:, 0, :], AF.Exp, [tot], [tot])
            self.act(ddS[:, 0:nt, :], ddS[:, 0:nt, :], AF.Exp, [ddS], [ddS])
            self.tt(ddS[:, 0:nt, :], ddS[:, 0:nt, :], dtS[:, 0:nt, :], ALU.mult, [ddS, dtS], [ddS])

        def stage0(un):
            (t0, nt, gi) = un
            NS = nt * 128
            if gi == 0:
                seg_prep(t0, nt)
                yield
            u = seg["u"]
            wt = wts.next()
            self.dma(wt[:, :, 0:256], w_ap[:, C_X + gi * 256:C_X + (gi + 1) * 256].rearrange("(k p) c -> p k c", p=128), [], [wt], eng="pool")
            self.dma(wt[:, :, 256:384], w_ap[:, C_B + gi * 128:C_B + (gi + 1) * 128].rearrange("(k p) c -> p k c", p=128), [], [wt], eng="pool")
            xr = rxr.next()
            seg[("st", t0, gi)] = dict(xr=xr, ps=seg["ps"], tot=seg["tot"])
            for b in range(3):
                for (n0, nn) in [(a, min(512, NS - a)) for a in range(0, NS, 512)]:
                    ps = self.pp.next()
                    for kc in range(KC):
                        self.mm(ps[:, 0:nn], wt[:, kc, b * 128:(b + 1) * 128], u[:, kc, n0:n0 + nn], kc == 0, kc == KC - 1, [wt, u], [ps])
                    self.act(xr[:, b, 3 + n0:3 + n0 + nn], ps[:, 0:nn], AF.Copy, [ps], [xr])
                    yield
            if expX is not None:
                self.dma(expX[gi, :, :, t0 * 128:t0 * 128 + NS].rearrange("b p t -> p b t"), xr[:, :, 3:3 + NS], [xr], [expX])

        def stage1(un):
            (t0, nt, gi) = un
            NS = nt * 128
            d = seg.pop(("st", t0, gi))
            xr = d["xr"]; ddS = d["ps"]["ddS"]; tot = d["tot"]
            acc = racc.next(); BT = rBT.next(); xdd = rxdd.next(); Btm = rBtm.next()
            for b in range(3):
                j = (gi * 2 + b) if b < 2 else 16 + gi
                self.act(xr[:, b, 0:3], self.carry[:, gi, b, :], AF.Copy, [self.carry], [xr])
                self.act(self.carry[:, gi, b, :], xr[:, b, NS:NS + 3], AF.Copy, [xr], [self.carry])
                tmpc = rtmpc.next()
                self.ts(tmpc[:, 0:NS], xr[:, b, 3:3 + NS], self.cw[3][:, j:j + 1], self.cbias[:, j:j + 1], ALU.mult, ALU.add,
                        [xr, self.cw[3], self.cbias], [tmpc])
                for k_ in range(3):
                    self.stt(tmpc[:, 0:NS], xr[:, b, k_:k_ + NS], self.cw[k_][:, j:j + 1], tmpc[:, 0:NS], ALU.mult, ALU.add,
                             [xr, self.cw[k_], tmpc], [tmpc])
                if b < 2:
                    self.act(acc[:, b, 0:NS], tmpc[:, 0:NS], AF.Silu, [tmpc], [acc])
                else:
                    self.act(BT[:, 0:NS], tmpc[:, 0:NS], AF.Silu, [tmpc], [BT])
                yield
            hs = slice(gi * 4, gi * 4 + 4)
            for ti in range(nt):
                tk = slice(ti * 128, (ti + 1) * 128)
                px = self.pq.next()
                self.tr(px[:, 0:128], acc[:, 0, tk], self.idf[:], [acc, self.idf], [px])
                self.tr(px[:, 128:256], acc[:, 1, tk], self.idf[:], [acc, self.idf], [px])
                pb = self.pt.next()
                self.tr(pb[:, 0:128], BT[:, tk], self.idb[:], [BT, self.idb], [pb])
                xs = rxs.next()
                self.act(xs[:], px[:, 0:256], AF.Copy, [px], [xs])
                self.act(Btm[:, ti, :], pb[:, 0:128], AF.Copy, [pb], [Btm])
                self.tt(xdd[:, ti, :].rearrange("p (h q) -> p h q", q=64), xs[:].rearrange("p (h q) -> p h q", q=64),
                        ddS[:, ti, hs].unsqueeze(2).to_broadcast([128, 4, 64]), ALU.mult, [xs, ddS], [xdd])
                yield
            pS = self.pq.next()
            for ti in range(nt):
                self.mm(pS[:, 0:256], Btm[:, ti, :], xdd[:, ti, :], ti == 0, ti == nt - 1, [Btm, xdd], [pS])
            Sg = S[:, gi * 4:(gi + 1) * 4, :]
            tS = rtS.next()
            self.tt(tS[:], Sg, tot[:, 1, hs].unsqueeze(2).to_broadcast([128, 4, 64]), ALU.mult, [S, tot], [tS])
            self.tt(Sg, tS[:], pS[:, 0:256].rearrange("p (h q) -> p h q", q=64), ALU.add, [tS, pS], [S])
            yield

        units = [(t0, nt, gi) for (t0, nt) in self.make_segs(MT) for gi in range(M2_G)]
        self.pipeline(units, stage0, stage1)

    def tok_chunks(self, n0, n1):
        return [(a, min(512, n1 - a)) for a in range(n0, n1, 512)]

    def dense_A_res(self, xT_d, KCx, W_ap, res_ap, res_r, out_d, seg_tiles):
        P = self.P
        NTs = seg_tiles
        xs_ = Ring([P.sb([128, KCx, NTs * 128], BF16, "dax") for _ in range(1)])
        wr = Ring([P.sb([128, KCx, 512], BF16, "daw") for _ in range(2)])
        rr = Ring([P.sb([128, 512], F32, "dar") for _ in range(3)])
        orr = Ring([P.sb([128, 512], F32, "dao") for _ in range(3)])
        t = 0
        while t < self.NT:
            nt = min(NTs, self.NT - t)
            x = xs_.next()
            self.dma(x[:, :, 0:nt * 128], xT_d[:, :, t * 128:(t + nt) * 128].rearrange("k p t -> p k t"), [xT_d], [x])
            for cb in range(4):
                w = wr.next()
                self.dma(w[:], W_ap[:, cb * 512:(cb + 1) * 512].rearrange("(k p) c -> p k c", p=128), [], [w], eng="pool")
                for ti in range(nt):
                    r = rr.next(); o = orr.next()
                    tok = slice((t + ti) * 128, (t + ti + 1) * 128)
                    self.dma(r[:], res_ap[tok, cb * 512:(cb + 1) * 512], [res_r], [r])
                    ps = self.pp.next()
                    for kc in range(KCx):
                        self.mm(ps[:], x[:, kc, ti * 128:(ti + 1) * 128], w[:, kc, :], kc == 0, kc == KCx - 1, [x, w], [ps])
                    self.tt(o[:], ps[:], r[:], ALU.add, [ps, r], [o])
                    self.dma(out_d[tok, cb * 512:(cb + 1) * 512], o[:], [o], [out_d])
            t += nt

    def dense_A_res_norm(self, xT_d, W_ap, res_ap, res_r, out_d, g_ap, uT_out_d):
        P = self.P
        x = self.load_xT(xT_d, "dnx")
        w = P.sb([128, KC, D], BF16, "dnw")
        for cb in range(4):
            self.dma(w[:, :, cb * 512:(cb + 1) * 512], W_ap[:, cb * 512:(cb + 1) * 512].rearrange("(k p) c -> p k c", p=128), [], [w], eng="pool")
        gbc = P.sb([128, D], F32, "dngbc")
        self.dma(gbc[:], g_ap.partition_broadcast(128), [], [gbc])
        rr = Ring([P.sb([128, 512], F32, "dnr") for _ in range(3)])
        ro = Ring([P.sb([128, D], F32, "dno") for _ in range(2)])
        ru = Ring([P.sb([128, D], BF16, "dnu") for _ in range(2)])
        NB_ = 2
        ruo = Ring([P.sb([128, KC, NB_ * 128], BF16, "dnuo") for _ in range(2)])
        cur = {}
        rs_ = Ring([P.sb([128, 4], F32, "dns") for _ in range(2)])

        def norm_tail(i, u):
            if i % NB_ == 0:
                cur["o"] = ruo.next()
            o_ = cur["o"]
            q4 = (i % NB_) * 128
            for half in range(2):
                pt = self.pt.next()
                for j in range(8):
                    kc = half * 8 + j
                    self.tr(pt[:, j * 128:(j + 1) * 128], u[:, kc * 128:(kc + 1) * 128], self.idb[:], [u, self.idb], [pt])
                self.act(o_[:, half * 8:(half + 1) * 8, q4:q4 + 128], pt[:].rearrange("p (j t) -> p j t", j=8), AF.Copy, [pt], [o_])
            if i % NB_ == NB_ - 1 or i == self.NT - 1:
                i0_ = i - (i % NB_)
                self.dma(uT_out_d[:, :, i0_ * 128:(i + 1) * 128].rearrange("k p t -> p k t"), o_[:, :, 0:q4 + 128], [o_], [uT_out_d])

        pend = None
        for i in range(self.NT):
            tok = slice(i * 128, (i + 1) * 128)
            o = ro.next(); u = ru.next(); s_ = rs_.next()
            for cb in range(4):
                r = rr.next()
                self.dma(r[:], res_ap[tok, cb * 512:(cb + 1) * 512], [res_r], [r])
                ps = self.pp.next()
                for kc in range(KC):
                    self.mm(ps[:], x[:, kc, tok], w[:, kc, cb * 512:(cb + 1) * 512], kc == 0, kc == KC - 1, [x, w], [ps])
                self.tt(o[:, cb * 512:(cb + 1) * 512], ps[:], r[:], ALU.add, [ps, r], [o])
            self.dma(out_d[tok, :], o[:], [o], [out_d])
            self.act(u[:], o[:], AF.Square, [o], [u, s_], accum_out=s_[:, 0:1])
            self.act(s_[:, 1:2], s_[:, 0:1], AF.Ln, [s_, self.epsc], [s_], scale=1.0 / D, bias=self.epsc[:, 0:1])
            self.act(s_[:, 2:3], s_[:, 1:2], AF.Exp, [s_], [s_], scale=-0.5)
            self.stt(u[:], o[:], s_[:, 2:3], gbc[:], ALU.mult, ALU.mult, [o, s_, gbc], [u])
            if pend is not None:
                norm_tail(*pend)
            pend = (i, u)
        norm_tail(*pend)

    def dense_B(self, x, W_ap, c0, ncols, evac, ntok=None):
        P = self.P
        wr = self._dbw
        ntok = self.TT if ntok is None else ntok
        for c in range(0, ncols, 512):
            wc = min(512, ncols - c)
            w = wr.next()
            self.dma(w[:, :, 0:wc], W_ap[:, c0 + c:c0 + c + wc].rearrange("(k p) c -> p k c", p=128), [], [w], eng="pool")
            for b in range(wc // 128):
                for (n0, nn) in self.tok_chunks(0, ntok):
                    ps = self.pp.next()
                    for kc in range(KC):
                        self.mm(ps[:, 0:nn], w[:, kc, b * 128:(b + 1) * 128], x[:, kc, n0:n0 + nn], kc == 0, kc == KC - 1, [w, x], [ps])
                    evac(ps, (c // 128) + b, n0, nn)

    def halves(self):
        a = ((self.NT + 1) // 2) * 128
        return [(0, a), (a, self.TT)] if a < self.TT else [(0, self.TT)]

    def load_xT(self, xT_d, name):
        x = self.P.sb([128, KC, self.TT], BF16, name)
        self.dma(x[:], xT_d[:, :, :].rearrange("k p t -> p k t"), [xT_d], [x])
        return x

    def mixer_merge(self, uT_d, hgT_d, m2T_d, w_ap, wbh_ap, wbm_ap, mT_d):
        P = self.P
        hv = self.halves()
        NSM = max(b - a for a, b in hv)
        ru = Ring([P.sb([128, KC, NSM], BF16, "mu") for _ in range(1)])
        rh = Ring([P.sb([128, KC, NSM], BF16, "mh") for _ in range(1)])
        rm = Ring([P.sb([128, KC, NSM], BF16, "mm") for _ in range(1)])
        rw = Ring([P.sb([128, KC, 4, 128], BF16, "mw") for _ in range(2)])
        rg = Ring([P.sb([128, 2, 512], F32, "mg") for _ in range(2)])
        rt = Ring([P.sb([128, 2, 512], F32, "mt") for _ in range(2)])
        ro = Ring([P.sb([128, NSM], BF16, "mo") for _ in range(2)])
        for (h0, h1) in hv:
            nh = h1 - h0
            u = ru.next(); hg = rh.next(); m2 = rm.next()
            for (dst, src) in ((u, uT_d), (hg, hgT_d), (m2, m2T_d)):
                self.dma(dst[:, :, 0:nh], src[:, :, h0:h1].rearrange("k p t -> p k t"), [src], [dst])
            for kb in range(16):
                c = kb * 128
                w = rw.next()
                for mi, (ap, cc) in enumerate(((w_ap, C_GHG + c), (wbh_ap, c), (w_ap, C_GM2 + c), (wbm_ap, c))):
                    self.dma(w[:, :, mi, :], ap[:, cc:cc + 128].rearrange("(k p) c -> p k c", p=128), [], [w], eng="pool")
                o = ro.next()
                for (n0, nn) in self.tok_chunks(0, nh):
                    g = rg.next(); t_ = rt.next()
                    for half, (xa, xb) in enumerate(((u, hg), (u, m2))):
                        pg = self.pp.next()
                        for kc in range(KC):
                            self.mm(pg[:, 0:nn], w[:, kc, 2 * half, :], xa[:, kc, n0:n0 + nn], kc == 0, kc == KC - 1, [w, xa], [pg])
                        self.act(g[:, half, 0:nn], pg[:, 0:nn], AF.Sigmoid, [pg], [g])
                        py_ = self.pq.next()
                        for kc in range(KC):
                            self.mm(py_[:, 0:nn], w[:, kc, 2 * half + 1, :], xb[:, kc, n0:n0 + nn], kc == 0, kc == KC - 1, [w, xb], [py_])
                        self.tt(t_[:, half, 0:nn], py_[:, 0:nn], g[:, half, 0:nn], ALU.mult, [py_, g], [t_])
                    self.tt(o[:, n0:n0 + nn], t_[:, 0, 0:nn], t_[:, 1, 0:nn], ALU.add, [t_], [o])
                self.dma(mT_d[kb, :, h0:h1], o[:, 0:nh], [o], [mT_d])

    def xattn(self, uT_d, memnT_d, wq_ap, wkv_ap, oT_d):
        P = self.P
        qT = P.sb([128, KC, self.TT], BF16, "xa_q")
        kT = P.sb([128, KC, 256], BF16, "xa_k")
        v = P.sb([128, 2, 2048], BF16, "xa_v")
        sc = 512 ** -0.5
        with P.scope():
            self._dbw = Ring([P.sb([128, KC, 512], BF16, "dbw") for _ in range(2)])
            mn = P.sb([128, KC, 256], BF16, "xa_mn")
            self.dma(mn[:], memnT_d[:, :, :].rearrange("k p t -> p k t"), [memnT_d], [mn])
            for c in range(0, 2048, 512):
                w = self._dbw.next()
                self.dma(w[:], wkv_ap[:, c:c + 512].rearrange("(k p) c -> p k c", p=128), [], [w], eng="pool")
                for b in range(4):
                    ps = self.pp.next()
                    for kc in range(KC):
                        self.mm(ps[:, 0:256], w[:, kc, b * 128:(b + 1) * 128], mn[:, kc, :], kc == 0, kc == KC - 1, [w, mn], [ps])
                    self.act(kT[:, c // 128 + b, :], ps[:, 0:256], AF.Copy, [ps], [kT])
            for c in range(0, 2048, 512):
                w = self._dbw.next()
                self.dma(w[:], wkv_ap[:, 2048 + c:2048 + c + 512].rearrange("(k p) c -> p k c", p=128), [], [w], eng="pool")
                for mb in range(2):
                    ps = self.pp.next()
                    for kc in range(KC):
                        self.mm(ps[:], mn[:, kc, mb * 128:(mb + 1) * 128], w[:, kc, :], kc == 0, kc == KC - 1, [w, mn], [ps])
                    self.act(v[:, mb, c:c + 512], ps[:], AF.Copy, [ps], [v])
            hv = self.halves()
            xr_ = Ring([P.sb([128, KC, max(b - a for a, b in hv)], BF16, "xa_u") for _ in range(1)])
            for (h0, h1) in hv:
                x = xr_.next()
                self.dma(x[:, :, 0:h1 - h0], uT_d[:, :, h0:h1].rearrange("k p t -> p k t"), [uT_d], [x])

                def evq(ps, cb, n0, nn, h0=h0):
                    self.act(qT[:, cb, h0 + n0:h0 + n0 + nn], ps[:, 0:nn], AF.Copy, [ps], [qT], scale=sc)
                self.dense_B(x, wq_ap, 0, 2048, evq, ntok=h1 - h0)
        rPT = Ring([P.sb([128, 2, 512], BF16, "xa_pt") for _ in range(2)])
        rden = Ring([P.sb([128, 512], F32, "xa_den") for _ in range(2)])
        ro = Ring([P.sb([128, 4, 512], BF16, "xa_o") for _ in range(2)])
        for hh in range(4):
            for (n0, nn) in self.tok_chunks(0, self.TT):
                PT = rPT.next()
                for mb in range(2):
                    ps = self.pp.next()
                    for dc in range(4):
                        self.mm(ps[:, 0:nn], kT[:, hh * 4 + dc, mb * 128:(mb + 1) * 128], qT[:, hh * 4 + dc, n0:n0 + nn], dc == 0, dc == 3, [kT, qT], [ps])
                    self.act(PT[:, mb, 0:nn], ps[:, 0:nn], AF.Exp, [ps], [PT])
                psd = self.pp.next()
                for mb in range(2):
                    self.mm(psd[:, 0:nn], self.ones_b[:], PT[:, mb, 0:nn], mb == 0, mb == 1, [self.ones_b, PT], [psd])
                den = rden.next()
                P.op("dve", lambda e, den=den, psd=psd, nn=nn: e.reciprocal(out=den[:, 0:nn], in_=psd[:, 0:nn]), [psd], [den])
                o = ro.next()
                for dc in range(4):
                    ps = self.pp.next()
                    for mb in range(2):
                        self.mm(ps[:, 0:nn], v[:, mb, hh * 512 + dc * 128:hh * 512 + (dc + 1) * 128], PT[:, mb, 0:nn], mb == 0, mb == 1, [v, PT], [ps])
                    self.tt(o[:, dc, 0:nn], ps[:, 0:nn], den[:, 0:nn], ALU.mult, [ps, den], [o])
                self.dma(oT_d[hh * 4:hh * 4 + 4, :, n0:n0 + nn].rearrange("j p t -> p j t"), o[:, :, 0:nn], [o], [oT_d])

    def ffn_up(self, uT_d, wup_ap, cw_ap, cb_ap, maskF_ap, aT_d):
        P = self.P
        TT = self.TT
        x = self.load_xT(uT_d, "ff_u")
        cw = [self.load_cols(cw_ap[k:k + 1, :].rearrange("o (j p) -> (o j) p", p=128), 44, f"fcw{k}") for k in range(3)]
        cbs = self.load_cols(cb_ap.rearrange("o (j p) -> (o j) p", p=128), 44, "fcb")
        mF = P.sb([128, 128], F32, "maskF")
        self.dma(mF[:], maskF_ap.partition_broadcast(128), [], [mF])
        rw = Ring([P.sb([128, KC, 2, 256], BF16, "fw") for _ in range(2)])
        rgr = Ring([P.sb([128, 2 + TT], F32, "fgr") for _ in range(2)])
        rup = Ring([P.sb([128, TT], F32, "fup") for _ in range(2)])
        rac = Ring([P.sb([128, TT], F32, "fac") for _ in range(2)])
        rao = Ring([P.sb([128, TT], BF16, "fao") for _ in range(2)])
        for jp in range(0, 44, 2):
            w = rw.next()
            self.dma(w[:, :, 0, :], wup_ap[:, jp * 128:jp * 128 + 256].rearrange("(k p) c -> p k c", p=128), [], [w], eng="pool")
            self.dma(w[:, :, 1, :], wup_ap[:, D_FF + jp * 128:D_FF + jp * 128 + 256].rearrange("(k p) c -> p k c", p=128), [], [w], eng="pool")
            for b in range(2):
                j = jp + b
                gr = rgr.next(); up = rup.next(); ac = rac.next(); ao = rao.next()
                P.op("dve", lambda e, gr=gr: e.memset(gr[:, 0:2], 0.0), (), [gr])
                for (n0, nn) in self.tok_chunks(0, TT):
                    ps = self.pp.next()
                    for kc in range(KC):
                        self.mm(ps[:, 0:nn], w[:, kc, 0, b * 128:(b + 1) * 128], x[:, kc, n0:n0 + nn], kc == 0, kc == KC - 1, [w, x], [ps])
                    self.act(gr[:, 2 + n0:2 + n0 + nn], ps[:, 0:nn], AF.Copy, [ps], [gr])
                    ps2 = self.pp.next()
                    for kc in range(KC):
                        self.mm(ps2[:, 0:nn], w[:, kc, 1, b * 128:(b + 1) * 128], x[:, kc, n0:n0 + nn], kc == 0, kc == KC - 1, [w, x], [ps2])
                    self.act(up[:, n0:n0 + nn], ps2[:, 0:nn], AF.Copy, [ps2], [up])
                self.tt(gr[:, 2:130], gr[:, 2:130], mF[:], ALU.mult, [gr, mF], [gr])
                self.ts(ac[:], gr[:, 2:2 + TT], cw[2][:, j:j + 1], cbs[:, j:j + 1], ALU.mult, ALU.add, [gr, cw[2], cbs], [ac])
                for k in range(2):
                    self.stt(ac[:], gr[:, k:k + TT], cw[k][:, j:j + 1], ac[:], ALU.mult, ALU.add, [gr, cw[k], ac], [ac])
                self.act(ac[:], ac[:], AF.Gelu, [ac], [ac])
                self.tt(ao[:], ac[:], up[:], ALU.mult, [ac, up], [ao])
                self.dma(aT_d[j, :, :], ao[:], [ao], [aT_d])

    def final_norm(self, h_ap, hres, g_ap, out_ap, out_r, tiles):
        P = self.P
        gbc = P.sb([128, D], F32, "fgbc")
        self.dma(gbc[:], g_ap.partition_broadcast(128), [], [gbc])
        hb = Ring([P.sb([128, D], F32, "fhb") for _ in range(2)])
        ob = Ring([P.sb([128, D], F32, "fob") for _ in range(2)])
        junk = P.sb([128, D], BF16, "fjunk")
        stt_ = Ring([P.sb([128, 4], F32, "fst") for _ in range(2)])
        outs = []
        for i in tiles:
            h = hb.next(); o = ob.next(); s = stt_.next()
            self.dma(h[:], h_ap[i * 128:(i + 1) * 128, :], [hres], [h])
            self.act(junk[:], h[:], AF.Square, [h], [junk, s], accum_out=s[:, 0:1])
            self.act(s[:, 1:2], s[:, 0:1], AF.Ln, [s, self.epsc], [s], scale=1.0 / D, bias=self.epsc[:, 0:1])
            self.act(s[:, 2:3], s[:, 1:2], AF.Exp, [s], [s], scale=-0.5)
            self.stt(o[:], h[:], s[:, 2:3], gbc[:], ALU.mult, ALU.mult, [h, s, gbc], [o])
            outs.append(self.dma(out_ap[i * 128:(i + 1) * 128, :], o[:], [o], [out_r]))
        return outs


def _inp(nc, name, shape):
    return nc.dram_tensor(name, list(shape), F32, kind="ExternalInput").ap()


def _outp(nc, name, shape):
    return nc.dram_tensor(name, list(shape), F32, kind="ExternalOutput").ap()


def build_state(l, NT):
    nc = bass.Bass("TRN2", target_bir_lowering=False)
    TT = NT * 128
    h = _inp(nc, "h", [TT, D]); mask = _inp(nc, "mask", [1, TT]); g = _inp(nc, "mix_g", [1, D])
    w_in = _inp(nc, "w_in", [D, N_IN]); lbl = _inp(nc, "lbl", [2, 2048])
    cw = _inp(nc, "m2_cw", [4, 4096]); cb = _inp(nc, "m2_cb", [1, 4096])
    dtb = _inp(nc, "m2_dtb", [1, 32]); alog = _inp(nc, "m2_alog", [1, 32])
    o_shg = _outp(nc, "o_shg", [128, 16, 128]); o_dhg = _outp(nc, "o_dhg", [128, 16])
    o_sm2 = _outp(nc, "o_sm2", [128, 32, 64]); o_dm2 = _outp(nc, "o_dm2", [128, 32])
    k = K(nc, NT); P = k.P
    uT_d = T(nc.dram_tensor("o_uT", [KC, 128, TT], BF16, kind="ExternalOutput").ap(), Res())
    xA = T(nc.dram_tensor("o_hgA", [16, 128, TT], F32, kind="ExternalOutput").ap(), Res())
    xI = T(nc.dram_tensor("o_hgI", [16, 128, TT], BF16, kind="ExternalOutput").ap(), Res())
    xX = T(nc.dram_tensor("o_m2x", [8, 3, 128, TT], F32, kind="ExternalOutput").ap(), Res())
    with P.scope():
        k.norm_to_uT(h, T(h, Res()), g, uT_d)
    k.hg_setup_lb(lbl, l)
    Shg = P.sb([128, 16, 128], F32, "Shg"); dhg = P.sb([128, 16], F32, "dhg")
    P.op("dve", lambda e: e.memset(Shg[:], 0.0), (), [Shg])
    P.op("dve", lambda e: e.memset(dhg[:], 0.0), (), [dhg])
    with P.scope():
        k.hg_state(uT_d, w_in, mask, Shg, dhg, expA=xA, expI=xI)
    outs = [k.dma(o_shg, Shg[:], [Shg], [T(o_shg, Res())]), k.dma(o_dhg, dhg[:], [dhg], [T(o_dhg, Res())])]
    k.m2_setup(w_in, cw, cb, dtb, alog, None, None, mask)
    Sm2 = P.sb([128, 32, 64], F32, "Sm2"); Sbf = P.sb([128, 32, 64], BF16, "Sm2b"); dm2 = P.sb([128, 32], F32, "dm2")
    P.op("dve", lambda e: e.memset(Sm2[:], 0.0), (), [Sm2])
    P.op("dve", lambda e: e.memset(dm2[:], 0.0), (), [dm2])
    with P.scope():
        k.m2_state(uT_d, w_in, Sm2, dm2, expX=xX)
    outs += [k.dma(o_sm2, Sm2[:], [Sm2], [T(o_sm2, Res())]), k.dma(o_dm2, dm2[:], [dm2], [T(o_dm2, Res())])]
    outs += [uT_d.r.last_write, xA.r.last_write, xI.r.last_write, xX.r.last_write]
    P.emit(outs)
    return nc


def build_main(l, NT, last):
    nc = bass.Bass("TRN2", target_bir_lowering=False)
    TT = NT * 128
    h = _inp(nc, "h", [TT, D]); mask = _inp(nc, "mask", [1, TT]); maskF = _inp(nc, "maskF", [1, 128])
    g = _inp(nc, "mix_g", [1, D]); w_in = _inp(nc, "w_in", [D, N_IN]); lbl = _inp(nc, "lbl", [2, 2048])
    hgng = _inp(nc, "hg_ng", [1, 2048])
    cw = _inp(nc, "m2_cw", [4, 4096]); cb = _inp(nc, "m2_cb", [1, 4096])
    dtb = _inp(nc, "m2_dtb", [1, 32]); alog = _inp(nc, "m2_alog", [1, 32]); dsk = _inp(nc, "m2_dsk", [1, 32])
    m2ng = _inp(nc, "m2_ng", [1, 2048])
    wbh = _inp(nc, "w_bhg", [2048, D]); wbm = _inp(nc, "w_bm2", [2048, D]); wout = _inp(nc, "w_out", [D, D])
    mem = _inp(nc, "mem", [N_MEM, D]); memg = _inp(nc, "mem_g", [1, D]); xag = _inp(nc, "xa_g", [1, D])
    wq = _inp(nc, "xa_wq", [D, D]); wkv = _inp(nc, "xa_wkv", [D, 2 * D]); wo = _inp(nc, "xa_wo", [D, D])
    ffg = _inp(nc, "ffn_g", [1, D]); wup = _inp(nc, "ffn_wup", [D, 2 * D_FF])
    fcw = _inp(nc, "ffn_cw", [3, D_FF]); fcb = _inp(nc, "ffn_cb", [1, D_FF]); wdn = _inp(nc, "ffn_wdn", [D_FF, D])
    ps_hg = _inp(nc, "ps_hg", [7, 128, 16, 128]); pd_hg = _inp(nc, "pd_hg", [7, 128, 16])
    ps_m2 = _inp(nc, "ps_m2", [7, 128, 32, 64]); pd_m2 = _inp(nc, "pd_m2", [7, 128, 32])
    if last:
        fing = _inp(nc, "fin_g", [1, D])
    h_out = _outp(nc, "h_out", [TT, D])
    k = K(nc, NT); P = k.P
    hin = T(h, Res())
    uT_d = T(nc.dram_tensor("uT_in", [KC, 128, TT], BF16, kind="ExternalInput").ap(), Res())
    xA = T(nc.dram_tensor("hgA_in", [16, 128, TT], F32, kind="ExternalInput").ap(), Res())
    xI = T(nc.dram_tensor("hgI_in", [16, 128, TT], BF16, kind="ExternalInput").ap(), Res())
    xX = T(nc.dram_tensor("m2x_in", [8, 3, 128, TT], F32, kind="ExternalInput").ap(), Res())
    hgT_d = P.dram([KC, 128, TT], BF16, "hgT")
    m2T_d = P.dram([KC, 128, TT], BF16, "m2T")
    mT_d = P.dram([KC, 128, TT], BF16, "mT")
    h1_d = P.dram([TT, D], F32, "h1")
    h2_d = P.dram([TT, D], F32, "h2")
    h3_d = T(h_out, Res()) if not last else P.dram([TT, D], F32, "h3")
    memT_d = P.dram([KC, 128, N_MEM], BF16, "memT")
    oT_d = P.dram([KC, 128, TT], BF16, "oT")
    aT_d = P.dram([D_FF // 128, 128, TT], BF16, "aT")
    with P.scope():
        k.hg_setup(lbl, l, hgng)
        Shg = P.sb([128, 16, 128], F32, "Shg")
        P.op("dve", lambda e: e.memset(Shg[:], 0.0), (), [Shg])
        Sm2 = P.sb([128, 32, 64], F32, "Sm2"); Sbf = P.sb([128, 32, 64], BF16, "Sm2b")
        P.op("dve", lambda e: e.memset(Sm2[:], 0.0), (), [Sm2])
        with P.scope():
            t1 = Ring([P.sb([128, 16, 128], F32, "pst") for _ in range(2)])
            d1 = Ring([P.sb([128, 16], F32, "pdt") for _ in range(2)])
            t2 = Ring([P.sb([128, 32, 64], F32, "pst2") for _ in range(2)])
            d2 = Ring([P.sb([128, 32], F32, "pdt2") for _ in range(2)])
            for j in range(7):
                a = t1.next(); b = d1.next(); c = t2.next(); d_ = d2.next()
                k.dma(a[:], ps_hg[j], [], [a]); k.dma(b[:], pd_hg[j], [], [b])
                k.dma(c[:], ps_m2[j], [], [c]); k.dma(d_[:], pd_m2[j], [], [d_])
                k.act(b[:], b[:], AF.Exp, [b], [b]); k.act(d_[:], d_[:], AF.Exp, [d_], [d_])
                k.tt(Shg[:], Shg[:], b[:].unsqueeze(2).to_broadcast([128, 16, 128]), ALU.mult, [Shg, b], [Shg])
                k.tt(Shg[:], Shg[:], a[:], ALU.add, [Shg, a], [Shg])
                k.tt(Sm2[:], Sm2[:], d_[:].unsqueeze(2).to_broadcast([128, 32, 64]), ALU.mult, [Sm2, d_], [Sm2])
                k.tt(Sm2[:], Sm2[:], c[:], ALU.add, [Sm2, c], [Sm2])
        k.act(Sbf[:], Sm2[:], AF.Copy, [Sm2], [Sbf])
        with P.scope():
            k.hg_mixer(uT_d, w_in, mask, Shg, hgT_d, impA=xA, impI=xI)
        k.m2_setup(w_in, cw, cb, dtb, alog, dsk, m2ng, mask)
        with P.scope():
            k.m2_mixer(uT_d, w_in, Sm2, Sbf, m2T_d, impX=xX)
    with P.scope():
        k.mixer_merge(uT_d, hgT_d, m2T_d, w_in, wbh, wbm, mT_d)
    u3_d = P.dram([KC, 128, TT], BF16, "u3T")
    with P.scope():
        k.dense_A_res_norm(mT_d, wout, h, hin, h1_d, xag, u3_d)
    with P.scope():
        k.norm_to_uT(mem, T(mem, Res()), memg, memT_d, tiles=range(N_MEM // 128))
    with P.scope():
        k.xattn(u3_d, memT_d, wq, wkv, oT_d)
    u4_d = P.dram([KC, 128, TT], BF16, "u4T")
    with P.scope():
        k.dense_A_res_norm(oT_d, wo, h1_d[:, :], h1_d, h2_d, ffg, u4_d)
    with P.scope():
        k.ffn_up(u4_d, wup, fcw, fcb, maskF, aT_d)
    with P.scope():
        k.dense_A_res(aT_d, D_FF // 128, wdn, h2_d[:, :], h2_d, h3_d, 6)
    if last:
        with P.scope():
            outs = k.final_norm(h3_d[:, :], h3_d, fing, h_out, T(h_out, Res()), range(NT))
    else:
        outs = [h3_d.r.last_write]
    stats = P.emit(outs)
    return nc, stats


_PROG_CACHE = {}


def _prog(kind, l, NT, last=False):
    key = (kind, l, NT, last)
    if key not in _PROG_CACHE:
        if kind == "state":
            _PROG_CACHE[key] = build_state(l, NT)
        else:
            _PROG_CACHE[key] = build_main(l, NT, last)[0]
    return _PROG_CACHE[key]


def _halo_slices(hfull, n_cores, TOWN):
    out = []
    for c in range(n_cores):
        s = c * TOWN
        if c == 0:
            blk = np.concatenate([np.zeros((128, hfull.shape[1]), np.float32), hfull[0:TOWN]], 0)
        else:
            blk = hfull[s - 128:s + TOWN]
        out.append(np.ascontiguousarray(blk, dtype=np.float32))
    return out


def kernel_impl(inputs, n_cores=8):
    f32 = np.float32
    x = np.asarray(inputs["x"], f32)
    SEQ = x.shape[1]
    TOWN = SEQ // n_cores
    NT = TOWN // 128 + 1
    TT = NT * 128
    mem = np.ascontiguousarray(np.asarray(inputs["mem"], f32)[0])
    g = lambda k: np.asarray(inputs[k], f32)
    row = lambda a: np.ascontiguousarray(a.reshape(1, -1))
    cores = list(range(n_cores))
    m_state, m_main, m_f = [], [], []
    for c in cores:
        ms = np.zeros((1, TT), f32); mm_ = np.zeros((1, TT), f32)
        lo = 128 if c == 0 else 3
        ms[0, lo:TOWN + 3] = 1.0
        mm_[0, lo:] = 1.0
        m_state.append(ms); m_main.append(mm_)
        m_f.append(np.zeros((1, 128), f32) if c == 0 else np.ones((1, 128), f32))
    hfull = np.ascontiguousarray(x[0])
    for l in range(DEPTH):
        hs = _halo_slices(hfull, n_cores, TOWN)
        common_state = dict(mix_g=row(g("mix_norm_g")[l]), w_in=np.ascontiguousarray(g("w_in")[l]), lbl=np.ascontiguousarray(g("hg_lb_logits")),
                            m2_cw=np.ascontiguousarray(g("m2_conv_w")[l]), m2_cb=row(g("m2_conv_b")[l]),
                            m2_dtb=row(g("m2_dt_bias")[l]), m2_alog=row(g("m2_A_log")[l]))
        nc = _prog("state", l, NT)
        res = run_bass_kernel_spmd(nc, [dict(common_state, h=hs[c], mask=m_state[c]) for c in cores], core_ids=cores)
        st = res.results
        last = (l == DEPTH - 1)
        common = dict(common_state, hg_ng=row(g("hg_norm_g")[l]), m2_dsk=row(g("m2_D")[l]), m2_ng=row(g("m2_norm_g")[l]),
                      w_bhg=np.ascontiguousarray(g("w_branch_hg")[l]), w_bm2=np.ascontiguousarray(g("w_branch_m2")[l]),
                      w_out=np.ascontiguousarray(g("w_out")[l]), mem=mem, mem_g=row(g("mem_norm_g")), xa_g=row(g("xa_norm_g")[l]),
                      xa_wq=np.ascontiguousarray(g("xa_wq")[l]), xa_wkv=np.ascontiguousarray(g("xa_wkv")[l]),
                      xa_wo=np.ascontiguousarray(g("xa_wo")[l]), ffn_g=row(g("ffn_norm_g")[l]),
                      ffn_wup=np.ascontiguousarray(g("ffn_w_up")[l]), ffn_cw=np.ascontiguousarray(g("ffn_conv_w")[l]),
                      ffn_cb=row(g("ffn_conv_b")[l]), ffn_wdn=np.ascontiguousarray(g("ffn_w_down")[l]))
        if last:
            common["fin_g"] = row(g("final_norm_g"))
        maps = []
        for c in cores:
            ps_hg = np.zeros((7, 128, 16, 128), f32); pd_hg = np.zeros((7, 128, 16), f32)
            ps_m2 = np.zeros((7, 128, 32, 64), f32); pd_m2 = np.zeros((7, 128, 32), f32)
            for j in range(7):
                src = c - 7 + j
                if src >= 0:
                    ps_hg[j] = st[src]["o_shg"]; pd_hg[j] = st[src]["o_dhg"]
                    ps_m2[j] = st[src]["o_sm2"]; pd_m2[j] = st[src]["o_dm2"]
            maps.append(dict(common, h=hs[c], mask=m_main[c], maskF=m_f[c], ps_hg=ps_hg, pd_hg=pd_hg, ps_m2=ps_m2, pd_m2=pd_m2,
                             uT_in=st[c]["o_uT"], hgA_in=st[c]["o_hgA"], hgI_in=st[c]["o_hgI"], m2x_in=st[c]["o_m2x"]))
        nc = _prog("main", l, NT, last)
        res = run_bass_kernel_spmd(nc, maps, core_ids=cores)
        hfull = np.concatenate([np.asarray(r["h_out"])[128:] for r in res.results], 0)
    return np.ascontiguousarray(hfull.reshape(1, SEQ, D).astype(np.float32))


def kernel(**inputs):
    return kernel_impl(inputs, 8)
```
